# Optimizing a Trainium2 kernel written in Bass

```python
import math
import jax, jax.numpy as jnp
from jax import lax
import numpy as np

D_MODEL = 2048
BATCH = 4
SEQ = 2048
DEPTH = 2
DEC_BATCH = 128
DEC_SEQ = 8
PAST_LEN = 16384
PAGE_SIZE = 128

N_MIXERS = 2
N_GMLP_LAYERS = (DEPTH + 1) // 2
N_SSD_LAYERS = DEPTH // 2

GMLP_WIDTH = D_MODEL
GMLP_GROUPS = 16
GMLP_GROUP_DIM = GMLP_WIDTH // GMLP_GROUPS
GMLP_CHUNK = 128

SSD_EXPAND = 2
SSD_INNER = SSD_EXPAND * D_MODEL
SSD_HEAD_DIM = 64
SSD_HEADS = SSD_INNER // SSD_HEAD_DIM
SSD_STATE = 128
SSD_GROUPS = 8
SSD_CONV = 4
SSD_CONV_DIM = SSD_INNER + 2 * SSD_GROUPS * SSD_STATE
SSD_IN_DIM = SSD_INNER + SSD_CONV_DIM + SSD_HEADS
SSD_CHUNK = 128
DT_MIN = 0.001
DT_MAX = 0.1

FFN_HIDDEN = int(math.ceil(8 * D_MODEL / 3 / 256) * 256)

NORM_EPS = 1e-6
LN_EPS = 1e-5

kernel_name = "hybrid_gmlp_ssd_adaln_step"


def _rmsnorm(x, w):
    xf = x.astype(jnp.float32)
    y = xf * lax.rsqrt(jnp.mean(xf * xf, axis=-1, keepdims=True) + NORM_EPS)
    return y.astype(x.dtype) * w


def _layernorm(x, w, b):
    xf = x.astype(jnp.float32)
    mu = jnp.mean(xf, axis=-1, keepdims=True)
    xc = xf - mu
    y = xc * lax.rsqrt(jnp.mean(xc * xc, axis=-1, keepdims=True) + LN_EPS)
    return y.astype(x.dtype) * w + b


def _group_rmsnorm(y, w):
    yg = y.reshape(y.shape[:-1] + (SSD_GROUPS, SSD_INNER // SSD_GROUPS))
    yg = yg * lax.rsqrt(jnp.mean(yg * yg, axis=-1, keepdims=True) + NORM_EPS)
    return yg.reshape(y.shape) * w.astype(jnp.float32)


def _chunk_gmlp_mixer(h, w_in, b_in, ln_w, ln_b, w_s, b_s, w_out):
    bt, seq_len, _ = h.shape
    z = jax.nn.gelu(h @ w_in + b_in, approximate=False)
    u, v = jnp.split(z, 2, axis=-1)
    v = _layernorm(v, ln_w, ln_b)
    t = min(GMLP_CHUNK, seq_len)
    n_chunks = seq_len // t
    causal = jnp.tril(jnp.ones((t, t), dtype=bool))
    ws = jnp.where(causal, w_s[:, :t, :t], 0)
    vc = v.reshape(bt, n_chunks, t, GMLP_GROUPS, GMLP_GROUP_DIM)
    s = jnp.einsum('gts,bcsgd->bctgd', ws, vc) + jnp.transpose(b_s[:, :t])[None, None, :, :, None]
    out = (u * s.reshape(bt, seq_len, GMLP_WIDTH)) @ w_out
    return out, v[:, seq_len - t:]


def _segsum(a):
    t = a.shape[-1]
    cs = jnp.cumsum(a, axis=-1)
    diff = cs[..., :, None] - cs[..., None, :]
    return jnp.where(jnp.tril(jnp.ones((t, t), dtype=bool)), diff, -jnp.inf)


def _ssd_scan(x, a, b, c, h0):
    bt, seq_len, n_heads, p = x.shape
    g, n = b.shape[2], b.shape[3]
    r = n_heads // g
    t = min(SSD_CHUNK, seq_len)
    nc = seq_len // t
    xc = x.reshape(bt, nc, t, g, r, p)
    ac = jnp.transpose(a.reshape(bt, nc, t, g, r), (0, 3, 4, 1, 2))
    bc = b.reshape(bt, nc, t, g, n)
    cc = c.reshape(bt, nc, t, g, n)
    a_cs = jnp.cumsum(ac, axis=-1)
    decay = jnp.exp(_segsum(ac))
    cb = jnp.einsum('bcign,bcjgn->bcgij', cc, bc)
    wmix = jnp.einsum('bcgij,bgrcij->bcgrij', cb, decay)
    y_diag = jnp.einsum('bcgrij,bcjgrp->bcigrp', wmix, xc)
    decay_to_end = jnp.exp(a_cs[..., -1:] - a_cs)
    states = jnp.einsum('bcjgn,bgrcj,bcjgrp->bcgrpn', bc, decay_to_end, xc)
    h0g = h0.reshape(bt, g, r, p, n)[:, None]
    states = jnp.concatenate([h0g, states], axis=1)
    chunk_tot = jnp.pad(a_cs[..., -1], ((0, 0), (0, 0), (0, 0), (1, 0)))
    decay_chunk = jnp.exp(_segsum(chunk_tot))
    states = jnp.einsum('bgrzc,bcgrpn->bzgrpn', decay_chunk, states)
    h_final = states[:, -1].reshape(bt, n_heads, p, n)
    states = states[:, :-1]
    y_off = jnp.einsum('bcign,bcgrpn,bgrci->bcigrp', cc, states, jnp.exp(a_cs))
    y = (y_diag + y_off).reshape(bt, seq_len, n_heads, p)
    return y, h_final


def _causal_dwconv(xbc, conv_state, w, bias):
    xp = jnp.concatenate([conv_state.astype(xbc.dtype), xbc], axis=1)
    out = lax.conv_general_dilated(xp, w[:, None, :].astype(xbc.dtype), (1,), 'VALID',
                                   dimension_numbers=('NWC', 'WIO', 'NWC'),
                                   feature_group_count=xbc.shape[-1])
    return out + bias, xp[:, xp.shape[1] - (SSD_CONV - 1):]


def _ssd_mixer(h, ssm0, conv0, w_in, conv_w, conv_b, dt_bias, a_log, d_skip, norm_w, w_out):
    bt, seq_len, _ = h.shape
    f32 = jnp.float32
    zxbcdt = h @ w_in
    z, xbc, dt = jnp.split(zxbcdt, [SSD_INNER, SSD_INNER + SSD_CONV_DIM], axis=-1)
    xbc, conv_new = _causal_dwconv(xbc, conv0, conv_w, conv_b)
    xbc = jax.nn.silu(xbc)
    xs, bmat, cmat = jnp.split(xbc, [SSD_INNER, SSD_INNER + SSD_GROUPS * SSD_STATE], axis=-1)
    dt = jax.nn.softplus(dt.astype(f32) + dt_bias.astype(f32))
    a = -jnp.exp(a_log.astype(f32))
    xh = xs.reshape(bt, seq_len, SSD_HEADS, SSD_HEAD_DIM).astype(f32)
    y, ssm_new = _ssd_scan(xh * dt[..., None], dt * a,
                           bmat.reshape(bt, seq_len, SSD_GROUPS, SSD_STATE).astype(f32),
                           cmat.reshape(bt, seq_len, SSD_GROUPS, SSD_STATE).astype(f32),
                           ssm0.astype(f32))
    y = y + d_skip.astype(f32)[:, None] * xh
    y = y.reshape(bt, seq_len, SSD_INNER) * jax.nn.silu(z.astype(f32))
    y = _group_rmsnorm(y, norm_w)
    out = y.astype(h.dtype) @ w_out
    return out, ssm_new.astype(ssm0.dtype), conv_new.astype(conv0.dtype)


def _swiglu(h, w_in, w_out):
    gate, up = jnp.split(h @ w_in, 2, axis=-1)
    return (jax.nn.silu(gate) * up) @ w_out


def _trunk(x, c, ssm0, conv0, mod_w, mod_b, norm_mix_w, norm_ffn_w,
           a_w_in, a_b_in, a_ln_w, a_ln_b, a_w_s, a_b_s, a_w_out,
           b_w_in, b_conv_w, b_conv_b, b_dt_bias, b_a_log, b_d, b_norm_w, b_w_out,
           f_w_in, f_w_out, final_norm_w):
    v_rows, ssm_states, conv_states = [], [], []
    for i in range(DEPTH):
        mod = (jax.nn.silu(c) @ mod_w[i] + mod_b[i])[:, None, :]
        shift_m, scale_m, gate_m, shift_f, scale_f, gate_f = jnp.split(mod, 6, axis=-1)
        h = _rmsnorm(x, norm_mix_w[i]) * (1 + scale_m) + shift_m
        j = i // N_MIXERS
        if i % N_MIXERS == 0:
            out, v = _chunk_gmlp_mixer(h, a_w_in[j], a_b_in[j], a_ln_w[j], a_ln_b[j],
                                       a_w_s[j], a_b_s[j], a_w_out[j])
            v_rows.append(v)
        else:
            out, s_new, cv_new = _ssd_mixer(h, ssm0[j], conv0[j], b_w_in[j], b_conv_w[j], b_conv_b[j],
                                            b_dt_bias[j], b_a_log[j], b_d[j], b_norm_w[j], b_w_out[j])
            ssm_states.append(s_new)
            conv_states.append(cv_new)
        x = x + gate_m * out
        h = _rmsnorm(x, norm_ffn_w[i]) * (1 + scale_f) + shift_f
        x = x + gate_f * _swiglu(h, f_w_in[i], f_w_out[i])
    y = _rmsnorm(x, final_norm_w)
    return y, jnp.stack(v_rows), jnp.stack(ssm_states), jnp.stack(conv_states)


def setup_inputs(seed: int = 0) -> dict:
    key = jax.random.key(seed)
    ks = jax.random.split(key, 32)
    f32 = jnp.float32
    nrm = lambda k, shape, s: s * jax.random.normal(k, shape, f32)
    D = D_MODEL
    u_dt = jax.random.uniform(ks[20], (N_SSD_LAYERS, SSD_HEADS), f32)
    dt0 = jnp.exp(u_dt * (math.log(DT_MAX) - math.log(DT_MIN)) + math.log(DT_MIN))
    dt_bias = dt0 + jnp.log(-jnp.expm1(-dt0))
    a_log = jnp.log(jax.random.uniform(ks[21], (N_SSD_LAYERS, SSD_HEADS), f32, 1.0, 16.0))
    return {
        "x_prompt": nrm(ks[0], (BATCH, SEQ, D), 1.0),
        "x_sample": nrm(ks[1], (DEC_BATCH, DEC_SEQ, D), 1.0),
        "c_prompt": nrm(ks[2], (BATCH, D), 1.0),
        "c_sample": nrm(ks[3], (DEC_BATCH, D), 1.0),
        "state_ssm": nrm(ks[4], (N_SSD_LAYERS, DEC_BATCH, SSD_HEADS, SSD_HEAD_DIM, SSD_STATE), 0.1),
        "state_conv": nrm(ks[5], (N_SSD_LAYERS, DEC_BATCH, SSD_CONV - 1, SSD_CONV_DIM), 1.0),
        "mod_w": nrm(ks[6], (DEPTH, D, 6 * D), D ** -0.5),
        "mod_b": nrm(ks[7], (DEPTH, 6 * D), 0.01),
        "norm_mix_w": 1.0 + nrm(ks[8], (DEPTH, D), 0.1),
        "norm_ffn_w": 1.0 + nrm(ks[9], (DEPTH, D), 0.1),
        "a_w_in": nrm(ks[10], (N_GMLP_LAYERS, D, 2 * GMLP_WIDTH), D ** -0.5),
        "a_b_in": nrm(ks[11], (N_GMLP_LAYERS, 2 * GMLP_WIDTH), 0.01),
        "a_ln_w": 1.0 + nrm(ks[12], (N_GMLP_LAYERS, GMLP_WIDTH), 0.1),
        "a_ln_b": nrm(ks[13], (N_GMLP_LAYERS, GMLP_WIDTH), 0.01),
        "a_w_s": nrm(ks[14], (N_GMLP_LAYERS, GMLP_GROUPS, GMLP_CHUNK, GMLP_CHUNK), GMLP_CHUNK ** -0.5),
        "a_b_s": 1.0 + nrm(ks[15], (N_GMLP_LAYERS, GMLP_GROUPS, GMLP_CHUNK), 0.1),
        "a_w_out": nrm(ks[16], (N_GMLP_LAYERS, GMLP_WIDTH, D), GMLP_WIDTH ** -0.5),
        "b_w_in": nrm(ks[17], (N_SSD_LAYERS, D, SSD_IN_DIM), D ** -0.5),
        "b_conv_w": nrm(ks[18], (N_SSD_LAYERS, SSD_CONV, SSD_CONV_DIM), SSD_CONV ** -0.5),
        "b_conv_b": nrm(ks[19], (N_SSD_LAYERS, SSD_CONV_DIM), 0.01),
        "b_dt_bias": dt_bias,
        "b_a_log": a_log,
        "b_d": 1.0 + nrm(ks[22], (N_SSD_LAYERS, SSD_HEADS), 0.1),
        "b_norm_w": 1.0 + nrm(ks[23], (N_SSD_LAYERS, SSD_INNER), 0.1),
        "b_w_out": nrm(ks[24], (N_SSD_LAYERS, SSD_INNER, D), SSD_INNER ** -0.5),
        "f_w_in": nrm(ks[25], (DEPTH, D, 2 * FFN_HIDDEN), D ** -0.5),
        "f_w_out": nrm(ks[26], (DEPTH, FFN_HIDDEN, D), FFN_HIDDEN ** -0.5),
        "final_norm_w": 1.0 + nrm(ks[27], (D,), 0.1),
    }


def reference(x_prompt, x_sample, c_prompt, c_sample, state_ssm, state_conv,
              mod_w, mod_b, norm_mix_w, norm_ffn_w,
              a_w_in, a_b_in, a_ln_w, a_ln_b, a_w_s, a_b_s, a_w_out,
              b_w_in, b_conv_w, b_conv_b, b_dt_bias, b_a_log, b_d, b_norm_w, b_w_out,
              f_w_in, f_w_out, final_norm_w):
    bp = x_prompt.shape[0]
    ssm0_prompt = jnp.zeros((N_SSD_LAYERS, bp) + state_ssm.shape[2:], state_ssm.dtype)
    conv0_prompt = jnp.zeros((N_SSD_LAYERS, bp) + state_conv.shape[2:], state_conv.dtype)
    y_prompt, v_prompt, ssm_prompt, conv_prompt = _trunk(
        x_prompt, c_prompt, ssm0_prompt, conv0_prompt, mod_w, mod_b, norm_mix_w, norm_ffn_w,
        a_w_in, a_b_in, a_ln_w, a_ln_b, a_w_s, a_b_s, a_w_out,
        b_w_in, b_conv_w, b_conv_b, b_dt_bias, b_a_log, b_d, b_norm_w, b_w_out,
        f_w_in, f_w_out, final_norm_w)
    y_sample, v_sample, ssm_sample, conv_sample = _trunk(
        x_sample, c_sample, state_ssm, state_conv, mod_w, mod_b, norm_mix_w, norm_ffn_w,
        a_w_in, a_b_in, a_ln_w, a_ln_b, a_w_s, a_b_s, a_w_out,
        b_w_in, b_conv_w, b_conv_b, b_dt_bias, b_a_log, b_d, b_norm_w, b_w_out,
        f_w_in, f_w_out, final_norm_w)
    return (y_prompt, y_sample, v_prompt, v_sample, ssm_prompt, ssm_sample, conv_prompt, conv_sample)
```

```python
import numpy as np
from contextlib import ExitStack
import concourse.bass as bass
import concourse.mybir as mybir
from concourse.bass_utils import run_bass_kernel_spmd

F32 = mybir.dt.float32
BF16 = mybir.dt.bfloat16
AF = mybir.ActivationFunctionType
ALU = mybir.AluOpType
AX = mybir.AxisListType

D = 2048
KC = 16
NPASS = 4
NCH = 4
SSQ = 4
NS = SSQ * 8
NPT = NCH * 128
NTOK = NPT + NS
NTILE = NCH + 1
TBS = [(0, 256), (256, NTOK)]
FFN_H = 5632
SSD_IN = 10304
NORM_EPS = 1e-6
LN_EPS = 1e-5
SAME_ENGINE_SYNC = True
STAGE = 2


class Sched:
    ENG = ('pe', 'dve', 'act', 'pool', 'sp')

    def __init__(self, nc, n_dma_sems=(24, 8, 16)):
        self.nc = nc
        self.ins = []
        self.last_w = {}
        self.readers = {}
        self.n_dma_sems = dict(sp=n_dma_sems[0], act=n_dma_sems[1], pool=n_dma_sems[2])

    def _add(self, eng, fn, reads, writes, kind):
        idx = len(self.ins)
        deps = set()
        raw = set()
        for k in reads:
            w = self.last_w.get(k)
            if w is not None:
                deps.add(w)
                raw.add(w)
        for k in writes:
            w = self.last_w.get(k)
            if w is not None:
                deps.add(w)
            for r in self.readers.get(k, {}).values():
                if isinstance(r, list):
                    deps.update(r)
                else:
                    deps.add(r)
        deps.discard(idx)
        if eng in ('dve', 'act', 'pool'):
            deps = set(d for d in deps if d in raw or self.ins[d]['eng'] != eng or self.ins[d]['kind'] == 'dma')
        self.ins.append(dict(eng=eng, fn=fn, deps=deps, kind=kind, needed=False))
        for k in writes:
            self.last_w[k] = idx
            self.readers[k] = {}
        for k in reads:
            if k not in writes:
                rd = self.readers.setdefault(k, {})
                if kind == 'dma':
                    rd.setdefault('dma', []).append(idx)
                else:
                    rd[eng] = idx
        return idx

    def op(self, eng, fn, reads=(), writes=()):
        return self._add(eng, fn, list(reads), list(writes), 'op')

    def dma(self, eng, out, in_, reads=(), writes=(), **kw):
        def fn(e, out=out, in_=in_, kw=kw):
            return e.dma_start(out=out, in_=in_, **kw)
        return self._add(eng, fn, list(reads), list(writes), 'dma')

    def barrier(self):
        last = {}
        for i, it in enumerate(self.ins):
            if it['kind'] != 'dma':
                last[it['eng']] = i
        deps = set(last.values())
        for k, w in self.last_w.items():
            if self.ins[w]['kind'] == 'dma':
                deps.add(w)
        for k, rs in self.readers.items():
            deps.update(rs.get('dma', []))
        for e in self.ENG:
            self.ins.append(dict(eng=e, fn=None, deps=set(deps), kind='nop', needed=False))
        self.last_w.clear()
        self.readers.clear()

    def emit(self):
        nc = self.nc
        ins = self.ins
        deps = set()
        last = {}
        for i, it in enumerate(ins):
            if it['kind'] == 'dma':
                deps.add(i)
            elif it['kind'] == 'op':
                last[it['eng']] = i
        deps |= set(last.values())
        ins.append(dict(eng='sp', fn=None, deps=deps, kind='nop', needed=False))
        for it in ins:
            for d in it['deps']:
                p = ins[d]
                if p['kind'] == 'dma':
                    p['needed'] = True
                elif p['eng'] == it['eng'] and (p['eng'] in ('pe', 'sp') or not SAME_ENGINE_SYNC):
                    pass
                else:
                    p['needed'] = True
        cnt = {e: 0 for e in self.ENG}
        dcnt = {e: 0 for e in self.ENG}
        for it in ins:
            e = it['eng']
            if it['kind'] == 'dma':
                n = self.n_dma_sems[e]
                j = dcnt[e]
                dcnt[e] += 1
                it['sig'] = ('d', e, j % n, 16 * (j // n + 1))
                it['prev_on_sem'] = 16 * (j // n)
            elif it['kind'] == 'op' and it['needed']:
                cnt[e] += 1
                it['sig'] = ('e', e, 0, cnt[e])
            else:
                it['sig'] = None
        self.stats = dict(cnt=cnt, dcnt=dcnt, n=len(ins))
        with ExitStack() as es:
            esem = {e: es.enter_context(nc.semaphore('s_' + e)) for e in self.ENG}
            dsem = {}
            for e in ('sp', 'act', 'pool'):
                nd = min(self.n_dma_sems[e], max(dcnt[e], 1))
                dsem[e] = [es.enter_context(nc.semaphore('d_%s%d' % (e, i))) for i in range(nd)]
            per = {e: [] for e in self.ENG}
            for i, it in enumerate(ins):
                per[it['eng']].append(i)
            block = es.enter_context(nc.Block())

            def make(e):
                def body(engobj):
                    wm = {}
                    for i in per[e]:
                        it = ins[i]
                        waits = {}
                        for d in it['deps']:
                            p = ins[d]
                            sg = p.get('sig')
                            if sg is None:
                                continue
                            if sg[0] == 'e' and p['eng'] == e and (e in ('pe', 'sp') or not SAME_ENGINE_SYNC):
                                continue
                            key = sg[:3]
                            if wm.get(key, 0) >= sg[3]:
                                continue
                            waits[key] = max(waits.get(key, 0), sg[3])
                        if it['kind'] == 'dma' and it['prev_on_sem'] > 0:
                            key = it['sig'][:3]
                            if wm.get(key, 0) < it['prev_on_sem']:
                                waits[key] = max(waits.get(key, 0), it['prev_on_sem'])
                        for key, v in waits.items():
                            sem = esem[key[1]] if key[0] == 'e' else dsem[key[1]][key[2]]
                            engobj.wait_ge(sem, v)
                            wm[key] = v
                        if it['fn'] is None:
                            continue
                        r = it['fn'](engobj)
                        sg = it['sig']
                        if sg is not None:
                            if sg[0] == 'e':
                                r.then_inc(esem[e], 1)
                            else:
                                r.then_inc(dsem[e][sg[2]], 16)
                return body

            block.tensor(make('pe'))
            block.vector(make('dve'))
            block.scalar(make('act'))
            block.gpsimd(make('pool'))
            block.sync(make('sp'))


VEC_ROWS = {}


def _vec_layout():
    off = 0
    lay = {}
    for name, n in [('mod_b0', 96), ('mod_b1', 96), ('nmw0', 16), ('nmw1', 16), ('nfw0', 16),
                    ('nfw1', 16), ('a_b_in', 32), ('fnw', 16), ('conv_w', 192), ('conv_b', 48)]:
        lay[name] = (off, n)
        off += n
    tot = ((off + 127) // 128) * 128
    return lay, tot


VEC_LAY, VEC_TOT = _vec_layout()


def build_program():
    nc = bass.Bass("TRN2", target_bir_lowering=False)
    din = lambda name, shape: nc.dram_tensor(name, list(shape), F32, kind="ExternalInput").ap()
    dout = lambda name, shape: nc.dram_tensor(name, list(shape), F32, kind="ExternalOutput").ap()
    xp = din("xp", [2048, D])
    xs = din("xs", [128, D])
    call = din("call", [17, D])
    vecs = din("vecs", [VEC_TOT, 128])
    mod_w = din("mod_w", [2, D, 6 * D])
    a_w_in = din("a_w_in", [D, 2 * D])
    a_ln_w = din("a_ln_w", [D])
    a_ln_b = din("a_ln_b", [D])
    a_b_in_r = din("a_b_in_r", [2 * D])
    a_w_s = din("a_w_s", [16, 128, 128])
    a_b_s = din("a_b_s", [16, 128])
    a_w_out = din("a_w_out", [D, D])
    f_w_in = din("f_w_in", [2, D, 2 * FFN_H])
    f_w_out = din("f_w_out", [2, FFN_H, D])
    y_p = dout("y_p", [2048, D])
    y_s = dout("y_s", [128, D])
    v_p = dout("v_p", [128, D])
    v_s = dout("v_s", [128, D])
    st_ssm = din("st_ssm", [16, 64, 64, 128])
    st_conv = din("st_conv", [16 * 3, 6144])
    b_w_in = din("b_w_in", [D, SSD_IN])
    b_w_out = din("b_w_out", [2 * D, D])
    b_dtb = din("b_dtb", [64])
    b_alog = din("b_alog", [64])
    b_dsk = din("b_dsk", [64])
    b_nw = din("b_nw", [2 * D])
    ssm_p = dout("ssm_p", [64 * 64, 128])
    ssm_s = dout("ssm_s", [16, 64, 64, 128])
    conv_p = dout("conv_p", [3, 6144])
    conv_s = dout("conv_s", [16 * 3, 6144])
    hst_d = nc.dram_tensor("hst_d", [8, 128, 512], F32).ap()

    S = Sched(nc)
    es = ExitStack()
    with es:
        def sb(name, shape, dt=F32):
            return es.enter_context(nc.sbuf_tensor(name, list(shape), dt))

        XT = sb("XT", [128, KC, NTOK])
        HT = sb("HT", [128, KC, NTOK], BF16)
        MOD = [sb("MOD%d" % l, [128, 96, 17]) for l in range(2)]
        COLS = sb("COLS", [128, VEC_TOT])
        WT = [sb("WT%d" % i, [128, KC, 256], BF16) for i in range(4)]
        WO = [sb("WO%d" % i, [128, 4, 512], BF16) for i in range(2)]
        GB = sb("GB", [128, 4, NTOK], BF16)
        IDF = sb("IDF", [128, 128])
        IDB = sb("IDB", [128, 128], BF16)
        ONESB = sb("ONESB", [128, 128], BF16)
        RSTD = sb("RSTD", [128, NTOK])
        ACOL = sb("ACOL", [128, KC, 17])
        TMPF = [sb("TMPF%d" % i, [128, 2048]) for i in range(2)]
        SQ = [sb("SQ%d" % i, [128, NTOK], BF16) for i in range(2)]
        EV = [sb("EV%d" % i, [128, 512]) for i in range(3)]
        CT = sb("CT", [128, KC, 17], BF16)
        VN = sb("VN", [128, NTILE, 2048], BF16)
        BINV = sb("BINV", [128, 2048], BF16)
        LNW = sb("LNW", [128, 2048], BF16)
        LNB = sb("LNB", [128, 2048], BF16)
        WST = sb("WST", [128, 16, 128], BF16)
        BDS = sb("BDS", [NS, 16, NS], BF16)
        BSR = sb("BSR", [1, 16, 128], BF16)
        BSS = sb("BSS", [1, 16, NS], BF16)
        CMASK = sb("CMASK", [128, 128])
        SEQM = sb("SEQM", [NS, SSQ])
        E8 = sb("E8", [8, SSQ, 8], BF16)
        STATS = sb("STATS", [128, NTILE, 4, 6])
        MV = sb("MV", [128, NTILE, 2])
        SMALL = sb("SMALL", [128, 64])
        UT = sb("UT", [128, 128])
        USB = sb("USB", [NS, NS])
        ONESF = sb("ONESF", [128, 128])
        SEQ32 = sb("SEQ32", [NS, NS])
        MASK3 = sb("MASK3", [128, SSQ, NS])
        DTB = sb("DTB", [128, 64]); AROW = sb("AROW", [128, 64]); DROW = sb("DROW", [128, 64])
        DT = sb("DT", [128, NTILE, 64]); DA = sb("DA", [128, NTILE, 64]); ACS = sb("ACS", [128, NTILE, 64])
        TOT = sb("TOT", [128, NTILE, 64]); EXPA = sb("EXPA", [128, NTILE, 64]); DTE = sb("DTE", [128, NTILE, 64])
        DEC = sb("DEC", [128, NTILE, 64])
        CTAIL = sb("CTAIL", [128, 48, 3])
        HB = sb("HB", [128, 512], BF16)
        BCT = sb("BCT", [128, 2, NTOK], BF16)
        DECS = sb("DECS", [128, 4, SSQ])
        DAB = sb("DAB", [NS, 128])
        H0T = sb("H0T", [128, 512], BF16)
        CMS = sb("CMS", [128, SSQ, NS], BF16)
        WM = sb("WM", [NS, 512], BF16)
        CBM = sb("CBM", [128, 128], BF16)
        XDT = sb("XDT", [128, 512], BF16)
        WW = sb("WW", [128, 512], BF16)
        SM12 = [sb("SM12_%d" % i, [128, 128]) for i in range(2)]
        VNF = VN[:].rearrange("p t c -> p (t c)")
        SZ = VNF[:, 0:NTILE * 512].rearrange("p (t c) -> p t c", c=512)
        XTOK = VNF[:, NTILE * 512:2 * NTILE * 512].rearrange("p (t c) -> p t c", c=512)
        BTOK = VNF[:, 2 * NTILE * 512:2 * NTILE * 512 + NTILE * 128].rearrange("p (t c) -> p t c", c=128)
        _o = 2 * NTILE * 512 + NTILE * 128
        EE = VNF[:, _o:_o + 2048].bitcast(F32).rearrange("p (r i) -> p r i", i=128)
        LT = VNF[:, _o + 2048:_o + 3072].rearrange("p (r i) -> p r i", i=128)
        MT = VNF[:, _o + 3072:_o + 4096].rearrange("p (r i) -> p r i", i=128)
        assert _o + 4096 <= NTILE * 2048
        RAW = TMPF[0][:, 0:3 + NPT + SSQ * 11]
        XC = TMPF[0][:, 560:560 + NTOK]
        HSTG = TMPF[0][:, 1104:1616]
        NWR = TMPF[1][:, 0:512]
        H0 = TMPF[1][:, 512:1024].rearrange("p (q n) -> p q n", n=128)
        SCV = TMPF[1][:, 1024:1024 + 48 * SSQ * 3].rearrange("p (c k) -> p c k", k=SSQ * 3)

        PS = [es.enter_context(nc.psum_tensor("PS%d" % i, [128, 512], F32)) for i in range(8)]
        ps_ctr = [0]

        def next_ps(excl=()):
            while True:
                i = ps_ctr[0] % 8
                ps_ctr[0] += 1
                if i not in excl:
                    return i

        ctr = {'wt': 0, 'wo': 0, 'ev': 0, 'evb': 0, 'tmpf': 0, 'sq': 0, 'sm12': 0}

        def rot(name, n):
            i = ctr[name] % n
            ctr[name] += 1
            return i

        def affine_mask(tile_ap, key, pattern, base, cm):
            S.op('pool', lambda e: e.memset(tile_ap, 1.0), writes=[key])
            S.op('pool', lambda e: e.affine_select(out=tile_ap, in_=tile_ap, pattern=pattern,
                                                   compare_op=ALU.is_ge, fill=0.0, base=base,
                                                   channel_multiplier=cm),
                 reads=[key], writes=[key])

        S.op('pool', lambda e: e.memset(IDF[:], 1.0), writes=['IDF'])
        S.op('pool', lambda e: e.affine_select(out=IDF[:], in_=IDF[:], pattern=[[-1, 128]], compare_op=ALU.is_ge,
                                               fill=0.0, base=0, channel_multiplier=1), reads=['IDF'], writes=['IDF'])
        S.op('pool', lambda e: e.affine_select(out=IDF[:], in_=IDF[:], pattern=[[1, 128]], compare_op=ALU.is_ge,
                                               fill=0.0, base=0, channel_multiplier=-1), reads=['IDF'], writes=['IDF'])
        S.op('dve', lambda e: e.tensor_copy(out=IDB[:], in_=IDF[:]), reads=['IDF'], writes=['IDB'])
        S.op('pool', lambda e: e.memset(ONESB[:], 1.0), writes=['ONESB'])
        affine_mask(CMASK[:], 'CMASK', [[1, 128]], 0, -1)
        S.op('pool', lambda e: e.memset(SEQM[:], 1.0), writes=['SEQM'])
        S.op('pool', lambda e: e.affine_select(out=SEQM[:], in_=SEQM[:], pattern=[[-8, SSQ]], compare_op=ALU.is_ge,
                                               fill=0.0, base=0, channel_multiplier=1), reads=['SEQM'], writes=['SEQM'])
        S.op('pool', lambda e: e.affine_select(out=SEQM[:], in_=SEQM[:], pattern=[[8, SSQ]], compare_op=ALU.is_ge,
                                               fill=0.0, base=7, channel_multiplier=-1), reads=['SEQM'], writes=['SEQM'])
        S.op('dve', lambda e: e.tensor_copy(out=E8[:], in_=IDF[0:8, 0:8].unsqueeze(1).to_broadcast([8, SSQ, 8])),
             reads=['IDF'], writes=['E8'])

        for i in range(VEC_TOT // 128):
            t = TMPF[rot('tmpf', 2)]
            tk = 'TMPF%d' % ((ctr['tmpf'] - 1) % 2)
            S.dma('sp', t[:, 0:128], vecs[i * 128:(i + 1) * 128, :], writes=[tk])
            b = next_ps()
            S.op('pe', lambda e, b=b, t=t: e.transpose(out=PS[b][:, 0:128], in_=t[:, 0:128], identity=IDF[:]),
                 reads=[tk, 'IDF'], writes=[('ps', b)])
            S.op('dve', lambda e, b=b, i=i: e.tensor_copy(out=COLS[:, i * 128:(i + 1) * 128], in_=PS[b][:, 0:128]),
                 reads=[('ps', b)], writes=['COLS'])

        def col(name, j=0, n=1):
            r0, nr = VEC_LAY[name]
            return COLS[:, r0 + j:r0 + j + n]

        S.dma('pool', BINV[:], a_b_in_r[2048:4096].partition_broadcast(128), writes=['BINV'])
        S.dma('pool', LNW[:], a_ln_w.partition_broadcast(128), writes=['LNW'])
        S.dma('pool', LNB[:], a_ln_b.partition_broadcast(128), writes=['LNB'])
        S.dma('pool', BSR[:], a_b_s.rearrange("(o g) t -> o g t", o=1), writes=['BSR'])
        S.op('dve', lambda e: e.tensor_copy(out=BSS[:].rearrange("o g (b t) -> o g b t", t=8),
                                            in_=BSR[:, :, 0:8].unsqueeze(2).to_broadcast([1, 16, SSQ, 8])),
             reads=['BSR'], writes=['BSS'])

        t = TMPF[rot('tmpf', 2)]
        tk = 'TMPF%d' % ((ctr['tmpf'] - 1) % 2)
        S.dma('sp', t[:].rearrange("p (g s) -> p g s", g=16), a_w_s.rearrange("g t s -> t g s"), writes=[tk])
        for g in range(16):
            b = next_ps()
            S.op('pe', lambda e, b=b, g=g, t=t: e.transpose(out=PS[b][:, 0:128], in_=t[:, g * 128:(g + 1) * 128], identity=IDF[:]),
                 reads=[tk, 'IDF'], writes=[('ps', b)])
            S.op('dve', lambda e, b=b, g=g: e.tensor_tensor(out=WST[:, g, :], in0=PS[b][:, 0:128], in1=CMASK[:], op=ALU.mult),
                 reads=[('ps', b), 'CMASK'], writes=['WST'])
        b = next_ps()
        S.op('pe', lambda e, b=b: e.matmul(PS[b][0:NS, 0:128], lhsT=E8[:].rearrange("s b t -> s (b t)"),
                                           rhs=WST[0:8, :, 0:8], start=True, stop=True),
             reads=['E8', 'WST'], writes=[('ps', b)])
        S.op('dve', lambda e, b=b: e.tensor_tensor(
            out=BDS[:].rearrange("p g (b t) -> p g b t", t=8),
            in0=PS[b][0:NS, 0:128].rearrange("p (g t) -> p g t", t=8).unsqueeze(2).to_broadcast([NS, 16, SSQ, 8]),
            in1=SEQM[:].unsqueeze(1).unsqueeze(3).to_broadcast([NS, 16, SSQ, 8]), op=ALU.mult),
            reads=[('ps', b), 'SEQM'], writes=['BDS'])

        class WTile:
            def __init__(self, halves):
                self.halves = halves

            def c(self, k, mi):
                tl, key = self.halves[mi // 2]
                o = (mi % 2) * 128
                return tl[:, k, o:o + 128]

            def key(self, mi):
                return self.halves[mi // 2][1]

        def load_w(wap, r0, nk, c0, ncols=512, pool='wt'):
            if pool == 'wo':
                i = rot('wo', 2)
                tl, key = WO[i], 'WO%d' % i
                src = wap[r0:r0 + nk * 128, c0:c0 + ncols].rearrange("(k p) c -> p k c", p=128)
                S.dma('pool', tl[:, 0:nk, 0:ncols], src, writes=[key])
                return tl, key
            halves = []
            for h0 in range(0, ncols, 256):
                w = min(256, ncols - h0)
                i = rot('wt', 4)
                tl, key = WT[i], 'WT%d' % i
                src = wap[r0:r0 + nk * 128, c0 + h0:c0 + h0 + w].rearrange("(k p) c -> p k c", p=128)
                S.dma('pool', tl[:, 0:nk, 0:w], src, writes=[key])
                halves.append((tl, key))
            return WTile(halves), None

        t = TMPF[rot('tmpf', 2)]
        tk = 'TMPF%d' % ((ctr['tmpf'] - 1) % 2)
        S.dma('sp', t[0:17, :], call, writes=[tk])
        S.op('act', lambda e, t=t: e.activation(out=t[0:17, :], in_=t[0:17, :], func=AF.Silu), reads=[tk], writes=[tk])
        for k4 in range(4):
            b = next_ps()
            for kk in range(4):
                k = k4 * 4 + kk
                S.op('pe', lambda e, b=b, k=k, kk=kk, t=t: e.transpose(out=PS[b][:, kk * 17:(kk + 1) * 17],
                                                                      in_=t[0:17, k * 128:(k + 1) * 128], identity=IDF[0:17, 0:17]),
                     reads=[tk, 'IDF'], writes=[('ps', b)])
            S.op('dve', lambda e, b=b, k4=k4: e.tensor_copy(out=CT[:, k4 * 4:(k4 + 1) * 4, :].rearrange("p k c -> p (k c)"),
                                                           in_=PS[b][:, 0:68]),
                 reads=[('ps', b)], writes=['CT'])
        for l in range(2):
            for nb in range(24):
                tl, key = load_w(mod_w[l], 0, KC, nb * 512)
                b = next_ps()
                for mi in range(4):
                    for k in range(KC):
                        S.op('pe', lambda e, b=b, mi=mi, k=k, tl=tl: e.matmul(
                            PS[b][:, mi * 17:(mi + 1) * 17], lhsT=tl.c(k, mi), rhs=CT[:, k, :],
                            start=(k == 0), stop=(k == KC - 1)),
                            reads=[tl.key(mi), 'CT'], writes=[('ps', b)])
                for mi in range(4):
                    m = nb * 4 + mi
                    S.op('act', lambda e, b=b, mi=mi, m=m, l=l: e.activation(
                        out=MOD[l][:, m, :], in_=PS[b][:, mi * 17:(mi + 1) * 17], func=AF.Identity,
                        bias=col('mod_b%d' % l, m), scale=1.0),
                        reads=[('ps', b), 'COLS'], writes=['MOD%d' % l])

        def load_x(hf):
            for tt in range(NTILE):
                rows = 128 if tt < NCH else NS
                t = TMPF[rot('tmpf', 2)]
                tk = 'TMPF%d' % ((ctr['tmpf'] - 1) % 2)
                if tt < NCH:
                    src = xp[hf * NPT + tt * 128: hf * NPT + (tt + 1) * 128, :]
                else:
                    src = xs[hf * NS:(hf + 1) * NS, :]
                S.dma('sp', t[0:rows, :], src, writes=[tk])
                for k4 in range(4):
                    b = next_ps()
                    for kk in range(4):
                        k = k4 * 4 + kk
                        S.op('pe', lambda e, b=b, k=k, kk=kk, t=t, rows=rows: e.transpose(
                            out=PS[b][:, kk * 128:kk * 128 + rows], in_=t[0:rows, k * 128:(k + 1) * 128],
                            identity=IDF[0:rows, 0:rows]),
                            reads=[tk, 'IDF'], writes=[('ps', b)])
                    S.op('dve', lambda e, b=b, k4=k4, tt=tt, rows=rows: e.tensor_copy(
                        out=XT[:, k4 * 4:(k4 + 1) * 4, tt * 128:tt * 128 + rows],
                        in_=PS[b][:].rearrange("p (k t) -> p k t", t=128)[:, :, 0:rows]),
                        reads=[('ps', b)], writes=[('XT', k4 * 4 + kk_, tb_of(tt * 128)) for kk_ in range(4)])

        def tb_of(tok):
            return 0 if tok < TBS[0][1] else 1

        def xk(m):
            return [('XT', m, 0), ('XT', m, 1)]

        def compute_rstd(eps):
            bs = [next_ps() for _ in TBS]
            for k in range(KC):
                i = rot('sq', 2)
                S.op('act', lambda e, i=i, k=k: e.activation(out=SQ[i][:], in_=XT[:, k, :], func=AF.Square),
                     reads=xk(k), writes=['SQ%d' % i])
                for bi, (t0, t1) in enumerate(TBS):
                    S.op('pe', lambda e, b=bs[bi], i=i, t0=t0, t1=t1, k=k: e.matmul(
                        PS[b][:, 0:t1 - t0], lhsT=ONESB[:], rhs=SQ[i][:, t0:t1], start=(k == 0), stop=(k == KC - 1)),
                        reads=['SQ%d' % i, 'ONESB'], writes=[('ps', bs[bi])])
            for bi, (t0, t1) in enumerate(TBS):
                b = bs[bi]
                S.op('dve', lambda e, b=b, t0=t0, t1=t1: e.tensor_scalar(
                    out=RSTD[:, t0:t1], in0=PS[b][:, 0:t1 - t0], scalar1=1.0 / D, scalar2=eps, op0=ALU.mult, op1=ALU.add),
                    reads=[('ps', b)], writes=['RSTD'])
            S.op('act', lambda e: e.activation(out=RSTD[:], in_=RSTD[:], func=AF.Ln), reads=['RSTD'], writes=['RSTD'])
            S.op('act', lambda e: e.activation(out=RSTD[:], in_=RSTD[:], func=AF.Exp, scale=-0.5), reads=['RSTD'], writes=['RSTD'])

        def norm_mod(hf, l, which, nw_name):
            compute_rstd(NORM_EPS)
            sh0, sc0 = which * 48, which * 48 + 16
            for k in range(KC):
                S.op('dve', lambda e, k=k: e.tensor_scalar(
                    out=ACOL[:, k, :], in0=MOD[l][:, sc0 + k, :], scalar1=1.0, scalar2=col(nw_name, k),
                    op0=ALU.add, op1=ALU.mult),
                    reads=['MOD%d' % l, 'COLS'], writes=['ACOL'])
            for k in range(KC):
                i = rot('tmpf', 2)
                t, tk = TMPF[i], 'TMPF%d' % i
                S.op('dve', lambda e, t=t, k=k: e.tensor_tensor(out=t[:, 0:NTOK], in0=XT[:, k, :], in1=RSTD[:], op=ALU.mult),
                     reads=xk(k) + ['RSTD'], writes=[tk])
                S.op('act', lambda e, t=t, k=k: e.activation(
                    out=HT[:, k, 0:NPT], in_=t[:, 0:NPT], func=AF.Identity,
                    bias=MOD[l][:, sh0 + k, 0:1], scale=ACOL[:, k, 0:1]),
                    reads=[tk, 'ACOL', 'MOD%d' % l], writes=[('HT', k)])
                c0 = 1 + hf * SSQ
                S.op('dve', lambda e, t=t, k=k, c0=c0: e.tensor_tensor(
                    out=t[:, NPT:NTOK].rearrange("p (b t) -> p b t", t=8),
                    in0=t[:, NPT:NTOK].rearrange("p (b t) -> p b t", t=8),
                    in1=ACOL[:, k, c0:c0 + SSQ].unsqueeze(2).to_broadcast([128, SSQ, 8]), op=ALU.mult),
                    reads=[tk, 'ACOL'], writes=[tk])
                S.op('dve', lambda e, t=t, k=k, c0=c0: e.tensor_tensor(
                    out=HT[:, k, NPT:NTOK].rearrange("p (b t) -> p b t", t=8),
                    in0=t[:, NPT:NTOK].rearrange("p (b t) -> p b t", t=8),
                    in1=MOD[l][:, sh0 + k, c0:c0 + SSQ].unsqueeze(2).to_broadcast([128, SSQ, 8]), op=ALU.add),
                    reads=[tk, 'MOD%d' % l], writes=[('HT', k)])

        def ht_keys():
            return [('HT', k) for k in range(KC)]

        def resid_evac(hf, l, gate0, b, m, t0, t1):
            n = t1 - t0
            npr = min(t1, NPT) - t0
            S.op('dve', lambda e: e.scalar_tensor_tensor(
                out=XT[:, m, t0:t0 + npr], in0=PS[b][:, 0:npr], scalar=MOD[l][:, gate0 + m, 0:1],
                in1=XT[:, m, t0:t0 + npr], op0=ALU.mult, op1=ALU.add),
                reads=[('ps', b), 'MOD%d' % l, ('XT', m, tb_of(t0))], writes=[('XT', m, tb_of(t0))])
            if t1 > NPT:
                c0 = 1 + hf * SSQ
                i = rot('ev', 3)
                S.op('dve', lambda e: e.tensor_tensor(
                    out=EV[i][:, 0:NS].rearrange("p (b t) -> p b t", t=8),
                    in0=PS[b][:, npr:n].rearrange("p (b t) -> p b t", t=8),
                    in1=MOD[l][:, gate0 + m, c0:c0 + SSQ].unsqueeze(2).to_broadcast([128, SSQ, 8]), op=ALU.mult),
                    reads=[('ps', b), 'MOD%d' % l], writes=['EV%d' % i])
                S.op('dve', lambda e: e.tensor_tensor(out=XT[:, m, NPT:NTOK], in0=XT[:, m, NPT:NTOK], in1=EV[i][:, 0:NS], op=ALU.add),
                     reads=['EV%d' % i, ('XT', m, 1)], writes=[('XT', m, 1)])

        def out_proj_partial(hf, l, gate0, wap, r0):
            for cb in range(4):
                tl, key = load_w(wap, r0, 4, cb * 512, pool='wo')
                for mi in range(4):
                    m = cb * 4 + mi
                    for (t0, t1) in TBS:
                        b = next_ps()
                        for k in range(4):
                            S.op('pe', lambda e, b=b, k=k, mi=mi, tl=tl, t0=t0, t1=t1: e.matmul(
                                PS[b][:, 0:t1 - t0], lhsT=tl[:, k, mi * 128:(mi + 1) * 128], rhs=GB[:, k, t0:t1],
                                start=(k == 0), stop=(k == 3)),
                                reads=[key, 'GB'], writes=[('ps', b)])
                        resid_evac(hf, l, gate0, b, m, t0, t1)

        def gmlp(hf):
            hk = ht_keys()
            for nb in range(4):
                tl, key = load_w(a_w_in, 0, KC, 2048 + nb * 512)
                for tt in range(NTILE):
                    rows = 128 if tt < NCH else NS
                    b = next_ps()
                    for hh, (th, kh) in enumerate(tl.halves):
                        for k in range(KC):
                            S.op('pe', lambda e, b=b, k=k, th=th, hh=hh, tt=tt, rows=rows: e.matmul(
                                PS[b][0:rows, hh * 256:(hh + 1) * 256], lhsT=HT[:, k, tt * 128:tt * 128 + rows], rhs=th[:, k, :],
                                start=(k == 0), stop=(k == KC - 1)),
                                reads=[kh, ('HT', k)], writes=[('ps', b)])
                    i = rot('ev', 3)
                    S.op('dve', lambda e, b=b, i=i, nb=nb, rows=rows: e.tensor_tensor(
                        out=EV[i][0:rows, :], in0=PS[b][0:rows, :], in1=BINV[0:rows, nb * 512:(nb + 1) * 512], op=ALU.add),
                        reads=[('ps', b), 'BINV'], writes=['EV%d' % i])
                    S.op('act', lambda e, i=i, rows=rows: e.activation(out=EV[i][0:rows, :], in_=EV[i][0:rows, :], func=AF.Gelu),
                         reads=['EV%d' % i], writes=['EV%d' % i])
                    S.op('dve', lambda e, i=i, tt=tt, nb=nb, rows=rows: e.bn_stats(out=STATS[0:rows, tt, nb, :], in_=EV[i][0:rows, :]),
                         reads=['EV%d' % i], writes=[('STATS', tt)])
                    S.op('act', lambda e, i=i, tt=tt, nb=nb, rows=rows: e.activation(
                        out=VN[0:rows, tt, nb * 512:(nb + 1) * 512], in_=EV[i][0:rows, :], func=AF.Copy),
                        reads=['EV%d' % i], writes=[('VN', tt)])
            for tt in range(NTILE):
                rows = 128 if tt < NCH else NS
                S.op('dve', lambda e, tt=tt, rows=rows: e.bn_aggr(out=MV[0:rows, tt, :], in_=STATS[0:rows, tt, :, :].rearrange("p a b -> p (a b)")),
                     reads=[('STATS', tt)], writes=[('MV', tt)])
                S.op('dve', lambda e, tt=tt, rows=rows: e.tensor_scalar(out=SMALL[0:rows, tt:tt + 1], in0=MV[0:rows, tt, 1:2], scalar1=LN_EPS, scalar2=None, op0=ALU.add),
                     reads=[('MV', tt)], writes=[('SM', tt)])
                S.op('act', lambda e, tt=tt, rows=rows: e.activation(out=SMALL[0:rows, tt:tt + 1], in_=SMALL[0:rows, tt:tt + 1], func=AF.Sqrt),
                     reads=[('SM', tt)], writes=[('SM', tt)])
                S.op('dve', lambda e, tt=tt, rows=rows: e.reciprocal(out=SMALL[0:rows, tt:tt + 1], in_=SMALL[0:rows, tt:tt + 1]),
                     reads=[('SM', tt)], writes=[('SM', tt)])
                i = rot('tmpf', 2)
                t, tk = TMPF[i], 'TMPF%d' % i
                S.op('dve', lambda e, tt=tt, rows=rows, t=t: e.tensor_scalar(
                    out=t[0:rows, :], in0=VN[0:rows, tt, :], scalar1=MV[0:rows, tt, 0:1], scalar2=SMALL[0:rows, tt:tt + 1],
                    op0=ALU.subtract, op1=ALU.mult),
                    reads=[('VN', tt), ('MV', tt), ('SM', tt)], writes=[tk])
                S.op('dve', lambda e, rows=rows, t=t: e.tensor_tensor(out=t[0:rows, :], in0=t[0:rows, :], in1=LNW[0:rows, :], op=ALU.mult),
                     reads=[tk, 'LNW'], writes=[tk])
                S.op('dve', lambda e, rows=rows, t=t: e.tensor_tensor(out=t[0:rows, :], in0=t[0:rows, :], in1=LNB[0:rows, :], op=ALU.add),
                     reads=[tk, 'LNB'], writes=[tk])
                S.op('act', lambda e, tt=tt, rows=rows, t=t: e.activation(out=VN[0:rows, tt, :], in_=t[0:rows, :], func=AF.Copy),
                     reads=[tk], writes=[('VN', tt)])
                if tt == NCH:
                    S.dma('sp', v_s[hf * NS:(hf + 1) * NS, :], t[0:NS, :], reads=[tk])
                elif tt == NCH - 1 and hf == NPASS - 1:
                    S.dma('sp', v_p, t[:, :], reads=[tk])
            for nb in range(4):
                tl, key = load_w(a_w_in, 0, KC, nb * 512)
                for mi in range(4):
                    m = nb * 4 + mi
                    for (t0, t1) in TBS:
                        n = t1 - t0
                        bu = next_ps()
                        for k in range(KC):
                            S.op('pe', lambda e, b=bu, k=k, mi=mi, tl=tl, t0=t0, t1=t1: e.matmul(
                                PS[b][:, 0:t1 - t0], lhsT=tl.c(k, mi), rhs=HT[:, k, t0:t1],
                                start=(k == 0), stop=(k == KC - 1)),
                                reads=[tl.key(mi), ('HT', k)], writes=[('ps', bu)])
                        i = rot('ev', 3)
                        S.op('act', lambda e, b=bu, i=i, n=n, m=m: e.activation(
                            out=EV[i][:, 0:n], in_=PS[b][:, 0:n], func=AF.Gelu, bias=col('a_b_in', m), scale=1.0),
                            reads=[('ps', bu), 'COLS'], writes=['EV%d' % i])
                        bs_ = next_ps()
                        for tt in range(t0 // 128, (t1 + 127) // 128):
                            rows = 128 if tt < NCH else NS
                            c0 = tt * 128 - t0
                            rhs = WST[:, m, :] if tt < NCH else BDS[:, m, :]
                            brow = BSR[:, m, :] if tt < NCH else BSS[:, m, :]
                            S.op('pe', lambda e, b=bs_, tt=tt, rows=rows, c0=c0, rhs=rhs, m=m: e.matmul(
                                PS[b][:, c0:c0 + rows], lhsT=VN[0:rows, tt, m * 128:(m + 1) * 128], rhs=rhs,
                                start=True, stop=False),
                                reads=[('VN', tt), 'WST', 'BDS'], writes=[('ps', bs_)])
                            S.op('pe', lambda e, b=bs_, rows=rows, c0=c0, brow=brow: e.matmul(
                                PS[b][:, c0:c0 + rows], lhsT=ONESB[0:1, :], rhs=brow, start=False, stop=True),
                                reads=['ONESB', 'BSR', 'BSS'], writes=[('ps', bs_)])
                        S.op('dve', lambda e, b=bs_, i=i, n=n, mi=mi, t0=t0, t1=t1: e.tensor_tensor(
                            out=GB[:, mi, t0:t1], in0=PS[b][:, 0:n], in1=EV[i][:, 0:n], op=ALU.mult),
                            reads=[('ps', bs_), 'EV%d' % i], writes=['GB'])
                out_proj_partial(hf, 0, 32, a_w_out, nb * 512)


        affine_mask(UT[:], 'UT', [[1, 128]], 0, -1)
        S.op('pool', lambda e: e.memset(ONESF[:], 1.0), writes=['ONESF'])
        S.op('dve', lambda e: e.tensor_copy(out=SEQ32[:].rearrange("p (b t) -> p b t", t=8),
                                            in_=SEQM[:].unsqueeze(2).to_broadcast([NS, SSQ, 8])),
             reads=['SEQM'], writes=['SEQ32'])
        S.op('dve', lambda e: e.tensor_tensor(out=USB[:], in0=SEQ32[:], in1=UT[0:NS, 0:NS], op=ALU.mult),
             reads=['SEQ32', 'UT'], writes=['USB'])
        S.op('pool', lambda e: e.memset(MASK3[:], 1.0), writes=['MASK3'])
        S.op('pool', lambda e: e.affine_select(out=MASK3[:], in_=MASK3[:], pattern=[[-8, SSQ], [1, NS]], compare_op=ALU.is_ge,
                                               fill=0.0, base=0, channel_multiplier=0), reads=['MASK3'], writes=['MASK3'])
        S.op('pool', lambda e: e.affine_select(out=MASK3[:], in_=MASK3[:], pattern=[[8, SSQ], [-1, NS]], compare_op=ALU.is_ge,
                                               fill=0.0, base=7, channel_multiplier=0), reads=['MASK3'], writes=['MASK3'])
        S.dma('sp', DTB[:], b_dtb.partition_broadcast(128), writes=['DTB'])
        S.dma('sp', AROW[:], b_alog.partition_broadcast(128), writes=['AROW'])
        S.dma('sp', DROW[:], b_dsk.partition_broadcast(128), writes=['DROW'])
        S.op('act', lambda e: e.activation(out=AROW[:], in_=AROW[:], func=AF.Exp), reads=['AROW'], writes=['AROW'])
        S.op('dve', lambda e: e.tensor_scalar(out=AROW[:], in0=AROW[:], scalar1=-1.0, scalar2=None, op0=ALU.mult),
             reads=['AROW'], writes=['AROW'])
        S.op('pool', lambda e: e.memset(CTAIL[:], 0.0), writes=['CTAIL'])

        LIVE = [set()]

        def tr_store(src_ap, nrow, dst_ap, rkeys):
            b = next_ps(LIVE[0])
            S.op('pe', lambda e: e.transpose(out=PS[b][0:nrow, 0:128], in_=src_ap, identity=IDF[:]),
                 reads=list(rkeys) + ['IDF'], writes=[('ps', b)])
            j = rot('sm12', 2)
            S.op('dve', lambda e: e.tensor_copy(out=SM12[j][0:nrow, :], in_=PS[b][0:nrow, 0:128]),
                 reads=[('ps', b)], writes=['SM12_%d' % j])
            S.dma('sp', dst_ap, SM12[j][0:nrow, :], reads=['SM12_%d' % j])

        def ssd(hf):
            last = (hf == NPASS - 1)
            rows_of = lambda tt: 128 if tt < NCH else NS
            for cb in range(3):
                t, tk = TMPF[0], 'TMPF0'
                S.dma('sp', t[0:SSQ * 3, :], st_conv[hf * SSQ * 3:(hf + 1) * SSQ * 3, cb * 2048:(cb + 1) * 2048], writes=[tk])
                for c4 in range(4):
                    b = next_ps()
                    for cc in range(4):
                        ch = c4 * 4 + cc
                        S.op('pe', lambda e, b=b, cc=cc, ch=ch, t=t: e.transpose(
                            out=PS[b][:, cc * 12:(cc + 1) * 12], in_=t[0:SSQ * 3, ch * 128:(ch + 1) * 128], identity=IDF[0:SSQ * 3, 0:SSQ * 3]),
                            reads=[tk, 'IDF'], writes=[('ps', b)])
                    S.op('dve', lambda e, b=b, cb=cb, c4=c4: e.tensor_copy(
                        out=SCV[:, cb * 16 + c4 * 4: cb * 16 + c4 * 4 + 4, :].rearrange("p c k -> p (c k)"), in_=PS[b][:, 0:48]),
                        reads=[('ps', b)], writes=['SCV'])
            tl, key = load_w(b_w_in, 0, KC, 10240, ncols=64)
            for tt in range(NTILE):
                rows = rows_of(tt)
                b = next_ps()
                for k in range(KC):
                    S.op('pe', lambda e, b=b, k=k, tt=tt, rows=rows, tl=tl: e.matmul(
                        PS[b][0:rows, 0:64], lhsT=HT[:, k, tt * 128:tt * 128 + rows], rhs=tl.halves[0][0][:, k, 0:64],
                        start=(k == 0), stop=(k == KC - 1)), reads=[tl.halves[0][1], ('HT', k)], writes=[('ps', b)])
                S.op('dve', lambda e, b=b, tt=tt, rows=rows: e.tensor_tensor(out=DT[0:rows, tt, :], in0=PS[b][0:rows, 0:64], in1=DTB[0:rows, :], op=ALU.add),
                     reads=[('ps', b), 'DTB'], writes=['DT'])
            S.op('act', lambda e: e.activation(out=DT[:], in_=DT[:], func=AF.Exp), reads=['DT'], writes=['DT'])
            S.op('act', lambda e: e.activation(out=DT[:], in_=DT[:], func=AF.Ln, bias=1.0, scale=1.0), reads=['DT'], writes=['DT'])
            S.op('dve', lambda e: e.tensor_tensor(out=DA[:], in0=DT[:], in1=AROW[:].unsqueeze(1).to_broadcast([128, NTILE, 64]), op=ALU.mult),
                 reads=['DT', 'AROW'], writes=['DA'])
            for tt in range(NTILE):
                rows = rows_of(tt)
                um = UT if tt < NCH else USB
                om = ONESF if tt < NCH else SEQ32
                b = next_ps()
                S.op('pe', lambda e, b=b, tt=tt, rows=rows, um=um: e.matmul(PS[b][0:rows, 0:64], lhsT=um[0:rows, 0:rows], rhs=DA[0:rows, tt, :], start=True, stop=True),
                     reads=['DA', 'UT', 'USB'], writes=[('ps', b)])
                S.op('pe', lambda e, b=b, tt=tt, rows=rows, om=om: e.matmul(PS[b][0:rows, 64:128], lhsT=om[0:rows, 0:rows], rhs=DA[0:rows, tt, :], start=True, stop=True),
                     reads=['DA', 'ONESF', 'SEQ32'], writes=[('ps', b)])
                S.op('dve', lambda e, b=b, tt=tt, rows=rows: e.tensor_copy(out=ACS[0:rows, tt, :], in_=PS[b][0:rows, 0:64]), reads=[('ps', b)], writes=['ACS'])
                S.op('dve', lambda e, b=b, tt=tt, rows=rows: e.tensor_copy(out=TOT[0:rows, tt, :], in_=PS[b][0:rows, 64:128]), reads=[('ps', b)], writes=['TOT'])
            S.op('act', lambda e: e.activation(out=EXPA[:], in_=ACS[:], func=AF.Exp), reads=['ACS'], writes=['EXPA'])
            S.op('act', lambda e: e.activation(out=DEC[:], in_=TOT[:], func=AF.Exp), reads=['TOT'], writes=['DEC'])
            S.op('dve', lambda e: e.tensor_tensor(out=DTE[:], in0=TOT[:], in1=ACS[:], op=ALU.subtract), reads=['TOT', 'ACS'], writes=['DTE'])
            S.op('act', lambda e: e.activation(out=DTE[:], in_=DTE[:], func=AF.Exp), reads=['DTE'], writes=['DTE'])

            S.barrier()
            for g in range(8):
                tl, key = load_w(b_w_in, 0, KC, g * 512)
                for tt in range(NTILE):
                    rows = rows_of(tt)
                    b = next_ps()
                    for hh, (th, kh) in enumerate(tl.halves):
                        for k in range(KC):
                            S.op('pe', lambda e, b=b, k=k, tt=tt, rows=rows, th=th, hh=hh: e.matmul(
                                PS[b][0:rows, hh * 256:(hh + 1) * 256], lhsT=HT[:, k, tt * 128:tt * 128 + rows], rhs=th[:, k, :],
                                start=(k == 0), stop=(k == KC - 1)), reads=[kh, ('HT', k)], writes=[('ps', b)])
                    S.op('act', lambda e, b=b, tt=tt, rows=rows: e.activation(out=SZ[0:rows, tt, :], in_=PS[b][0:rows, :], func=AF.Silu),
                         reads=[('ps', b)], writes=['SZ'])
                tlx, keyx = load_w(b_w_in, 0, KC, 4096 + g * 512)
                cinfo = []
                for ci in range(6):
                    if ci < 4:
                        cinfo.append(dict(c0=ci * 128, ch=g * 4 + ci, kind='x', mi=ci))
                    elif ci == 4:
                        cinfo.append(dict(c0=0, ch=32 + g, kind='B', mi=0))
                    else:
                        cinfo.append(dict(c0=0, ch=40 + g, kind='C', mi=1))

                def stage_p(ci):
                    inf = cinfo[ci]
                    if ci < 4:
                        tl = tlx
                    elif ci == 4:
                        tl, _ = load_w(b_w_in, 0, KC, 8192 + g * 128, ncols=128)
                    else:
                        tl, _ = load_w(b_w_in, 0, KC, 9216 + g * 128, ncols=128)
                    c0 = inf['c0']
                    banks = []
                    for (t0, t1) in TBS:
                        b = next_ps(live_banks)
                        banks.append(b)
                        for k in range(KC):
                            S.op('pe', lambda e, b=b, k=k, tl=tl, c0=c0, t0=t0, t1=t1: e.matmul(
                                PS[b][:, 0:t1 - t0], lhsT=tl.c(k, c0 // 128), rhs=HT[:, k, t0:t1],
                                start=(k == 0), stop=(k == KC - 1)), reads=[tl.key(c0 // 128), ('HT', k)], writes=[('ps', b)])
                    inf['banks'] = banks
                    live_banks.update(banks)

                def stage_q(ci):
                    inf = cinfo[ci]
                    ch, kind, mi = inf['ch'], inf['kind'], inf['mi']
                    S.op('dve', lambda e, ch=ch: e.tensor_copy(out=RAW[:, 0:3], in_=CTAIL[:, ch, :]), reads=['CTAIL'], writes=['RAW'])
                    S.op('dve', lambda e, ch=ch: e.tensor_copy(
                        out=RAW[:, 3 + NPT:].rearrange("p (b t) -> p b t", t=11)[:, :, 0:3],
                        in_=SCV[:, ch, :].rearrange("p (b k) -> p b k", k=3)), reads=['SCV'], writes=['RAW'])
                    for bi, (t0, t1) in enumerate(TBS):
                        n = t1 - t0
                        npr = min(t1, NPT) - t0
                        b = inf['banks'][bi]
                        S.op('act', lambda e, b=b, t0=t0, npr=npr: e.activation(out=RAW[:, 3 + t0:3 + t0 + npr], in_=PS[b][:, 0:npr], func=AF.Copy),
                             reads=[('ps', b)], writes=['RAW'])
                        if t1 > NPT:
                            S.op('act', lambda e, b=b, npr=npr, n=n: e.activation(
                                out=RAW[:, 3 + NPT:].rearrange("p (b t) -> p b t", t=11)[:, :, 3:11],
                                in_=PS[b][:, npr:n].rearrange("p (b t) -> p b t", t=8), func=AF.Copy),
                                reads=[('ps', b)], writes=['RAW'])
                    for b in inf['banks']:
                        live_banks.discard(b)
                    S.op('dve', lambda e, ch=ch: e.tensor_copy(out=CTAIL[:, ch, :], in_=RAW[:, NPT:NPT + 3]), reads=['RAW'], writes=['CTAIL'])
                    j2 = rot('ev', 3)
                    S.op('dve', lambda e, j2=j2: e.tensor_copy(
                        out=EV[j2][:, 0:SSQ * 3].rearrange("p (b k) -> p b k", k=3),
                        in_=RAW[:, 3 + NPT:].rearrange("p (b t) -> p b t", t=11)[:, :, 8:11]), reads=['RAW'], writes=['EV%d' % j2])
                    tr_store(EV[j2][:, 0:SSQ * 3], SSQ * 3, conv_s[hf * SSQ * 3:(hf + 1) * SSQ * 3, ch * 128:(ch + 1) * 128], ['EV%d' % j2])
                    if last:
                        tr_store(CTAIL[:, ch, :], 3, conv_p[:, ch * 128:(ch + 1) * 128], ['CTAIL'])
                    cw = lambda kk, ch=ch: col('conv_w', kk * 48 + ch)
                    cbias = col('conv_b', ch)
                    pr_in = lambda kk: RAW[:, kk:kk + NPT]
                    sm_in = lambda kk: RAW[:, 3 + NPT:].rearrange("p (b t) -> p b t", t=11)[:, :, kk:kk + 8]
                    pr_out = XC[:, 0:NPT]
                    sm_out = XC[:, NPT:NTOK].rearrange("p (b t) -> p b t", t=8)
                    for (oin, oout) in ((pr_in, pr_out), (sm_in, sm_out)):
                        S.op('dve', lambda e, oin=oin, oout=oout, cw=cw, cbias=cbias: e.tensor_scalar(
                            out=oout, in0=oin(0), scalar1=cw(0), scalar2=cbias, op0=ALU.mult, op1=ALU.add),
                            reads=['RAW', 'COLS'], writes=['XC'])
                        for kk in range(1, 4):
                            S.op('dve', lambda e, oin=oin, oout=oout, cw=cw, kk=kk: e.scalar_tensor_tensor(
                                out=oout, in0=oin(kk), scalar=cw(kk), in1=oout, op0=ALU.mult, op1=ALU.add),
                                reads=['RAW', 'COLS', 'XC'], writes=['XC'])
                    if kind == 'C':
                        S.op('act', lambda e: e.activation(out=BCT[:, 1, :], in_=XC[:], func=AF.Silu), reads=['XC'], writes=['BCT'])
                        return
                    if kind == 'B':
                        S.op('act', lambda e: e.activation(out=BCT[:, 0, :], in_=XC[:], func=AF.Silu), reads=['XC'], writes=['BCT'])
                    S.op('act', lambda e: e.activation(out=XC[:], in_=XC[:], func=AF.Silu), reads=['XC'], writes=['XC'])
                    b = next_ps(live_banks)
                    for tt in range(NCH):
                        S.op('pe', lambda e, b=b, tt=tt: e.transpose(out=PS[b][:, tt * 128:(tt + 1) * 128], in_=XC[:, tt * 128:(tt + 1) * 128], identity=IDF[:]),
                             reads=['XC', 'IDF'], writes=[('ps', b)])
                    b2 = next_ps(live_banks)
                    S.op('pe', lambda e, b2=b2: e.transpose(out=PS[b2][0:NS, 0:128], in_=XC[:, NPT:NTOK], identity=IDF[:]),
                         reads=['XC', 'IDF'], writes=[('ps', b2)])
                    if kind == 'x':
                        S.op('dve', lambda e, b=b, mi=mi: e.tensor_copy(out=XTOK[:, 0:NCH, mi * 128:(mi + 1) * 128],
                                                                      in_=PS[b][:, :].rearrange("p (t c) -> p t c", c=128)),
                             reads=[('ps', b)], writes=['XTOK'])
                        S.op('dve', lambda e, b2=b2, mi=mi: e.tensor_copy(out=XTOK[0:NS, NCH, mi * 128:(mi + 1) * 128], in_=PS[b2][0:NS, 0:128]),
                             reads=[('ps', b2)], writes=['XTOK'])
                    else:
                        S.op('dve', lambda e, b=b: e.tensor_copy(out=BTOK[:, 0:NCH, :], in_=PS[b][:, :].rearrange("p (t c) -> p t c", c=128)),
                             reads=[('ps', b)], writes=['BTOK'])
                        S.op('dve', lambda e, b2=b2: e.tensor_copy(out=BTOK[0:NS, NCH, :], in_=PS[b2][0:NS, 0:128]),
                             reads=[('ps', b2)], writes=['BTOK'])

                live_banks = set()
                LIVE[0] = live_banks
                stage_p(0)
                for ci in range(6):
                    if ci + 1 < 6:
                        stage_p(ci + 1)
                    stage_q(ci)
                S.dma('sp', NWR[:], b_nw[g * 512:(g + 1) * 512].partition_broadcast(128), writes=['NWR'])
                if hf == 0:
                    S.op('pool', lambda e: e.memset(HSTG[:], 0.0), writes=['HSTG'])
                else:
                    S.dma('sp', HSTG[:], hst_d[g], reads=[('hst_d', g)], writes=['HSTG'])
                S.op('act', lambda e: e.activation(out=HB[:], in_=HSTG[:], func=AF.Copy), reads=['HSTG'], writes=['HB'])
                hs = slice(g * 8, g * 8 + 8)
                for tt in range(NTILE):
                    rows = rows_of(tt)
                    samp = (tt == NCH)
                    tk0 = tt * 128
                    um = USB if samp else UT
                    b = next_ps()
                    S.op('pe', lambda e, b=b, rows=rows, tk0=tk0: e.matmul(PS[b][0:rows, 0:rows], lhsT=BCT[:, 0, tk0:tk0 + rows], rhs=BCT[:, 1, tk0:tk0 + rows], start=True, stop=True),
                         reads=['BCT'], writes=[('ps', b)])
                    S.op('dve', lambda e, b=b, rows=rows, um=um: e.tensor_tensor(out=CBM[0:rows, 0:rows], in0=PS[b][0:rows, 0:rows], in1=um[0:rows, 0:rows], op=ALU.mult),
                         reads=[('ps', b), 'UT', 'USB'], writes=['CBM'])
                    for r4 in range(2):
                        b = next_ps()
                        for rr in range(4):
                            r = r4 * 4 + rr
                            h = g * 8 + r
                            S.op('pe', lambda e, b=b, rr=rr, h=h, tt=tt, rows=rows, um=um: e.matmul(
                                PS[b][0:rows, rr * 128:rr * 128 + rows], lhsT=DA[0:rows, tt, h:h + 1].to_broadcast([rows, rows]),
                                rhs=um[0:rows, 0:rows], start=True, stop=True), reads=['DA', 'UT', 'USB'], writes=[('ps', b)])
                        for rr in range(4):
                            r = r4 * 4 + rr
                            h = g * 8 + r
                            S.op('dve', lambda e, b=b, rr=rr, r=r, h=h, tt=tt, rows=rows: e.tensor_scalar(
                                out=EE[0:rows, r, 0:rows], in0=PS[b][0:rows, rr * 128:rr * 128 + rows], scalar1=ACS[0:rows, tt, h:h + 1], scalar2=0.0,
                                op0=ALU.subtract, op1=ALU.min), reads=[('ps', b), 'ACS'], writes=['EE'])
                    S.op('act', lambda e, rows=rows: e.activation(out=LT[0:rows, :, 0:rows], in_=EE[0:rows, :, 0:rows], func=AF.Exp), reads=['EE'], writes=['LT'])
                    S.op('dve', lambda e, rows=rows: e.tensor_tensor(out=MT[0:rows, :, 0:rows], in0=LT[0:rows, :, 0:rows],
                                                                    in1=CBM[0:rows, 0:rows].unsqueeze(1).to_broadcast([rows, 8, rows]), op=ALU.mult),
                         reads=['LT', 'CBM'], writes=['MT'])
                    S.op('dve', lambda e, tt=tt, rows=rows, hs=hs: e.tensor_tensor(
                        out=XDT[0:rows, :].rearrange("p (r q) -> p r q", q=64), in0=XTOK[0:rows, tt, :].rearrange("p (r q) -> p r q", q=64),
                        in1=DT[0:rows, tt, hs].unsqueeze(2).to_broadcast([rows, 8, 64]), op=ALU.mult), reads=['XTOK', 'DT'], writes=['XDT'])
                    S.op('dve', lambda e, tt=tt, rows=rows, hs=hs: e.tensor_tensor(
                        out=WW[0:rows, :].rearrange("p (r q) -> p r q", q=64), in0=XDT[0:rows, :].rearrange("p (r q) -> p r q", q=64),
                        in1=DTE[0:rows, tt, hs].unsqueeze(2).to_broadcast([rows, 8, 64]), op=ALU.mult), reads=['XDT', 'DTE'], writes=['WW'])
                    by = next_ps()
                    for r in range(8):
                        S.op('pe', lambda e, by=by, r=r, rows=rows: e.matmul(PS[by][0:rows, r * 64:(r + 1) * 64], lhsT=MT[0:rows, r, 0:rows], rhs=XDT[0:rows, r * 64:(r + 1) * 64], start=True, stop=True),
                             reads=['MT', 'XDT'], writes=[('ps', by)])
                    bo = next_ps()
                    if not samp:
                        S.op('pe', lambda e, bo=bo, rows=rows, tk0=tk0: e.matmul(PS[bo][0:rows, :], lhsT=BCT[:, 1, tk0:tk0 + rows], rhs=HB[:], start=True, stop=True),
                             reads=['BCT', 'HB'], writes=[('ps', bo)])
                    else:
                        S.op('dve', lambda e, tk0=tk0: e.tensor_tensor(out=CMS[:], in0=BCT[:, 1, tk0:tk0 + NS].unsqueeze(1).to_broadcast([128, SSQ, NS]), in1=MASK3[:], op=ALU.mult),
                             reads=['BCT', 'MASK3'], writes=['CMS'])
                        bd = next_ps((by, bo))
                        for q4 in range(4):
                            h2 = g * 8 + q4 * 2
                            S.op('dve', lambda e, h2=h2, tt=tt: e.tensor_copy(
                                out=DAB[:].rearrange("p (h q) -> p h q", q=64), in_=DA[0:NS, tt, h2:h2 + 2].unsqueeze(2).to_broadcast([NS, 2, 64])),
                                reads=['DA'], writes=['DAB'])
                            S.op('pe', lambda e, bd=bd, q4=q4: e.matmul(
                                PS[bd][:, q4 * SSQ:(q4 + 1) * SSQ], lhsT=DAB[:], rhs=SEQM[:], start=True, stop=True),
                                reads=['DAB', 'SEQM'], writes=[('ps', bd)])
                        S.op('act', lambda e, bd=bd: e.activation(out=DECS[:].rearrange("p q b -> p (q b)"), in_=PS[bd][:, 0:4 * SSQ], func=AF.Exp),
                             reads=[('ps', bd)], writes=['DECS'])
                        for bq in range(SSQ):
                            sq_ = hf * SSQ + bq
                            src = st_ssm[sq_, g * 8:(g + 1) * 8].rearrange("(q h) p n -> (h p) q n", h=2)
                            S.dma('sp', H0[:], src, writes=['H0'])
                            bt_ = next_ps((by, bo))
                            for q4 in range(4):
                                S.op('pe', lambda e, bt_=bt_, q4=q4: e.transpose(out=PS[bt_][:, q4 * 128:(q4 + 1) * 128], in_=H0[:, q4, :], identity=IDF[:]),
                                     reads=['H0', 'IDF'], writes=[('ps', bt_)])
                            S.op('act', lambda e, bt_=bt_: e.activation(out=H0T[:], in_=PS[bt_][:], func=AF.Copy), reads=[('ps', bt_)], writes=['H0T'])
                            S.op('pe', lambda e, bo=bo, bq=bq: e.matmul(PS[bo][0:NS, :], lhsT=CMS[:, bq, :], rhs=H0T[:], start=(bq == 0), stop=(bq == SSQ - 1)),
                                 reads=['CMS', 'H0T'], writes=[('ps', bo)])
                            S.op('dve', lambda e, bq=bq: e.tensor_scalar(out=WM[:], in0=WW[0:NS, :], scalar1=SEQM[:, bq:bq + 1], scalar2=None, op0=ALU.mult),
                                 reads=['WW', 'SEQM'], writes=['WM'])
                            bn = next_ps((by, bo))
                            for q4 in range(4):
                                S.op('pe', lambda e, bn=bn, q4=q4, tt=tt: e.matmul(PS[bn][:, q4 * 128:(q4 + 1) * 128], lhsT=WM[:, q4 * 128:(q4 + 1) * 128], rhs=BTOK[0:NS, tt, :], start=True, stop=True),
                                     reads=['WM', 'BTOK'], writes=[('ps', bn)])
                            for q4 in range(4):
                                S.op('dve', lambda e, bn=bn, q4=q4, bq=bq: e.scalar_tensor_tensor(
                                    out=H0[:, q4, :], in0=H0[:, q4, :], scalar=DECS[:, q4, bq:bq + 1], in1=PS[bn][:, q4 * 128:(q4 + 1) * 128],
                                    op0=ALU.mult, op1=ALU.add), reads=['H0', 'DECS', ('ps', bn)], writes=['H0'])
                            dst = ssm_s[sq_, g * 8:(g + 1) * 8].rearrange("(q h) p n -> (h p) q n", h=2)
                            S.dma('sp', dst, H0[:], reads=['H0'])
                    jy, jo = rot('ev', 3), rot('ev', 3)
                    Y, YO = EV[jy], EV[jo]
                    S.op('dve', lambda e, bo=bo, YO=YO, tt=tt, rows=rows, hs=hs: e.tensor_tensor(
                        out=YO[0:rows, :].rearrange("p (r q) -> p r q", q=64), in0=PS[bo][0:rows, :].rearrange("p (r q) -> p r q", q=64),
                        in1=EXPA[0:rows, tt, hs].unsqueeze(2).to_broadcast([rows, 8, 64]), op=ALU.mult), reads=[('ps', bo), 'EXPA'], writes=['EV%d' % jo])
                    S.op('dve', lambda e, by=by, Y=Y, YO=YO, rows=rows: e.tensor_tensor(out=Y[0:rows, :], in0=PS[by][0:rows, :], in1=YO[0:rows, :], op=ALU.add),
                         reads=[('ps', by), 'EV%d' % jo], writes=['EV%d' % jy])
                    S.op('dve', lambda e, YO=YO, tt=tt, rows=rows, hs=hs: e.tensor_tensor(
                        out=YO[0:rows, :].rearrange("p (r q) -> p r q", q=64), in0=XTOK[0:rows, tt, :].rearrange("p (r q) -> p r q", q=64),
                        in1=DROW[0:rows, hs].unsqueeze(2).to_broadcast([rows, 8, 64]), op=ALU.mult), reads=['XTOK', 'DROW', 'EV%d' % jo], writes=['EV%d' % jo])
                    S.op('dve', lambda e, Y=Y, YO=YO, rows=rows: e.tensor_tensor(out=Y[0:rows, :], in0=Y[0:rows, :], in1=YO[0:rows, :], op=ALU.add),
                         reads=['EV%d' % jo, 'EV%d' % jy], writes=['EV%d' % jy])
                    S.op('dve', lambda e, Y=Y, tt=tt, rows=rows: e.tensor_tensor(out=Y[0:rows, :], in0=Y[0:rows, :], in1=SZ[0:rows, tt, :], op=ALU.mult),
                         reads=['SZ', 'EV%d' % jy], writes=['EV%d' % jy])
                    S.op('act', lambda e, Y=Y, YO=YO, rows=rows: e.activation(out=YO[0:rows, :], in_=Y[0:rows, :], func=AF.Square, accum_out=SMALL[0:rows, 32:33]),
                         reads=['EV%d' % jy], writes=['EV%d' % jo, ('SM', 32)])
                    S.op('dve', lambda e, rows=rows: e.tensor_scalar(out=SMALL[0:rows, 32:33], in0=SMALL[0:rows, 32:33], scalar1=1.0 / 512, scalar2=NORM_EPS, op0=ALU.mult, op1=ALU.add),
                         reads=[('SM', 32)], writes=[('SM', 32)])
                    S.op('act', lambda e, rows=rows: e.activation(out=SMALL[0:rows, 32:33], in_=SMALL[0:rows, 32:33], func=AF.Ln), reads=[('SM', 32)], writes=[('SM', 32)])
                    S.op('act', lambda e, rows=rows: e.activation(out=SMALL[0:rows, 32:33], in_=SMALL[0:rows, 32:33], func=AF.Exp, scale=-0.5), reads=[('SM', 32)], writes=[('SM', 32)])
                    S.op('dve', lambda e, Y=Y, rows=rows: e.scalar_tensor_tensor(out=Y[0:rows, :], in0=Y[0:rows, :], scalar=SMALL[0:rows, 32:33], in1=NWR[0:rows, :], op0=ALU.mult, op1=ALU.mult),
                         reads=['EV%d' % jy, ('SM', 32), 'NWR'], writes=['EV%d' % jy])
                    bt2 = next_ps()
                    for kk in range(4):
                        S.op('pe', lambda e, bt2=bt2, kk=kk, Y=Y, rows=rows: e.transpose(out=PS[bt2][:, kk * 128:kk * 128 + rows], in_=Y[0:rows, kk * 128:(kk + 1) * 128], identity=IDF[0:rows, 0:rows]),
                             reads=['EV%d' % jy, 'IDF'], writes=[('ps', bt2)])
                    S.op('act', lambda e, bt2=bt2, tk0=tk0, rows=rows: e.activation(
                        out=GB[:, :, tk0:tk0 + rows], in_=PS[bt2][:].rearrange("p (k t) -> p k t", t=128)[:, :, 0:rows], func=AF.Copy),
                        reads=[('ps', bt2)], writes=['GB'])
                    if not samp:
                        bs3 = next_ps()
                        S.op('pe', lambda e, bs3=bs3, tt=tt: e.matmul(PS[bs3][:, :], lhsT=BTOK[:, tt, :], rhs=WW[:, :], start=True, stop=True),
                             reads=['BTOK', 'WW'], writes=[('ps', bs3)])
                        S.op('dve', lambda e, tt=tt, hs=hs: e.tensor_tensor(
                            out=HSTG[:].rearrange("p (r q) -> p r q", q=64), in0=HSTG[:].rearrange("p (r q) -> p r q", q=64),
                            in1=DEC[:, tt, hs].unsqueeze(2).to_broadcast([128, 8, 64]), op=ALU.mult), reads=['HSTG', 'DEC'], writes=['HSTG'])
                        S.op('dve', lambda e, bs3=bs3: e.tensor_tensor(out=HSTG[:], in0=HSTG[:], in1=PS[bs3][:, :], op=ALU.add),
                             reads=['HSTG', ('ps', bs3)], writes=['HSTG'])
                        S.op('act', lambda e: e.activation(out=HB[:], in_=HSTG[:], func=AF.Copy), reads=['HSTG'], writes=['HB'])
                if not last:
                    S.dma('sp', hst_d[g], HSTG[:], reads=['HSTG'], writes=[('hst_d', g)])
                else:
                    for q4 in range(4):
                        b = next_ps()
                        S.op('pe', lambda e, b=b, q4=q4: e.transpose(out=PS[b][:, 0:128], in_=HSTG[:, q4 * 128:(q4 + 1) * 128], identity=IDF[:]),
                             reads=['HSTG', 'IDF'], writes=[('ps', b)])
                        j = rot('sm12', 2)
                        S.op('dve', lambda e, b=b, j=j: e.tensor_copy(out=SM12[j][:, :], in_=PS[b][:, 0:128]), reads=[('ps', b)], writes=['SM12_%d' % j])
                        r0 = (g * 8 + q4 * 2) * 64
                        S.dma('sp', ssm_p[r0:r0 + 128, :], SM12[j][:, :], reads=['SM12_%d' % j])
                out_proj_partial(hf, 1, 32, b_w_out, g * 512)

        def ffn(hf, l):
            for blk in range(11):
                for half in range(2):
                    tg, _ = load_w(f_w_in[l], 0, KC, blk * 512 + half * 256, ncols=256)
                    tu, _ = load_w(f_w_in[l], 0, KC, FFN_H + blk * 512 + half * 256, ncols=256)
                    for mi2 in range(2):
                        mi = half * 2 + mi2
                        for (t0, t1) in TBS:
                            n = t1 - t0
                            bg, bu = next_ps(), next_ps()
                            for k in range(KC):
                                S.op('pe', lambda e, b=bg, k=k, mi2=mi2, tl=tg, t0=t0, t1=t1: e.matmul(
                                    PS[b][:, 0:t1 - t0], lhsT=tl.c(k, mi2), rhs=HT[:, k, t0:t1],
                                    start=(k == 0), stop=(k == KC - 1)),
                                    reads=[tg.key(mi2), ('HT', k)], writes=[('ps', bg)])
                            for k in range(KC):
                                S.op('pe', lambda e, b=bu, k=k, mi2=mi2, tl=tu, t0=t0, t1=t1: e.matmul(
                                    PS[b][:, 0:t1 - t0], lhsT=tl.c(k, mi2), rhs=HT[:, k, t0:t1],
                                    start=(k == 0), stop=(k == KC - 1)),
                                    reads=[tu.key(mi2), ('HT', k)], writes=[('ps', bu)])
                            i = rot('ev', 3)
                            S.op('act', lambda e, b=bg, i=i, n=n: e.activation(out=EV[i][:, 0:n], in_=PS[b][:, 0:n], func=AF.Silu),
                                 reads=[('ps', bg)], writes=['EV%d' % i])
                            S.op('dve', lambda e, b=bu, i=i, n=n, mi=mi, t0=t0, t1=t1: e.tensor_tensor(
                                out=GB[:, mi, t0:t1], in0=PS[b][:, 0:n], in1=EV[i][:, 0:n], op=ALU.mult),
                                reads=[('ps', bu), 'EV%d' % i], writes=['GB'])
                out_proj_partial(hf, l, 80, f_w_out[l], blk * 512)

        def final_out(hf):
            compute_rstd(NORM_EPS)
            for tt in range(NTILE):
                rows = 128 if tt < NCH else NS
                i = rot('tmpf', 2)
                t, tk = TMPF[i], 'TMPF%d' % i
                for k4 in range(4):
                    j = rot('ev', 3)
                    S.op('dve', lambda e, j=j, k4=k4, tt=tt, rows=rows: e.tensor_tensor(
                        out=EV[j][:, :].rearrange("p (k t) -> p k t", t=128)[:, :, 0:rows],
                        in0=XT[:, k4 * 4:(k4 + 1) * 4, tt * 128:tt * 128 + rows],
                        in1=RSTD[:, tt * 128:tt * 128 + rows].unsqueeze(1).to_broadcast([128, 4, rows]), op=ALU.mult),
                        reads=[kx for kk_ in range(4) for kx in xk(k4 * 4 + kk_)] + ['RSTD'], writes=['EV%d' % j])
                    b = next_ps()
                    for kk in range(4):
                        k = k4 * 4 + kk
                        S.op('act', lambda e, j=j, kk=kk, k=k, rows=rows: e.activation(
                            out=EV[j][:, kk * 128:kk * 128 + rows], in_=EV[j][:, kk * 128:kk * 128 + rows],
                            func=AF.Copy, scale=col('fnw', k)),
                            reads=['EV%d' % j, 'COLS'], writes=['EV%d' % j])
                        S.op('pe', lambda e, b=b, j=j, kk=kk, rows=rows: e.transpose(
                            out=PS[b][0:rows, kk * 128:(kk + 1) * 128], in_=EV[j][:, kk * 128:kk * 128 + rows], identity=IDF[:]),
                            reads=['EV%d' % j, 'IDF'], writes=[('ps', b)])
                    S.op('dve', lambda e, b=b, k4=k4, rows=rows, t=t: e.tensor_copy(out=t[0:rows, k4 * 512:(k4 + 1) * 512], in_=PS[b][0:rows, :]),
                         reads=[('ps', b)], writes=[tk])
                if tt < NCH:
                    S.dma('sp', y_p[hf * NPT + tt * 128: hf * NPT + (tt + 1) * 128, :], t[:, :], reads=[tk])
                else:
                    S.dma('sp', y_s[hf * NS:(hf + 1) * NS, :], t[0:NS, :], reads=[tk])

        for hf in range(NPASS):
            load_x(hf)
            norm_mod(hf, 0, 0, 'nmw0')
            gmlp(hf)
            norm_mod(hf, 0, 1, 'nfw0')
            ffn(hf, 0)
            if STAGE >= 2:
                norm_mod(hf, 1, 0, 'nmw1')
                S.barrier()
                ssd(hf)
                S.barrier()
                norm_mod(hf, 1, 1, 'nfw1')
                ffn(hf, 1)
            final_out(hf)
            S.barrier()

        S.emit()
    return nc, S


_CACHE = {}


def _get_program():
    if 'nc' not in _CACHE:
        _CACHE['nc'] = build_program()
    return _CACHE['nc']


def kernel(x_prompt, x_sample, c_prompt, c_sample, state_ssm, state_conv,
           mod_w, mod_b, norm_mix_w, norm_ffn_w,
           a_w_in, a_b_in, a_ln_w, a_ln_b, a_w_s, a_b_s, a_w_out,
           b_w_in, b_conv_w, b_conv_b, b_dt_bias, b_a_log, b_d, b_norm_w, b_w_out,
           f_w_in, f_w_out, final_norm_w):
    f = lambda a: np.ascontiguousarray(np.asarray(a, dtype=np.float32))
    nc, S = _get_program()
    vecs = np.zeros((VEC_TOT, 128), np.float32)

    def put(name, arr):
        r0, n = VEC_LAY[name]
        vecs[r0:r0 + n] = np.asarray(arr, np.float32).reshape(n, 128)

    put('mod_b0', mod_b[0]); put('mod_b1', mod_b[1])
    put('nmw0', norm_mix_w[0]); put('nmw1', norm_mix_w[1])
    put('nfw0', norm_ffn_w[0]); put('nfw1', norm_ffn_w[1])
    put('a_b_in', a_b_in[0]); put('fnw', final_norm_w)
    put('conv_w', b_conv_w[0]); put('conv_b', b_conv_b[0])
    shared = dict(vecs=vecs, mod_w=f(mod_w), a_w_in=f(a_w_in[0]), a_ln_w=f(a_ln_w[0]), a_ln_b=f(a_ln_b[0]),
                  a_b_in_r=f(a_b_in[0]), a_w_s=f(a_w_s[0]), a_b_s=f(a_b_s[0]), a_w_out=f(a_w_out[0]),
                  f_w_in=f(f_w_in), f_w_out=f(f_w_out))
    shared.update(b_w_in=f(b_w_in[0]), b_w_out=f(b_w_out[0]), b_dtb=f(b_dt_bias[0]), b_alog=f(b_a_log[0]),
                  b_dsk=f(b_d[0]), b_nw=f(b_norm_w[0]))
    in_maps = []
    for c in range(8):
        m = dict(shared)
        m['st_ssm'] = f(np.asarray(state_ssm)[0, 16 * c:16 * (c + 1)])
        m['st_conv'] = f(np.asarray(state_conv)[0, 16 * c:16 * (c + 1)].reshape(48, 6144))
        m['xp'] = f(x_prompt[c % 4])
        m['xs'] = f(np.asarray(x_sample)[16 * c:16 * (c + 1)].reshape(128, D))
        m['call'] = f(np.concatenate([np.asarray(c_prompt)[c % 4][None], np.asarray(c_sample)[16 * c:16 * (c + 1)]], 0))
        in_maps.append(m)
    res = run_bass_kernel_spmd(nc, in_maps, core_ids=list(range(8)))
    R = res.results
    y_prompt = np.stack([R[c]['y_p'] for c in range(4)], 0)
    y_sample = np.concatenate([R[c]['y_s'].reshape(16, 8, D) for c in range(8)], 0)
    v_prompt = np.stack([R[c]['v_p'] for c in range(4)], 0)[None]
    v_sample = np.concatenate([R[c]['v_s'].reshape(16, 8, D) for c in range(8)], 0)[None]
    ssm_prompt = np.stack([R[c]['ssm_p'].reshape(64, 64, 128) for c in range(4)], 0)[None]
    ssm_sample = np.concatenate([R[c]['ssm_s'] for c in range(8)], 0)[None]
    conv_prompt = np.stack([R[c]['conv_p'] for c in range(4)], 0)[None]
    conv_sample = np.concatenate([R[c]['conv_s'].reshape(16, 3, 6144) for c in range(8)], 0)[None]
    return (y_prompt, y_sample, v_prompt, v_sample, ssm_prompt, ssm_sample, conv_prompt, conv_sample)
```

```python
import numpy as np
from contextlib import ExitStack
import concourse.bass as bass
import concourse.mybir as mybir
from concourse.bass_utils import run_bass_kernel_spmd

F32 = mybir.dt.float32
BF16 = mybir.dt.bfloat16
AF = mybir.ActivationFunctionType
ALU = mybir.AluOpType
AX = mybir.AxisListType

D = 2048
KC = 16
NPASS = 4
NCH = 4
SSQ = 4
NS = SSQ * 8
NPT = NCH * 128
NTOK = NPT + NS
NTILE = NCH + 1
TBS = [(0, 256), (256, NTOK)]
FFN_H = 5632
SSD_IN = 10304
NORM_EPS = 1e-6
LN_EPS = 1e-5
SAME_ENGINE_SYNC = True
STAGE = 2


class Sched:
    ENG = ('pe', 'dve', 'act', 'pool', 'sp')

    def __init__(self, nc, n_dma_sems=(24, 8, 16)):
        self.nc = nc
        self.ins = []
        self.last_w = {}
        self.readers = {}
        self.n_dma_sems = dict(sp=n_dma_sems[0], act=n_dma_sems[1], pool=n_dma_sems[2])

    def _add(self, eng, fn, reads, writes, kind):
        idx = len(self.ins)
        deps = set()
        raw = set()
        for k in reads:
            w = self.last_w.get(k)
            if w is not None:
                deps.add(w)
                raw.add(w)
        for k in writes:
            w = self.last_w.get(k)
            if w is not None:
                deps.add(w)
            for r in self.readers.get(k, {}).values():
                if isinstance(r, list):
                    deps.update(r)
                else:
                    deps.add(r)
        deps.discard(idx)
        if eng in ('dve', 'act', 'pool'):
            deps = set(d for d in deps if d in raw or self.ins[d]['eng'] != eng or self.ins[d]['kind'] == 'dma')
        self.ins.append(dict(eng=eng, fn=fn, deps=deps, kind=kind, needed=False))
        for k in writes:
            self.last_w[k] = idx
            self.readers[k] = {}
        for k in reads:
            if k not in writes:
                rd = self.readers.setdefault(k, {})
                if kind == 'dma':
                    rd.setdefault('dma', []).append(idx)
                else:
                    rd[eng] = idx
        return idx

    def op(self, eng, fn, reads=(), writes=()):
        return self._add(eng, fn, list(reads), list(writes), 'op')

    def dma(self, eng, out, in_, reads=(), writes=(), **kw):
        def fn(e, out=out, in_=in_, kw=kw):
            return e.dma_start(out=out, in_=in_, **kw)
        return self._add(eng, fn, list(reads), list(writes), 'dma')

    def barrier(self):
        last = {}
        for i, it in enumerate(self.ins):
            if it['kind'] != 'dma':
                last[it['eng']] = i
        deps = set(last.values())
        for k, w in self.last_w.items():
            if self.ins[w]['kind'] == 'dma':
                deps.add(w)
        for k, rs in self.readers.items():
            deps.update(rs.get('dma', []))
        for e in self.ENG:
            self.ins.append(dict(eng=e, fn=None, deps=set(deps), kind='nop', needed=False))
        self.last_w.clear()
        self.readers.clear()

    def emit(self):
        nc = self.nc
        ins = self.ins
        deps = set()
        last = {}
        for i, it in enumerate(ins):
            if it['kind'] == 'dma':
                deps.add(i)
            elif it['kind'] == 'op':
                last[it['eng']] = i
        deps |= set(last.values())
        ins.append(dict(eng='sp', fn=None, deps=deps, kind='nop', needed=False))
        for it in ins:
            for d in it['deps']:
                p = ins[d]
                if p['kind'] == 'dma':
                    p['needed'] = True
                elif p['eng'] == it['eng'] and (p['eng'] in ('pe', 'sp') or not SAME_ENGINE_SYNC):
                    pass
                else:
                    p['needed'] = True
        cnt = {e: 0 for e in self.ENG}
        dcnt = {e: 0 for e in self.ENG}
        for it in ins:
            e = it['eng']
            if it['kind'] == 'dma':
                n = self.n_dma_sems[e]
                j = dcnt[e]
                dcnt[e] += 1
                it['sig'] = ('d', e, j % n, 16 * (j // n + 1))
                it['prev_on_sem'] = 16 * (j // n)
            elif it['kind'] == 'op' and it['needed']:
                cnt[e] += 1
                it['sig'] = ('e', e, 0, cnt[e])
            else:
                it['sig'] = None
        self.stats = dict(cnt=cnt, dcnt=dcnt, n=len(ins))
        with ExitStack() as es:
            esem = {e: es.enter_context(nc.semaphore('s_' + e)) for e in self.ENG}
            dsem = {}
            for e in ('sp', 'act', 'pool'):
                nd = min(self.n_dma_sems[e], max(dcnt[e], 1))
                dsem[e] = [es.enter_context(nc.semaphore('d_%s%d' % (e, i))) for i in range(nd)]
            per = {e: [] for e in self.ENG}
            for i, it in enumerate(ins):
                per[it['eng']].append(i)
            block = es.enter_context(nc.Block())

            def make(e):
                def body(engobj):
                    wm = {}
                    for i in per[e]:
                        it = ins[i]
                        waits = {}
                        for d in it['deps']:
                            p = ins[d]
                            sg = p.get('sig')
                            if sg is None:
                                continue
                            if sg[0] == 'e' and p['eng'] == e and (e in ('pe', 'sp') or not SAME_ENGINE_SYNC):
                                continue
                            key = sg[:3]
                            if wm.get(key, 0) >= sg[3]:
                                continue
                            waits[key] = max(waits.get(key, 0), sg[3])
                        if it['kind'] == 'dma' and it['prev_on_sem'] > 0:
                            key = it['sig'][:3]
                            if wm.get(key, 0) < it['prev_on_sem']:
                                waits[key] = max(waits.get(key, 0), it['prev_on_sem'])
                        for key, v in waits.items():
                            sem = esem[key[1]] if key[0] == 'e' else dsem[key[1]][key[2]]
                            engobj.wait_ge(sem, v)
                            wm[key] = v
                        if it['fn'] is None:
                            continue
                        r = it['fn'](engobj)
                        sg = it['sig']
                        if sg is not None:
                            if sg[0] == 'e':
                                r.then_inc(esem[e], 1)
                            else:
                                r.then_inc(dsem[e][sg[2]], 16)
                return body

            block.tensor(make('pe'))
            block.vector(make('dve'))
            block.scalar(make('act'))
            block.gpsimd(make('pool'))
            block.sync(make('sp'))


VEC_ROWS = {}


def _vec_layout():
    off = 0
    lay = {}
    for name, n in [('mod_b0', 96), ('mod_b1', 96), ('nmw0', 16), ('nmw1', 16), ('nfw0', 16),
                    ('nfw1', 16), ('a_b_in', 32), ('fnw', 16), ('conv_w', 192), ('conv_b', 48)]:
        lay[name] = (off, n)
        off += n
    tot = ((off + 127) // 128) * 128
    return lay, tot


VEC_LAY, VEC_TOT = _vec_layout()


def build_program():
    nc = bass.Bass("TRN2", target_bir_lowering=False)
    din = lambda name, shape: nc.dram_tensor(name, list(shape), F32, kind="ExternalInput").ap()
    dout = lambda name, shape: nc.dram_tensor(name, list(shape), F32, kind="ExternalOutput").ap()
    xp = din("xp", [2048, D])
    xs = din("xs", [128, D])
    call = din("call", [17, D])
    vecs = din("vecs", [VEC_TOT, 128])
    mod_w = din("mod_w", [2, D, 6 * D])
    a_w_in = din("a_w_in", [D, 2 * D])
    a_ln_w = din("a_ln_w", [D])
    a_ln_b = din("a_ln_b", [D])
    a_b_in_r = din("a_b_in_r", [2 * D])
    a_w_s = din("a_w_s", [16, 128, 128])
    a_b_s = din("a_b_s", [16, 128])
    a_w_out = din("a_w_out", [D, D])
    f_w_in = din("f_w_in", [2, D, 2 * FFN_H])
    f_w_out = din("f_w_out", [2, FFN_H, D])
    y_p = dout("y_p", [2048, D])
    y_s = dout("y_s", [128, D])
    v_p = dout("v_p", [128, D])
    v_s = dout("v_s", [128, D])
    st_ssm = din("st_ssm", [16, 64, 64, 128])
    st_conv = din("st_conv", [16 * 3, 6144])
    b_w_in = din("b_w_in", [D, SSD_IN])
    b_w_out = din("b_w_out", [2 * D, D])
    b_dtb = din("b_dtb", [64])
    b_alog = din("b_alog", [64])
    b_dsk = din("b_dsk", [64])
    b_nw = din("b_nw", [2 * D])
    ssm_p = dout("ssm_p", [64 * 64, 128])
    ssm_s = dout("ssm_s", [16, 64, 64, 128])
    conv_p = dout("conv_p", [3, 6144])
    conv_s = dout("conv_s", [16 * 3, 6144])
    hst_d = nc.dram_tensor("hst_d", [8, 128, 512], F32).ap()

    S = Sched(nc)
    es = ExitStack()
    with es:
        def sb(name, shape, dt=F32):
            return es.enter_context(nc.sbuf_tensor(name, list(shape), dt))

        XT = sb("XT", [128, KC, NTOK])
        HT = sb("HT", [128, KC, NTOK], BF16)
        MOD = [sb("MOD%d" % l, [128, 96, 17]) for l in range(2)]
        COLS = sb("COLS", [128, VEC_TOT])
        WT = [sb("WT%d" % i, [128, KC, 256], BF16) for i in range(4)]
        WO = [sb("WO%d" % i, [128, 4, 512], BF16) for i in range(2)]
        GB = sb("GB", [128, 4, NTOK], BF16)
        IDF = sb("IDF", [128, 128])
        IDB = sb("IDB", [128, 128], BF16)
        ONESB = sb("ONESB", [128, 128], BF16)
        RSTD = sb("RSTD", [128, NTOK])
        ACOL = sb("ACOL", [128, KC, 17])
        TMPF = [sb("TMPF%d" % i, [128, 2048]) for i in range(2)]
        SQ = [sb("SQ%d" % i, [128, NTOK], BF16) for i in range(2)]
        EV = [sb("EV%d" % i, [128, 512]) for i in range(3)]
        CT = sb("CT", [128, KC, 17], BF16)
        VN = sb("VN", [128, NTILE, 2048], BF16)
        BINV = sb("BINV", [128, 2048], BF16)
        LNW = sb("LNW", [128, 2048], BF16)
        LNB = sb("LNB", [128, 2048], BF16)
        WST = sb("WST", [128, 16, 128], BF16)
        BDS = sb("BDS", [NS, 16, NS], BF16)
        BSR = sb("BSR", [1, 16, 128], BF16)
        BSS = sb("BSS", [1, 16, NS], BF16)
        CMASK = sb("CMASK", [128, 128])
        SEQM = sb("SEQM", [NS, SSQ])
        E8 = sb("E8", [8, SSQ, 8], BF16)
        STATS = sb("STATS", [128, NTILE, 4, 6])
        MV = sb("MV", [128, NTILE, 2])
        SMALL = sb("SMALL", [128, 64])
        UT = sb("UT", [128, 128])
        USB = sb("USB", [NS, NS])
        ONESF = sb("ONESF", [128, 128])
        SEQ32 = sb("SEQ32", [NS, NS])
        MASK3 = sb("MASK3", [128, SSQ, NS])
        DTB = sb("DTB", [128, 64]); AROW = sb("AROW", [128, 64]); DROW = sb("DROW", [128, 64])
        DT = sb("DT", [128, NTILE, 64]); DA = sb("DA", [128, NTILE, 64]); ACS = sb("ACS", [128, NTILE, 64])
        TOT = sb("TOT", [128, NTILE, 64]); EXPA = sb("EXPA", [128, NTILE, 64]); DTE = sb("DTE", [128, NTILE, 64])
        DEC = sb("DEC", [128, NTILE, 64])
        CTAIL = sb("CTAIL", [128, 48, 3])
        HB = sb("HB", [128, 512], BF16)
        BCT = sb("BCT", [128, 2, NTOK], BF16)
        DECS = sb("DECS", [128, 4, SSQ])
        DAB = sb("DAB", [NS, 128])
        H0T = sb("H0T", [128, 512], BF16)
        CMS = sb("CMS", [128, SSQ, NS], BF16)
        WM = sb("WM", [NS, 512], BF16)
        CBM = sb("CBM", [128, 128], BF16)
        XDT = sb("XDT", [128, 512], BF16)
        XDB = sb("XDB", [128, 512], BF16)
        WW = sb("WW", [128, 512], BF16)
        SM12 = [sb("SM12_%d" % i, [128, 128]) for i in range(2)]
        VNF = VN[:].rearrange("p t c -> p (t c)")
        SZ = VNF[:, 0:NTILE * 512].rearrange("p (t c) -> p t c", c=512)
        XTOK = VNF[:, NTILE * 512:2 * NTILE * 512].rearrange("p (t c) -> p t c", c=512)
        BTOK = VNF[:, 2 * NTILE * 512:2 * NTILE * 512 + NTILE * 128].rearrange("p (t c) -> p t c", c=128)
        _o = 2 * NTILE * 512 + NTILE * 128
        EE = VNF[:, _o:_o + 2048].bitcast(F32).rearrange("p (r i) -> p r i", i=128)
        LT = VNF[:, _o + 2048:_o + 3072].rearrange("p (r i) -> p r i", i=128)
        MT = VNF[:, _o + 3072:_o + 4096].rearrange("p (r i) -> p r i", i=128)
        assert _o + 4096 <= NTILE * 2048
        RAW = TMPF[0][:, 0:3 + NPT + SSQ * 11]
        XC = TMPF[0][:, 560:560 + NTOK]
        HSTG = TMPF[0][:, 1104:1616]
        NWR = TMPF[1][:, 0:512]
        H0 = TMPF[1][:, 512:1024].rearrange("p (q n) -> p q n", n=128)
        SCV = TMPF[1][:, 1024:1024 + 48 * SSQ * 3].rearrange("p (c k) -> p c k", k=SSQ * 3)

        PS = [es.enter_context(nc.psum_tensor("PS%d" % i, [128, 512], F32)) for i in range(8)]
        ps_ctr = [0]

        def next_ps(excl=()):
            while True:
                i = ps_ctr[0] % 8
                ps_ctr[0] += 1
                if i not in excl:
                    return i

        ctr = {'wt': 0, 'wo': 0, 'ev': 0, 'evb': 0, 'tmpf': 0, 'sq': 0, 'sm12': 0}

        def rot(name, n):
            i = ctr[name] % n
            ctr[name] += 1
            return i

        def affine_mask(tile_ap, key, pattern, base, cm):
            S.op('pool', lambda e: e.memset(tile_ap, 1.0), writes=[key])
            S.op('pool', lambda e: e.affine_select(out=tile_ap, in_=tile_ap, pattern=pattern,
                                                   compare_op=ALU.is_ge, fill=0.0, base=base,
                                                   channel_multiplier=cm),
                 reads=[key], writes=[key])

        S.op('pool', lambda e: e.memset(IDF[:], 1.0), writes=['IDF'])
        S.op('pool', lambda e: e.affine_select(out=IDF[:], in_=IDF[:], pattern=[[-1, 128]], compare_op=ALU.is_ge,
                                               fill=0.0, base=0, channel_multiplier=1), reads=['IDF'], writes=['IDF'])
        S.op('pool', lambda e: e.affine_select(out=IDF[:], in_=IDF[:], pattern=[[1, 128]], compare_op=ALU.is_ge,
                                               fill=0.0, base=0, channel_multiplier=-1), reads=['IDF'], writes=['IDF'])
        S.op('dve', lambda e: e.tensor_copy(out=IDB[:], in_=IDF[:]), reads=['IDF'], writes=['IDB'])
        S.op('pool', lambda e: e.memset(ONESB[:], 1.0), writes=['ONESB'])
        affine_mask(CMASK[:], 'CMASK', [[1, 128]], 0, -1)
        S.op('pool', lambda e: e.memset(SEQM[:], 1.0), writes=['SEQM'])
        S.op('pool', lambda e: e.affine_select(out=SEQM[:], in_=SEQM[:], pattern=[[-8, SSQ]], compare_op=ALU.is_ge,
                                               fill=0.0, base=0, channel_multiplier=1), reads=['SEQM'], writes=['SEQM'])
        S.op('pool', lambda e: e.affine_select(out=SEQM[:], in_=SEQM[:], pattern=[[8, SSQ]], compare_op=ALU.is_ge,
                                               fill=0.0, base=7, channel_multiplier=-1), reads=['SEQM'], writes=['SEQM'])
        S.op('dve', lambda e: e.tensor_copy(out=E8[:], in_=IDF[0:8, 0:8].unsqueeze(1).to_broadcast([8, SSQ, 8])),
             reads=['IDF'], writes=['E8'])

        for i in range(VEC_TOT // 128):
            t = TMPF[rot('tmpf', 2)]
            tk = 'TMPF%d' % ((ctr['tmpf'] - 1) % 2)
            S.dma('sp', t[:, 0:128], vecs[i * 128:(i + 1) * 128, :], writes=[tk])
            b = next_ps()
            S.op('pe', lambda e, b=b, t=t: e.transpose(out=PS[b][:, 0:128], in_=t[:, 0:128], identity=IDF[:]),
                 reads=[tk, 'IDF'], writes=[('ps', b)])
            S.op('dve', lambda e, b=b, i=i: e.tensor_copy(out=COLS[:, i * 128:(i + 1) * 128], in_=PS[b][:, 0:128]),
                 reads=[('ps', b)], writes=['COLS'])

        def col(name, j=0, n=1):
            r0, nr = VEC_LAY[name]
            return COLS[:, r0 + j:r0 + j + n]

        S.dma('pool', BINV[:], a_b_in_r[2048:4096].partition_broadcast(128), writes=['BINV'])
        S.dma('pool', LNW[:], a_ln_w.partition_broadcast(128), writes=['LNW'])
        S.dma('pool', LNB[:], a_ln_b.partition_broadcast(128), writes=['LNB'])
        S.dma('pool', BSR[:], a_b_s.rearrange("(o g) t -> o g t", o=1), writes=['BSR'])
        S.op('dve', lambda e: e.tensor_copy(out=BSS[:].rearrange("o g (b t) -> o g b t", t=8),
                                            in_=BSR[:, :, 0:8].unsqueeze(2).to_broadcast([1, 16, SSQ, 8])),
             reads=['BSR'], writes=['BSS'])

        t = TMPF[rot('tmpf', 2)]
        tk = 'TMPF%d' % ((ctr['tmpf'] - 1) % 2)
        S.dma('sp', t[:].rearrange("p (g s) -> p g s", g=16), a_w_s.rearrange("g t s -> t g s"), writes=[tk])
        for g in range(16):
            b = next_ps()
            S.op('pe', lambda e, b=b, g=g, t=t: e.transpose(out=PS[b][:, 0:128], in_=t[:, g * 128:(g + 1) * 128], identity=IDF[:]),
                 reads=[tk, 'IDF'], writes=[('ps', b)])
            S.op('dve', lambda e, b=b, g=g: e.tensor_tensor(out=WST[:, g, :], in0=PS[b][:, 0:128], in1=CMASK[:], op=ALU.mult),
                 reads=[('ps', b), 'CMASK'], writes=['WST'])
        b = next_ps()
        S.op('pe', lambda e, b=b: e.matmul(PS[b][0:NS, 0:128], lhsT=E8[:].rearrange("s b t -> s (b t)"),
                                           rhs=WST[0:8, :, 0:8], start=True, stop=True),
             reads=['E8', 'WST'], writes=[('ps', b)])
        S.op('dve', lambda e, b=b: e.tensor_tensor(
            out=BDS[:].rearrange("p g (b t) -> p g b t", t=8),
            in0=PS[b][0:NS, 0:128].rearrange("p (g t) -> p g t", t=8).unsqueeze(2).to_broadcast([NS, 16, SSQ, 8]),
            in1=SEQM[:].unsqueeze(1).unsqueeze(3).to_broadcast([NS, 16, SSQ, 8]), op=ALU.mult),
            reads=[('ps', b), 'SEQM'], writes=['BDS'])

        class WTile:
            def __init__(self, halves):
                self.halves = halves

            def c(self, k, mi):
                tl, key = self.halves[mi // 2]
                o = (mi % 2) * 128
                return tl[:, k, o:o + 128]

            def key(self, mi):
                return self.halves[mi // 2][1]

        def load_w(wap, r0, nk, c0, ncols=512, pool='wt'):
            if pool == 'wo':
                i = rot('wo', 2)
                tl, key = WO[i], 'WO%d' % i
                src = wap[r0:r0 + nk * 128, c0:c0 + ncols].rearrange("(k p) c -> p k c", p=128)
                S.dma('pool', tl[:, 0:nk, 0:ncols], src, writes=[key])
                return tl, key
            halves = []
            for h0 in range(0, ncols, 256):
                w = min(256, ncols - h0)
                i = rot('wt', 4)
                tl, key = WT[i], 'WT%d' % i
                src = wap[r0:r0 + nk * 128, c0 + h0:c0 + h0 + w].rearrange("(k p) c -> p k c", p=128)
                S.dma('pool', tl[:, 0:nk, 0:w], src, writes=[key])
                halves.append((tl, key))
            return WTile(halves), None

        t = TMPF[rot('tmpf', 2)]
        tk = 'TMPF%d' % ((ctr['tmpf'] - 1) % 2)
        S.dma('sp', t[0:17, :], call, writes=[tk])
        S.op('act', lambda e, t=t: e.activation(out=t[0:17, :], in_=t[0:17, :], func=AF.Silu), reads=[tk], writes=[tk])
        for k4 in range(4):
            b = next_ps()
            for kk in range(4):
                k = k4 * 4 + kk
                S.op('pe', lambda e, b=b, k=k, kk=kk, t=t: e.transpose(out=PS[b][:, kk * 17:(kk + 1) * 17],
                                                                      in_=t[0:17, k * 128:(k + 1) * 128], identity=IDF[0:17, 0:17]),
                     reads=[tk, 'IDF'], writes=[('ps', b)])
            S.op('dve', lambda e, b=b, k4=k4: e.tensor_copy(out=CT[:, k4 * 4:(k4 + 1) * 4, :].rearrange("p k c -> p (k c)"),
                                                           in_=PS[b][:, 0:68]),
                 reads=[('ps', b)], writes=['CT'])
        for l in range(2):
            for nb in range(24):
                tl, key = load_w(mod_w[l], 0, KC, nb * 512)
                b = next_ps()
                for mi in range(4):
                    for k in range(KC):
                        S.op('pe', lambda e, b=b, mi=mi, k=k, tl=tl: e.matmul(
                            PS[b][:, mi * 17:(mi + 1) * 17], lhsT=tl.c(k, mi), rhs=CT[:, k, :],
                            start=(k == 0), stop=(k == KC - 1)),
                            reads=[tl.key(mi), 'CT'], writes=[('ps', b)])
                for mi in range(4):
                    m = nb * 4 + mi
                    S.op('act', lambda e, b=b, mi=mi, m=m, l=l: e.activation(
                        out=MOD[l][:, m, :], in_=PS[b][:, mi * 17:(mi + 1) * 17], func=AF.Identity,
                        bias=col('mod_b%d' % l, m), scale=1.0),
                        reads=[('ps', b), 'COLS'], writes=['MOD%d' % l])

        def load_x(hf):
            for tt in range(NTILE):
                rows = 128 if tt < NCH else NS
                t = TMPF[rot('tmpf', 2)]
                tk = 'TMPF%d' % ((ctr['tmpf'] - 1) % 2)
                if tt < NCH:
                    src = xp[hf * NPT + tt * 128: hf * NPT + (tt + 1) * 128, :]
                else:
                    src = xs[hf * NS:(hf + 1) * NS, :]
                S.dma('sp', t[0:rows, :], src, writes=[tk])
                for k4 in range(4):
                    b = next_ps()
                    for kk in range(4):
                        k = k4 * 4 + kk
                        S.op('pe', lambda e, b=b, k=k, kk=kk, t=t, rows=rows: e.transpose(
                            out=PS[b][:, kk * 128:kk * 128 + rows], in_=t[0:rows, k * 128:(k + 1) * 128],
                            identity=IDF[0:rows, 0:rows]),
                            reads=[tk, 'IDF'], writes=[('ps', b)])
                    S.op('dve', lambda e, b=b, k4=k4, tt=tt, rows=rows: e.tensor_copy(
                        out=XT[:, k4 * 4:(k4 + 1) * 4, tt * 128:tt * 128 + rows],
                        in_=PS[b][:].rearrange("p (k t) -> p k t", t=128)[:, :, 0:rows]),
                        reads=[('ps', b)], writes=[('XT', k4 * 4 + kk_, tb_of(tt * 128)) for kk_ in range(4)])

        def tb_of(tok):
            return 0 if tok < TBS[0][1] else 1

        def xk(m):
            return [('XT', m, 0), ('XT', m, 1)]

        def compute_rstd(eps):
            bs = [next_ps() for _ in TBS]
            for k in range(KC):
                i = rot('sq', 2)
                S.op('act', lambda e, i=i, k=k: e.activation(out=SQ[i][:], in_=XT[:, k, :], func=AF.Square),
                     reads=xk(k), writes=['SQ%d' % i])
                for bi, (t0, t1) in enumerate(TBS):
                    S.op('pe', lambda e, b=bs[bi], i=i, t0=t0, t1=t1, k=k: e.matmul(
                        PS[b][:, 0:t1 - t0], lhsT=ONESB[:], rhs=SQ[i][:, t0:t1], start=(k == 0), stop=(k == KC - 1)),
                        reads=['SQ%d' % i, 'ONESB'], writes=[('ps', bs[bi])])
            for bi, (t0, t1) in enumerate(TBS):
                b = bs[bi]
                S.op('dve', lambda e, b=b, t0=t0, t1=t1: e.tensor_scalar(
                    out=RSTD[:, t0:t1], in0=PS[b][:, 0:t1 - t0], scalar1=1.0 / D, scalar2=eps, op0=ALU.mult, op1=ALU.add),
                    reads=[('ps', b)], writes=['RSTD'])
            S.op('act', lambda e: e.activation(out=RSTD[:], in_=RSTD[:], func=AF.Ln), reads=['RSTD'], writes=['RSTD'])
            S.op('act', lambda e: e.activation(out=RSTD[:], in_=RSTD[:], func=AF.Exp, scale=-0.5), reads=['RSTD'], writes=['RSTD'])

        def norm_mod(hf, l, which, nw_name):
            compute_rstd(NORM_EPS)
            sh0, sc0 = which * 48, which * 48 + 16
            for k in range(KC):
                S.op('dve', lambda e, k=k: e.tensor_scalar(
                    out=ACOL[:, k, :], in0=MOD[l][:, sc0 + k, :], scalar1=1.0, scalar2=col(nw_name, k),
                    op0=ALU.add, op1=ALU.mult),
                    reads=['MOD%d' % l, 'COLS'], writes=['ACOL'])
            for k in range(KC):
                i = rot('tmpf', 2)
                t, tk = TMPF[i], 'TMPF%d' % i
                S.op('dve', lambda e, t=t, k=k: e.tensor_tensor(out=t[:, 0:NTOK], in0=XT[:, k, :], in1=RSTD[:], op=ALU.mult),
                     reads=xk(k) + ['RSTD'], writes=[tk])
                S.op('act', lambda e, t=t, k=k: e.activation(
                    out=HT[:, k, 0:NPT], in_=t[:, 0:NPT], func=AF.Identity,
                    bias=MOD[l][:, sh0 + k, 0:1], scale=ACOL[:, k, 0:1]),
                    reads=[tk, 'ACOL', 'MOD%d' % l], writes=[('HT', k)])
                c0 = 1 + hf * SSQ
                S.op('dve', lambda e, t=t, k=k, c0=c0: e.tensor_tensor(
                    out=t[:, NPT:NTOK].rearrange("p (b t) -> p b t", t=8),
                    in0=t[:, NPT:NTOK].rearrange("p (b t) -> p b t", t=8),
                    in1=ACOL[:, k, c0:c0 + SSQ].unsqueeze(2).to_broadcast([128, SSQ, 8]), op=ALU.mult),
                    reads=[tk, 'ACOL'], writes=[tk])
                S.op('dve', lambda e, t=t, k=k, c0=c0: e.tensor_tensor(
                    out=HT[:, k, NPT:NTOK].rearrange("p (b t) -> p b t", t=8),
                    in0=t[:, NPT:NTOK].rearrange("p (b t) -> p b t", t=8),
                    in1=MOD[l][:, sh0 + k, c0:c0 + SSQ].unsqueeze(2).to_broadcast([128, SSQ, 8]), op=ALU.add),
                    reads=[tk, 'MOD%d' % l], writes=[('HT', k)])

        def ht_keys():
            return [('HT', k) for k in range(KC)]

        def resid_evac(hf, l, gate0, b, m, t0, t1):
            n = t1 - t0
            npr = min(t1, NPT) - t0
            S.op('dve', lambda e: e.scalar_tensor_tensor(
                out=XT[:, m, t0:t0 + npr], in0=PS[b][:, 0:npr], scalar=MOD[l][:, gate0 + m, 0:1],
                in1=XT[:, m, t0:t0 + npr], op0=ALU.mult, op1=ALU.add),
                reads=[('ps', b), 'MOD%d' % l, ('XT', m, tb_of(t0))], writes=[('XT', m, tb_of(t0))])
            if t1 > NPT:
                c0 = 1 + hf * SSQ
                i = rot('ev', 3)
                S.op('dve', lambda e: e.tensor_tensor(
                    out=EV[i][:, 0:NS].rearrange("p (b t) -> p b t", t=8),
                    in0=PS[b][:, npr:n].rearrange("p (b t) -> p b t", t=8),
                    in1=MOD[l][:, gate0 + m, c0:c0 + SSQ].unsqueeze(2).to_broadcast([128, SSQ, 8]), op=ALU.mult),
                    reads=[('ps', b), 'MOD%d' % l], writes=['EV%d' % i])
                S.op('dve', lambda e: e.tensor_tensor(out=XT[:, m, NPT:NTOK], in0=XT[:, m, NPT:NTOK], in1=EV[i][:, 0:NS], op=ALU.add),
                     reads=['EV%d' % i, ('XT', m, 1)], writes=[('XT', m, 1)])

        def out_proj_partial(hf, l, gate0, wap, r0):
            for cb in range(4):
                tl, key = load_w(wap, r0, 4, cb * 512, pool='wo')
                for mi in range(4):
                    m = cb * 4 + mi
                    for (t0, t1) in TBS:
                        b = next_ps()
                        for k in range(4):
                            S.op('pe', lambda e, b=b, k=k, mi=mi, tl=tl, t0=t0, t1=t1: e.matmul(
                                PS[b][:, 0:t1 - t0], lhsT=tl[:, k, mi * 128:(mi + 1) * 128], rhs=GB[:, k, t0:t1],
                                start=(k == 0), stop=(k == 3)),
                                reads=[key, 'GB'], writes=[('ps', b)])
                        resid_evac(hf, l, gate0, b, m, t0, t1)

        def gmlp(hf):
            hk = ht_keys()
            for nb in range(4):
                tl, key = load_w(a_w_in, 0, KC, 2048 + nb * 512)
                for tt in range(NTILE):
                    rows = 128 if tt < NCH else NS
                    b = next_ps()
                    for hh, (th, kh) in enumerate(tl.halves):
                        for k in range(KC):
                            S.op('pe', lambda e, b=b, k=k, th=th, hh=hh, tt=tt, rows=rows: e.matmul(
                                PS[b][0:rows, hh * 256:(hh + 1) * 256], lhsT=HT[:, k, tt * 128:tt * 128 + rows], rhs=th[:, k, :],
                                start=(k == 0), stop=(k == KC - 1)),
                                reads=[kh, ('HT', k)], writes=[('ps', b)])
                    i = rot('ev', 3)
                    S.op('dve', lambda e, b=b, i=i, nb=nb, rows=rows: e.tensor_tensor(
                        out=EV[i][0:rows, :], in0=PS[b][0:rows, :], in1=BINV[0:rows, nb * 512:(nb + 1) * 512], op=ALU.add),
                        reads=[('ps', b), 'BINV'], writes=['EV%d' % i])
                    S.op('act', lambda e, i=i, rows=rows: e.activation(out=EV[i][0:rows, :], in_=EV[i][0:rows, :], func=AF.Gelu),
                         reads=['EV%d' % i], writes=['EV%d' % i])
                    S.op('dve', lambda e, i=i, tt=tt, nb=nb, rows=rows: e.bn_stats(out=STATS[0:rows, tt, nb, :], in_=EV[i][0:rows, :]),
                         reads=['EV%d' % i], writes=[('STATS', tt)])
                    S.op('act', lambda e, i=i, tt=tt, nb=nb, rows=rows: e.activation(
                        out=VN[0:rows, tt, nb * 512:(nb + 1) * 512], in_=EV[i][0:rows, :], func=AF.Copy),
                        reads=['EV%d' % i], writes=[('VN', tt)])
            for tt in range(NTILE):
                rows = 128 if tt < NCH else NS
                S.op('dve', lambda e, tt=tt, rows=rows: e.bn_aggr(out=MV[0:rows, tt, :], in_=STATS[0:rows, tt, :, :].rearrange("p a b -> p (a b)")),
                     reads=[('STATS', tt)], writes=[('MV', tt)])
                S.op('dve', lambda e, tt=tt, rows=rows: e.tensor_scalar(out=SMALL[0:rows, tt:tt + 1], in0=MV[0:rows, tt, 1:2], scalar1=LN_EPS, scalar2=None, op0=ALU.add),
                     reads=[('MV', tt)], writes=[('SM', tt)])
                S.op('act', lambda e, tt=tt, rows=rows: e.activation(out=SMALL[0:rows, tt:tt + 1], in_=SMALL[0:rows, tt:tt + 1], func=AF.Sqrt),
                     reads=[('SM', tt)], writes=[('SM', tt)])
                S.op('dve', lambda e, tt=tt, rows=rows: e.reciprocal(out=SMALL[0:rows, tt:tt + 1], in_=SMALL[0:rows, tt:tt + 1]),
                     reads=[('SM', tt)], writes=[('SM', tt)])
                i = rot('tmpf', 2)
                t, tk = TMPF[i], 'TMPF%d' % i
                S.op('dve', lambda e, tt=tt, rows=rows, t=t: e.tensor_scalar(
                    out=t[0:rows, :], in0=VN[0:rows, tt, :], scalar1=MV[0:rows, tt, 0:1], scalar2=SMALL[0:rows, tt:tt + 1],
                    op0=ALU.subtract, op1=ALU.mult),
                    reads=[('VN', tt), ('MV', tt), ('SM', tt)], writes=[tk])
                S.op('dve', lambda e, rows=rows, t=t: e.tensor_tensor(out=t[0:rows, :], in0=t[0:rows, :], in1=LNW[0:rows, :], op=ALU.mult),
                     reads=[tk, 'LNW'], writes=[tk])
                S.op('dve', lambda e, rows=rows, t=t: e.tensor_tensor(out=t[0:rows, :], in0=t[0:rows, :], in1=LNB[0:rows, :], op=ALU.add),
                     reads=[tk, 'LNB'], writes=[tk])
                S.op('act', lambda e, tt=tt, rows=rows, t=t: e.activation(out=VN[0:rows, tt, :], in_=t[0:rows, :], func=AF.Copy),
                     reads=[tk], writes=[('VN', tt)])
                if tt == NCH:
                    S.dma('sp', v_s[hf * NS:(hf + 1) * NS, :], t[0:NS, :], reads=[tk])
                elif tt == NCH - 1 and hf == NPASS - 1:
                    S.dma('sp', v_p, t[:, :], reads=[tk])
            for nb in range(4):
                tl, key = load_w(a_w_in, 0, KC, nb * 512)
                for mi in range(4):
                    m = nb * 4 + mi
                    for (t0, t1) in TBS:
                        n = t1 - t0
                        bu = next_ps()
                        for k in range(KC):
                            S.op('pe', lambda e, b=bu, k=k, mi=mi, tl=tl, t0=t0, t1=t1: e.matmul(
                                PS[b][:, 0:t1 - t0], lhsT=tl.c(k, mi), rhs=HT[:, k, t0:t1],
                                start=(k == 0), stop=(k == KC - 1)),
                                reads=[tl.key(mi), ('HT', k)], writes=[('ps', bu)])
                        i = rot('ev', 3)
                        S.op('act', lambda e, b=bu, i=i, n=n, m=m: e.activation(
                            out=EV[i][:, 0:n], in_=PS[b][:, 0:n], func=AF.Gelu, bias=col('a_b_in', m), scale=1.0),
                            reads=[('ps', bu), 'COLS'], writes=['EV%d' % i])
                        bs_ = next_ps()
                        for tt in range(t0 // 128, (t1 + 127) // 128):
                            rows = 128 if tt < NCH else NS
                            c0 = tt * 128 - t0
                            rhs = WST[:, m, :] if tt < NCH else BDS[:, m, :]
                            brow = BSR[:, m, :] if tt < NCH else BSS[:, m, :]
                            S.op('pe', lambda e, b=bs_, tt=tt, rows=rows, c0=c0, rhs=rhs, m=m: e.matmul(
                                PS[b][:, c0:c0 + rows], lhsT=VN[0:rows, tt, m * 128:(m + 1) * 128], rhs=rhs,
                                start=True, stop=False),
                                reads=[('VN', tt), 'WST', 'BDS'], writes=[('ps', bs_)])
                            S.op('pe', lambda e, b=bs_, rows=rows, c0=c0, brow=brow: e.matmul(
                                PS[b][:, c0:c0 + rows], lhsT=ONESB[0:1, :], rhs=brow, start=False, stop=True),
                                reads=['ONESB', 'BSR', 'BSS'], writes=[('ps', bs_)])
                        S.op('dve', lambda e, b=bs_, i=i, n=n, mi=mi, t0=t0, t1=t1: e.tensor_tensor(
                            out=GB[:, mi, t0:t1], in0=PS[b][:, 0:n], in1=EV[i][:, 0:n], op=ALU.mult),
                            reads=[('ps', bs_), 'EV%d' % i], writes=['GB'])
                out_proj_partial(hf, 0, 32, a_w_out, nb * 512)


        affine_mask(UT[:], 'UT', [[1, 128]], 0, -1)
        S.op('pool', lambda e: e.memset(ONESF[:], 1.0), writes=['ONESF'])
        S.op('dve', lambda e: e.tensor_copy(out=SEQ32[:].rearrange("p (b t) -> p b t", t=8),
                                            in_=SEQM[:].unsqueeze(2).to_broadcast([NS, SSQ, 8])),
             reads=['SEQM'], writes=['SEQ32'])
        S.op('dve', lambda e: e.tensor_tensor(out=USB[:], in0=SEQ32[:], in1=UT[0:NS, 0:NS], op=ALU.mult),
             reads=['SEQ32', 'UT'], writes=['USB'])
        S.op('pool', lambda e: e.memset(MASK3[:], 1.0), writes=['MASK3'])
        S.op('pool', lambda e: e.affine_select(out=MASK3[:], in_=MASK3[:], pattern=[[-8, SSQ], [1, NS]], compare_op=ALU.is_ge,
                                               fill=0.0, base=0, channel_multiplier=0), reads=['MASK3'], writes=['MASK3'])
        S.op('pool', lambda e: e.affine_select(out=MASK3[:], in_=MASK3[:], pattern=[[8, SSQ], [-1, NS]], compare_op=ALU.is_ge,
                                               fill=0.0, base=7, channel_multiplier=0), reads=['MASK3'], writes=['MASK3'])
        S.dma('sp', DTB[:], b_dtb.partition_broadcast(128), writes=['DTB'])
        S.dma('sp', AROW[:], b_alog.partition_broadcast(128), writes=['AROW'])
        S.dma('sp', DROW[:], b_dsk.partition_broadcast(128), writes=['DROW'])
        S.op('act', lambda e: e.activation(out=AROW[:], in_=AROW[:], func=AF.Exp), reads=['AROW'], writes=['AROW'])
        S.op('dve', lambda e: e.tensor_scalar(out=AROW[:], in0=AROW[:], scalar1=-1.0, scalar2=None, op0=ALU.mult),
             reads=['AROW'], writes=['AROW'])
        S.op('pool', lambda e: e.memset(CTAIL[:], 0.0), writes=['CTAIL'])

        LIVE = [set()]

        def tr_store(src_ap, nrow, dst_ap, rkeys):
            b = next_ps(LIVE[0])
            S.op('pe', lambda e: e.transpose(out=PS[b][0:nrow, 0:128], in_=src_ap, identity=IDF[:]),
                 reads=list(rkeys) + ['IDF'], writes=[('ps', b)])
            j = rot('sm12', 2)
            S.op('dve', lambda e: e.tensor_copy(out=SM12[j][0:nrow, :], in_=PS[b][0:nrow, 0:128]),
                 reads=[('ps', b)], writes=['SM12_%d' % j])
            S.dma('sp', dst_ap, SM12[j][0:nrow, :], reads=['SM12_%d' % j])

        def ssd(hf):
            last = (hf == NPASS - 1)
            rows_of = lambda tt: 128 if tt < NCH else NS
            for cb in range(3):
                t, tk = TMPF[0], 'TMPF0'
                S.dma('sp', t[0:SSQ * 3, :], st_conv[hf * SSQ * 3:(hf + 1) * SSQ * 3, cb * 2048:(cb + 1) * 2048], writes=[tk])
                for c4 in range(4):
                    b = next_ps()
                    for cc in range(4):
                        ch = c4 * 4 + cc
                        S.op('pe', lambda e, b=b, cc=cc, ch=ch, t=t: e.transpose(
                            out=PS[b][:, cc * 12:(cc + 1) * 12], in_=t[0:SSQ * 3, ch * 128:(ch + 1) * 128], identity=IDF[0:SSQ * 3, 0:SSQ * 3]),
                            reads=[tk, 'IDF'], writes=[('ps', b)])
                    S.op('dve', lambda e, b=b, cb=cb, c4=c4: e.tensor_copy(
                        out=SCV[:, cb * 16 + c4 * 4: cb * 16 + c4 * 4 + 4, :].rearrange("p c k -> p (c k)"), in_=PS[b][:, 0:48]),
                        reads=[('ps', b)], writes=['SCV'])
            tl, key = load_w(b_w_in, 0, KC, 10240, ncols=64)
            for tt in range(NTILE):
                rows = rows_of(tt)
                b = next_ps()
                for k in range(KC):
                    S.op('pe', lambda e, b=b, k=k, tt=tt, rows=rows, tl=tl: e.matmul(
                        PS[b][0:rows, 0:64], lhsT=HT[:, k, tt * 128:tt * 128 + rows], rhs=tl.halves[0][0][:, k, 0:64],
                        start=(k == 0), stop=(k == KC - 1)), reads=[tl.halves[0][1], ('HT', k)], writes=[('ps', b)])
                S.op('dve', lambda e, b=b, tt=tt, rows=rows: e.tensor_tensor(out=DT[0:rows, tt, :], in0=PS[b][0:rows, 0:64], in1=DTB[0:rows, :], op=ALU.add),
                     reads=[('ps', b), 'DTB'], writes=['DT'])
            S.op('act', lambda e: e.activation(out=DT[:], in_=DT[:], func=AF.Exp), reads=['DT'], writes=['DT'])
            S.op('act', lambda e: e.activation(out=DT[:], in_=DT[:], func=AF.Ln, bias=1.0, scale=1.0), reads=['DT'], writes=['DT'])
            S.op('dve', lambda e: e.tensor_tensor(out=DA[:], in0=DT[:], in1=AROW[:].unsqueeze(1).to_broadcast([128, NTILE, 64]), op=ALU.mult),
                 reads=['DT', 'AROW'], writes=['DA'])
            for tt in range(NTILE):
                rows = rows_of(tt)
                um = UT if tt < NCH else USB
                om = ONESF if tt < NCH else SEQ32
                b = next_ps()
                S.op('pe', lambda e, b=b, tt=tt, rows=rows, um=um: e.matmul(PS[b][0:rows, 0:64], lhsT=um[0:rows, 0:rows], rhs=DA[0:rows, tt, :], start=True, stop=True),
                     reads=['DA', 'UT', 'USB'], writes=[('ps', b)])
                S.op('pe', lambda e, b=b, tt=tt, rows=rows, om=om: e.matmul(PS[b][0:rows, 64:128], lhsT=om[0:rows, 0:rows], rhs=DA[0:rows, tt, :], start=True, stop=True),
                     reads=['DA', 'ONESF', 'SEQ32'], writes=[('ps', b)])
                S.op('dve', lambda e, b=b, tt=tt, rows=rows: e.tensor_copy(out=ACS[0:rows, tt, :], in_=PS[b][0:rows, 0:64]), reads=[('ps', b)], writes=['ACS'])
                S.op('dve', lambda e, b=b, tt=tt, rows=rows: e.tensor_copy(out=TOT[0:rows, tt, :], in_=PS[b][0:rows, 64:128]), reads=[('ps', b)], writes=['TOT'])
            S.op('act', lambda e: e.activation(out=EXPA[:], in_=ACS[:], func=AF.Exp), reads=['ACS'], writes=['EXPA'])
            S.op('act', lambda e: e.activation(out=DEC[:], in_=TOT[:], func=AF.Exp), reads=['TOT'], writes=['DEC'])
            S.op('dve', lambda e: e.tensor_tensor(out=DTE[:], in0=TOT[:], in1=ACS[:], op=ALU.subtract), reads=['TOT', 'ACS'], writes=['DTE'])
            S.op('act', lambda e: e.activation(out=DTE[:], in_=DTE[:], func=AF.Exp), reads=['DTE'], writes=['DTE'])

            S.barrier()
            for g in range(8):
                tl, key = load_w(b_w_in, 0, KC, g * 512)
                for tt in range(NTILE):
                    rows = rows_of(tt)
                    b = next_ps()
                    for hh, (th, kh) in enumerate(tl.halves):
                        for k in range(KC):
                            S.op('pe', lambda e, b=b, k=k, tt=tt, rows=rows, th=th, hh=hh: e.matmul(
                                PS[b][0:rows, hh * 256:(hh + 1) * 256], lhsT=HT[:, k, tt * 128:tt * 128 + rows], rhs=th[:, k, :],
                                start=(k == 0), stop=(k == KC - 1)), reads=[kh, ('HT', k)], writes=[('ps', b)])
                    S.op('act', lambda e, b=b, tt=tt, rows=rows: e.activation(out=SZ[0:rows, tt, :], in_=PS[b][0:rows, :], func=AF.Silu),
                         reads=[('ps', b)], writes=['SZ'])
                tlx, keyx = load_w(b_w_in, 0, KC, 4096 + g * 512)
                cinfo = []
                for ci in range(6):
                    if ci < 4:
                        cinfo.append(dict(c0=ci * 128, ch=g * 4 + ci, kind='x', mi=ci))
                    elif ci == 4:
                        cinfo.append(dict(c0=0, ch=32 + g, kind='B', mi=0))
                    else:
                        cinfo.append(dict(c0=0, ch=40 + g, kind='C', mi=1))

                def stage_p(ci):
                    inf = cinfo[ci]
                    if ci < 4:
                        tl = tlx
                    elif ci == 4:
                        tl, _ = load_w(b_w_in, 0, KC, 8192 + g * 128, ncols=128)
                    else:
                        tl, _ = load_w(b_w_in, 0, KC, 9216 + g * 128, ncols=128)
                    c0 = inf['c0']
                    banks = []
                    for (t0, t1) in TBS:
                        b = next_ps(live_banks)
                        banks.append(b)
                        for k in range(KC):
                            S.op('pe', lambda e, b=b, k=k, tl=tl, c0=c0, t0=t0, t1=t1: e.matmul(
                                PS[b][:, 0:t1 - t0], lhsT=tl.c(k, c0 // 128), rhs=HT[:, k, t0:t1],
                                start=(k == 0), stop=(k == KC - 1)), reads=[tl.key(c0 // 128), ('HT', k)], writes=[('ps', b)])
                    inf['banks'] = banks
                    live_banks.update(banks)

                def stage_q(ci):
                    inf = cinfo[ci]
                    ch, kind, mi = inf['ch'], inf['kind'], inf['mi']
                    S.op('dve', lambda e, ch=ch: e.tensor_copy(out=RAW[:, 0:3], in_=CTAIL[:, ch, :]), reads=['CTAIL'], writes=['RAW'])
                    S.op('dve', lambda e, ch=ch: e.tensor_copy(
                        out=RAW[:, 3 + NPT:].rearrange("p (b t) -> p b t", t=11)[:, :, 0:3],
                        in_=SCV[:, ch, :].rearrange("p (b k) -> p b k", k=3)), reads=['SCV'], writes=['RAW'])
                    for bi, (t0, t1) in enumerate(TBS):
                        n = t1 - t0
                        npr = min(t1, NPT) - t0
                        b = inf['banks'][bi]
                        S.op('act', lambda e, b=b, t0=t0, npr=npr: e.activation(out=RAW[:, 3 + t0:3 + t0 + npr], in_=PS[b][:, 0:npr], func=AF.Copy),
                             reads=[('ps', b)], writes=['RAW'])
                        if t1 > NPT:
                            S.op('act', lambda e, b=b, npr=npr, n=n: e.activation(
                                out=RAW[:, 3 + NPT:].rearrange("p (b t) -> p b t", t=11)[:, :, 3:11],
                                in_=PS[b][:, npr:n].rearrange("p (b t) -> p b t", t=8), func=AF.Copy),
                                reads=[('ps', b)], writes=['RAW'])
                    for b in inf['banks']:
                        live_banks.discard(b)
                    S.op('dve', lambda e, ch=ch: e.tensor_copy(out=CTAIL[:, ch, :], in_=RAW[:, NPT:NPT + 3]), reads=['RAW'], writes=['CTAIL'])
                    j2 = rot('ev', 3)
                    S.op('dve', lambda e, j2=j2: e.tensor_copy(
                        out=EV[j2][:, 0:SSQ * 3].rearrange("p (b k) -> p b k", k=3),
                        in_=RAW[:, 3 + NPT:].rearrange("p (b t) -> p b t", t=11)[:, :, 8:11]), reads=['RAW'], writes=['EV%d' % j2])
                    tr_store(EV[j2][:, 0:SSQ * 3], SSQ * 3, conv_s[hf * SSQ * 3:(hf + 1) * SSQ * 3, ch * 128:(ch + 1) * 128], ['EV%d' % j2])
                    if last:
                        tr_store(CTAIL[:, ch, :], 3, conv_p[:, ch * 128:(ch + 1) * 128], ['CTAIL'])
                    cw = lambda kk, ch=ch: col('conv_w', kk * 48 + ch)
                    cbias = col('conv_b', ch)
                    pr_in = lambda kk: RAW[:, kk:kk + NPT]
                    sm_in = lambda kk: RAW[:, 3 + NPT:].rearrange("p (b t) -> p b t", t=11)[:, :, kk:kk + 8]
                    pr_out = XC[:, 0:NPT]
                    sm_out = XC[:, NPT:NTOK].rearrange("p (b t) -> p b t", t=8)
                    for (oin, oout) in ((pr_in, pr_out), (sm_in, sm_out)):
                        S.op('dve', lambda e, oin=oin, oout=oout, cw=cw, cbias=cbias: e.tensor_scalar(
                            out=oout, in0=oin(0), scalar1=cw(0), scalar2=cbias, op0=ALU.mult, op1=ALU.add),
                            reads=['RAW', 'COLS'], writes=['XC'])
                        for kk in range(1, 4):
                            S.op('dve', lambda e, oin=oin, oout=oout, cw=cw, kk=kk: e.scalar_tensor_tensor(
                                out=oout, in0=oin(kk), scalar=cw(kk), in1=oout, op0=ALU.mult, op1=ALU.add),
                                reads=['RAW', 'COLS', 'XC'], writes=['XC'])
                    if kind == 'C':
                        S.op('act', lambda e: e.activation(out=BCT[:, 1, :], in_=XC[:], func=AF.Silu), reads=['XC'], writes=['BCT'])
                        return
                    if kind == 'B':
                        S.op('act', lambda e: e.activation(out=BCT[:, 0, :], in_=XC[:], func=AF.Silu), reads=['XC'], writes=['BCT'])
                    S.op('act', lambda e: e.activation(out=XC[:], in_=XC[:], func=AF.Silu), reads=['XC'], writes=['XC'])
                    b = next_ps(live_banks)
                    for tt in range(NCH):
                        S.op('pe', lambda e, b=b, tt=tt: e.transpose(out=PS[b][:, tt * 128:(tt + 1) * 128], in_=XC[:, tt * 128:(tt + 1) * 128], identity=IDF[:]),
                             reads=['XC', 'IDF'], writes=[('ps', b)])
                    b2 = next_ps(live_banks)
                    S.op('pe', lambda e, b2=b2: e.transpose(out=PS[b2][0:NS, 0:128], in_=XC[:, NPT:NTOK], identity=IDF[:]),
                         reads=['XC', 'IDF'], writes=[('ps', b2)])
                    if kind == 'x':
                        S.op('dve', lambda e, b=b, mi=mi: e.tensor_copy(out=XTOK[:, 0:NCH, mi * 128:(mi + 1) * 128],
                                                                      in_=PS[b][:, :].rearrange("p (t c) -> p t c", c=128)),
                             reads=[('ps', b)], writes=['XTOK'])
                        S.op('dve', lambda e, b2=b2, mi=mi: e.tensor_copy(out=XTOK[0:NS, NCH, mi * 128:(mi + 1) * 128], in_=PS[b2][0:NS, 0:128]),
                             reads=[('ps', b2)], writes=['XTOK'])
                    else:
                        S.op('dve', lambda e, b=b: e.tensor_copy(out=BTOK[:, 0:NCH, :], in_=PS[b][:, :].rearrange("p (t c) -> p t c", c=128)),
                             reads=[('ps', b)], writes=['BTOK'])
                        S.op('dve', lambda e, b2=b2: e.tensor_copy(out=BTOK[0:NS, NCH, :], in_=PS[b2][0:NS, 0:128]),
                             reads=[('ps', b2)], writes=['BTOK'])

                live_banks = set()
                LIVE[0] = live_banks
                stage_p(0)
                for ci in range(6):
                    if ci + 1 < 6:
                        stage_p(ci + 1)
                    stage_q(ci)
                S.dma('sp', NWR[:], b_nw[g * 512:(g + 1) * 512].partition_broadcast(128), writes=['NWR'])
                if hf == 0:
                    S.op('pool', lambda e: e.memset(HSTG[:], 0.0), writes=['HSTG'])
                else:
                    S.dma('sp', HSTG[:], hst_d[g], reads=[('hst_d', g)], writes=['HSTG'])
                S.op('act', lambda e: e.activation(out=HB[:], in_=HSTG[:], func=AF.Copy), reads=['HSTG'], writes=['HB'])
                hs = slice(g * 8, g * 8 + 8)
                held = set()
                LIVE[0] = held
                tinfo = {}

                def front(tt, g=g, hs=hs):
                    rows = rows_of(tt)
                    samp = (tt == NCH)
                    tk0 = tt * 128
                    um = USB if samp else UT
                    S.op('pool', lambda e: e.tensor_tensor(
                        out=XDT[0:rows, :].rearrange("p (r q) -> p r q", q=64), in0=XTOK[0:rows, tt, :].rearrange("p (r q) -> p r q", q=64),
                        in1=DT[0:rows, tt, hs].unsqueeze(2).to_broadcast([rows, 8, 64]), op=ALU.mult), reads=['XTOK', 'DT'], writes=['XDT'])
                    S.op('pool', lambda e: e.tensor_tensor(
                        out=WW[0:rows, :].rearrange("p (r q) -> p r q", q=64), in0=XDT[0:rows, :].rearrange("p (r q) -> p r q", q=64),
                        in1=DTE[0:rows, tt, hs].unsqueeze(2).to_broadcast([rows, 8, 64]), op=ALU.mult), reads=['XDT', 'DTE'], writes=['WW'])
                    S.op('pool', lambda e: e.tensor_tensor(
                        out=XDB[0:rows, :].rearrange("p (r q) -> p r q", q=64), in0=XTOK[0:rows, tt, :].rearrange("p (r q) -> p r q", q=64),
                        in1=DROW[0:rows, hs].unsqueeze(2).to_broadcast([rows, 8, 64]), op=ALU.mult), reads=['XTOK', 'DROW'], writes=['XDB'])
                    b = next_ps(held)
                    S.op('pe', lambda e, b=b: e.matmul(PS[b][0:rows, 0:rows], lhsT=BCT[:, 0, tk0:tk0 + rows], rhs=BCT[:, 1, tk0:tk0 + rows], start=True, stop=True),
                         reads=['BCT'], writes=[('ps', b)])
                    S.op('dve', lambda e, b=b: e.tensor_tensor(out=CBM[0:rows, 0:rows], in0=PS[b][0:rows, 0:rows], in1=um[0:rows, 0:rows], op=ALU.mult),
                         reads=[('ps', b), 'UT', 'USB'], writes=['CBM'])
                    for r4 in range(2):
                        b = next_ps(held)
                        for rr in range(4):
                            h = g * 8 + r4 * 4 + rr
                            S.op('pe', lambda e, b=b, rr=rr, h=h: e.matmul(
                                PS[b][0:rows, rr * 128:rr * 128 + rows], lhsT=DA[0:rows, tt, h:h + 1].to_broadcast([rows, rows]),
                                rhs=um[0:rows, 0:rows], start=True, stop=True), reads=['DA', 'UT', 'USB'], writes=[('ps', b)])
                        for rr in range(4):
                            r = r4 * 4 + rr
                            h = g * 8 + r
                            S.op('dve', lambda e, b=b, rr=rr, r=r, h=h: e.tensor_scalar(
                                out=EE[0:rows, r, 0:rows], in0=PS[b][0:rows, rr * 128:rr * 128 + rows], scalar1=ACS[0:rows, tt, h:h + 1], scalar2=0.0,
                                op0=ALU.subtract, op1=ALU.min), reads=[('ps', b), 'ACS'], writes=['EE'])
                    S.op('act', lambda e: e.activation(out=LT[0:rows, :, 0:rows], in_=EE[0:rows, :, 0:rows], func=AF.Exp), reads=['EE'], writes=['LT'])
                    S.op('dve', lambda e: e.tensor_tensor(out=MT[0:rows, :, 0:rows], in0=LT[0:rows, :, 0:rows],
                                                         in1=CBM[0:rows, 0:rows].unsqueeze(1).to_broadcast([rows, 8, rows]), op=ALU.mult),
                         reads=['LT', 'CBM'], writes=['MT'])
                    by = next_ps(held)
                    S.op('pe', lambda e: e.matmul(PS[by][0:rows, :], lhsT=IDB[0:rows, 0:rows], rhs=XDB[0:rows, :], start=True, stop=False),
                         reads=['IDB', 'XDB'], writes=[('ps', by)])
                    for r in range(8):
                        S.op('pe', lambda e, r=r: e.matmul(PS[by][0:rows, r * 64:(r + 1) * 64], lhsT=MT[0:rows, r, 0:rows], rhs=XDT[0:rows, r * 64:(r + 1) * 64],
                                                            start=False, stop=(r == 7)),
                             reads=['MT', 'XDT'], writes=[('ps', by)])
                    held.add(by)
                    bs3 = None
                    if not samp:
                        bs3 = next_ps(held)
                        S.op('pe', lambda e: e.matmul(PS[bs3][:, :], lhsT=BTOK[:, tt, :], rhs=WW[:, :], start=True, stop=True),
                             reads=['BTOK', 'WW'], writes=[('ps', bs3)])
                        held.add(bs3)
                    tinfo[tt] = (by, bs3)

                def back(tt, g=g, hs=hs):
                    rows = rows_of(tt)
                    samp = (tt == NCH)
                    tk0 = tt * 128
                    by, bs3 = tinfo[tt]
                    bo = next_ps(held)
                    if not samp:
                        S.op('pe', lambda e: e.matmul(PS[bo][0:rows, :], lhsT=BCT[:, 1, tk0:tk0 + rows], rhs=HB[:], start=True, stop=True),
                             reads=['BCT', 'HB'], writes=[('ps', bo)])
                    else:
                        held.add(bo)
                        S.op('dve', lambda e: e.tensor_tensor(out=CMS[:], in0=BCT[:, 1, tk0:tk0 + NS].unsqueeze(1).to_broadcast([128, SSQ, NS]), in1=MASK3[:], op=ALU.mult),
                             reads=['BCT', 'MASK3'], writes=['CMS'])
                        bd = next_ps(held)
                        for q4 in range(4):
                            h2 = g * 8 + q4 * 2
                            S.op('dve', lambda e, h2=h2: e.tensor_copy(
                                out=DAB[:].rearrange("p (h q) -> p h q", q=64), in_=DA[0:NS, tt, h2:h2 + 2].unsqueeze(2).to_broadcast([NS, 2, 64])),
                                reads=['DA'], writes=['DAB'])
                            S.op('pe', lambda e, q4=q4: e.matmul(
                                PS[bd][:, q4 * SSQ:(q4 + 1) * SSQ], lhsT=DAB[:], rhs=SEQM[:], start=True, stop=True),
                                reads=['DAB', 'SEQM'], writes=[('ps', bd)])
                        S.op('act', lambda e: e.activation(out=DECS[:].rearrange("p q b -> p (q b)"), in_=PS[bd][:, 0:4 * SSQ], func=AF.Exp),
                             reads=[('ps', bd)], writes=['DECS'])
                        for bq in range(SSQ):
                            sq_ = hf * SSQ + bq
                            jh = rot('ev', 3)
                            H0v = EV[jh][:, :].rearrange("p (q n) -> p q n", n=128)
                            hk_ = 'EV%d' % jh
                            src = st_ssm[sq_, g * 8:(g + 1) * 8].rearrange("(q h) p n -> (h p) q n", h=2)
                            S.dma('sp', H0v, src, writes=[hk_])
                            bt_ = next_ps(held)
                            for q4 in range(4):
                                S.op('pe', lambda e, bt_=bt_, q4=q4, H0v=H0v: e.transpose(out=PS[bt_][:, q4 * 128:(q4 + 1) * 128], in_=H0v[:, q4, :], identity=IDF[:]),
                                     reads=[hk_, 'IDF'], writes=[('ps', bt_)])
                            S.op('act', lambda e, bt_=bt_: e.activation(out=H0T[:], in_=PS[bt_][:], func=AF.Copy), reads=[('ps', bt_)], writes=['H0T'])
                            S.op('pe', lambda e, bq=bq: e.matmul(PS[bo][0:NS, :], lhsT=CMS[:, bq, :], rhs=H0T[:], start=(bq == 0), stop=(bq == SSQ - 1)),
                                 reads=['CMS', 'H0T'], writes=[('ps', bo)])
                            S.op('dve', lambda e, bq=bq: e.tensor_scalar(out=WM[:], in0=WW[0:NS, :], scalar1=SEQM[:, bq:bq + 1], scalar2=None, op0=ALU.mult),
                                 reads=['WW', 'SEQM'], writes=['WM'])
                            bn = next_ps(held)
                            for q4 in range(4):
                                S.op('pe', lambda e, bn=bn, q4=q4: e.matmul(PS[bn][:, q4 * 128:(q4 + 1) * 128], lhsT=WM[:, q4 * 128:(q4 + 1) * 128], rhs=BTOK[0:NS, tt, :], start=True, stop=True),
                                     reads=['WM', 'BTOK'], writes=[('ps', bn)])
                            for q4 in range(4):
                                S.op('dve', lambda e, bn=bn, q4=q4, bq=bq, H0v=H0v: e.scalar_tensor_tensor(
                                    out=H0v[:, q4, :], in0=H0v[:, q4, :], scalar=DECS[:, q4, bq:bq + 1], in1=PS[bn][:, q4 * 128:(q4 + 1) * 128],
                                    op0=ALU.mult, op1=ALU.add), reads=[hk_, 'DECS', ('ps', bn)], writes=[hk_])
                            dst = ssm_s[sq_, g * 8:(g + 1) * 8].rearrange("(q h) p n -> (h p) q n", h=2)
                            S.dma('sp', dst, H0v, reads=[hk_])
                        held.discard(bo)
                    jy, jo = rot('ev', 3), rot('ev', 3)
                    Y, YO = EV[jy], EV[jo]
                    S.op('dve', lambda e: e.tensor_tensor(
                        out=YO[0:rows, :].rearrange("p (r q) -> p r q", q=64), in0=PS[bo][0:rows, :].rearrange("p (r q) -> p r q", q=64),
                        in1=EXPA[0:rows, tt, hs].unsqueeze(2).to_broadcast([rows, 8, 64]), op=ALU.mult), reads=[('ps', bo), 'EXPA'], writes=['EV%d' % jo])
                    S.op('dve', lambda e: e.tensor_tensor(out=Y[0:rows, :], in0=PS[by][0:rows, :], in1=YO[0:rows, :], op=ALU.add),
                         reads=[('ps', by), 'EV%d' % jo], writes=['EV%d' % jy])
                    held.discard(by)
                    S.op('dve', lambda e: e.tensor_tensor(out=Y[0:rows, :], in0=Y[0:rows, :], in1=SZ[0:rows, tt, :], op=ALU.mult),
                         reads=['SZ', 'EV%d' % jy], writes=['EV%d' % jy])
                    S.op('act', lambda e: e.activation(out=YO[0:rows, :], in_=Y[0:rows, :], func=AF.Square, accum_out=SMALL[0:rows, 32:33]),
                         reads=['EV%d' % jy], writes=['EV%d' % jo, ('SM', 32)])
                    S.op('dve', lambda e: e.tensor_scalar(out=SMALL[0:rows, 32:33], in0=SMALL[0:rows, 32:33], scalar1=1.0 / 512, scalar2=NORM_EPS, op0=ALU.mult, op1=ALU.add),
                         reads=[('SM', 32)], writes=[('SM', 32)])
                    S.op('act', lambda e: e.activation(out=SMALL[0:rows, 32:33], in_=SMALL[0:rows, 32:33], func=AF.Ln), reads=[('SM', 32)], writes=[('SM', 32)])
                    S.op('act', lambda e: e.activation(out=SMALL[0:rows, 32:33], in_=SMALL[0:rows, 32:33], func=AF.Exp, scale=-0.5), reads=[('SM', 32)], writes=[('SM', 32)])
                    S.op('dve', lambda e: e.scalar_tensor_tensor(out=Y[0:rows, :], in0=Y[0:rows, :], scalar=SMALL[0:rows, 32:33], in1=NWR[0:rows, :], op0=ALU.mult, op1=ALU.mult),
                         reads=['EV%d' % jy, ('SM', 32), 'NWR'], writes=['EV%d' % jy])
                    bt2 = next_ps(held)
                    for kk in range(4):
                        S.op('pe', lambda e, kk=kk: e.transpose(out=PS[bt2][:, kk * 128:kk * 128 + rows], in_=Y[0:rows, kk * 128:(kk + 1) * 128], identity=IDF[0:rows, 0:rows]),
                             reads=['EV%d' % jy, 'IDF'], writes=[('ps', bt2)])
                    S.op('act', lambda e: e.activation(
                        out=GB[:, :, tk0:tk0 + rows], in_=PS[bt2][:].rearrange("p (k t) -> p k t", t=128)[:, :, 0:rows], func=AF.Copy),
                        reads=[('ps', bt2)], writes=['GB'])
                    if not samp:
                        S.op('dve', lambda e: e.tensor_tensor(
                            out=HSTG[:].rearrange("p (r q) -> p r q", q=64), in0=HSTG[:].rearrange("p (r q) -> p r q", q=64),
                            in1=DEC[:, tt, hs].unsqueeze(2).to_broadcast([128, 8, 64]), op=ALU.mult), reads=['HSTG', 'DEC'], writes=['HSTG'])
                        S.op('dve', lambda e: e.tensor_tensor(out=HSTG[:], in0=HSTG[:], in1=PS[bs3][:, :], op=ALU.add),
                             reads=['HSTG', ('ps', bs3)], writes=['HSTG'])
                        held.discard(bs3)
                        S.op('act', lambda e: e.activation(out=HB[:], in_=HSTG[:], func=AF.Copy), reads=['HSTG'], writes=['HB'])

                front(0)
                for tt in range(NTILE):
                    if tt + 1 < NTILE:
                        front(tt + 1)
                    back(tt)
                LIVE[0] = set()
                if not last:
                    S.dma('sp', hst_d[g], HSTG[:], reads=['HSTG'], writes=[('hst_d', g)])
                else:
                    for q4 in range(4):
                        b = next_ps()
                        S.op('pe', lambda e, b=b, q4=q4: e.transpose(out=PS[b][:, 0:128], in_=HSTG[:, q4 * 128:(q4 + 1) * 128], identity=IDF[:]),
                             reads=['HSTG', 'IDF'], writes=[('ps', b)])
                        j = rot('sm12', 2)
                        S.op('dve', lambda e, b=b, j=j: e.tensor_copy(out=SM12[j][:, :], in_=PS[b][:, 0:128]), reads=[('ps', b)], writes=['SM12_%d' % j])
                        r0 = (g * 8 + q4 * 2) * 64
                        S.dma('sp', ssm_p[r0:r0 + 128, :], SM12[j][:, :], reads=['SM12_%d' % j])
                out_proj_partial(hf, 1, 32, b_w_out, g * 512)

        def ffn(hf, l):
            for blk in range(11):
                for half in range(2):
                    tg, _ = load_w(f_w_in[l], 0, KC, blk * 512 + half * 256, ncols=256)
                    tu, _ = load_w(f_w_in[l], 0, KC, FFN_H + blk * 512 + half * 256, ncols=256)
                    for mi2 in range(2):
                        mi = half * 2 + mi2
                        for (t0, t1) in TBS:
                            n = t1 - t0
                            bg, bu = next_ps(), next_ps()
                            for k in range(KC):
                                S.op('pe', lambda e, b=bg, k=k, mi2=mi2, tl=tg, t0=t0, t1=t1: e.matmul(
                                    PS[b][:, 0:t1 - t0], lhsT=tl.c(k, mi2), rhs=HT[:, k, t0:t1],
                                    start=(k == 0), stop=(k == KC - 1)),
                                    reads=[tg.key(mi2), ('HT', k)], writes=[('ps', bg)])
                            for k in range(KC):
                                S.op('pe', lambda e, b=bu, k=k, mi2=mi2, tl=tu, t0=t0, t1=t1: e.matmul(
                                    PS[b][:, 0:t1 - t0], lhsT=tl.c(k, mi2), rhs=HT[:, k, t0:t1],
                                    start=(k == 0), stop=(k == KC - 1)),
                                    reads=[tu.key(mi2), ('HT', k)], writes=[('ps', bu)])
                            i = rot('ev', 3)
                            S.op('act', lambda e, b=bg, i=i, n=n: e.activation(out=EV[i][:, 0:n], in_=PS[b][:, 0:n], func=AF.Silu),
                                 reads=[('ps', bg)], writes=['EV%d' % i])
                            S.op('dve', lambda e, b=bu, i=i, n=n, mi=mi, t0=t0, t1=t1: e.tensor_tensor(
                                out=GB[:, mi, t0:t1], in0=PS[b][:, 0:n], in1=EV[i][:, 0:n], op=ALU.mult),
                                reads=[('ps', bu), 'EV%d' % i], writes=['GB'])
                out_proj_partial(hf, l, 80, f_w_out[l], blk * 512)

        def final_out(hf):
            compute_rstd(NORM_EPS)
            for tt in range(NTILE):
                rows = 128 if tt < NCH else NS
                i = rot('tmpf', 2)
                t, tk = TMPF[i], 'TMPF%d' % i
                for k4 in range(4):
                    j = rot('ev', 3)
                    S.op('dve', lambda e, j=j, k4=k4, tt=tt, rows=rows: e.tensor_tensor(
                        out=EV[j][:, :].rearrange("p (k t) -> p k t", t=128)[:, :, 0:rows],
                        in0=XT[:, k4 * 4:(k4 + 1) * 4, tt * 128:tt * 128 + rows],
                        in1=RSTD[:, tt * 128:tt * 128 + rows].unsqueeze(1).to_broadcast([128, 4, rows]), op=ALU.mult),
                        reads=[kx for kk_ in range(4) for kx in xk(k4 * 4 + kk_)] + ['RSTD'], writes=['EV%d' % j])
                    b = next_ps()
                    for kk in range(4):
                        k = k4 * 4 + kk
                        S.op('act', lambda e, j=j, kk=kk, k=k, rows=rows: e.activation(
                            out=EV[j][:, kk * 128:kk * 128 + rows], in_=EV[j][:, kk * 128:kk * 128 + rows],
                            func=AF.Copy, scale=col('fnw', k)),
                            reads=['EV%d' % j, 'COLS'], writes=['EV%d' % j])
                        S.op('pe', lambda e, b=b, j=j, kk=kk, rows=rows: e.transpose(
                            out=PS[b][0:rows, kk * 128:(kk + 1) * 128], in_=EV[j][:, kk * 128:kk * 128 + rows], identity=IDF[:]),
                            reads=['EV%d' % j, 'IDF'], writes=[('ps', b)])
                    S.op('dve', lambda e, b=b, k4=k4, rows=rows, t=t: e.tensor_copy(out=t[0:rows, k4 * 512:(k4 + 1) * 512], in_=PS[b][0:rows, :]),
                         reads=[('ps', b)], writes=[tk])
                if tt < NCH:
                    S.dma('sp', y_p[hf * NPT + tt * 128: hf * NPT + (tt + 1) * 128, :], t[:, :], reads=[tk])
                else:
                    S.dma('sp', y_s[hf * NS:(hf + 1) * NS, :], t[0:NS, :], reads=[tk])

        for hf in range(NPASS):
            load_x(hf)
            norm_mod(hf, 0, 0, 'nmw0')
            gmlp(hf)
            norm_mod(hf, 0, 1, 'nfw0')
            ffn(hf, 0)
            if STAGE >= 2:
                norm_mod(hf, 1, 0, 'nmw1')
                S.barrier()
                ssd(hf)
                S.barrier()
                norm_mod(hf, 1, 1, 'nfw1')
                ffn(hf, 1)
            final_out(hf)
            S.barrier()

        S.emit()
    return nc, S


_CACHE = {}


def _get_program():
    if 'nc' not in _CACHE:
        _CACHE['nc'] = build_program()
    return _CACHE['nc']


def kernel(x_prompt, x_sample, c_prompt, c_sample, state_ssm, state_conv,
           mod_w, mod_b, norm_mix_w, norm_ffn_w,
           a_w_in, a_b_in, a_ln_w, a_ln_b, a_w_s, a_b_s, a_w_out,
           b_w_in, b_conv_w, b_conv_b, b_dt_bias, b_a_log, b_d, b_norm_w, b_w_out,
           f_w_in, f_w_out, final_norm_w):
    f = lambda a: np.ascontiguousarray(np.asarray(a, dtype=np.float32))
    nc, S = _get_program()
    vecs = np.zeros((VEC_TOT, 128), np.float32)

    def put(name, arr):
        r0, n = VEC_LAY[name]
        vecs[r0:r0 + n] = np.asarray(arr, np.float32).reshape(n, 128)

    put('mod_b0', mod_b[0]); put('mod_b1', mod_b[1])
    put('nmw0', norm_mix_w[0]); put('nmw1', norm_mix_w[1])
    put('nfw0', norm_ffn_w[0]); put('nfw1', norm_ffn_w[1])
    put('a_b_in', a_b_in[0]); put('fnw', final_norm_w)
    put('conv_w', b_conv_w[0]); put('conv_b', b_conv_b[0])
    shared = dict(vecs=vecs, mod_w=f(mod_w), a_w_in=f(a_w_in[0]), a_ln_w=f(a_ln_w[0]), a_ln_b=f(a_ln_b[0]),
                  a_b_in_r=f(a_b_in[0]), a_w_s=f(a_w_s[0]), a_b_s=f(a_b_s[0]), a_w_out=f(a_w_out[0]),
                  f_w_in=f(f_w_in), f_w_out=f(f_w_out))
    shared.update(b_w_in=f(b_w_in[0]), b_w_out=f(b_w_out[0]), b_dtb=f(b_dt_bias[0]), b_alog=f(b_a_log[0]),
                  b_dsk=f(b_d[0]), b_nw=f(b_norm_w[0]))
    in_maps = []
    for c in range(8):
        m = dict(shared)
        m['st_ssm'] = f(np.asarray(state_ssm)[0, 16 * c:16 * (c + 1)])
        m['st_conv'] = f(np.asarray(state_conv)[0, 16 * c:16 * (c + 1)].reshape(48, 6144))
        m['xp'] = f(x_prompt[c % 4])
        m['xs'] = f(np.asarray(x_sample)[16 * c:16 * (c + 1)].reshape(128, D))
        m['call'] = f(np.concatenate([np.asarray(c_prompt)[c % 4][None], np.asarray(c_sample)[16 * c:16 * (c + 1)]], 0))
        in_maps.append(m)
    res = run_bass_kernel_spmd(nc, in_maps, core_ids=list(range(8)))
    R = res.results
    y_prompt = np.stack([R[c]['y_p'] for c in range(4)], 0)
    y_sample = np.concatenate([R[c]['y_s'].reshape(16, 8, D) for c in range(8)], 0)
    v_prompt = np.stack([R[c]['v_p'] for c in range(4)], 0)[None]
    v_sample = np.concatenate([R[c]['v_s'].reshape(16, 8, D) for c in range(8)], 0)[None]
    ssm_prompt = np.stack([R[c]['ssm_p'].reshape(64, 64, 128) for c in range(4)], 0)[None]
    ssm_sample = np.concatenate([R[c]['ssm_s'] for c in range(8)], 0)[None]
    conv_prompt = np.stack([R[c]['conv_p'] for c in range(4)], 0)[None]
    conv_sample = np.concatenate([R[c]['conv_s'].reshape(16, 3, 6144) for c in range(8)], 0)[None]
    return (y_prompt, y_sample, v_prompt, v_sample, ssm_prompt, ssm_sample, conv_prompt, conv_sample)
```

```python
import numpy as np
from contextlib import ExitStack
import concourse.bass as bass
import concourse.mybir as mybir
from concourse.bass_utils import run_bass_kernel_spmd

F32 = mybir.dt.float32
BF16 = mybir.dt.bfloat16
AF = mybir.ActivationFunctionType
ALU = mybir.AluOpType
AX = mybir.AxisListType

D = 2048
KC = 16
NPASS = 4
NCH = 4
SSQ = 4
NS = SSQ * 8
NPT = NCH * 128
NTOK = NPT + NS
NTILE = NCH + 1
TBS = [(0, 256), (256, NTOK)]
FFN_H = 5632
SSD_IN = 10304
NORM_EPS = 1e-6
LN_EPS = 1e-5
SAME_ENGINE_SYNC = True
STAGE = 2


class Sched:
    ENG = ('pe', 'dve', 'act', 'pool', 'sp')

    def __init__(self, nc, n_dma_sems=(24, 8, 16)):
        self.nc = nc
        self.ins = []
        self.last_w = {}
        self.readers = {}
        self.n_dma_sems = dict(sp=n_dma_sems[0], act=n_dma_sems[1], pool=n_dma_sems[2])

    def _add(self, eng, fn, reads, writes, kind):
        idx = len(self.ins)
        deps = set()
        raw = set()
        for k in reads:
            w = self.last_w.get(k)
            if w is not None:
                deps.add(w)
                raw.add(w)
        for k in writes:
            w = self.last_w.get(k)
            if w is not None:
                deps.add(w)
            for r in self.readers.get(k, {}).values():
                if isinstance(r, list):
                    deps.update(r)
                else:
                    deps.add(r)
        deps.discard(idx)
        if eng in ('dve', 'act', 'pool'):
            deps = set(d for d in deps if d in raw or self.ins[d]['eng'] != eng or self.ins[d]['kind'] == 'dma')
        self.ins.append(dict(eng=eng, fn=fn, deps=deps, kind=kind, needed=False))
        for k in writes:
            self.last_w[k] = idx
            self.readers[k] = {}
        for k in reads:
            if k not in writes:
                rd = self.readers.setdefault(k, {})
                if kind == 'dma':
                    rd.setdefault('dma', []).append(idx)
                else:
                    rd[eng] = idx
        return idx

    def op(self, eng, fn, reads=(), writes=()):
        return self._add(eng, fn, list(reads), list(writes), 'op')

    def dma(self, eng, out, in_, reads=(), writes=(), **kw):
        def fn(e, out=out, in_=in_, kw=kw):
            return e.dma_start(out=out, in_=in_, **kw)
        return self._add(eng, fn, list(reads), list(writes), 'dma')

    def barrier(self):
        last = {}
        for i, it in enumerate(self.ins):
            if it['kind'] != 'dma':
                last[it['eng']] = i
        deps = set(last.values())
        for k, w in self.last_w.items():
            if self.ins[w]['kind'] == 'dma':
                deps.add(w)
        for k, rs in self.readers.items():
            deps.update(rs.get('dma', []))
        for e in self.ENG:
            self.ins.append(dict(eng=e, fn=None, deps=set(deps), kind='nop', needed=False))
        self.last_w.clear()
        self.readers.clear()

    def emit(self):
        nc = self.nc
        ins = self.ins
        deps = set()
        last = {}
        for i, it in enumerate(ins):
            if it['kind'] == 'dma':
                deps.add(i)
            elif it['kind'] == 'op':
                last[it['eng']] = i
        deps |= set(last.values())
        ins.append(dict(eng='sp', fn=None, deps=deps, kind='nop', needed=False))
        for it in ins:
            for d in it['deps']:
                p = ins[d]
                if p['kind'] == 'dma':
                    p['needed'] = True
                elif p['eng'] == it['eng'] and (p['eng'] in ('pe', 'sp') or not SAME_ENGINE_SYNC):
                    pass
                else:
                    p['needed'] = True
        cnt = {e: 0 for e in self.ENG}
        dcnt = {e: 0 for e in self.ENG}
        for it in ins:
            e = it['eng']
            if it['kind'] == 'dma':
                n = self.n_dma_sems[e]
                j = dcnt[e]
                dcnt[e] += 1
                it['sig'] = ('d', e, j % n, 16 * (j // n + 1))
                it['prev_on_sem'] = 16 * (j // n)
            elif it['kind'] == 'op' and it['needed']:
                cnt[e] += 1
                it['sig'] = ('e', e, 0, cnt[e])
            else:
                it['sig'] = None
        self.stats = dict(cnt=cnt, dcnt=dcnt, n=len(ins))
        with ExitStack() as es:
            esem = {e: es.enter_context(nc.semaphore('s_' + e)) for e in self.ENG}
            dsem = {}
            for e in ('sp', 'act', 'pool'):
                nd = min(self.n_dma_sems[e], max(dcnt[e], 1))
                dsem[e] = [es.enter_context(nc.semaphore('d_%s%d' % (e, i))) for i in range(nd)]
            per = {e: [] for e in self.ENG}
            for i, it in enumerate(ins):
                per[it['eng']].append(i)
            block = es.enter_context(nc.Block())

            def make(e):
                def body(engobj):
                    wm = {}
                    for i in per[e]:
                        it = ins[i]
                        waits = {}
                        for d in it['deps']:
                            p = ins[d]
                            sg = p.get('sig')
                            if sg is None:
                                continue
                            if sg[0] == 'e' and p['eng'] == e and (e in ('pe', 'sp') or not SAME_ENGINE_SYNC):
                                continue
                            key = sg[:3]
                            if wm.get(key, 0) >= sg[3]:
                                continue
                            waits[key] = max(waits.get(key, 0), sg[3])
                        if it['kind'] == 'dma' and it['prev_on_sem'] > 0:
                            key = it['sig'][:3]
                            if wm.get(key, 0) < it['prev_on_sem']:
                                waits[key] = max(waits.get(key, 0), it['prev_on_sem'])
                        for key, v in waits.items():
                            sem = esem[key[1]] if key[0] == 'e' else dsem[key[1]][key[2]]
                            engobj.wait_ge(sem, v)
                            wm[key] = v
                        if it['fn'] is None:
                            continue
                        r = it['fn'](engobj)
                        sg = it['sig']
                        if sg is not None:
                            if sg[0] == 'e':
                                r.then_inc(esem[e], 1)
                            else:
                                r.then_inc(dsem[e][sg[2]], 16)
                return body

            block.tensor(make('pe'))
            block.vector(make('dve'))
            block.scalar(make('act'))
            block.gpsimd(make('pool'))
            block.sync(make('sp'))


VEC_ROWS = {}


def _vec_layout():
    off = 0
    lay = {}
    for name, n in [('mod_b0', 96), ('mod_b1', 96), ('nmw0', 16), ('nmw1', 16), ('nfw0', 16),
                    ('nfw1', 16), ('a_b_in', 32), ('fnw', 16), ('conv_w', 192), ('conv_b', 48)]:
        lay[name] = (off, n)
        off += n
    tot = ((off + 127) // 128) * 128
    return lay, tot


VEC_LAY, VEC_TOT = _vec_layout()


def build_program():
    nc = bass.Bass("TRN2", target_bir_lowering=False)
    din = lambda name, shape: nc.dram_tensor(name, list(shape), F32, kind="ExternalInput").ap()
    dout = lambda name, shape: nc.dram_tensor(name, list(shape), F32, kind="ExternalOutput").ap()
    xp = din("xp", [2048, D])
    xs = din("xs", [128, D])
    call = din("call", [17, D])
    vecs = din("vecs", [VEC_TOT, 128])
    mod_w = din("mod_w", [2, D, 6 * D])
    a_w_in = din("a_w_in", [D, 2 * D])
    a_ln_w = din("a_ln_w", [D])
    a_ln_b = din("a_ln_b", [D])
    a_b_in_r = din("a_b_in_r", [2 * D])
    a_w_s = din("a_w_s", [16, 128, 128])
    a_b_s = din("a_b_s", [16, 128])
    a_w_out = din("a_w_out", [D, D])
    f_w_in = din("f_w_in", [2, D, 2 * FFN_H])
    f_w_out = din("f_w_out", [2, FFN_H, D])
    y_p = dout("y_p", [2048, D])
    y_s = dout("y_s", [128, D])
    v_p = dout("v_p", [128, D])
    v_s = dout("v_s", [128, D])
    st_ssm = din("st_ssm", [16, 64, 64, 128])
    st_conv = din("st_conv", [16 * 3, 6144])
    b_w_in = din("b_w_in", [D, SSD_IN])
    b_w_out = din("b_w_out", [2 * D, D])
    b_dtb = din("b_dtb", [64])
    b_alog = din("b_alog", [64])
    b_dsk = din("b_dsk", [64])
    b_nw = din("b_nw", [2 * D])
    ssm_p = dout("ssm_p", [64 * 64, 128])
    ssm_s = dout("ssm_s", [16, 64, 64, 128])
    conv_p = dout("conv_p", [3, 6144])
    conv_s = dout("conv_s", [16 * 3, 6144])
    hst_d = nc.dram_tensor("hst_d", [8, 128, 512], F32).ap()

    S = Sched(nc)
    es = ExitStack()
    with es:
        def sb(name, shape, dt=F32):
            return es.enter_context(nc.sbuf_tensor(name, list(shape), dt))

        XT = sb("XT", [128, KC, NTOK])
        HT = sb("HT", [128, KC, NTOK], BF16)
        MOD = [sb("MOD%d" % l, [128, 96, 17]) for l in range(2)]
        COLS = sb("COLS", [128, VEC_TOT])
        WT = [sb("WT%d" % i, [128, KC, 256], BF16) for i in range(4)]
        WO = [sb("WO%d" % i, [128, 4, 512], BF16) for i in range(2)]
        GB = sb("GB", [128, 4, NTOK], BF16)
        IDF = sb("IDF", [128, 128])
        IDB = sb("IDB", [128, 128], BF16)
        ONESB = sb("ONESB", [128, 128], BF16)
        RSTD = sb("RSTD", [128, NTOK])
        ACOL = sb("ACOL", [128, KC, 17])
        TMPF = [sb("TMPF%d" % i, [128, 2048]) for i in range(2)]
        SQ = [sb("SQ%d" % i, [128, NTOK], BF16) for i in range(2)]
        EV = [sb("EV%d" % i, [128, 512]) for i in range(3)]
        CT = sb("CT", [128, KC, 17], BF16)
        VN = sb("VN", [128, NTILE, 2048], BF16)
        BINV = sb("BINV", [128, 2048], BF16)
        LNW = sb("LNW", [128, 2048], BF16)
        LNB = sb("LNB", [128, 2048], BF16)
        WST = sb("WST", [128, 16, 128], BF16)
        BDS = sb("BDS", [NS, 16, NS], BF16)
        BSR = sb("BSR", [1, 16, 128], BF16)
        BSS = sb("BSS", [1, 16, NS], BF16)
        CMASK = sb("CMASK", [128, 128])
        SEQM = sb("SEQM", [NS, SSQ])
        E8 = sb("E8", [8, SSQ, 8], BF16)
        STATS = sb("STATS", [128, NTILE, 4, 6])
        MV = sb("MV", [128, NTILE, 2])
        SMALL = sb("SMALL", [128, 64])
        UT = sb("UT", [128, 128])
        USB = sb("USB", [NS, NS])
        ONESF = sb("ONESF", [128, 128])
        SEQ32 = sb("SEQ32", [NS, NS])
        MASK3 = sb("MASK3", [128, SSQ, NS])
        DTB = sb("DTB", [128, 64]); AROW = sb("AROW", [128, 64]); DROW = sb("DROW", [128, 64])
        DT = sb("DT", [128, NTILE, 64]); DA = sb("DA", [128, NTILE, 64]); ACS = sb("ACS", [128, NTILE, 64])
        TOT = sb("TOT", [128, NTILE, 64]); EXPA = sb("EXPA", [128, NTILE, 64]); DTE = sb("DTE", [128, NTILE, 64])
        DEC = sb("DEC", [128, NTILE, 64])
        CTAIL = sb("CTAIL", [128, 48, 3])
        HB = sb("HB", [128, 512], BF16)
        BCT = sb("BCT", [128, 2, NTOK], BF16)
        DECS = sb("DECS", [128, 4, SSQ])
        DAB = sb("DAB", [NS, 128])
        H0T = sb("H0T", [128, 512], BF16)
        CMS = sb("CMS", [128, SSQ, NS], BF16)
        WM = sb("WM", [NS, 512], BF16)
        CBM = sb("CBM", [128, 128], BF16)
        XDT = sb("XDT", [128, 512], BF16)
        XDB = sb("XDB", [128, 512], BF16)
        XC2 = sb("XC2", [128, NTOK])
        WW = sb("WW", [128, 512], BF16)
        SM12 = [sb("SM12_%d" % i, [128, 128]) for i in range(2)]
        VNF = VN[:].rearrange("p t c -> p (t c)")
        SZ = VNF[:, 0:NTILE * 512].rearrange("p (t c) -> p t c", c=512)
        XTOK = VNF[:, NTILE * 512:2 * NTILE * 512].rearrange("p (t c) -> p t c", c=512)
        BTOK = VNF[:, 2 * NTILE * 512:2 * NTILE * 512 + NTILE * 128].rearrange("p (t c) -> p t c", c=128)
        _o = 2 * NTILE * 512 + NTILE * 128
        EE = VNF[:, _o:_o + 2048].bitcast(F32).rearrange("p (r i) -> p r i", i=128)
        LT = VNF[:, _o + 2048:_o + 3072].rearrange("p (r i) -> p r i", i=128)
        MT = VNF[:, _o + 3072:_o + 4096].rearrange("p (r i) -> p r i", i=128)
        assert _o + 4096 <= NTILE * 2048
        RAW = TMPF[0][:, 0:3 + NPT + SSQ * 11]
        XC_A = TMPF[0][:, 560:560 + NTOK]
        HSTG = TMPF[0][:, 1104:1616]
        NWR = TMPF[1][:, 0:512]
        H0 = TMPF[1][:, 512:1024].rearrange("p (q n) -> p q n", n=128)
        SCV = TMPF[1][:, 1024:1024 + 48 * SSQ * 3].rearrange("p (c k) -> p c k", k=SSQ * 3)

        PS = [es.enter_context(nc.psum_tensor("PS%d" % i, [128, 512], F32)) for i in range(8)]
        ps_ctr = [0]

        def next_ps(excl=()):
            while True:
                i = ps_ctr[0] % 8
                ps_ctr[0] += 1
                if i not in excl:
                    return i

        ctr = {'wt': 0, 'wo': 0, 'ev': 0, 'evb': 0, 'tmpf': 0, 'sq': 0, 'sm12': 0}

        def rot(name, n):
            i = ctr[name] % n
            ctr[name] += 1
            return i

        def affine_mask(tile_ap, key, pattern, base, cm):
            S.op('pool', lambda e: e.memset(tile_ap, 1.0), writes=[key])
            S.op('pool', lambda e: e.affine_select(out=tile_ap, in_=tile_ap, pattern=pattern,
                                                   compare_op=ALU.is_ge, fill=0.0, base=base,
                                                   channel_multiplier=cm),
                 reads=[key], writes=[key])

        S.op('pool', lambda e: e.memset(IDF[:], 1.0), writes=['IDF'])
        S.op('pool', lambda e: e.affine_select(out=IDF[:], in_=IDF[:], pattern=[[-1, 128]], compare_op=ALU.is_ge,
                                               fill=0.0, base=0, channel_multiplier=1), reads=['IDF'], writes=['IDF'])
        S.op('pool', lambda e: e.affine_select(out=IDF[:], in_=IDF[:], pattern=[[1, 128]], compare_op=ALU.is_ge,
                                               fill=0.0, base=0, channel_multiplier=-1), reads=['IDF'], writes=['IDF'])
        S.op('dve', lambda e: e.tensor_copy(out=IDB[:], in_=IDF[:]), reads=['IDF'], writes=['IDB'])
        S.op('pool', lambda e: e.memset(ONESB[:], 1.0), writes=['ONESB'])
        affine_mask(CMASK[:], 'CMASK', [[1, 128]], 0, -1)
        S.op('pool', lambda e: e.memset(SEQM[:], 1.0), writes=['SEQM'])
        S.op('pool', lambda e: e.affine_select(out=SEQM[:], in_=SEQM[:], pattern=[[-8, SSQ]], compare_op=ALU.is_ge,
                                               fill=0.0, base=0, channel_multiplier=1), reads=['SEQM'], writes=['SEQM'])
        S.op('pool', lambda e: e.affine_select(out=SEQM[:], in_=SEQM[:], pattern=[[8, SSQ]], compare_op=ALU.is_ge,
                                               fill=0.0, base=7, channel_multiplier=-1), reads=['SEQM'], writes=['SEQM'])
        S.op('dve', lambda e: e.tensor_copy(out=E8[:], in_=IDF[0:8, 0:8].unsqueeze(1).to_broadcast([8, SSQ, 8])),
             reads=['IDF'], writes=['E8'])

        for i in range(VEC_TOT // 128):
            t = TMPF[rot('tmpf', 2)]
            tk = 'TMPF%d' % ((ctr['tmpf'] - 1) % 2)
            S.dma('sp', t[:, 0:128], vecs[i * 128:(i + 1) * 128, :], writes=[tk])
            b = next_ps()
            S.op('pe', lambda e, b=b, t=t: e.transpose(out=PS[b][:, 0:128], in_=t[:, 0:128], identity=IDF[:]),
                 reads=[tk, 'IDF'], writes=[('ps', b)])
            S.op('dve', lambda e, b=b, i=i: e.tensor_copy(out=COLS[:, i * 128:(i + 1) * 128], in_=PS[b][:, 0:128]),
                 reads=[('ps', b)], writes=['COLS'])

        def col(name, j=0, n=1):
            r0, nr = VEC_LAY[name]
            return COLS[:, r0 + j:r0 + j + n]

        S.dma('pool', BINV[:], a_b_in_r[2048:4096].partition_broadcast(128), writes=['BINV'])
        S.dma('pool', LNW[:], a_ln_w.partition_broadcast(128), writes=['LNW'])
        S.dma('pool', LNB[:], a_ln_b.partition_broadcast(128), writes=['LNB'])
        S.dma('pool', BSR[:], a_b_s.rearrange("(o g) t -> o g t", o=1), writes=['BSR'])
        S.op('dve', lambda e: e.tensor_copy(out=BSS[:].rearrange("o g (b t) -> o g b t", t=8),
                                            in_=BSR[:, :, 0:8].unsqueeze(2).to_broadcast([1, 16, SSQ, 8])),
             reads=['BSR'], writes=['BSS'])

        t = TMPF[rot('tmpf', 2)]
        tk = 'TMPF%d' % ((ctr['tmpf'] - 1) % 2)
        S.dma('sp', t[:].rearrange("p (g s) -> p g s", g=16), a_w_s.rearrange("g t s -> t g s"), writes=[tk])
        for g in range(16):
            b = next_ps()
            S.op('pe', lambda e, b=b, g=g, t=t: e.transpose(out=PS[b][:, 0:128], in_=t[:, g * 128:(g + 1) * 128], identity=IDF[:]),
                 reads=[tk, 'IDF'], writes=[('ps', b)])
            S.op('dve', lambda e, b=b, g=g: e.tensor_tensor(out=WST[:, g, :], in0=PS[b][:, 0:128], in1=CMASK[:], op=ALU.mult),
                 reads=[('ps', b), 'CMASK'], writes=['WST'])
        b = next_ps()
        S.op('pe', lambda e, b=b: e.matmul(PS[b][0:NS, 0:128], lhsT=E8[:].rearrange("s b t -> s (b t)"),
                                           rhs=WST[0:8, :, 0:8], start=True, stop=True),
             reads=['E8', 'WST'], writes=[('ps', b)])
        S.op('dve', lambda e, b=b: e.tensor_tensor(
            out=BDS[:].rearrange("p g (b t) -> p g b t", t=8),
            in0=PS[b][0:NS, 0:128].rearrange("p (g t) -> p g t", t=8).unsqueeze(2).to_broadcast([NS, 16, SSQ, 8]),
            in1=SEQM[:].unsqueeze(1).unsqueeze(3).to_broadcast([NS, 16, SSQ, 8]), op=ALU.mult),
            reads=[('ps', b), 'SEQM'], writes=['BDS'])

        class WTile:
            def __init__(self, halves):
                self.halves = halves

            def c(self, k, mi):
                tl, key = self.halves[mi // 2]
                o = (mi % 2) * 128
                return tl[:, k, o:o + 128]

            def key(self, mi):
                return self.halves[mi // 2][1]

        def load_w(wap, r0, nk, c0, ncols=512, pool='wt'):
            if pool == 'wo':
                i = rot('wo', 2)
                tl, key = WO[i], 'WO%d' % i
                src = wap[r0:r0 + nk * 128, c0:c0 + ncols].rearrange("(k p) c -> p k c", p=128)
                S.dma('pool', tl[:, 0:nk, 0:ncols], src, writes=[key])
                return tl, key
            halves = []
            for h0 in range(0, ncols, 256):
                w = min(256, ncols - h0)
                i = rot('wt', 4)
                tl, key = WT[i], 'WT%d' % i
                src = wap[r0:r0 + nk * 128, c0 + h0:c0 + h0 + w].rearrange("(k p) c -> p k c", p=128)
                S.dma('pool', tl[:, 0:nk, 0:w], src, writes=[key])
                halves.append((tl, key))
            return WTile(halves), None

        t = TMPF[rot('tmpf', 2)]
        tk = 'TMPF%d' % ((ctr['tmpf'] - 1) % 2)
        S.dma('sp', t[0:17, :], call, writes=[tk])
        S.op('act', lambda e, t=t: e.activation(out=t[0:17, :], in_=t[0:17, :], func=AF.Silu), reads=[tk], writes=[tk])
        for k4 in range(4):
            b = next_ps()
            for kk in range(4):
                k = k4 * 4 + kk
                S.op('pe', lambda e, b=b, k=k, kk=kk, t=t: e.transpose(out=PS[b][:, kk * 17:(kk + 1) * 17],
                                                                      in_=t[0:17, k * 128:(k + 1) * 128], identity=IDF[0:17, 0:17]),
                     reads=[tk, 'IDF'], writes=[('ps', b)])
            S.op('dve', lambda e, b=b, k4=k4: e.tensor_copy(out=CT[:, k4 * 4:(k4 + 1) * 4, :].rearrange("p k c -> p (k c)"),
                                                           in_=PS[b][:, 0:68]),
                 reads=[('ps', b)], writes=['CT'])
        for l in range(2):
            for nb in range(24):
                tl, key = load_w(mod_w[l], 0, KC, nb * 512)
                b = next_ps()
                for mi in range(4):
                    for k in range(KC):
                        S.op('pe', lambda e, b=b, mi=mi, k=k, tl=tl: e.matmul(
                            PS[b][:, mi * 17:(mi + 1) * 17], lhsT=tl.c(k, mi), rhs=CT[:, k, :],
                            start=(k == 0), stop=(k == KC - 1)),
                            reads=[tl.key(mi), 'CT'], writes=[('ps', b)])
                for mi in range(4):
                    m = nb * 4 + mi
                    S.op('act', lambda e, b=b, mi=mi, m=m, l=l: e.activation(
                        out=MOD[l][:, m, :], in_=PS[b][:, mi * 17:(mi + 1) * 17], func=AF.Identity,
                        bias=col('mod_b%d' % l, m), scale=1.0),
                        reads=[('ps', b), 'COLS'], writes=['MOD%d' % l])

        def load_x(hf):
            for tt in range(NTILE):
                rows = 128 if tt < NCH else NS
                t = TMPF[rot('tmpf', 2)]
                tk = 'TMPF%d' % ((ctr['tmpf'] - 1) % 2)
                if tt < NCH:
                    src = xp[hf * NPT + tt * 128: hf * NPT + (tt + 1) * 128, :]
                else:
                    src = xs[hf * NS:(hf + 1) * NS, :]
                S.dma('sp', t[0:rows, :], src, writes=[tk])
                for k4 in range(4):
                    b = next_ps()
                    for kk in range(4):
                        k = k4 * 4 + kk
                        S.op('pe', lambda e, b=b, k=k, kk=kk, t=t, rows=rows: e.transpose(
                            out=PS[b][:, kk * 128:kk * 128 + rows], in_=t[0:rows, k * 128:(k + 1) * 128],
                            identity=IDF[0:rows, 0:rows]),
                            reads=[tk, 'IDF'], writes=[('ps', b)])
                    S.op('dve', lambda e, b=b, k4=k4, tt=tt, rows=rows: e.tensor_copy(
                        out=XT[:, k4 * 4:(k4 + 1) * 4, tt * 128:tt * 128 + rows],
                        in_=PS[b][:].rearrange("p (k t) -> p k t", t=128)[:, :, 0:rows]),
                        reads=[('ps', b)], writes=[('XT', k4 * 4 + kk_, tb_of(tt * 128)) for kk_ in range(4)])

        def tb_of(tok):
            return 0 if tok < TBS[0][1] else 1

        def xk(m):
            return [('XT', m, 0), ('XT', m, 1)]

        def compute_rstd(eps):
            bs = [next_ps() for _ in TBS]
            for k in range(KC):
                i = rot('sq', 2)
                S.op('act', lambda e, i=i, k=k: e.activation(out=SQ[i][:], in_=XT[:, k, :], func=AF.Square),
                     reads=xk(k), writes=['SQ%d' % i])
                for bi, (t0, t1) in enumerate(TBS):
                    S.op('pe', lambda e, b=bs[bi], i=i, t0=t0, t1=t1, k=k: e.matmul(
                        PS[b][:, 0:t1 - t0], lhsT=ONESB[:], rhs=SQ[i][:, t0:t1], start=(k == 0), stop=(k == KC - 1)),
                        reads=['SQ%d' % i, 'ONESB'], writes=[('ps', bs[bi])])
            for bi, (t0, t1) in enumerate(TBS):
                b = bs[bi]
                S.op('dve', lambda e, b=b, t0=t0, t1=t1: e.tensor_scalar(
                    out=RSTD[:, t0:t1], in0=PS[b][:, 0:t1 - t0], scalar1=1.0 / D, scalar2=eps, op0=ALU.mult, op1=ALU.add),
                    reads=[('ps', b)], writes=['RSTD'])
            S.op('act', lambda e: e.activation(out=RSTD[:], in_=RSTD[:], func=AF.Ln), reads=['RSTD'], writes=['RSTD'])
            S.op('act', lambda e: e.activation(out=RSTD[:], in_=RSTD[:], func=AF.Exp, scale=-0.5), reads=['RSTD'], writes=['RSTD'])

        def norm_mod(hf, l, which, nw_name):
            compute_rstd(NORM_EPS)
            sh0, sc0 = which * 48, which * 48 + 16
            for k in range(KC):
                S.op('dve', lambda e, k=k: e.tensor_scalar(
                    out=ACOL[:, k, :], in0=MOD[l][:, sc0 + k, :], scalar1=1.0, scalar2=col(nw_name, k),
                    op0=ALU.add, op1=ALU.mult),
                    reads=['MOD%d' % l, 'COLS'], writes=['ACOL'])
            for k in range(KC):
                i = rot('tmpf', 2)
                t, tk = TMPF[i], 'TMPF%d' % i
                S.op('dve', lambda e, t=t, k=k: e.tensor_tensor(out=t[:, 0:NTOK], in0=XT[:, k, :], in1=RSTD[:], op=ALU.mult),
                     reads=xk(k) + ['RSTD'], writes=[tk])
                S.op('act', lambda e, t=t, k=k: e.activation(
                    out=HT[:, k, 0:NPT], in_=t[:, 0:NPT], func=AF.Identity,
                    bias=MOD[l][:, sh0 + k, 0:1], scale=ACOL[:, k, 0:1]),
                    reads=[tk, 'ACOL', 'MOD%d' % l], writes=[('HT', k)])
                c0 = 1 + hf * SSQ
                S.op('dve', lambda e, t=t, k=k, c0=c0: e.tensor_tensor(
                    out=t[:, NPT:NTOK].rearrange("p (b t) -> p b t", t=8),
                    in0=t[:, NPT:NTOK].rearrange("p (b t) -> p b t", t=8),
                    in1=ACOL[:, k, c0:c0 + SSQ].unsqueeze(2).to_broadcast([128, SSQ, 8]), op=ALU.mult),
                    reads=[tk, 'ACOL'], writes=[tk])
                S.op('dve', lambda e, t=t, k=k, c0=c0: e.tensor_tensor(
                    out=HT[:, k, NPT:NTOK].rearrange("p (b t) -> p b t", t=8),
                    in0=t[:, NPT:NTOK].rearrange("p (b t) -> p b t", t=8),
                    in1=MOD[l][:, sh0 + k, c0:c0 + SSQ].unsqueeze(2).to_broadcast([128, SSQ, 8]), op=ALU.add),
                    reads=[tk, 'MOD%d' % l], writes=[('HT', k)])

        def ht_keys():
            return [('HT', k) for k in range(KC)]

        def resid_evac(hf, l, gate0, b, m, t0, t1):
            n = t1 - t0
            npr = min(t1, NPT) - t0
            S.op('dve', lambda e: e.scalar_tensor_tensor(
                out=XT[:, m, t0:t0 + npr], in0=PS[b][:, 0:npr], scalar=MOD[l][:, gate0 + m, 0:1],
                in1=XT[:, m, t0:t0 + npr], op0=ALU.mult, op1=ALU.add),
                reads=[('ps', b), 'MOD%d' % l, ('XT', m, tb_of(t0))], writes=[('XT', m, tb_of(t0))])
            if t1 > NPT:
                c0 = 1 + hf * SSQ
                i = rot('ev', 3)
                S.op('dve', lambda e: e.tensor_tensor(
                    out=EV[i][:, 0:NS].rearrange("p (b t) -> p b t", t=8),
                    in0=PS[b][:, npr:n].rearrange("p (b t) -> p b t", t=8),
                    in1=MOD[l][:, gate0 + m, c0:c0 + SSQ].unsqueeze(2).to_broadcast([128, SSQ, 8]), op=ALU.mult),
                    reads=[('ps', b), 'MOD%d' % l], writes=['EV%d' % i])
                S.op('dve', lambda e: e.tensor_tensor(out=XT[:, m, NPT:NTOK], in0=XT[:, m, NPT:NTOK], in1=EV[i][:, 0:NS], op=ALU.add),
                     reads=['EV%d' % i, ('XT', m, 1)], writes=[('XT', m, 1)])

        def out_proj_partial(hf, l, gate0, wap, r0):
            for cb in range(4):
                tl, key = load_w(wap, r0, 4, cb * 512, pool='wo')
                for mi in range(4):
                    m = cb * 4 + mi
                    for (t0, t1) in TBS:
                        b = next_ps()
                        for k in range(4):
                            S.op('pe', lambda e, b=b, k=k, mi=mi, tl=tl, t0=t0, t1=t1: e.matmul(
                                PS[b][:, 0:t1 - t0], lhsT=tl[:, k, mi * 128:(mi + 1) * 128], rhs=GB[:, k, t0:t1],
                                start=(k == 0), stop=(k == 3)),
                                reads=[key, 'GB'], writes=[('ps', b)])
                        resid_evac(hf, l, gate0, b, m, t0, t1)

        def gmlp(hf):
            hk = ht_keys()
            for nb in range(4):
                tl, key = load_w(a_w_in, 0, KC, 2048 + nb * 512)
                for tt in range(NTILE):
                    rows = 128 if tt < NCH else NS
                    b = next_ps()
                    for hh, (th, kh) in enumerate(tl.halves):
                        for k in range(KC):
                            S.op('pe', lambda e, b=b, k=k, th=th, hh=hh, tt=tt, rows=rows: e.matmul(
                                PS[b][0:rows, hh * 256:(hh + 1) * 256], lhsT=HT[:, k, tt * 128:tt * 128 + rows], rhs=th[:, k, :],
                                start=(k == 0), stop=(k == KC - 1)),
                                reads=[kh, ('HT', k)], writes=[('ps', b)])
                    i = rot('ev', 3)
                    S.op('dve', lambda e, b=b, i=i, nb=nb, rows=rows: e.tensor_tensor(
                        out=EV[i][0:rows, :], in0=PS[b][0:rows, :], in1=BINV[0:rows, nb * 512:(nb + 1) * 512], op=ALU.add),
                        reads=[('ps', b), 'BINV'], writes=['EV%d' % i])
                    S.op('act', lambda e, i=i, rows=rows: e.activation(out=EV[i][0:rows, :], in_=EV[i][0:rows, :], func=AF.Gelu),
                         reads=['EV%d' % i], writes=['EV%d' % i])
                    S.op('dve', lambda e, i=i, tt=tt, nb=nb, rows=rows: e.bn_stats(out=STATS[0:rows, tt, nb, :], in_=EV[i][0:rows, :]),
                         reads=['EV%d' % i], writes=[('STATS', tt)])
                    S.op('act', lambda e, i=i, tt=tt, nb=nb, rows=rows: e.activation(
                        out=VN[0:rows, tt, nb * 512:(nb + 1) * 512], in_=EV[i][0:rows, :], func=AF.Copy),
                        reads=['EV%d' % i], writes=[('VN', tt)])
            for tt in range(NTILE):
                rows = 128 if tt < NCH else NS
                S.op('dve', lambda e, tt=tt, rows=rows: e.bn_aggr(out=MV[0:rows, tt, :], in_=STATS[0:rows, tt, :, :].rearrange("p a b -> p (a b)")),
                     reads=[('STATS', tt)], writes=[('MV', tt)])
                S.op('dve', lambda e, tt=tt, rows=rows: e.tensor_scalar(out=SMALL[0:rows, tt:tt + 1], in0=MV[0:rows, tt, 1:2], scalar1=LN_EPS, scalar2=None, op0=ALU.add),
                     reads=[('MV', tt)], writes=[('SM', tt)])
                S.op('act', lambda e, tt=tt, rows=rows: e.activation(out=SMALL[0:rows, tt:tt + 1], in_=SMALL[0:rows, tt:tt + 1], func=AF.Sqrt),
                     reads=[('SM', tt)], writes=[('SM', tt)])
                S.op('dve', lambda e, tt=tt, rows=rows: e.reciprocal(out=SMALL[0:rows, tt:tt + 1], in_=SMALL[0:rows, tt:tt + 1]),
                     reads=[('SM', tt)], writes=[('SM', tt)])
                i = rot('tmpf', 2)
                t, tk = TMPF[i], 'TMPF%d' % i
                S.op('dve', lambda e, tt=tt, rows=rows, t=t: e.tensor_scalar(
                    out=t[0:rows, :], in0=VN[0:rows, tt, :], scalar1=MV[0:rows, tt, 0:1], scalar2=SMALL[0:rows, tt:tt + 1],
                    op0=ALU.subtract, op1=ALU.mult),
                    reads=[('VN', tt), ('MV', tt), ('SM', tt)], writes=[tk])
                S.op('dve', lambda e, rows=rows, t=t: e.tensor_tensor(out=t[0:rows, :], in0=t[0:rows, :], in1=LNW[0:rows, :], op=ALU.mult),
                     reads=[tk, 'LNW'], writes=[tk])
                S.op('dve', lambda e, rows=rows, t=t: e.tensor_tensor(out=t[0:rows, :], in0=t[0:rows, :], in1=LNB[0:rows, :], op=ALU.add),
                     reads=[tk, 'LNB'], writes=[tk])
                S.op('act', lambda e, tt=tt, rows=rows, t=t: e.activation(out=VN[0:rows, tt, :], in_=t[0:rows, :], func=AF.Copy),
                     reads=[tk], writes=[('VN', tt)])
                if tt == NCH:
                    S.dma('sp', v_s[hf * NS:(hf + 1) * NS, :], t[0:NS, :], reads=[tk])
                elif tt == NCH - 1 and hf == NPASS - 1:
                    S.dma('sp', v_p, t[:, :], reads=[tk])
            for nb in range(4):
                tl, key = load_w(a_w_in, 0, KC, nb * 512)
                for mi in range(4):
                    m = nb * 4 + mi
                    for (t0, t1) in TBS:
                        n = t1 - t0
                        bu = next_ps()
                        for k in range(KC):
                            S.op('pe', lambda e, b=bu, k=k, mi=mi, tl=tl, t0=t0, t1=t1: e.matmul(
                                PS[b][:, 0:t1 - t0], lhsT=tl.c(k, mi), rhs=HT[:, k, t0:t1],
                                start=(k == 0), stop=(k == KC - 1)),
                                reads=[tl.key(mi), ('HT', k)], writes=[('ps', bu)])
                        i = rot('ev', 3)
                        S.op('act', lambda e, b=bu, i=i, n=n, m=m: e.activation(
                            out=EV[i][:, 0:n], in_=PS[b][:, 0:n], func=AF.Gelu, bias=col('a_b_in', m), scale=1.0),
                            reads=[('ps', bu), 'COLS'], writes=['EV%d' % i])
                        bs_ = next_ps()
                        for tt in range(t0 // 128, (t1 + 127) // 128):
                            rows = 128 if tt < NCH else NS
                            c0 = tt * 128 - t0
                            rhs = WST[:, m, :] if tt < NCH else BDS[:, m, :]
                            brow = BSR[:, m, :] if tt < NCH else BSS[:, m, :]
                            S.op('pe', lambda e, b=bs_, tt=tt, rows=rows, c0=c0, rhs=rhs, m=m: e.matmul(
                                PS[b][:, c0:c0 + rows], lhsT=VN[0:rows, tt, m * 128:(m + 1) * 128], rhs=rhs,
                                start=True, stop=False),
                                reads=[('VN', tt), 'WST', 'BDS'], writes=[('ps', bs_)])
                            S.op('pe', lambda e, b=bs_, rows=rows, c0=c0, brow=brow: e.matmul(
                                PS[b][:, c0:c0 + rows], lhsT=ONESB[0:1, :], rhs=brow, start=False, stop=True),
                                reads=['ONESB', 'BSR', 'BSS'], writes=[('ps', bs_)])
                        S.op('dve', lambda e, b=bs_, i=i, n=n, mi=mi, t0=t0, t1=t1: e.tensor_tensor(
                            out=GB[:, mi, t0:t1], in0=PS[b][:, 0:n], in1=EV[i][:, 0:n], op=ALU.mult),
                            reads=[('ps', bs_), 'EV%d' % i], writes=['GB'])
                out_proj_partial(hf, 0, 32, a_w_out, nb * 512)


        affine_mask(UT[:], 'UT', [[1, 128]], 0, -1)
        S.op('pool', lambda e: e.memset(ONESF[:], 1.0), writes=['ONESF'])
        S.op('dve', lambda e: e.tensor_copy(out=SEQ32[:].rearrange("p (b t) -> p b t", t=8),
                                            in_=SEQM[:].unsqueeze(2).to_broadcast([NS, SSQ, 8])),
             reads=['SEQM'], writes=['SEQ32'])
        S.op('dve', lambda e: e.tensor_tensor(out=USB[:], in0=SEQ32[:], in1=UT[0:NS, 0:NS], op=ALU.mult),
             reads=['SEQ32', 'UT'], writes=['USB'])
        S.op('pool', lambda e: e.memset(MASK3[:], 1.0), writes=['MASK3'])
        S.op('pool', lambda e: e.affine_select(out=MASK3[:], in_=MASK3[:], pattern=[[-8, SSQ], [1, NS]], compare_op=ALU.is_ge,
                                               fill=0.0, base=0, channel_multiplier=0), reads=['MASK3'], writes=['MASK3'])
        S.op('pool', lambda e: e.affine_select(out=MASK3[:], in_=MASK3[:], pattern=[[8, SSQ], [-1, NS]], compare_op=ALU.is_ge,
                                               fill=0.0, base=7, channel_multiplier=0), reads=['MASK3'], writes=['MASK3'])
        S.dma('sp', DTB[:], b_dtb.partition_broadcast(128), writes=['DTB'])
        S.dma('sp', AROW[:], b_alog.partition_broadcast(128), writes=['AROW'])
        S.dma('sp', DROW[:], b_dsk.partition_broadcast(128), writes=['DROW'])
        S.op('act', lambda e: e.activation(out=AROW[:], in_=AROW[:], func=AF.Exp), reads=['AROW'], writes=['AROW'])
        S.op('dve', lambda e: e.tensor_scalar(out=AROW[:], in0=AROW[:], scalar1=-1.0, scalar2=None, op0=ALU.mult),
             reads=['AROW'], writes=['AROW'])
        S.op('pool', lambda e: e.memset(CTAIL[:], 0.0), writes=['CTAIL'])

        LIVE = [set()]

        def tr_store(src_ap, nrow, dst_ap, rkeys):
            b = next_ps(LIVE[0])
            S.op('pe', lambda e: e.transpose(out=PS[b][0:nrow, 0:128], in_=src_ap, identity=IDF[:]),
                 reads=list(rkeys) + ['IDF'], writes=[('ps', b)])
            j = rot('sm12', 2)
            S.op('dve', lambda e: e.tensor_copy(out=SM12[j][0:nrow, :], in_=PS[b][0:nrow, 0:128]),
                 reads=[('ps', b)], writes=['SM12_%d' % j])
            S.dma('sp', dst_ap, SM12[j][0:nrow, :], reads=['SM12_%d' % j])

        def ssd(hf):
            last = (hf == NPASS - 1)
            rows_of = lambda tt: 128 if tt < NCH else NS
            for cb in range(3):
                t, tk = TMPF[0], 'TMPF0'
                S.dma('sp', t[0:SSQ * 3, :], st_conv[hf * SSQ * 3:(hf + 1) * SSQ * 3, cb * 2048:(cb + 1) * 2048], writes=[tk])
                for c4 in range(4):
                    b = next_ps()
                    for cc in range(4):
                        ch = c4 * 4 + cc
                        S.op('pe', lambda e, b=b, cc=cc, ch=ch, t=t: e.transpose(
                            out=PS[b][:, cc * 12:(cc + 1) * 12], in_=t[0:SSQ * 3, ch * 128:(ch + 1) * 128], identity=IDF[0:SSQ * 3, 0:SSQ * 3]),
                            reads=[tk, 'IDF'], writes=[('ps', b)])
                    S.op('dve', lambda e, b=b, cb=cb, c4=c4: e.tensor_copy(
                        out=SCV[:, cb * 16 + c4 * 4: cb * 16 + c4 * 4 + 4, :].rearrange("p c k -> p (c k)"), in_=PS[b][:, 0:48]),
                        reads=[('ps', b)], writes=['SCV'])
            tl, key = load_w(b_w_in, 0, KC, 10240, ncols=64)
            for tt in range(NTILE):
                rows = rows_of(tt)
                b = next_ps()
                for k in range(KC):
                    S.op('pe', lambda e, b=b, k=k, tt=tt, rows=rows, tl=tl: e.matmul(
                        PS[b][0:rows, 0:64], lhsT=HT[:, k, tt * 128:tt * 128 + rows], rhs=tl.halves[0][0][:, k, 0:64],
                        start=(k == 0), stop=(k == KC - 1)), reads=[tl.halves[0][1], ('HT', k)], writes=[('ps', b)])
                S.op('dve', lambda e, b=b, tt=tt, rows=rows: e.tensor_tensor(out=DT[0:rows, tt, :], in0=PS[b][0:rows, 0:64], in1=DTB[0:rows, :], op=ALU.add),
                     reads=[('ps', b), 'DTB'], writes=['DT'])
            S.op('act', lambda e: e.activation(out=DT[:], in_=DT[:], func=AF.Exp), reads=['DT'], writes=['DT'])
            S.op('act', lambda e: e.activation(out=DT[:], in_=DT[:], func=AF.Ln, bias=1.0, scale=1.0), reads=['DT'], writes=['DT'])
            S.op('dve', lambda e: e.tensor_tensor(out=DA[:], in0=DT[:], in1=AROW[:].unsqueeze(1).to_broadcast([128, NTILE, 64]), op=ALU.mult),
                 reads=['DT', 'AROW'], writes=['DA'])
            for tt in range(NTILE):
                rows = rows_of(tt)
                um = UT if tt < NCH else USB
                om = ONESF if tt < NCH else SEQ32
                b = next_ps()
                S.op('pe', lambda e, b=b, tt=tt, rows=rows, um=um: e.matmul(PS[b][0:rows, 0:64], lhsT=um[0:rows, 0:rows], rhs=DA[0:rows, tt, :], start=True, stop=True),
                     reads=['DA', 'UT', 'USB'], writes=[('ps', b)])
                S.op('pe', lambda e, b=b, tt=tt, rows=rows, om=om: e.matmul(PS[b][0:rows, 64:128], lhsT=om[0:rows, 0:rows], rhs=DA[0:rows, tt, :], start=True, stop=True),
                     reads=['DA', 'ONESF', 'SEQ32'], writes=[('ps', b)])
                S.op('dve', lambda e, b=b, tt=tt, rows=rows: e.tensor_copy(out=ACS[0:rows, tt, :], in_=PS[b][0:rows, 0:64]), reads=[('ps', b)], writes=['ACS'])
                S.op('dve', lambda e, b=b, tt=tt, rows=rows: e.tensor_copy(out=TOT[0:rows, tt, :], in_=PS[b][0:rows, 64:128]), reads=[('ps', b)], writes=['TOT'])
            S.op('act', lambda e: e.activation(out=EXPA[:], in_=ACS[:], func=AF.Exp), reads=['ACS'], writes=['EXPA'])
            S.op('act', lambda e: e.activation(out=DEC[:], in_=TOT[:], func=AF.Exp), reads=['TOT'], writes=['DEC'])
            S.op('dve', lambda e: e.tensor_tensor(out=DTE[:], in0=TOT[:], in1=ACS[:], op=ALU.subtract), reads=['TOT', 'ACS'], writes=['DTE'])
            S.op('act', lambda e: e.activation(out=DTE[:], in_=DTE[:], func=AF.Exp), reads=['DTE'], writes=['DTE'])

            S.barrier()
            for g in range(8):
                tl, key = load_w(b_w_in, 0, KC, g * 512)
                for tt in range(NTILE):
                    rows = rows_of(tt)
                    b = next_ps()
                    for hh, (th, kh) in enumerate(tl.halves):
                        for k in range(KC):
                            S.op('pe', lambda e, b=b, k=k, tt=tt, rows=rows, th=th, hh=hh: e.matmul(
                                PS[b][0:rows, hh * 256:(hh + 1) * 256], lhsT=HT[:, k, tt * 128:tt * 128 + rows], rhs=th[:, k, :],
                                start=(k == 0), stop=(k == KC - 1)), reads=[kh, ('HT', k)], writes=[('ps', b)])
                    S.op('act', lambda e, b=b, tt=tt, rows=rows: e.activation(out=SZ[0:rows, tt, :], in_=PS[b][0:rows, :], func=AF.Silu),
                         reads=[('ps', b)], writes=['SZ'])
                tlx, keyx = load_w(b_w_in, 0, KC, 4096 + g * 512)
                cinfo = []
                for ci in range(6):
                    if ci < 4:
                        cinfo.append(dict(c0=ci * 128, ch=g * 4 + ci, kind='x', mi=ci))
                    elif ci == 4:
                        cinfo.append(dict(c0=0, ch=32 + g, kind='B', mi=0))
                    else:
                        cinfo.append(dict(c0=0, ch=40 + g, kind='C', mi=1))

                def stage_p(ci):
                    inf = cinfo[ci]
                    if ci < 4:
                        tl = tlx
                    elif ci == 4:
                        tl, _ = load_w(b_w_in, 0, KC, 8192 + g * 128, ncols=128)
                    else:
                        tl, _ = load_w(b_w_in, 0, KC, 9216 + g * 128, ncols=128)
                    c0 = inf['c0']
                    banks = []
                    for (t0, t1) in TBS:
                        b = next_ps(live_banks)
                        banks.append(b)
                        for k in range(KC):
                            S.op('pe', lambda e, b=b, k=k, tl=tl, c0=c0, t0=t0, t1=t1: e.matmul(
                                PS[b][:, 0:t1 - t0], lhsT=tl.c(k, c0 // 128), rhs=HT[:, k, t0:t1],
                                start=(k == 0), stop=(k == KC - 1)), reads=[tl.key(c0 // 128), ('HT', k)], writes=[('ps', b)])
                    inf['banks'] = banks
                    live_banks.update(banks)

                def stage_q(ci):
                    inf = cinfo[ci]
                    ch, kind, mi = inf['ch'], inf['kind'], inf['mi']
                    XC = (XC_A, XC2)[ci % 2]
                    xck = 'XC%d' % (ci % 2)
                    S.op('dve', lambda e, ch=ch: e.tensor_copy(out=RAW[:, 0:3], in_=CTAIL[:, ch, :]), reads=['CTAIL'], writes=['RAW'])
                    S.op('dve', lambda e, ch=ch: e.tensor_copy(
                        out=RAW[:, 3 + NPT:].rearrange("p (b t) -> p b t", t=11)[:, :, 0:3],
                        in_=SCV[:, ch, :].rearrange("p (b k) -> p b k", k=3)), reads=['SCV'], writes=['RAW'])
                    for bi, (t0, t1) in enumerate(TBS):
                        n = t1 - t0
                        npr = min(t1, NPT) - t0
                        b = inf['banks'][bi]
                        S.op('act', lambda e, b=b, t0=t0, npr=npr: e.activation(out=RAW[:, 3 + t0:3 + t0 + npr], in_=PS[b][:, 0:npr], func=AF.Copy),
                             reads=[('ps', b)], writes=['RAW'])
                        if t1 > NPT:
                            S.op('act', lambda e, b=b, npr=npr, n=n: e.activation(
                                out=RAW[:, 3 + NPT:].rearrange("p (b t) -> p b t", t=11)[:, :, 3:11],
                                in_=PS[b][:, npr:n].rearrange("p (b t) -> p b t", t=8), func=AF.Copy),
                                reads=[('ps', b)], writes=['RAW'])
                    for b in inf['banks']:
                        live_banks.discard(b)
                    S.op('dve', lambda e, ch=ch: e.tensor_copy(out=CTAIL[:, ch, :], in_=RAW[:, NPT:NPT + 3]), reads=['RAW'], writes=['CTAIL'])
                    j2 = rot('ev', 3)
                    S.op('dve', lambda e, j2=j2: e.tensor_copy(
                        out=EV[j2][:, 0:SSQ * 3].rearrange("p (b k) -> p b k", k=3),
                        in_=RAW[:, 3 + NPT:].rearrange("p (b t) -> p b t", t=11)[:, :, 8:11]), reads=['RAW'], writes=['EV%d' % j2])
                    tr_store(EV[j2][:, 0:SSQ * 3], SSQ * 3, conv_s[hf * SSQ * 3:(hf + 1) * SSQ * 3, ch * 128:(ch + 1) * 128], ['EV%d' % j2])
                    if last:
                        tr_store(CTAIL[:, ch, :], 3, conv_p[:, ch * 128:(ch + 1) * 128], ['CTAIL'])
                    cw = lambda kk, ch=ch: col('conv_w', kk * 48 + ch)
                    cbias = col('conv_b', ch)
                    pr_in = lambda kk: RAW[:, kk:kk + NPT]
                    sm_in = lambda kk: RAW[:, 3 + NPT:].rearrange("p (b t) -> p b t", t=11)[:, :, kk:kk + 8]
                    pr_out = XC[:, 0:NPT]
                    sm_out = XC[:, NPT:NTOK].rearrange("p (b t) -> p b t", t=8)
                    for (oin, oout) in ((pr_in, pr_out), (sm_in, sm_out)):
                        S.op('dve', lambda e, oin=oin, oout=oout, cw=cw, cbias=cbias: e.tensor_scalar(
                            out=oout, in0=oin(0), scalar1=cw(0), scalar2=cbias, op0=ALU.mult, op1=ALU.add),
                            reads=['RAW', 'COLS'], writes=[xck])
                        for kk in range(1, 4):
                            S.op('dve', lambda e, oin=oin, oout=oout, cw=cw, kk=kk: e.scalar_tensor_tensor(
                                out=oout, in0=oin(kk), scalar=cw(kk), in1=oout, op0=ALU.mult, op1=ALU.add),
                                reads=['RAW', 'COLS', xck], writes=[xck])
                    if kind == 'C':
                        S.op('act', lambda e: e.activation(out=BCT[:, 1, :], in_=XC[:], func=AF.Silu), reads=[xck], writes=['BCT'])
                        return
                    if kind == 'B':
                        S.op('act', lambda e: e.activation(out=BCT[:, 0, :], in_=XC[:], func=AF.Silu), reads=[xck], writes=['BCT'])
                    S.op('act', lambda e: e.activation(out=XC[:], in_=XC[:], func=AF.Silu), reads=[xck], writes=[xck])
                    b = next_ps(live_banks)
                    for tt in range(NCH):
                        S.op('pe', lambda e, b=b, tt=tt: e.transpose(out=PS[b][:, tt * 128:(tt + 1) * 128], in_=XC[:, tt * 128:(tt + 1) * 128], identity=IDF[:]),
                             reads=[xck, 'IDF'], writes=[('ps', b)])
                    b2 = next_ps(live_banks)
                    S.op('pe', lambda e, b2=b2: e.transpose(out=PS[b2][0:NS, 0:128], in_=XC[:, NPT:NTOK], identity=IDF[:]),
                         reads=[xck, 'IDF'], writes=[('ps', b2)])
                    if kind == 'x':
                        S.op('dve', lambda e, b=b, mi=mi: e.tensor_copy(out=XTOK[:, 0:NCH, mi * 128:(mi + 1) * 128],
                                                                      in_=PS[b][:, :].rearrange("p (t c) -> p t c", c=128)),
                             reads=[('ps', b)], writes=['XTOK'])
                        S.op('dve', lambda e, b2=b2, mi=mi: e.tensor_copy(out=XTOK[0:NS, NCH, mi * 128:(mi + 1) * 128], in_=PS[b2][0:NS, 0:128]),
                             reads=[('ps', b2)], writes=['XTOK'])
                    else:
                        S.op('dve', lambda e, b=b: e.tensor_copy(out=BTOK[:, 0:NCH, :], in_=PS[b][:, :].rearrange("p (t c) -> p t c", c=128)),
                             reads=[('ps', b)], writes=['BTOK'])
                        S.op('dve', lambda e, b2=b2: e.tensor_copy(out=BTOK[0:NS, NCH, :], in_=PS[b2][0:NS, 0:128]),
                             reads=[('ps', b2)], writes=['BTOK'])

                live_banks = set()
                LIVE[0] = live_banks
                stage_p(0)
                for ci in range(6):
                    if ci + 1 < 6:
                        stage_p(ci + 1)
                    stage_q(ci)
                S.dma('sp', NWR[:], b_nw[g * 512:(g + 1) * 512].partition_broadcast(128), writes=['NWR'])
                if hf == 0:
                    S.op('pool', lambda e: e.memset(HSTG[:], 0.0), writes=['HSTG'])
                else:
                    S.dma('sp', HSTG[:], hst_d[g], reads=[('hst_d', g)], writes=['HSTG'])
                S.op('act', lambda e: e.activation(out=HB[:], in_=HSTG[:], func=AF.Copy), reads=['HSTG'], writes=['HB'])
                hs = slice(g * 8, g * 8 + 8)
                held = set()
                LIVE[0] = held
                tinfo = {}

                def front(tt, g=g, hs=hs):
                    rows = rows_of(tt)
                    samp = (tt == NCH)
                    tk0 = tt * 128
                    um = USB if samp else UT
                    S.op('pool', lambda e: e.tensor_tensor(
                        out=XDT[0:rows, :].rearrange("p (r q) -> p r q", q=64), in0=XTOK[0:rows, tt, :].rearrange("p (r q) -> p r q", q=64),
                        in1=DT[0:rows, tt, hs].unsqueeze(2).to_broadcast([rows, 8, 64]), op=ALU.mult), reads=['XTOK', 'DT'], writes=['XDT'])
                    S.op('pool', lambda e: e.tensor_tensor(
                        out=WW[0:rows, :].rearrange("p (r q) -> p r q", q=64), in0=XDT[0:rows, :].rearrange("p (r q) -> p r q", q=64),
                        in1=DTE[0:rows, tt, hs].unsqueeze(2).to_broadcast([rows, 8, 64]), op=ALU.mult), reads=['XDT', 'DTE'], writes=['WW'])
                    S.op('pool', lambda e: e.tensor_tensor(
                        out=XDB[0:rows, :].rearrange("p (r q) -> p r q", q=64), in0=XTOK[0:rows, tt, :].rearrange("p (r q) -> p r q", q=64),
                        in1=DROW[0:rows, hs].unsqueeze(2).to_broadcast([rows, 8, 64]), op=ALU.mult), reads=['XTOK', 'DROW'], writes=['XDB'])
                    b = next_ps(held)
                    S.op('pe', lambda e, b=b: e.matmul(PS[b][0:rows, 0:rows], lhsT=BCT[:, 0, tk0:tk0 + rows], rhs=BCT[:, 1, tk0:tk0 + rows], start=True, stop=True),
                         reads=['BCT'], writes=[('ps', b)])
                    S.op('dve', lambda e, b=b: e.tensor_tensor(out=CBM[0:rows, 0:rows], in0=PS[b][0:rows, 0:rows], in1=um[0:rows, 0:rows], op=ALU.mult),
                         reads=[('ps', b), 'UT', 'USB'], writes=['CBM'])
                    for r4 in range(2):
                        b = next_ps(held)
                        for rr in range(4):
                            h = g * 8 + r4 * 4 + rr
                            S.op('pe', lambda e, b=b, rr=rr, h=h: e.matmul(
                                PS[b][0:rows, rr * 128:rr * 128 + rows], lhsT=DA[0:rows, tt, h:h + 1].to_broadcast([rows, rows]),
                                rhs=um[0:rows, 0:rows], start=True, stop=True), reads=['DA', 'UT', 'USB'], writes=[('ps', b)])
                        for rr in range(4):
                            r = r4 * 4 + rr
                            h = g * 8 + r
                            S.op('dve', lambda e, b=b, rr=rr, r=r, h=h: e.tensor_scalar(
                                out=EE[0:rows, r, 0:rows], in0=PS[b][0:rows, rr * 128:rr * 128 + rows], scalar1=ACS[0:rows, tt, h:h + 1], scalar2=0.0,
                                op0=ALU.subtract, op1=ALU.min), reads=[('ps', b), 'ACS'], writes=['EE'])
                    S.op('act', lambda e: e.activation(out=LT[0:rows, :, 0:rows], in_=EE[0:rows, :, 0:rows], func=AF.Exp), reads=['EE'], writes=['LT'])
                    S.op('dve', lambda e: e.tensor_tensor(out=MT[0:rows, :, 0:rows], in0=LT[0:rows, :, 0:rows],
                                                         in1=CBM[0:rows, 0:rows].unsqueeze(1).to_broadcast([rows, 8, rows]), op=ALU.mult),
                         reads=['LT', 'CBM'], writes=['MT'])
                    by = next_ps(held)
                    S.op('pe', lambda e: e.matmul(PS[by][0:rows, :], lhsT=IDB[0:rows, 0:rows], rhs=XDB[0:rows, :], start=True, stop=False),
                         reads=['IDB', 'XDB'], writes=[('ps', by)])
                    for r in range(8):
                        S.op('pe', lambda e, r=r: e.matmul(PS[by][0:rows, r * 64:(r + 1) * 64], lhsT=MT[0:rows, r, 0:rows], rhs=XDT[0:rows, r * 64:(r + 1) * 64],
                                                            start=False, stop=(r == 7)),
                             reads=['MT', 'XDT'], writes=[('ps', by)])
                    held.add(by)
                    bs3 = None
                    if not samp:
                        bs3 = next_ps(held)
                        S.op('pe', lambda e: e.matmul(PS[bs3][:, :], lhsT=BTOK[:, tt, :], rhs=WW[:, :], start=True, stop=True),
                             reads=['BTOK', 'WW'], writes=[('ps', bs3)])
                        held.add(bs3)
                    tinfo[tt] = (by, bs3)

                def back(tt, g=g, hs=hs):
                    rows = rows_of(tt)
                    samp = (tt == NCH)
                    tk0 = tt * 128
                    by, bs3 = tinfo[tt]
                    bo = next_ps(held)
                    if not samp:
                        S.op('pe', lambda e: e.matmul(PS[bo][0:rows, :], lhsT=BCT[:, 1, tk0:tk0 + rows], rhs=HB[:], start=True, stop=True),
                             reads=['BCT', 'HB'], writes=[('ps', bo)])
                    else:
                        held.add(bo)
                        S.op('dve', lambda e: e.tensor_tensor(out=CMS[:], in0=BCT[:, 1, tk0:tk0 + NS].unsqueeze(1).to_broadcast([128, SSQ, NS]), in1=MASK3[:], op=ALU.mult),
                             reads=['BCT', 'MASK3'], writes=['CMS'])
                        bd = next_ps(held)
                        for q4 in range(4):
                            h2 = g * 8 + q4 * 2
                            S.op('dve', lambda e, h2=h2: e.tensor_copy(
                                out=DAB[:].rearrange("p (h q) -> p h q", q=64), in_=DA[0:NS, tt, h2:h2 + 2].unsqueeze(2).to_broadcast([NS, 2, 64])),
                                reads=['DA'], writes=['DAB'])
                            S.op('pe', lambda e, q4=q4: e.matmul(
                                PS[bd][:, q4 * SSQ:(q4 + 1) * SSQ], lhsT=DAB[:], rhs=SEQM[:], start=True, stop=True),
                                reads=['DAB', 'SEQM'], writes=[('ps', bd)])
                        S.op('act', lambda e: e.activation(out=DECS[:].rearrange("p q b -> p (q b)"), in_=PS[bd][:, 0:4 * SSQ], func=AF.Exp),
                             reads=[('ps', bd)], writes=['DECS'])
                        h0bufs = {}

                        def issue_in(bq):
                            sq_ = hf * SSQ + bq
                            jh = rot('ev', 3)
                            H0v = EV[jh][:, :].rearrange("p (q n) -> p q n", n=128)
                            src = st_ssm[sq_, g * 8:(g + 1) * 8].rearrange("(q h) p n -> (h p) q n", h=2)
                            S.dma('sp', H0v, src, writes=['EV%d' % jh])
                            h0bufs[bq] = (H0v, 'EV%d' % jh)

                        issue_in(0)
                        for bq in range(SSQ):
                            if bq + 1 < SSQ:
                                issue_in(bq + 1)
                            sq_ = hf * SSQ + bq
                            H0v, hk_ = h0bufs[bq]
                            bt_ = next_ps(held)
                            for q4 in range(4):
                                S.op('pe', lambda e, bt_=bt_, q4=q4, H0v=H0v: e.transpose(out=PS[bt_][:, q4 * 128:(q4 + 1) * 128], in_=H0v[:, q4, :], identity=IDF[:]),
                                     reads=[hk_, 'IDF'], writes=[('ps', bt_)])
                            S.op('act', lambda e, bt_=bt_: e.activation(out=H0T[:], in_=PS[bt_][:], func=AF.Copy), reads=[('ps', bt_)], writes=['H0T'])
                            S.op('pe', lambda e, bq=bq: e.matmul(PS[bo][0:NS, :], lhsT=CMS[:, bq, :], rhs=H0T[:], start=(bq == 0), stop=(bq == SSQ - 1)),
                                 reads=['CMS', 'H0T'], writes=[('ps', bo)])
                            S.op('dve', lambda e, bq=bq: e.tensor_scalar(out=WM[:], in0=WW[0:NS, :], scalar1=SEQM[:, bq:bq + 1], scalar2=None, op0=ALU.mult),
                                 reads=['WW', 'SEQM'], writes=['WM'])
                            bn = next_ps(held)
                            for q4 in range(4):
                                S.op('pe', lambda e, bn=bn, q4=q4: e.matmul(PS[bn][:, q4 * 128:(q4 + 1) * 128], lhsT=WM[:, q4 * 128:(q4 + 1) * 128], rhs=BTOK[0:NS, tt, :], start=True, stop=True),
                                     reads=['WM', 'BTOK'], writes=[('ps', bn)])
                            for q4 in range(4):
                                S.op('dve', lambda e, bn=bn, q4=q4, bq=bq, H0v=H0v: e.scalar_tensor_tensor(
                                    out=H0v[:, q4, :], in0=H0v[:, q4, :], scalar=DECS[:, q4, bq:bq + 1], in1=PS[bn][:, q4 * 128:(q4 + 1) * 128],
                                    op0=ALU.mult, op1=ALU.add), reads=[hk_, 'DECS', ('ps', bn)], writes=[hk_])
                            dst = ssm_s[sq_, g * 8:(g + 1) * 8].rearrange("(q h) p n -> (h p) q n", h=2)
                            S.dma('sp', dst, H0v, reads=[hk_])
                        held.discard(bo)
                    jy, jo = rot('ev', 3), rot('ev', 3)
                    Y, YO = EV[jy], EV[jo]
                    S.op('dve', lambda e: e.tensor_tensor(
                        out=YO[0:rows, :].rearrange("p (r q) -> p r q", q=64), in0=PS[bo][0:rows, :].rearrange("p (r q) -> p r q", q=64),
                        in1=EXPA[0:rows, tt, hs].unsqueeze(2).to_broadcast([rows, 8, 64]), op=ALU.mult), reads=[('ps', bo), 'EXPA'], writes=['EV%d' % jo])
                    S.op('dve', lambda e: e.tensor_tensor(out=Y[0:rows, :], in0=PS[by][0:rows, :], in1=YO[0:rows, :], op=ALU.add),
                         reads=[('ps', by), 'EV%d' % jo], writes=['EV%d' % jy])
                    held.discard(by)
                    S.op('dve', lambda e: e.tensor_tensor(out=Y[0:rows, :], in0=Y[0:rows, :], in1=SZ[0:rows, tt, :], op=ALU.mult),
                         reads=['SZ', 'EV%d' % jy], writes=['EV%d' % jy])
                    S.op('act', lambda e: e.activation(out=YO[0:rows, :], in_=Y[0:rows, :], func=AF.Square, accum_out=SMALL[0:rows, 32:33]),
                         reads=['EV%d' % jy], writes=['EV%d' % jo, ('SM', 32)])
                    S.op('dve', lambda e: e.tensor_scalar(out=SMALL[0:rows, 32:33], in0=SMALL[0:rows, 32:33], scalar1=1.0 / 512, scalar2=NORM_EPS, op0=ALU.mult, op1=ALU.add),
                         reads=[('SM', 32)], writes=[('SM', 32)])
                    S.op('act', lambda e: e.activation(out=SMALL[0:rows, 32:33], in_=SMALL[0:rows, 32:33], func=AF.Ln), reads=[('SM', 32)], writes=[('SM', 32)])
                    S.op('act', lambda e: e.activation(out=SMALL[0:rows, 32:33], in_=SMALL[0:rows, 32:33], func=AF.Exp, scale=-0.5), reads=[('SM', 32)], writes=[('SM', 32)])
                    S.op('dve', lambda e: e.scalar_tensor_tensor(out=Y[0:rows, :], in0=Y[0:rows, :], scalar=SMALL[0:rows, 32:33], in1=NWR[0:rows, :], op0=ALU.mult, op1=ALU.mult),
                         reads=['EV%d' % jy, ('SM', 32), 'NWR'], writes=['EV%d' % jy])
                    bt2 = next_ps(held)
                    for kk in range(4):
                        S.op('pe', lambda e, kk=kk: e.transpose(out=PS[bt2][:, kk * 128:kk * 128 + rows], in_=Y[0:rows, kk * 128:(kk + 1) * 128], identity=IDF[0:rows, 0:rows]),
                             reads=['EV%d' % jy, 'IDF'], writes=[('ps', bt2)])
                    S.op('act', lambda e: e.activation(
                        out=GB[:, :, tk0:tk0 + rows], in_=PS[bt2][:].rearrange("p (k t) -> p k t", t=128)[:, :, 0:rows], func=AF.Copy),
                        reads=[('ps', bt2)], writes=['GB'])
                    if not samp:
                        S.op('dve', lambda e: e.tensor_tensor(
                            out=HSTG[:].rearrange("p (r q) -> p r q", q=64), in0=HSTG[:].rearrange("p (r q) -> p r q", q=64),
                            in1=DEC[:, tt, hs].unsqueeze(2).to_broadcast([128, 8, 64]), op=ALU.mult), reads=['HSTG', 'DEC'], writes=['HSTG'])
                        S.op('dve', lambda e: e.tensor_tensor(out=HSTG[:], in0=HSTG[:], in1=PS[bs3][:, :], op=ALU.add),
                             reads=['HSTG', ('ps', bs3)], writes=['HSTG'])
                        held.discard(bs3)
                        S.op('act', lambda e: e.activation(out=HB[:], in_=HSTG[:], func=AF.Copy), reads=['HSTG'], writes=['HB'])

                front(0)
                for tt in range(NTILE):
                    if tt + 1 < NTILE:
                        front(tt + 1)
                    back(tt)
                LIVE[0] = set()
                if not last:
                    S.dma('sp', hst_d[g], HSTG[:], reads=['HSTG'], writes=[('hst_d', g)])
                else:
                    for q4 in range(4):
                        b = next_ps()
                        S.op('pe', lambda e, b=b, q4=q4: e.transpose(out=PS[b][:, 0:128], in_=HSTG[:, q4 * 128:(q4 + 1) * 128], identity=IDF[:]),
                             reads=['HSTG', 'IDF'], writes=[('ps', b)])
                        j = rot('sm12', 2)
                        S.op('dve', lambda e, b=b, j=j: e.tensor_copy(out=SM12[j][:, :], in_=PS[b][:, 0:128]), reads=[('ps', b)], writes=['SM12_%d' % j])
                        r0 = (g * 8 + q4 * 2) * 64
                        S.dma('sp', ssm_p[r0:r0 + 128, :], SM12[j][:, :], reads=['SM12_%d' % j])
                out_proj_partial(hf, 1, 32, b_w_out, g * 512)

        def ffn(hf, l):
            for blk in range(11):
                for half in range(2):
                    tg, _ = load_w(f_w_in[l], 0, KC, blk * 512 + half * 256, ncols=256)
                    tu, _ = load_w(f_w_in[l], 0, KC, FFN_H + blk * 512 + half * 256, ncols=256)
                    for mi2 in range(2):
                        mi = half * 2 + mi2
                        for (t0, t1) in TBS:
                            n = t1 - t0
                            bg, bu = next_ps(), next_ps()
                            for k in range(KC):
                                S.op('pe', lambda e, b=bg, k=k, mi2=mi2, tl=tg, t0=t0, t1=t1: e.matmul(
                                    PS[b][:, 0:t1 - t0], lhsT=tl.c(k, mi2), rhs=HT[:, k, t0:t1],
                                    start=(k == 0), stop=(k == KC - 1)),
                                    reads=[tg.key(mi2), ('HT', k)], writes=[('ps', bg)])
                            for k in range(KC):
                                S.op('pe', lambda e, b=bu, k=k, mi2=mi2, tl=tu, t0=t0, t1=t1: e.matmul(
                                    PS[b][:, 0:t1 - t0], lhsT=tl.c(k, mi2), rhs=HT[:, k, t0:t1],
                                    start=(k == 0), stop=(k == KC - 1)),
                                    reads=[tu.key(mi2), ('HT', k)], writes=[('ps', bu)])
                            i = rot('ev', 3)
                            S.op('act', lambda e, b=bg, i=i, n=n: e.activation(out=EV[i][:, 0:n], in_=PS[b][:, 0:n], func=AF.Silu),
                                 reads=[('ps', bg)], writes=['EV%d' % i])
                            S.op('dve', lambda e, b=bu, i=i, n=n, mi=mi, t0=t0, t1=t1: e.tensor_tensor(
                                out=GB[:, mi, t0:t1], in0=PS[b][:, 0:n], in1=EV[i][:, 0:n], op=ALU.mult),
                                reads=[('ps', bu), 'EV%d' % i], writes=['GB'])
                out_proj_partial(hf, l, 80, f_w_out[l], blk * 512)

        def final_out(hf):
            compute_rstd(NORM_EPS)
            for tt in range(NTILE):
                rows = 128 if tt < NCH else NS
                i = rot('tmpf', 2)
                t, tk = TMPF[i], 'TMPF%d' % i
                for k4 in range(4):
                    j = rot('ev', 3)
                    S.op('dve', lambda e, j=j, k4=k4, tt=tt, rows=rows: e.tensor_tensor(
                        out=EV[j][:, :].rearrange("p (k t) -> p k t", t=128)[:, :, 0:rows],
                        in0=XT[:, k4 * 4:(k4 + 1) * 4, tt * 128:tt * 128 + rows],
                        in1=RSTD[:, tt * 128:tt * 128 + rows].unsqueeze(1).to_broadcast([128, 4, rows]), op=ALU.mult),
                        reads=[kx for kk_ in range(4) for kx in xk(k4 * 4 + kk_)] + ['RSTD'], writes=['EV%d' % j])
                    b = next_ps()
                    for kk in range(4):
                        k = k4 * 4 + kk
                        S.op('act', lambda e, j=j, kk=kk, k=k, rows=rows: e.activation(
                            out=EV[j][:, kk * 128:kk * 128 + rows], in_=EV[j][:, kk * 128:kk * 128 + rows],
                            func=AF.Copy, scale=col('fnw', k)),
                            reads=['EV%d' % j, 'COLS'], writes=['EV%d' % j])
                        S.op('pe', lambda e, b=b, j=j, kk=kk, rows=rows: e.transpose(
                            out=PS[b][0:rows, kk * 128:(kk + 1) * 128], in_=EV[j][:, kk * 128:kk * 128 + rows], identity=IDF[:]),
                            reads=['EV%d' % j, 'IDF'], writes=[('ps', b)])
                    S.op('dve', lambda e, b=b, k4=k4, rows=rows, t=t: e.tensor_copy(out=t[0:rows, k4 * 512:(k4 + 1) * 512], in_=PS[b][0:rows, :]),
                         reads=[('ps', b)], writes=[tk])
                if tt < NCH:
                    S.dma('sp', y_p[hf * NPT + tt * 128: hf * NPT + (tt + 1) * 128, :], t[:, :], reads=[tk])
                else:
                    S.dma('sp', y_s[hf * NS:(hf + 1) * NS, :], t[0:NS, :], reads=[tk])

        for hf in range(NPASS):
            load_x(hf)
            norm_mod(hf, 0, 0, 'nmw0')
            gmlp(hf)
            norm_mod(hf, 0, 1, 'nfw0')
            ffn(hf, 0)
            if STAGE >= 2:
                norm_mod(hf, 1, 0, 'nmw1')
                S.barrier()
                ssd(hf)
                S.barrier()
                norm_mod(hf, 1, 1, 'nfw1')
                ffn(hf, 1)
            final_out(hf)
            S.barrier()

        S.emit()
    return nc, S


_CACHE = {}


def _get_program():
    if 'nc' not in _CACHE:
        _CACHE['nc'] = build_program()
    return _CACHE['nc']


def kernel(x_prompt, x_sample, c_prompt, c_sample, state_ssm, state_conv,
           mod_w, mod_b, norm_mix_w, norm_ffn_w,
           a_w_in, a_b_in, a_ln_w, a_ln_b, a_w_s, a_b_s, a_w_out,
           b_w_in, b_conv_w, b_conv_b, b_dt_bias, b_a_log, b_d, b_norm_w, b_w_out,
           f_w_in, f_w_out, final_norm_w):
    f = lambda a: np.ascontiguousarray(np.asarray(a, dtype=np.float32))
    nc, S = _get_program()
    vecs = np.zeros((VEC_TOT, 128), np.float32)

    def put(name, arr):
        r0, n = VEC_LAY[name]
        vecs[r0:r0 + n] = np.asarray(arr, np.float32).reshape(n, 128)

    put('mod_b0', mod_b[0]); put('mod_b1', mod_b[1])
    put('nmw0', norm_mix_w[0]); put('nmw1', norm_mix_w[1])
    put('nfw0', norm_ffn_w[0]); put('nfw1', norm_ffn_w[1])
    put('a_b_in', a_b_in[0]); put('fnw', final_norm_w)
    put('conv_w', b_conv_w[0]); put('conv_b', b_conv_b[0])
    shared = dict(vecs=vecs, mod_w=f(mod_w), a_w_in=f(a_w_in[0]), a_ln_w=f(a_ln_w[0]), a_ln_b=f(a_ln_b[0]),
                  a_b_in_r=f(a_b_in[0]), a_w_s=f(a_w_s[0]), a_b_s=f(a_b_s[0]), a_w_out=f(a_w_out[0]),
                  f_w_in=f(f_w_in), f_w_out=f(f_w_out))
    shared.update(b_w_in=f(b_w_in[0]), b_w_out=f(b_w_out[0]), b_dtb=f(b_dt_bias[0]), b_alog=f(b_a_log[0]),
                  b_dsk=f(b_d[0]), b_nw=f(b_norm_w[0]))
    in_maps = []
    for c in range(8):
        m = dict(shared)
        m['st_ssm'] = f(np.asarray(state_ssm)[0, 16 * c:16 * (c + 1)])
        m['st_conv'] = f(np.asarray(state_conv)[0, 16 * c:16 * (c + 1)].reshape(48, 6144))
        m['xp'] = f(x_prompt[c % 4])
        m['xs'] = f(np.asarray(x_sample)[16 * c:16 * (c + 1)].reshape(128, D))
        m['call'] = f(np.concatenate([np.asarray(c_prompt)[c % 4][None], np.asarray(c_sample)[16 * c:16 * (c + 1)]], 0))
        in_maps.append(m)
    res = run_bass_kernel_spmd(nc, in_maps, core_ids=list(range(8)))
    R = res.results
    y_prompt = np.stack([R[c]['y_p'] for c in range(4)], 0)
    y_sample = np.concatenate([R[c]['y_s'].reshape(16, 8, D) for c in range(8)], 0)
    v_prompt = np.stack([R[c]['v_p'] for c in range(4)], 0)[None]
    v_sample = np.concatenate([R[c]['v_s'].reshape(16, 8, D) for c in range(8)], 0)[None]
    ssm_prompt = np.stack([R[c]['ssm_p'].reshape(64, 64, 128) for c in range(4)], 0)[None]
    ssm_sample = np.concatenate([R[c]['ssm_s'] for c in range(8)], 0)[None]
    conv_prompt = np.stack([R[c]['conv_p'] for c in range(4)], 0)[None]
    conv_sample = np.concatenate([R[c]['conv_s'].reshape(16, 3, 6144) for c in range(8)], 0)[None]
    return (y_prompt, y_sample, v_prompt, v_sample, ssm_prompt, ssm_sample, conv_prompt, conv_sample)
```

```python
import numpy as np
from contextlib import ExitStack
import concourse.bass as bass
import concourse.mybir as mybir
from concourse.bass_utils import run_bass_kernel_spmd

F32 = mybir.dt.float32
BF16 = mybir.dt.bfloat16
AF = mybir.ActivationFunctionType
ALU = mybir.AluOpType
AX = mybir.AxisListType

D = 2048
KC = 16
NPASS = 4
NCH = 4
SSQ = 4
NS = SSQ * 8
NPT = NCH * 128
NTOK = NPT + NS
NTILE = NCH + 1
TBS = [(0, 256), (256, NTOK)]
FFN_H = 5632
SSD_IN = 10304
NORM_EPS = 1e-6
LN_EPS = 1e-5
SAME_ENGINE_SYNC = True
STAGE = 2


class Sched:
    ENG = ('pe', 'dve', 'act', 'pool', 'sp')

    def __init__(self, nc, n_dma_sems=(24, 8, 16)):
        self.nc = nc
        self.ins = []
        self.last_w = {}
        self.readers = {}
        self.n_dma_sems = dict(sp=n_dma_sems[0], act=n_dma_sems[1], pool=n_dma_sems[2])

    def _add(self, eng, fn, reads, writes, kind):
        idx = len(self.ins)
        deps = set()
        raw = set()
        for k in reads:
            w = self.last_w.get(k)
            if w is not None:
                deps.add(w)
                raw.add(w)
        for k in writes:
            w = self.last_w.get(k)
            if w is not None:
                deps.add(w)
            for r in self.readers.get(k, {}).values():
                if isinstance(r, list):
                    deps.update(r)
                else:
                    deps.add(r)
        deps.discard(idx)
        if eng in ('dve', 'act', 'pool'):
            deps = set(d for d in deps if d in raw or self.ins[d]['eng'] != eng or self.ins[d]['kind'] == 'dma')
        self.ins.append(dict(eng=eng, fn=fn, deps=deps, kind=kind, needed=False))
        for k in writes:
            self.last_w[k] = idx
            self.readers[k] = {}
        for k in reads:
            if k not in writes:
                rd = self.readers.setdefault(k, {})
                if kind == 'dma':
                    rd.setdefault('dma', []).append(idx)
                else:
                    rd[eng] = idx
        return idx

    def op(self, eng, fn, reads=(), writes=()):
        return self._add(eng, fn, list(reads), list(writes), 'op')

    def dma(self, eng, out, in_, reads=(), writes=(), **kw):
        def fn(e, out=out, in_=in_, kw=kw):
            return e.dma_start(out=out, in_=in_, **kw)
        return self._add(eng, fn, list(reads), list(writes), 'dma')

    def barrier(self):
        last = {}
        for i, it in enumerate(self.ins):
            if it['kind'] != 'dma':
                last[it['eng']] = i
        deps = set(last.values())
        for k, w in self.last_w.items():
            if self.ins[w]['kind'] == 'dma':
                deps.add(w)
        for k, rs in self.readers.items():
            deps.update(rs.get('dma', []))
        for e in self.ENG:
            self.ins.append(dict(eng=e, fn=None, deps=set(deps), kind='nop', needed=False))
        self.last_w.clear()
        self.readers.clear()

    def emit(self):
        nc = self.nc
        ins = self.ins
        deps = set()
        last = {}
        for i, it in enumerate(ins):
            if it['kind'] == 'dma':
                deps.add(i)
            elif it['kind'] == 'op':
                last[it['eng']] = i
        deps |= set(last.values())
        ins.append(dict(eng='sp', fn=None, deps=deps, kind='nop', needed=False))
        for it in ins:
            for d in it['deps']:
                p = ins[d]
                if p['kind'] == 'dma':
                    p['needed'] = True
                elif p['eng'] == it['eng'] and (p['eng'] in ('pe', 'sp') or not SAME_ENGINE_SYNC):
                    pass
                else:
                    p['needed'] = True
        cnt = {e: 0 for e in self.ENG}
        dcnt = {e: 0 for e in self.ENG}
        for it in ins:
            e = it['eng']
            if it['kind'] == 'dma':
                n = self.n_dma_sems[e]
                j = dcnt[e]
                dcnt[e] += 1
                it['sig'] = ('d', e, j % n, 16 * (j // n + 1))
                it['prev_on_sem'] = 16 * (j // n)
            elif it['kind'] == 'op' and it['needed']:
                cnt[e] += 1
                it['sig'] = ('e', e, 0, cnt[e])
            else:
                it['sig'] = None
        self.stats = dict(cnt=cnt, dcnt=dcnt, n=len(ins))
        with ExitStack() as es:
            esem = {e: es.enter_context(nc.semaphore('s_' + e)) for e in self.ENG}
            dsem = {}
            for e in ('sp', 'act', 'pool'):
                nd = min(self.n_dma_sems[e], max(dcnt[e], 1))
                dsem[e] = [es.enter_context(nc.semaphore('d_%s%d' % (e, i))) for i in range(nd)]
            per = {e: [] for e in self.ENG}
            for i, it in enumerate(ins):
                per[it['eng']].append(i)
            block = es.enter_context(nc.Block())

            def make(e):
                def body(engobj):
                    wm = {}
                    for i in per[e]:
                        it = ins[i]
                        waits = {}
                        for d in it['deps']:
                            p = ins[d]
                            sg = p.get('sig')
                            if sg is None:
                                continue
                            if sg[0] == 'e' and p['eng'] == e and (e in ('pe', 'sp') or not SAME_ENGINE_SYNC):
                                continue
                            key = sg[:3]
                            if wm.get(key, 0) >= sg[3]:
                                continue
                            waits[key] = max(waits.get(key, 0), sg[3])
                        if it['kind'] == 'dma' and it['prev_on_sem'] > 0:
                            key = it['sig'][:3]
                            if wm.get(key, 0) < it['prev_on_sem']:
                                waits[key] = max(waits.get(key, 0), it['prev_on_sem'])
                        for key, v in waits.items():
                            sem = esem[key[1]] if key[0] == 'e' else dsem[key[1]][key[2]]
                            engobj.wait_ge(sem, v)
                            wm[key] = v
                        if it['fn'] is None:
                            continue
                        r = it['fn'](engobj)
                        sg = it['sig']
                        if sg is not None:
                            if sg[0] == 'e':
                                r.then_inc(esem[e], 1)
                            else:
                                r.then_inc(dsem[e][sg[2]], 16)
                return body

            block.tensor(make('pe'))
            block.vector(make('dve'))
            block.scalar(make('act'))
            block.gpsimd(make('pool'))
            block.sync(make('sp'))


VEC_ROWS = {}


def _vec_layout():
    off = 0
    lay = {}
    for name, n in [('mod_b0', 96), ('mod_b1', 96), ('nmw0', 16), ('nmw1', 16), ('nfw0', 16),
                    ('nfw1', 16), ('a_b_in', 32), ('fnw', 16), ('conv_w', 192), ('conv_b', 48)]:
        lay[name] = (off, n)
        off += n
    tot = ((off + 127) // 128) * 128
    return lay, tot


VEC_LAY, VEC_TOT = _vec_layout()


def build_program():
    nc = bass.Bass("TRN2", target_bir_lowering=False)
    din = lambda name, shape: nc.dram_tensor(name, list(shape), F32, kind="ExternalInput").ap()
    dout = lambda name, shape: nc.dram_tensor(name, list(shape), F32, kind="ExternalOutput").ap()
    xp = din("xp", [2048, D])
    xs = din("xs", [128, D])
    call = din("call", [17, D])
    vecs = din("vecs", [VEC_TOT, 128])
    mod_w = din("mod_w", [2, D, 6 * D])
    a_w_in = din("a_w_in", [D, 2 * D])
    a_ln_w = din("a_ln_w", [D])
    a_ln_b = din("a_ln_b", [D])
    a_b_in_r = din("a_b_in_r", [2 * D])
    a_w_s = din("a_w_s", [16, 128, 128])
    a_b_s = din("a_b_s", [16, 128])
    a_w_out = din("a_w_out", [D, D])
    f_w_in = din("f_w_in", [2, D, 2 * FFN_H])
    f_w_out = din("f_w_out", [2, FFN_H, D])
    y_p = dout("y_p", [2048, D])
    y_s = dout("y_s", [128, D])
    v_p = dout("v_p", [128, D])
    v_s = dout("v_s", [128, D])
    st_ssm = din("st_ssm", [16, 64, 64, 128])
    st_conv = din("st_conv", [16 * 3, 6144])
    b_w_in = din("b_w_in", [D, SSD_IN])
    b_w_out = din("b_w_out", [2 * D, D])
    b_dtb = din("b_dtb", [64])
    b_alog = din("b_alog", [64])
    b_dsk = din("b_dsk", [64])
    b_nw = din("b_nw", [2 * D])
    ssm_p = dout("ssm_p", [64 * 64, 128])
    ssm_s = dout("ssm_s", [16, 64, 64, 128])
    conv_p = dout("conv_p", [3, 6144])
    conv_s = dout("conv_s", [16 * 3, 6144])
    hst_d = nc.dram_tensor("hst_d", [8, 128, 512], F32).ap()

    S = Sched(nc)
    es = ExitStack()
    with es:
        def sb(name, shape, dt=F32):
            return es.enter_context(nc.sbuf_tensor(name, list(shape), dt))

        XT = sb("XT", [128, KC, NTOK])
        HT = sb("HT", [128, KC, NTOK], BF16)
        MOD = [sb("MOD%d" % l, [128, 96, 17]) for l in range(2)]
        COLS = sb("COLS", [128, VEC_TOT])
        WT = [sb("WT%d" % i, [128, KC, 256], BF16) for i in range(4)]
        WO = [sb("WO%d" % i, [128, 4, 512], BF16) for i in range(2)]
        GB = sb("GB", [128, 4, NTOK], BF16)
        IDF = sb("IDF", [128, 128])
        IDB = sb("IDB", [128, 128], BF16)
        ONESB = sb("ONESB", [128, 128], BF16)
        RSTD = sb("RSTD", [128, NTOK])
        ACOL = sb("ACOL", [128, KC, 17])
        TMPF = [sb("TMPF%d" % i, [128, 2048]) for i in range(2)]
        SQ = [sb("SQ%d" % i, [128, NTOK], BF16) for i in range(2)]
        EV = [sb("EV%d" % i, [128, 512]) for i in range(3)]
        CT = sb("CT", [128, KC, 17], BF16)
        VN = sb("VN", [128, NTILE, 2048], BF16)
        BINV = sb("BINV", [128, 2048], BF16)
        LNW = sb("LNW", [128, 2048], BF16)
        LNB = sb("LNB", [128, 2048], BF16)
        WST = sb("WST", [128, 16, 128], BF16)
        BDS = sb("BDS", [NS, 16, NS], BF16)
        BSR = sb("BSR", [1, 16, 128], BF16)
        BSS = sb("BSS", [1, 16, NS], BF16)
        CMASK = sb("CMASK", [128, 128])
        SEQM = sb("SEQM", [NS, SSQ])
        E8 = sb("E8", [8, SSQ, 8], BF16)
        STATS = sb("STATS", [128, NTILE, 4, 6])
        MV = sb("MV", [128, NTILE, 2])
        SMALL = sb("SMALL", [128, 64])
        UT = sb("UT", [128, 128])
        USB = sb("USB", [NS, NS])
        ONESF = sb("ONESF", [128, 128])
        SEQ32 = sb("SEQ32", [NS, NS])
        MASK3 = sb("MASK3", [128, SSQ, NS])
        DTB = sb("DTB", [128, 64]); AROW = sb("AROW", [128, 64]); DROW = sb("DROW", [128, 64])
        DT = sb("DT", [128, NTILE, 64]); DA = sb("DA", [128, NTILE, 64]); ACS = sb("ACS", [128, NTILE, 64])
        TOT = sb("TOT", [128, NTILE, 64]); EXPA = sb("EXPA", [128, NTILE, 64]); DTE = sb("DTE", [128, NTILE, 64])
        DEC = sb("DEC", [128, NTILE, 64])
        CTAIL = sb("CTAIL", [128, 48, 3])
        HB = sb("HB", [128, 512], BF16)
        BCT = sb("BCT", [128, 2, NTOK], BF16)
        DECS = sb("DECS", [128, 4, SSQ])
        DAB = sb("DAB", [NS, 128])
        H0T = sb("H0T", [128, 512], BF16)
        CMS = sb("CMS", [128, SSQ, NS], BF16)
        WM = sb("WM", [NS, 512], BF16)
        CBM = sb("CBM", [128, 128], BF16)
        XDT = sb("XDT", [128, 512], BF16)
        XDB = sb("XDB", [128, 512], BF16)
        XC2 = sb("XC2", [128, NTOK])
        WW = sb("WW", [128, 512], BF16)
        SM12 = [sb("SM12_%d" % i, [128, 128]) for i in range(2)]
        VNF = VN[:].rearrange("p t c -> p (t c)")
        SZ = VNF[:, 0:NTILE * 512].rearrange("p (t c) -> p t c", c=512)
        XTOK = VNF[:, NTILE * 512:2 * NTILE * 512].rearrange("p (t c) -> p t c", c=512)
        BTOK = VNF[:, 2 * NTILE * 512:2 * NTILE * 512 + NTILE * 128].rearrange("p (t c) -> p t c", c=128)
        _o = 2 * NTILE * 512 + NTILE * 128
        EE = VNF[:, _o:_o + 2048].bitcast(F32).rearrange("p (r i) -> p r i", i=128)
        LT = VNF[:, _o + 2048:_o + 3072].rearrange("p (r i) -> p r i", i=128)
        MT = VNF[:, _o + 3072:_o + 4096].rearrange("p (r i) -> p r i", i=128)
        assert _o + 4096 <= NTILE * 2048
        RAW = TMPF[0][:, 0:3 + NPT + SSQ * 11]
        XC_A = TMPF[0][:, 560:560 + NTOK]
        HSTG = TMPF[0][:, 1104:1616]
        NWR = TMPF[1][:, 0:512]
        H0 = TMPF[1][:, 512:1024].rearrange("p (q n) -> p q n", n=128)
        SCV = TMPF[1][:, 1024:1024 + 48 * SSQ * 3].rearrange("p (c k) -> p c k", k=SSQ * 3)
        CSO = SCV

        PS = [es.enter_context(nc.psum_tensor("PS%d" % i, [128, 512], F32)) for i in range(8)]
        ps_ctr = [0]

        def next_ps(excl=()):
            while True:
                i = ps_ctr[0] % 8
                ps_ctr[0] += 1
                if i not in excl:
                    return i

        ctr = {'wt': 0, 'wo': 0, 'ev': 0, 'evb': 0, 'tmpf': 0, 'sq': 0, 'sm12': 0}

        def rot(name, n):
            i = ctr[name] % n
            ctr[name] += 1
            return i

        def affine_mask(tile_ap, key, pattern, base, cm):
            S.op('pool', lambda e: e.memset(tile_ap, 1.0), writes=[key])
            S.op('pool', lambda e: e.affine_select(out=tile_ap, in_=tile_ap, pattern=pattern,
                                                   compare_op=ALU.is_ge, fill=0.0, base=base,
                                                   channel_multiplier=cm),
                 reads=[key], writes=[key])

        S.op('pool', lambda e: e.memset(IDF[:], 1.0), writes=['IDF'])
        S.op('pool', lambda e: e.affine_select(out=IDF[:], in_=IDF[:], pattern=[[-1, 128]], compare_op=ALU.is_ge,
                                               fill=0.0, base=0, channel_multiplier=1), reads=['IDF'], writes=['IDF'])
        S.op('pool', lambda e: e.affine_select(out=IDF[:], in_=IDF[:], pattern=[[1, 128]], compare_op=ALU.is_ge,
                                               fill=0.0, base=0, channel_multiplier=-1), reads=['IDF'], writes=['IDF'])
        S.op('dve', lambda e: e.tensor_copy(out=IDB[:], in_=IDF[:]), reads=['IDF'], writes=['IDB'])
        S.op('pool', lambda e: e.memset(ONESB[:], 1.0), writes=['ONESB'])
        affine_mask(CMASK[:], 'CMASK', [[1, 128]], 0, -1)
        S.op('pool', lambda e: e.memset(SEQM[:], 1.0), writes=['SEQM'])
        S.op('pool', lambda e: e.affine_select(out=SEQM[:], in_=SEQM[:], pattern=[[-8, SSQ]], compare_op=ALU.is_ge,
                                               fill=0.0, base=0, channel_multiplier=1), reads=['SEQM'], writes=['SEQM'])
        S.op('pool', lambda e: e.affine_select(out=SEQM[:], in_=SEQM[:], pattern=[[8, SSQ]], compare_op=ALU.is_ge,
                                               fill=0.0, base=7, channel_multiplier=-1), reads=['SEQM'], writes=['SEQM'])
        S.op('dve', lambda e: e.tensor_copy(out=E8[:], in_=IDF[0:8, 0:8].unsqueeze(1).to_broadcast([8, SSQ, 8])),
             reads=['IDF'], writes=['E8'])

        for i in range(VEC_TOT // 128):
            t = TMPF[rot('tmpf', 2)]
            tk = 'TMPF%d' % ((ctr['tmpf'] - 1) % 2)
            S.dma('sp', t[:, 0:128], vecs[i * 128:(i + 1) * 128, :], writes=[tk])
            b = next_ps()
            S.op('pe', lambda e, b=b, t=t: e.transpose(out=PS[b][:, 0:128], in_=t[:, 0:128], identity=IDF[:]),
                 reads=[tk, 'IDF'], writes=[('ps', b)])
            S.op('dve', lambda e, b=b, i=i: e.tensor_copy(out=COLS[:, i * 128:(i + 1) * 128], in_=PS[b][:, 0:128]),
                 reads=[('ps', b)], writes=['COLS'])

        def col(name, j=0, n=1):
            r0, nr = VEC_LAY[name]
            return COLS[:, r0 + j:r0 + j + n]

        S.dma('pool', BINV[:], a_b_in_r[2048:4096].partition_broadcast(128), writes=['BINV'])
        S.dma('pool', LNW[:], a_ln_w.partition_broadcast(128), writes=['LNW'])
        S.dma('pool', LNB[:], a_ln_b.partition_broadcast(128), writes=['LNB'])
        S.dma('pool', BSR[:], a_b_s.rearrange("(o g) t -> o g t", o=1), writes=['BSR'])
        S.op('dve', lambda e: e.tensor_copy(out=BSS[:].rearrange("o g (b t) -> o g b t", t=8),
                                            in_=BSR[:, :, 0:8].unsqueeze(2).to_broadcast([1, 16, SSQ, 8])),
             reads=['BSR'], writes=['BSS'])

        t = TMPF[rot('tmpf', 2)]
        tk = 'TMPF%d' % ((ctr['tmpf'] - 1) % 2)
        S.dma('sp', t[:].rearrange("p (g s) -> p g s", g=16), a_w_s.rearrange("g t s -> t g s"), writes=[tk])
        for g in range(16):
            b = next_ps()
            S.op('pe', lambda e, b=b, g=g, t=t: e.transpose(out=PS[b][:, 0:128], in_=t[:, g * 128:(g + 1) * 128], identity=IDF[:]),
                 reads=[tk, 'IDF'], writes=[('ps', b)])
            S.op('dve', lambda e, b=b, g=g: e.tensor_tensor(out=WST[:, g, :], in0=PS[b][:, 0:128], in1=CMASK[:], op=ALU.mult),
                 reads=[('ps', b), 'CMASK'], writes=['WST'])
        b = next_ps()
        S.op('pe', lambda e, b=b: e.matmul(PS[b][0:NS, 0:128], lhsT=E8[:].rearrange("s b t -> s (b t)"),
                                           rhs=WST[0:8, :, 0:8], start=True, stop=True),
             reads=['E8', 'WST'], writes=[('ps', b)])
        S.op('dve', lambda e, b=b: e.tensor_tensor(
            out=BDS[:].rearrange("p g (b t) -> p g b t", t=8),
            in0=PS[b][0:NS, 0:128].rearrange("p (g t) -> p g t", t=8).unsqueeze(2).to_broadcast([NS, 16, SSQ, 8]),
            in1=SEQM[:].unsqueeze(1).unsqueeze(3).to_broadcast([NS, 16, SSQ, 8]), op=ALU.mult),
            reads=[('ps', b), 'SEQM'], writes=['BDS'])

        class WTile:
            def __init__(self, halves):
                self.halves = halves

            def c(self, k, mi):
                tl, key = self.halves[mi // 2]
                o = (mi % 2) * 128
                return tl[:, k, o:o + 128]

            def key(self, mi):
                return self.halves[mi // 2][1]

        def load_w(wap, r0, nk, c0, ncols=512, pool='wt'):
            if pool == 'wo':
                i = rot('wo', 2)
                tl, key = WO[i], 'WO%d' % i
                src = wap[r0:r0 + nk * 128, c0:c0 + ncols].rearrange("(k p) c -> p k c", p=128)
                S.dma('pool', tl[:, 0:nk, 0:ncols], src, writes=[key])
                return tl, key
            halves = []
            for h0 in range(0, ncols, 256):
                w = min(256, ncols - h0)
                i = rot('wt', 4)
                tl, key = WT[i], 'WT%d' % i
                src = wap[r0:r0 + nk * 128, c0 + h0:c0 + h0 + w].rearrange("(k p) c -> p k c", p=128)
                S.dma('pool', tl[:, 0:nk, 0:w], src, writes=[key])
                halves.append((tl, key))
            return WTile(halves), None

        t = TMPF[rot('tmpf', 2)]
        tk = 'TMPF%d' % ((ctr['tmpf'] - 1) % 2)
        S.dma('sp', t[0:17, :], call, writes=[tk])
        S.op('act', lambda e, t=t: e.activation(out=t[0:17, :], in_=t[0:17, :], func=AF.Silu), reads=[tk], writes=[tk])
        for k4 in range(4):
            b = next_ps()
            for kk in range(4):
                k = k4 * 4 + kk
                S.op('pe', lambda e, b=b, k=k, kk=kk, t=t: e.transpose(out=PS[b][:, kk * 17:(kk + 1) * 17],
                                                                      in_=t[0:17, k * 128:(k + 1) * 128], identity=IDF[0:17, 0:17]),
                     reads=[tk, 'IDF'], writes=[('ps', b)])
            S.op('dve', lambda e, b=b, k4=k4: e.tensor_copy(out=CT[:, k4 * 4:(k4 + 1) * 4, :].rearrange("p k c -> p (k c)"),
                                                           in_=PS[b][:, 0:68]),
                 reads=[('ps', b)], writes=['CT'])
        for l in range(2):
            for nb in range(24):
                tl, key = load_w(mod_w[l], 0, KC, nb * 512)
                b = next_ps()
                for mi in range(4):
                    for k in range(KC):
                        S.op('pe', lambda e, b=b, mi=mi, k=k, tl=tl: e.matmul(
                            PS[b][:, mi * 17:(mi + 1) * 17], lhsT=tl.c(k, mi), rhs=CT[:, k, :],
                            start=(k == 0), stop=(k == KC - 1)),
                            reads=[tl.key(mi), 'CT'], writes=[('ps', b)])
                for mi in range(4):
                    m = nb * 4 + mi
                    S.op('act', lambda e, b=b, mi=mi, m=m, l=l: e.activation(
                        out=MOD[l][:, m, :], in_=PS[b][:, mi * 17:(mi + 1) * 17], func=AF.Identity,
                        bias=col('mod_b%d' % l, m), scale=1.0),
                        reads=[('ps', b), 'COLS'], writes=['MOD%d' % l])

        def load_x(hf):
            for tt in range(NTILE):
                rows = 128 if tt < NCH else NS
                t = TMPF[rot('tmpf', 2)]
                tk = 'TMPF%d' % ((ctr['tmpf'] - 1) % 2)
                if tt < NCH:
                    src = xp[hf * NPT + tt * 128: hf * NPT + (tt + 1) * 128, :]
                else:
                    src = xs[hf * NS:(hf + 1) * NS, :]
                S.dma('sp', t[0:rows, :], src, writes=[tk])
                for k4 in range(4):
                    b = next_ps()
                    for kk in range(4):
                        k = k4 * 4 + kk
                        S.op('pe', lambda e, b=b, k=k, kk=kk, t=t, rows=rows: e.transpose(
                            out=PS[b][:, kk * 128:kk * 128 + rows], in_=t[0:rows, k * 128:(k + 1) * 128],
                            identity=IDF[0:rows, 0:rows]),
                            reads=[tk, 'IDF'], writes=[('ps', b)])
                    S.op('dve', lambda e, b=b, k4=k4, tt=tt, rows=rows: e.tensor_copy(
                        out=XT[:, k4 * 4:(k4 + 1) * 4, tt * 128:tt * 128 + rows],
                        in_=PS[b][:].rearrange("p (k t) -> p k t", t=128)[:, :, 0:rows]),
                        reads=[('ps', b)], writes=[('XT', k4 * 4 + kk_, tb_of(tt * 128)) for kk_ in range(4)])

        def tb_of(tok):
            return 0 if tok < TBS[0][1] else 1

        def xk(m):
            return [('XT', m, 0), ('XT', m, 1)]

        def compute_rstd(eps):
            bs = [next_ps() for _ in TBS]
            for k in range(KC):
                i = rot('sq', 2)
                S.op('act', lambda e, i=i, k=k: e.activation(out=SQ[i][:], in_=XT[:, k, :], func=AF.Square),
                     reads=xk(k), writes=['SQ%d' % i])
                for bi, (t0, t1) in enumerate(TBS):
                    S.op('pe', lambda e, b=bs[bi], i=i, t0=t0, t1=t1, k=k: e.matmul(
                        PS[b][:, 0:t1 - t0], lhsT=ONESB[:], rhs=SQ[i][:, t0:t1], start=(k == 0), stop=(k == KC - 1)),
                        reads=['SQ%d' % i, 'ONESB'], writes=[('ps', bs[bi])])
            for bi, (t0, t1) in enumerate(TBS):
                b = bs[bi]
                S.op('dve', lambda e, b=b, t0=t0, t1=t1: e.tensor_scalar(
                    out=RSTD[:, t0:t1], in0=PS[b][:, 0:t1 - t0], scalar1=1.0 / D, scalar2=eps, op0=ALU.mult, op1=ALU.add),
                    reads=[('ps', b)], writes=['RSTD'])
            S.op('act', lambda e: e.activation(out=RSTD[:], in_=RSTD[:], func=AF.Ln), reads=['RSTD'], writes=['RSTD'])
            S.op('act', lambda e: e.activation(out=RSTD[:], in_=RSTD[:], func=AF.Exp, scale=-0.5), reads=['RSTD'], writes=['RSTD'])

        def norm_mod(hf, l, which, nw_name):
            compute_rstd(NORM_EPS)
            sh0, sc0 = which * 48, which * 48 + 16
            for k in range(KC):
                S.op('dve', lambda e, k=k: e.tensor_scalar(
                    out=ACOL[:, k, :], in0=MOD[l][:, sc0 + k, :], scalar1=1.0, scalar2=col(nw_name, k),
                    op0=ALU.add, op1=ALU.mult),
                    reads=['MOD%d' % l, 'COLS'], writes=['ACOL'])
            for k in range(KC):
                i = rot('tmpf', 2)
                t, tk = TMPF[i], 'TMPF%d' % i
                S.op('dve', lambda e, t=t, k=k: e.tensor_tensor(out=t[:, 0:NTOK], in0=XT[:, k, :], in1=RSTD[:], op=ALU.mult),
                     reads=xk(k) + ['RSTD'], writes=[tk])
                S.op('act', lambda e, t=t, k=k: e.activation(
                    out=HT[:, k, 0:NPT], in_=t[:, 0:NPT], func=AF.Identity,
                    bias=MOD[l][:, sh0 + k, 0:1], scale=ACOL[:, k, 0:1]),
                    reads=[tk, 'ACOL', 'MOD%d' % l], writes=[('HT', k)])
                c0 = 1 + hf * SSQ
                S.op('dve', lambda e, t=t, k=k, c0=c0: e.tensor_tensor(
                    out=t[:, NPT:NTOK].rearrange("p (b t) -> p b t", t=8),
                    in0=t[:, NPT:NTOK].rearrange("p (b t) -> p b t", t=8),
                    in1=ACOL[:, k, c0:c0 + SSQ].unsqueeze(2).to_broadcast([128, SSQ, 8]), op=ALU.mult),
                    reads=[tk, 'ACOL'], writes=[tk])
                S.op('dve', lambda e, t=t, k=k, c0=c0: e.tensor_tensor(
                    out=HT[:, k, NPT:NTOK].rearrange("p (b t) -> p b t", t=8),
                    in0=t[:, NPT:NTOK].rearrange("p (b t) -> p b t", t=8),
                    in1=MOD[l][:, sh0 + k, c0:c0 + SSQ].unsqueeze(2).to_broadcast([128, SSQ, 8]), op=ALU.add),
                    reads=[tk, 'MOD%d' % l], writes=[('HT', k)])

        def ht_keys():
            return [('HT', k) for k in range(KC)]

        def resid_evac(hf, l, gate0, b, m, t0, t1):
            n = t1 - t0
            npr = min(t1, NPT) - t0
            S.op('dve', lambda e: e.scalar_tensor_tensor(
                out=XT[:, m, t0:t0 + npr], in0=PS[b][:, 0:npr], scalar=MOD[l][:, gate0 + m, 0:1],
                in1=XT[:, m, t0:t0 + npr], op0=ALU.mult, op1=ALU.add),
                reads=[('ps', b), 'MOD%d' % l, ('XT', m, tb_of(t0))], writes=[('XT', m, tb_of(t0))])
            if t1 > NPT:
                c0 = 1 + hf * SSQ
                i = rot('ev', 3)
                S.op('dve', lambda e: e.tensor_tensor(
                    out=EV[i][:, 0:NS].rearrange("p (b t) -> p b t", t=8),
                    in0=PS[b][:, npr:n].rearrange("p (b t) -> p b t", t=8),
                    in1=MOD[l][:, gate0 + m, c0:c0 + SSQ].unsqueeze(2).to_broadcast([128, SSQ, 8]), op=ALU.mult),
                    reads=[('ps', b), 'MOD%d' % l], writes=['EV%d' % i])
                S.op('dve', lambda e: e.tensor_tensor(out=XT[:, m, NPT:NTOK], in0=XT[:, m, NPT:NTOK], in1=EV[i][:, 0:NS], op=ALU.add),
                     reads=['EV%d' % i, ('XT', m, 1)], writes=[('XT', m, 1)])

        def out_proj_partial(hf, l, gate0, wap, r0):
            for cb in range(4):
                tl, key = load_w(wap, r0, 4, cb * 512, pool='wo')
                for mi in range(4):
                    m = cb * 4 + mi
                    for (t0, t1) in TBS:
                        b = next_ps()
                        for k in range(4):
                            S.op('pe', lambda e, b=b, k=k, mi=mi, tl=tl, t0=t0, t1=t1: e.matmul(
                                PS[b][:, 0:t1 - t0], lhsT=tl[:, k, mi * 128:(mi + 1) * 128], rhs=GB[:, k, t0:t1],
                                start=(k == 0), stop=(k == 3)),
                                reads=[key, 'GB'], writes=[('ps', b)])
                        resid_evac(hf, l, gate0, b, m, t0, t1)

        def gmlp(hf):
            hk = ht_keys()
            for nb in range(4):
                tl, key = load_w(a_w_in, 0, KC, 2048 + nb * 512)
                for tt in range(NTILE):
                    rows = 128 if tt < NCH else NS
                    b = next_ps()
                    for hh, (th, kh) in enumerate(tl.halves):
                        for k in range(KC):
                            S.op('pe', lambda e, b=b, k=k, th=th, hh=hh, tt=tt, rows=rows: e.matmul(
                                PS[b][0:rows, hh * 256:(hh + 1) * 256], lhsT=HT[:, k, tt * 128:tt * 128 + rows], rhs=th[:, k, :],
                                start=(k == 0), stop=(k == KC - 1)),
                                reads=[kh, ('HT', k)], writes=[('ps', b)])
                    i = rot('ev', 3)
                    S.op('dve', lambda e, b=b, i=i, nb=nb, rows=rows: e.tensor_tensor(
                        out=EV[i][0:rows, :], in0=PS[b][0:rows, :], in1=BINV[0:rows, nb * 512:(nb + 1) * 512], op=ALU.add),
                        reads=[('ps', b), 'BINV'], writes=['EV%d' % i])
                    S.op('act', lambda e, i=i, rows=rows: e.activation(out=EV[i][0:rows, :], in_=EV[i][0:rows, :], func=AF.Gelu),
                         reads=['EV%d' % i], writes=['EV%d' % i])
                    S.op('dve', lambda e, i=i, tt=tt, nb=nb, rows=rows: e.bn_stats(out=STATS[0:rows, tt, nb, :], in_=EV[i][0:rows, :]),
                         reads=['EV%d' % i], writes=[('STATS', tt)])
                    S.op('act', lambda e, i=i, tt=tt, nb=nb, rows=rows: e.activation(
                        out=VN[0:rows, tt, nb * 512:(nb + 1) * 512], in_=EV[i][0:rows, :], func=AF.Copy),
                        reads=['EV%d' % i], writes=[('VN', tt)])
            for tt in range(NTILE):
                rows = 128 if tt < NCH else NS
                S.op('dve', lambda e, tt=tt, rows=rows: e.bn_aggr(out=MV[0:rows, tt, :], in_=STATS[0:rows, tt, :, :].rearrange("p a b -> p (a b)")),
                     reads=[('STATS', tt)], writes=[('MV', tt)])
                S.op('dve', lambda e, tt=tt, rows=rows: e.tensor_scalar(out=SMALL[0:rows, tt:tt + 1], in0=MV[0:rows, tt, 1:2], scalar1=LN_EPS, scalar2=None, op0=ALU.add),
                     reads=[('MV', tt)], writes=[('SM', tt)])
                S.op('act', lambda e, tt=tt, rows=rows: e.activation(out=SMALL[0:rows, tt:tt + 1], in_=SMALL[0:rows, tt:tt + 1], func=AF.Sqrt),
                     reads=[('SM', tt)], writes=[('SM', tt)])
                S.op('dve', lambda e, tt=tt, rows=rows: e.reciprocal(out=SMALL[0:rows, tt:tt + 1], in_=SMALL[0:rows, tt:tt + 1]),
                     reads=[('SM', tt)], writes=[('SM', tt)])
                i = rot('tmpf', 2)
                t, tk = TMPF[i], 'TMPF%d' % i
                S.op('dve', lambda e, tt=tt, rows=rows, t=t: e.tensor_scalar(
                    out=t[0:rows, :], in0=VN[0:rows, tt, :], scalar1=MV[0:rows, tt, 0:1], scalar2=SMALL[0:rows, tt:tt + 1],
                    op0=ALU.subtract, op1=ALU.mult),
                    reads=[('VN', tt), ('MV', tt), ('SM', tt)], writes=[tk])
                S.op('dve', lambda e, rows=rows, t=t: e.tensor_tensor(out=t[0:rows, :], in0=t[0:rows, :], in1=LNW[0:rows, :], op=ALU.mult),
                     reads=[tk, 'LNW'], writes=[tk])
                S.op('dve', lambda e, rows=rows, t=t: e.tensor_tensor(out=t[0:rows, :], in0=t[0:rows, :], in1=LNB[0:rows, :], op=ALU.add),
                     reads=[tk, 'LNB'], writes=[tk])
                S.op('act', lambda e, tt=tt, rows=rows, t=t: e.activation(out=VN[0:rows, tt, :], in_=t[0:rows, :], func=AF.Copy),
                     reads=[tk], writes=[('VN', tt)])
                if tt == NCH:
                    S.dma('sp', v_s[hf * NS:(hf + 1) * NS, :], t[0:NS, :], reads=[tk])
                elif tt == NCH - 1 and hf == NPASS - 1:
                    S.dma('sp', v_p, t[:, :], reads=[tk])
            for nb in range(4):
                tl, key = load_w(a_w_in, 0, KC, nb * 512)
                for mi in range(4):
                    m = nb * 4 + mi
                    for (t0, t1) in TBS:
                        n = t1 - t0
                        bu = next_ps()
                        for k in range(KC):
                            S.op('pe', lambda e, b=bu, k=k, mi=mi, tl=tl, t0=t0, t1=t1: e.matmul(
                                PS[b][:, 0:t1 - t0], lhsT=tl.c(k, mi), rhs=HT[:, k, t0:t1],
                                start=(k == 0), stop=(k == KC - 1)),
                                reads=[tl.key(mi), ('HT', k)], writes=[('ps', bu)])
                        i = rot('ev', 3)
                        S.op('act', lambda e, b=bu, i=i, n=n, m=m: e.activation(
                            out=EV[i][:, 0:n], in_=PS[b][:, 0:n], func=AF.Gelu, bias=col('a_b_in', m), scale=1.0),
                            reads=[('ps', bu), 'COLS'], writes=['EV%d' % i])
                        bs_ = next_ps()
                        for tt in range(t0 // 128, (t1 + 127) // 128):
                            rows = 128 if tt < NCH else NS
                            c0 = tt * 128 - t0
                            rhs = WST[:, m, :] if tt < NCH else BDS[:, m, :]
                            brow = BSR[:, m, :] if tt < NCH else BSS[:, m, :]
                            S.op('pe', lambda e, b=bs_, tt=tt, rows=rows, c0=c0, rhs=rhs, m=m: e.matmul(
                                PS[b][:, c0:c0 + rows], lhsT=VN[0:rows, tt, m * 128:(m + 1) * 128], rhs=rhs,
                                start=True, stop=False),
                                reads=[('VN', tt), 'WST', 'BDS'], writes=[('ps', bs_)])
                            S.op('pe', lambda e, b=bs_, rows=rows, c0=c0, brow=brow: e.matmul(
                                PS[b][:, c0:c0 + rows], lhsT=ONESB[0:1, :], rhs=brow, start=False, stop=True),
                                reads=['ONESB', 'BSR', 'BSS'], writes=[('ps', bs_)])
                        S.op('dve', lambda e, b=bs_, i=i, n=n, mi=mi, t0=t0, t1=t1: e.tensor_tensor(
                            out=GB[:, mi, t0:t1], in0=PS[b][:, 0:n], in1=EV[i][:, 0:n], op=ALU.mult),
                            reads=[('ps', bs_), 'EV%d' % i], writes=['GB'])
                out_proj_partial(hf, 0, 32, a_w_out, nb * 512)


        affine_mask(UT[:], 'UT', [[1, 128]], 0, -1)
        S.op('pool', lambda e: e.memset(ONESF[:], 1.0), writes=['ONESF'])
        S.op('dve', lambda e: e.tensor_copy(out=SEQ32[:].rearrange("p (b t) -> p b t", t=8),
                                            in_=SEQM[:].unsqueeze(2).to_broadcast([NS, SSQ, 8])),
             reads=['SEQM'], writes=['SEQ32'])
        S.op('dve', lambda e: e.tensor_tensor(out=USB[:], in0=SEQ32[:], in1=UT[0:NS, 0:NS], op=ALU.mult),
             reads=['SEQ32', 'UT'], writes=['USB'])
        S.op('pool', lambda e: e.memset(MASK3[:], 1.0), writes=['MASK3'])
        S.op('pool', lambda e: e.affine_select(out=MASK3[:], in_=MASK3[:], pattern=[[-8, SSQ], [1, NS]], compare_op=ALU.is_ge,
                                               fill=0.0, base=0, channel_multiplier=0), reads=['MASK3'], writes=['MASK3'])
        S.op('pool', lambda e: e.affine_select(out=MASK3[:], in_=MASK3[:], pattern=[[8, SSQ], [-1, NS]], compare_op=ALU.is_ge,
                                               fill=0.0, base=7, channel_multiplier=0), reads=['MASK3'], writes=['MASK3'])
        S.dma('sp', DTB[:], b_dtb.partition_broadcast(128), writes=['DTB'])
        S.dma('sp', AROW[:], b_alog.partition_broadcast(128), writes=['AROW'])
        S.dma('sp', DROW[:], b_dsk.partition_broadcast(128), writes=['DROW'])
        S.op('act', lambda e: e.activation(out=AROW[:], in_=AROW[:], func=AF.Exp), reads=['AROW'], writes=['AROW'])
        S.op('dve', lambda e: e.tensor_scalar(out=AROW[:], in0=AROW[:], scalar1=-1.0, scalar2=None, op0=ALU.mult),
             reads=['AROW'], writes=['AROW'])
        S.op('pool', lambda e: e.memset(CTAIL[:], 0.0), writes=['CTAIL'])

        LIVE = [set()]

        def tr_store(src_ap, nrow, dst_ap, rkeys):
            b = next_ps(LIVE[0])
            S.op('pe', lambda e: e.transpose(out=PS[b][0:nrow, 0:128], in_=src_ap, identity=IDF[:]),
                 reads=list(rkeys) + ['IDF'], writes=[('ps', b)])
            j = rot('sm12', 2)
            S.op('dve', lambda e: e.tensor_copy(out=SM12[j][0:nrow, :], in_=PS[b][0:nrow, 0:128]),
                 reads=[('ps', b)], writes=['SM12_%d' % j])
            S.dma('sp', dst_ap, SM12[j][0:nrow, :], reads=['SM12_%d' % j])

        def ssd(hf):
            last = (hf == NPASS - 1)
            rows_of = lambda tt: 128 if tt < NCH else NS
            for cb in range(3):
                t, tk = TMPF[0], 'TMPF0'
                S.dma('sp', t[0:SSQ * 3, :], st_conv[hf * SSQ * 3:(hf + 1) * SSQ * 3, cb * 2048:(cb + 1) * 2048], writes=[tk])
                for c4 in range(4):
                    b = next_ps()
                    for cc in range(4):
                        ch = c4 * 4 + cc
                        S.op('pe', lambda e, b=b, cc=cc, ch=ch, t=t: e.transpose(
                            out=PS[b][:, cc * 12:(cc + 1) * 12], in_=t[0:SSQ * 3, ch * 128:(ch + 1) * 128], identity=IDF[0:SSQ * 3, 0:SSQ * 3]),
                            reads=[tk, 'IDF'], writes=[('ps', b)])
                    S.op('dve', lambda e, b=b, cb=cb, c4=c4: e.tensor_copy(
                        out=SCV[:, cb * 16 + c4 * 4: cb * 16 + c4 * 4 + 4, :].rearrange("p c k -> p (c k)"), in_=PS[b][:, 0:48]),
                        reads=[('ps', b)], writes=['SCV'])
            tl, key = load_w(b_w_in, 0, KC, 10240, ncols=64)
            for tt in range(NTILE):
                rows = rows_of(tt)
                b = next_ps()
                for k in range(KC):
                    S.op('pe', lambda e, b=b, k=k, tt=tt, rows=rows, tl=tl: e.matmul(
                        PS[b][0:rows, 0:64], lhsT=HT[:, k, tt * 128:tt * 128 + rows], rhs=tl.halves[0][0][:, k, 0:64],
                        start=(k == 0), stop=(k == KC - 1)), reads=[tl.halves[0][1], ('HT', k)], writes=[('ps', b)])
                S.op('dve', lambda e, b=b, tt=tt, rows=rows: e.tensor_tensor(out=DT[0:rows, tt, :], in0=PS[b][0:rows, 0:64], in1=DTB[0:rows, :], op=ALU.add),
                     reads=[('ps', b), 'DTB'], writes=['DT'])
            S.op('act', lambda e: e.activation(out=DT[:], in_=DT[:], func=AF.Exp), reads=['DT'], writes=['DT'])
            S.op('act', lambda e: e.activation(out=DT[:], in_=DT[:], func=AF.Ln, bias=1.0, scale=1.0), reads=['DT'], writes=['DT'])
            S.op('dve', lambda e: e.tensor_tensor(out=DA[:], in0=DT[:], in1=AROW[:].unsqueeze(1).to_broadcast([128, NTILE, 64]), op=ALU.mult),
                 reads=['DT', 'AROW'], writes=['DA'])
            for tt in range(NTILE):
                rows = rows_of(tt)
                um = UT if tt < NCH else USB
                om = ONESF if tt < NCH else SEQ32
                b = next_ps()
                S.op('pe', lambda e, b=b, tt=tt, rows=rows, um=um: e.matmul(PS[b][0:rows, 0:64], lhsT=um[0:rows, 0:rows], rhs=DA[0:rows, tt, :], start=True, stop=True),
                     reads=['DA', 'UT', 'USB'], writes=[('ps', b)])
                S.op('pe', lambda e, b=b, tt=tt, rows=rows, om=om: e.matmul(PS[b][0:rows, 64:128], lhsT=om[0:rows, 0:rows], rhs=DA[0:rows, tt, :], start=True, stop=True),
                     reads=['DA', 'ONESF', 'SEQ32'], writes=[('ps', b)])
                S.op('dve', lambda e, b=b, tt=tt, rows=rows: e.tensor_copy(out=ACS[0:rows, tt, :], in_=PS[b][0:rows, 0:64]), reads=[('ps', b)], writes=['ACS'])
                S.op('dve', lambda e, b=b, tt=tt, rows=rows: e.tensor_copy(out=TOT[0:rows, tt, :], in_=PS[b][0:rows, 64:128]), reads=[('ps', b)], writes=['TOT'])
            S.op('act', lambda e: e.activation(out=EXPA[:], in_=ACS[:], func=AF.Exp), reads=['ACS'], writes=['EXPA'])
            S.op('act', lambda e: e.activation(out=DEC[:], in_=TOT[:], func=AF.Exp), reads=['TOT'], writes=['DEC'])
            S.op('dve', lambda e: e.tensor_tensor(out=DTE[:], in0=TOT[:], in1=ACS[:], op=ALU.subtract), reads=['TOT', 'ACS'], writes=['DTE'])
            S.op('act', lambda e: e.activation(out=DTE[:], in_=DTE[:], func=AF.Exp), reads=['DTE'], writes=['DTE'])

            S.barrier()
            for g in range(8):
                tl, key = load_w(b_w_in, 0, KC, g * 512)
                for tt in range(NTILE):
                    rows = rows_of(tt)
                    b = next_ps()
                    for hh, (th, kh) in enumerate(tl.halves):
                        for k in range(KC):
                            S.op('pe', lambda e, b=b, k=k, tt=tt, rows=rows, th=th, hh=hh: e.matmul(
                                PS[b][0:rows, hh * 256:(hh + 1) * 256], lhsT=HT[:, k, tt * 128:tt * 128 + rows], rhs=th[:, k, :],
                                start=(k == 0), stop=(k == KC - 1)), reads=[kh, ('HT', k)], writes=[('ps', b)])
                    S.op('act', lambda e, b=b, tt=tt, rows=rows: e.activation(out=SZ[0:rows, tt, :], in_=PS[b][0:rows, :], func=AF.Silu),
                         reads=[('ps', b)], writes=['SZ'])
                tlx, keyx = load_w(b_w_in, 0, KC, 4096 + g * 512)
                cinfo = []
                for ci in range(6):
                    if ci < 4:
                        cinfo.append(dict(c0=ci * 128, ch=g * 4 + ci, kind='x', mi=ci))
                    elif ci == 4:
                        cinfo.append(dict(c0=0, ch=32 + g, kind='B', mi=0))
                    else:
                        cinfo.append(dict(c0=0, ch=40 + g, kind='C', mi=1))

                def stage_p(ci):
                    inf = cinfo[ci]
                    if ci < 4:
                        tl = tlx
                    elif ci == 4:
                        tl, _ = load_w(b_w_in, 0, KC, 8192 + g * 128, ncols=128)
                    else:
                        tl, _ = load_w(b_w_in, 0, KC, 9216 + g * 128, ncols=128)
                    c0 = inf['c0']
                    banks = []
                    for (t0, t1) in TBS:
                        b = next_ps(live_banks)
                        banks.append(b)
                        for k in range(KC):
                            S.op('pe', lambda e, b=b, k=k, tl=tl, c0=c0, t0=t0, t1=t1: e.matmul(
                                PS[b][:, 0:t1 - t0], lhsT=tl.c(k, c0 // 128), rhs=HT[:, k, t0:t1],
                                start=(k == 0), stop=(k == KC - 1)), reads=[tl.key(c0 // 128), ('HT', k)], writes=[('ps', b)])
                    inf['banks'] = banks
                    live_banks.update(banks)

                def stage_q(ci):
                    inf = cinfo[ci]
                    ch, kind, mi = inf['ch'], inf['kind'], inf['mi']
                    XC = (XC_A, XC2)[ci % 2]
                    xck = 'XC%d' % (ci % 2)
                    S.op('dve', lambda e, ch=ch: e.tensor_copy(out=RAW[:, 0:3], in_=CTAIL[:, ch, :]), reads=['CTAIL'], writes=['RAW'])
                    S.op('dve', lambda e, ch=ch: e.tensor_copy(
                        out=RAW[:, 3 + NPT:].rearrange("p (b t) -> p b t", t=11)[:, :, 0:3],
                        in_=SCV[:, ch, :].rearrange("p (b k) -> p b k", k=3)), reads=['SCV'], writes=['RAW'])
                    for bi, (t0, t1) in enumerate(TBS):
                        n = t1 - t0
                        npr = min(t1, NPT) - t0
                        b = inf['banks'][bi]
                        S.op('act', lambda e, b=b, t0=t0, npr=npr: e.activation(out=RAW[:, 3 + t0:3 + t0 + npr], in_=PS[b][:, 0:npr], func=AF.Copy),
                             reads=[('ps', b)], writes=['RAW'])
                        if t1 > NPT:
                            S.op('act', lambda e, b=b, npr=npr, n=n: e.activation(
                                out=RAW[:, 3 + NPT:].rearrange("p (b t) -> p b t", t=11)[:, :, 3:11],
                                in_=PS[b][:, npr:n].rearrange("p (b t) -> p b t", t=8), func=AF.Copy),
                                reads=[('ps', b)], writes=['RAW'])
                    for b in inf['banks']:
                        live_banks.discard(b)
                    def conv_state_out():
                        S.op('dve', lambda e, ch=ch: e.tensor_copy(out=CTAIL[:, ch, :], in_=RAW[:, NPT:NPT + 3]), reads=['RAW'], writes=['CTAIL'])
                        S.op('dve', lambda e, ch=ch: e.tensor_copy(
                            out=CSO[:, ch, :].rearrange("p (b k) -> p b k", k=3),
                            in_=RAW[:, 3 + NPT:].rearrange("p (b t) -> p b t", t=11)[:, :, 8:11]), reads=['RAW'], writes=['CSO'])
                    cw = lambda kk, ch=ch: col('conv_w', kk * 48 + ch)
                    cbias = col('conv_b', ch)
                    pr_in = lambda kk: RAW[:, kk:kk + NPT]
                    sm_in = lambda kk: RAW[:, 3 + NPT:].rearrange("p (b t) -> p b t", t=11)[:, :, kk:kk + 8]
                    pr_out = XC[:, 0:NPT]
                    sm_out = XC[:, NPT:NTOK].rearrange("p (b t) -> p b t", t=8)
                    for (oin, oout) in ((pr_in, pr_out), (sm_in, sm_out)):
                        S.op('dve', lambda e, oin=oin, oout=oout, cw=cw, cbias=cbias: e.tensor_scalar(
                            out=oout, in0=oin(0), scalar1=cw(0), scalar2=cbias, op0=ALU.mult, op1=ALU.add),
                            reads=['RAW', 'COLS'], writes=[xck])
                        for kk in range(1, 4):
                            S.op('dve', lambda e, oin=oin, oout=oout, cw=cw, kk=kk: e.scalar_tensor_tensor(
                                out=oout, in0=oin(kk), scalar=cw(kk), in1=oout, op0=ALU.mult, op1=ALU.add),
                                reads=['RAW', 'COLS', xck], writes=[xck])
                    conv_state_out()
                    if kind == 'C':
                        S.op('act', lambda e: e.activation(out=BCT[:, 1, :], in_=XC[:], func=AF.Silu), reads=[xck], writes=['BCT'])
                        return
                    if kind == 'B':
                        S.op('act', lambda e: e.activation(out=BCT[:, 0, :], in_=XC[:], func=AF.Silu), reads=[xck], writes=['BCT'])
                    S.op('act', lambda e: e.activation(out=XC[:], in_=XC[:], func=AF.Silu), reads=[xck], writes=[xck])
                    b = next_ps(live_banks)
                    for tt in range(NCH):
                        S.op('pe', lambda e, b=b, tt=tt: e.transpose(out=PS[b][:, tt * 128:(tt + 1) * 128], in_=XC[:, tt * 128:(tt + 1) * 128], identity=IDF[:]),
                             reads=[xck, 'IDF'], writes=[('ps', b)])
                    b2 = next_ps(live_banks)
                    S.op('pe', lambda e, b2=b2: e.transpose(out=PS[b2][0:NS, 0:128], in_=XC[:, NPT:NTOK], identity=IDF[:]),
                         reads=[xck, 'IDF'], writes=[('ps', b2)])
                    if kind == 'x':
                        S.op('dve', lambda e, b=b, mi=mi: e.tensor_copy(out=XTOK[:, 0:NCH, mi * 128:(mi + 1) * 128],
                                                                      in_=PS[b][:, :].rearrange("p (t c) -> p t c", c=128)),
                             reads=[('ps', b)], writes=['XTOK'])
                        S.op('dve', lambda e, b2=b2, mi=mi: e.tensor_copy(out=XTOK[0:NS, NCH, mi * 128:(mi + 1) * 128], in_=PS[b2][0:NS, 0:128]),
                             reads=[('ps', b2)], writes=['XTOK'])
                    else:
                        S.op('dve', lambda e, b=b: e.tensor_copy(out=BTOK[:, 0:NCH, :], in_=PS[b][:, :].rearrange("p (t c) -> p t c", c=128)),
                             reads=[('ps', b)], writes=['BTOK'])
                        S.op('dve', lambda e, b2=b2: e.tensor_copy(out=BTOK[0:NS, NCH, :], in_=PS[b2][0:NS, 0:128]),
                             reads=[('ps', b2)], writes=['BTOK'])

                live_banks = set()
                LIVE[0] = live_banks
                stage_p(0)
                for ci in range(6):
                    if ci + 1 < 6:
                        stage_p(ci + 1)
                    stage_q(ci)
                S.dma('sp', NWR[:], b_nw[g * 512:(g + 1) * 512].partition_broadcast(128), writes=['NWR'])
                if hf == 0:
                    S.op('pool', lambda e: e.memset(HSTG[:], 0.0), writes=['HSTG'])
                else:
                    S.dma('sp', HSTG[:], hst_d[g], reads=[('hst_d', g)], writes=['HSTG'])
                S.op('act', lambda e: e.activation(out=HB[:], in_=HSTG[:], func=AF.Copy), reads=['HSTG'], writes=['HB'])
                hs = slice(g * 8, g * 8 + 8)
                held = set()
                LIVE[0] = held
                tinfo = {}

                def front(tt, g=g, hs=hs):
                    rows = rows_of(tt)
                    samp = (tt == NCH)
                    tk0 = tt * 128
                    um = USB if samp else UT
                    S.op('pool', lambda e: e.tensor_tensor(
                        out=XDT[0:rows, :].rearrange("p (r q) -> p r q", q=64), in0=XTOK[0:rows, tt, :].rearrange("p (r q) -> p r q", q=64),
                        in1=DT[0:rows, tt, hs].unsqueeze(2).to_broadcast([rows, 8, 64]), op=ALU.mult), reads=['XTOK', 'DT'], writes=['XDT'])
                    S.op('pool', lambda e: e.tensor_tensor(
                        out=WW[0:rows, :].rearrange("p (r q) -> p r q", q=64), in0=XDT[0:rows, :].rearrange("p (r q) -> p r q", q=64),
                        in1=DTE[0:rows, tt, hs].unsqueeze(2).to_broadcast([rows, 8, 64]), op=ALU.mult), reads=['XDT', 'DTE'], writes=['WW'])
                    S.op('pool', lambda e: e.tensor_tensor(
                        out=XDB[0:rows, :].rearrange("p (r q) -> p r q", q=64), in0=XTOK[0:rows, tt, :].rearrange("p (r q) -> p r q", q=64),
                        in1=DROW[0:rows, hs].unsqueeze(2).to_broadcast([rows, 8, 64]), op=ALU.mult), reads=['XTOK', 'DROW'], writes=['XDB'])
                    b = next_ps(held)
                    S.op('pe', lambda e, b=b: e.matmul(PS[b][0:rows, 0:rows], lhsT=BCT[:, 0, tk0:tk0 + rows], rhs=BCT[:, 1, tk0:tk0 + rows], start=True, stop=True),
                         reads=['BCT'], writes=[('ps', b)])
                    S.op('dve', lambda e, b=b: e.tensor_tensor(out=CBM[0:rows, 0:rows], in0=PS[b][0:rows, 0:rows], in1=um[0:rows, 0:rows], op=ALU.mult),
                         reads=[('ps', b), 'UT', 'USB'], writes=['CBM'])
                    for r4 in range(2):
                        b = next_ps(held)
                        for rr in range(4):
                            h = g * 8 + r4 * 4 + rr
                            S.op('pe', lambda e, b=b, rr=rr, h=h: e.matmul(
                                PS[b][0:rows, rr * 128:rr * 128 + rows], lhsT=DA[0:rows, tt, h:h + 1].to_broadcast([rows, rows]),
                                rhs=um[0:rows, 0:rows], start=True, stop=True), reads=['DA', 'UT', 'USB'], writes=[('ps', b)])
                        for rr in range(4):
                            r = r4 * 4 + rr
                            h = g * 8 + r
                            S.op('dve', lambda e, b=b, rr=rr, r=r, h=h: e.tensor_scalar(
                                out=EE[0:rows, r, 0:rows], in0=PS[b][0:rows, rr * 128:rr * 128 + rows], scalar1=ACS[0:rows, tt, h:h + 1], scalar2=0.0,
                                op0=ALU.subtract, op1=ALU.min), reads=[('ps', b), 'ACS'], writes=['EE'])
                    S.op('act', lambda e: e.activation(out=LT[0:rows, :, 0:rows], in_=EE[0:rows, :, 0:rows], func=AF.Exp), reads=['EE'], writes=['LT'])
                    S.op('dve', lambda e: e.tensor_tensor(out=MT[0:rows, :, 0:rows], in0=LT[0:rows, :, 0:rows],
                                                         in1=CBM[0:rows, 0:rows].unsqueeze(1).to_broadcast([rows, 8, rows]), op=ALU.mult),
                         reads=['LT', 'CBM'], writes=['MT'])
                    by = next_ps(held)
                    S.op('pe', lambda e: e.matmul(PS[by][0:rows, :], lhsT=IDB[0:rows, 0:rows], rhs=XDB[0:rows, :], start=True, stop=False),
                         reads=['IDB', 'XDB'], writes=[('ps', by)])
                    for r in range(8):
                        S.op('pe', lambda e, r=r: e.matmul(PS[by][0:rows, r * 64:(r + 1) * 64], lhsT=MT[0:rows, r, 0:rows], rhs=XDT[0:rows, r * 64:(r + 1) * 64],
                                                            start=False, stop=(r == 7)),
                             reads=['MT', 'XDT'], writes=[('ps', by)])
                    held.add(by)
                    bs3 = None
                    if not samp:
                        bs3 = next_ps(held)
                        S.op('pe', lambda e: e.matmul(PS[bs3][:, :], lhsT=BTOK[:, tt, :], rhs=WW[:, :], start=True, stop=True),
                             reads=['BTOK', 'WW'], writes=[('ps', bs3)])
                        held.add(bs3)
                    tinfo[tt] = (by, bs3)

                def back(tt, g=g, hs=hs):
                    rows = rows_of(tt)
                    samp = (tt == NCH)
                    tk0 = tt * 128
                    by, bs3 = tinfo[tt]
                    bo = next_ps(held)
                    if not samp:
                        S.op('pe', lambda e: e.matmul(PS[bo][0:rows, :], lhsT=BCT[:, 1, tk0:tk0 + rows], rhs=HB[:], start=True, stop=True),
                             reads=['BCT', 'HB'], writes=[('ps', bo)])
                    else:
                        held.add(bo)
                        S.op('dve', lambda e: e.tensor_tensor(out=CMS[:], in0=BCT[:, 1, tk0:tk0 + NS].unsqueeze(1).to_broadcast([128, SSQ, NS]), in1=MASK3[:], op=ALU.mult),
                             reads=['BCT', 'MASK3'], writes=['CMS'])
                        bd = next_ps(held)
                        for q4 in range(4):
                            h2 = g * 8 + q4 * 2
                            S.op('dve', lambda e, h2=h2: e.tensor_copy(
                                out=DAB[:].rearrange("p (h q) -> p h q", q=64), in_=DA[0:NS, tt, h2:h2 + 2].unsqueeze(2).to_broadcast([NS, 2, 64])),
                                reads=['DA'], writes=['DAB'])
                            S.op('pe', lambda e, q4=q4: e.matmul(
                                PS[bd][:, q4 * SSQ:(q4 + 1) * SSQ], lhsT=DAB[:], rhs=SEQM[:], start=True, stop=True),
                                reads=['DAB', 'SEQM'], writes=[('ps', bd)])
                        S.op('act', lambda e: e.activation(out=DECS[:].rearrange("p q b -> p (q b)"), in_=PS[bd][:, 0:4 * SSQ], func=AF.Exp),
                             reads=[('ps', bd)], writes=['DECS'])
                        h0bufs = {}

                        def issue_in(bq):
                            sq_ = hf * SSQ + bq
                            jh = rot('ev', 3)
                            H0v = EV[jh][:, :].rearrange("p (q n) -> p q n", n=128)
                            src = st_ssm[sq_, g * 8:(g + 1) * 8].rearrange("(q h) p n -> (h p) q n", h=2)
                            S.dma('sp', H0v, src, writes=['EV%d' % jh])
                            h0bufs[bq] = (H0v, 'EV%d' % jh)

                        issue_in(0)
                        for bq in range(SSQ):
                            if bq + 1 < SSQ:
                                issue_in(bq + 1)
                            sq_ = hf * SSQ + bq
                            H0v, hk_ = h0bufs[bq]
                            bt_ = next_ps(held)
                            for q4 in range(4):
                                S.op('pe', lambda e, bt_=bt_, q4=q4, H0v=H0v: e.transpose(out=PS[bt_][:, q4 * 128:(q4 + 1) * 128], in_=H0v[:, q4, :], identity=IDF[:]),
                                     reads=[hk_, 'IDF'], writes=[('ps', bt_)])
                            S.op('act', lambda e, bt_=bt_: e.activation(out=H0T[:], in_=PS[bt_][:], func=AF.Copy), reads=[('ps', bt_)], writes=['H0T'])
                            S.op('pe', lambda e, bq=bq: e.matmul(PS[bo][0:NS, :], lhsT=CMS[:, bq, :], rhs=H0T[:], start=(bq == 0), stop=(bq == SSQ - 1)),
                                 reads=['CMS', 'H0T'], writes=[('ps', bo)])
                            S.op('dve', lambda e, bq=bq: e.tensor_scalar(out=WM[:], in0=WW[0:NS, :], scalar1=SEQM[:, bq:bq + 1], scalar2=None, op0=ALU.mult),
                                 reads=['WW', 'SEQM'], writes=['WM'])
                            bn = next_ps(held)
                            for q4 in range(4):
                                S.op('pe', lambda e, bn=bn, q4=q4: e.matmul(PS[bn][:, q4 * 128:(q4 + 1) * 128], lhsT=WM[:, q4 * 128:(q4 + 1) * 128], rhs=BTOK[0:NS, tt, :], start=True, stop=True),
                                     reads=['WM', 'BTOK'], writes=[('ps', bn)])
                            for q4 in range(4):
                                S.op('dve', lambda e, bn=bn, q4=q4, bq=bq, H0v=H0v: e.scalar_tensor_tensor(
                                    out=H0v[:, q4, :], in0=H0v[:, q4, :], scalar=DECS[:, q4, bq:bq + 1], in1=PS[bn][:, q4 * 128:(q4 + 1) * 128],
                                    op0=ALU.mult, op1=ALU.add), reads=[hk_, 'DECS', ('ps', bn)], writes=[hk_])
                            dst = ssm_s[sq_, g * 8:(g + 1) * 8].rearrange("(q h) p n -> (h p) q n", h=2)
                            S.dma('sp', dst, H0v, reads=[hk_])
                        held.discard(bo)
                    jy, jo = rot('ev', 3), rot('ev', 3)
                    Y, YO = EV[jy], EV[jo]
                    S.op('dve', lambda e: e.tensor_tensor(
                        out=YO[0:rows, :].rearrange("p (r q) -> p r q", q=64), in0=PS[bo][0:rows, :].rearrange("p (r q) -> p r q", q=64),
                        in1=EXPA[0:rows, tt, hs].unsqueeze(2).to_broadcast([rows, 8, 64]), op=ALU.mult), reads=[('ps', bo), 'EXPA'], writes=['EV%d' % jo])
                    S.op('dve', lambda e: e.tensor_tensor(out=Y[0:rows, :], in0=PS[by][0:rows, :], in1=YO[0:rows, :], op=ALU.add),
                         reads=[('ps', by), 'EV%d' % jo], writes=['EV%d' % jy])
                    held.discard(by)
                    S.op('dve', lambda e: e.tensor_tensor(out=Y[0:rows, :], in0=Y[0:rows, :], in1=SZ[0:rows, tt, :], op=ALU.mult),
                         reads=['SZ', 'EV%d' % jy], writes=['EV%d' % jy])
                    S.op('act', lambda e: e.activation(out=YO[0:rows, :], in_=Y[0:rows, :], func=AF.Square, accum_out=SMALL[0:rows, 32:33]),
                         reads=['EV%d' % jy], writes=['EV%d' % jo, ('SM', 32)])
                    S.op('dve', lambda e: e.tensor_scalar(out=SMALL[0:rows, 32:33], in0=SMALL[0:rows, 32:33], scalar1=1.0 / 512, scalar2=NORM_EPS, op0=ALU.mult, op1=ALU.add),
                         reads=[('SM', 32)], writes=[('SM', 32)])
                    S.op('act', lambda e: e.activation(out=SMALL[0:rows, 32:33], in_=SMALL[0:rows, 32:33], func=AF.Ln), reads=[('SM', 32)], writes=[('SM', 32)])
                    S.op('act', lambda e: e.activation(out=SMALL[0:rows, 32:33], in_=SMALL[0:rows, 32:33], func=AF.Exp, scale=-0.5), reads=[('SM', 32)], writes=[('SM', 32)])
                    S.op('dve', lambda e: e.scalar_tensor_tensor(out=Y[0:rows, :], in0=Y[0:rows, :], scalar=SMALL[0:rows, 32:33], in1=NWR[0:rows, :], op0=ALU.mult, op1=ALU.mult),
                         reads=['EV%d' % jy, ('SM', 32), 'NWR'], writes=['EV%d' % jy])
                    bt2 = next_ps(held)
                    for kk in range(4):
                        S.op('pe', lambda e, kk=kk: e.transpose(out=PS[bt2][:, kk * 128:kk * 128 + rows], in_=Y[0:rows, kk * 128:(kk + 1) * 128], identity=IDF[0:rows, 0:rows]),
                             reads=['EV%d' % jy, 'IDF'], writes=[('ps', bt2)])
                    S.op('act', lambda e: e.activation(
                        out=GB[:, :, tk0:tk0 + rows], in_=PS[bt2][:].rearrange("p (k t) -> p k t", t=128)[:, :, 0:rows], func=AF.Copy),
                        reads=[('ps', bt2)], writes=['GB'])
                    if not samp:
                        S.op('dve', lambda e: e.tensor_tensor(
                            out=HSTG[:].rearrange("p (r q) -> p r q", q=64), in0=HSTG[:].rearrange("p (r q) -> p r q", q=64),
                            in1=DEC[:, tt, hs].unsqueeze(2).to_broadcast([128, 8, 64]), op=ALU.mult), reads=['HSTG', 'DEC'], writes=['HSTG'])
                        S.op('dve', lambda e: e.tensor_tensor(out=HSTG[:], in0=HSTG[:], in1=PS[bs3][:, :], op=ALU.add),
                             reads=['HSTG', ('ps', bs3)], writes=['HSTG'])
                        held.discard(bs3)
                        S.op('act', lambda e: e.activation(out=HB[:], in_=HSTG[:], func=AF.Copy), reads=['HSTG'], writes=['HB'])

                front(0)
                for tt in range(NTILE):
                    if tt + 1 < NTILE:
                        front(tt + 1)
                    back(tt)
                LIVE[0] = set()
                if not last:
                    S.dma('sp', hst_d[g], HSTG[:], reads=['HSTG'], writes=[('hst_d', g)])
                else:
                    for q4 in range(4):
                        b = next_ps()
                        S.op('pe', lambda e, b=b, q4=q4: e.transpose(out=PS[b][:, 0:128], in_=HSTG[:, q4 * 128:(q4 + 1) * 128], identity=IDF[:]),
                             reads=['HSTG', 'IDF'], writes=[('ps', b)])
                        j = rot('sm12', 2)
                        S.op('dve', lambda e, b=b, j=j: e.tensor_copy(out=SM12[j][:, :], in_=PS[b][:, 0:128]), reads=[('ps', b)], writes=['SM12_%d' % j])
                        r0 = (g * 8 + q4 * 2) * 64
                        S.dma('sp', ssm_p[r0:r0 + 128, :], SM12[j][:, :], reads=['SM12_%d' % j])
                out_proj_partial(hf, 1, 32, b_w_out, g * 512)
            def store_rows(src3, nrow, dst2, rk):
                for c4 in range(12):
                    b = next_ps()
                    for cc in range(4):
                        S.op('pe', lambda e, b=b, cc=cc, c4=c4: e.transpose(out=PS[b][0:nrow, cc * 128:(cc + 1) * 128], in_=src3[:, c4 * 4 + cc, :], identity=IDF[:]),
                             reads=[rk, 'IDF'], writes=[('ps', b)])
                    j = rot('ev', 3)
                    S.op('dve', lambda e, b=b, j=j: e.tensor_copy(out=EV[j][0:nrow, :], in_=PS[b][0:nrow, :]), reads=[('ps', b)], writes=['EV%d' % j])
                    S.dma('sp', dst2[:, c4 * 512:(c4 + 1) * 512], EV[j][0:nrow, :], reads=['EV%d' % j])
            store_rows(CSO, SSQ * 3, conv_s[hf * SSQ * 3:(hf + 1) * SSQ * 3, :], 'CSO')
            if last:
                store_rows(CTAIL, 3, conv_p, 'CTAIL')

        def ffn(hf, l):
            for blk in range(11):
                for half in range(2):
                    tg, _ = load_w(f_w_in[l], 0, KC, blk * 512 + half * 256, ncols=256)
                    tu, _ = load_w(f_w_in[l], 0, KC, FFN_H + blk * 512 + half * 256, ncols=256)
                    for mi2 in range(2):
                        mi = half * 2 + mi2
                        for (t0, t1) in TBS:
                            n = t1 - t0
                            bg, bu = next_ps(), next_ps()
                            for k in range(KC):
                                S.op('pe', lambda e, b=bg, k=k, mi2=mi2, tl=tg, t0=t0, t1=t1: e.matmul(
                                    PS[b][:, 0:t1 - t0], lhsT=tl.c(k, mi2), rhs=HT[:, k, t0:t1],
                                    start=(k == 0), stop=(k == KC - 1)),
                                    reads=[tg.key(mi2), ('HT', k)], writes=[('ps', bg)])
                            for k in range(KC):
                                S.op('pe', lambda e, b=bu, k=k, mi2=mi2, tl=tu, t0=t0, t1=t1: e.matmul(
                                    PS[b][:, 0:t1 - t0], lhsT=tl.c(k, mi2), rhs=HT[:, k, t0:t1],
                                    start=(k == 0), stop=(k == KC - 1)),
                                    reads=[tu.key(mi2), ('HT', k)], writes=[('ps', bu)])
                            i = rot('ev', 3)
                            S.op('act', lambda e, b=bg, i=i, n=n: e.activation(out=EV[i][:, 0:n], in_=PS[b][:, 0:n], func=AF.Silu),
                                 reads=[('ps', bg)], writes=['EV%d' % i])
                            S.op('dve', lambda e, b=bu, i=i, n=n, mi=mi, t0=t0, t1=t1: e.tensor_tensor(
                                out=GB[:, mi, t0:t1], in0=PS[b][:, 0:n], in1=EV[i][:, 0:n], op=ALU.mult),
                                reads=[('ps', bu), 'EV%d' % i], writes=['GB'])
                out_proj_partial(hf, l, 80, f_w_out[l], blk * 512)

        def final_out(hf):
            compute_rstd(NORM_EPS)
            for tt in range(NTILE):
                rows = 128 if tt < NCH else NS
                i = rot('tmpf', 2)
                t, tk = TMPF[i], 'TMPF%d' % i
                for k4 in range(4):
                    j = rot('ev', 3)
                    S.op('dve', lambda e, j=j, k4=k4, tt=tt, rows=rows: e.tensor_tensor(
                        out=EV[j][:, :].rearrange("p (k t) -> p k t", t=128)[:, :, 0:rows],
                        in0=XT[:, k4 * 4:(k4 + 1) * 4, tt * 128:tt * 128 + rows],
                        in1=RSTD[:, tt * 128:tt * 128 + rows].unsqueeze(1).to_broadcast([128, 4, rows]), op=ALU.mult),
                        reads=[kx for kk_ in range(4) for kx in xk(k4 * 4 + kk_)] + ['RSTD'], writes=['EV%d' % j])
                    b = next_ps()
                    for kk in range(4):
                        k = k4 * 4 + kk
                        S.op('act', lambda e, j=j, kk=kk, k=k, rows=rows: e.activation(
                            out=EV[j][:, kk * 128:kk * 128 + rows], in_=EV[j][:, kk * 128:kk * 128 + rows],
                            func=AF.Copy, scale=col('fnw', k)),
                            reads=['EV%d' % j, 'COLS'], writes=['EV%d' % j])
                        S.op('pe', lambda e, b=b, j=j, kk=kk, rows=rows: e.transpose(
                            out=PS[b][0:rows, kk * 128:(kk + 1) * 128], in_=EV[j][:, kk * 128:kk * 128 + rows], identity=IDF[:]),
                            reads=['EV%d' % j, 'IDF'], writes=[('ps', b)])
                    S.op('dve', lambda e, b=b, k4=k4, rows=rows, t=t: e.tensor_copy(out=t[0:rows, k4 * 512:(k4 + 1) * 512], in_=PS[b][0:rows, :]),
                         reads=[('ps', b)], writes=[tk])
                if tt < NCH:
                    S.dma('sp', y_p[hf * NPT + tt * 128: hf * NPT + (tt + 1) * 128, :], t[:, :], reads=[tk])
                else:
                    S.dma('sp', y_s[hf * NS:(hf + 1) * NS, :], t[0:NS, :], reads=[tk])

        for hf in range(NPASS):
            load_x(hf)
            norm_mod(hf, 0, 0, 'nmw0')
            gmlp(hf)
            norm_mod(hf, 0, 1, 'nfw0')
            ffn(hf, 0)
            if STAGE >= 2:
                norm_mod(hf, 1, 0, 'nmw1')
                S.barrier()
                ssd(hf)
                S.barrier()
                norm_mod(hf, 1, 1, 'nfw1')
                ffn(hf, 1)
            final_out(hf)
            S.barrier()

        S.emit()
    return nc, S


_CACHE = {}


def _get_program():
    if 'nc' not in _CACHE:
        _CACHE['nc'] = build_program()
    return _CACHE['nc']


def kernel(x_prompt, x_sample, c_prompt, c_sample, state_ssm, state_conv,
           mod_w, mod_b, norm_mix_w, norm_ffn_w,
           a_w_in, a_b_in, a_ln_w, a_ln_b, a_w_s, a_b_s, a_w_out,
           b_w_in, b_conv_w, b_conv_b, b_dt_bias, b_a_log, b_d, b_norm_w, b_w_out,
           f_w_in, f_w_out, final_norm_w):
    f = lambda a: np.ascontiguousarray(np.asarray(a, dtype=np.float32))
    nc, S = _get_program()
    vecs = np.zeros((VEC_TOT, 128), np.float32)

    def put(name, arr):
        r0, n = VEC_LAY[name]
        vecs[r0:r0 + n] = np.asarray(arr, np.float32).reshape(n, 128)

    put('mod_b0', mod_b[0]); put('mod_b1', mod_b[1])
    put('nmw0', norm_mix_w[0]); put('nmw1', norm_mix_w[1])
    put('nfw0', norm_ffn_w[0]); put('nfw1', norm_ffn_w[1])
    put('a_b_in', a_b_in[0]); put('fnw', final_norm_w)
    put('conv_w', b_conv_w[0]); put('conv_b', b_conv_b[0])
    shared = dict(vecs=vecs, mod_w=f(mod_w), a_w_in=f(a_w_in[0]), a_ln_w=f(a_ln_w[0]), a_ln_b=f(a_ln_b[0]),
                  a_b_in_r=f(a_b_in[0]), a_w_s=f(a_w_s[0]), a_b_s=f(a_b_s[0]), a_w_out=f(a_w_out[0]),
                  f_w_in=f(f_w_in), f_w_out=f(f_w_out))
    shared.update(b_w_in=f(b_w_in[0]), b_w_out=f(b_w_out[0]), b_dtb=f(b_dt_bias[0]), b_alog=f(b_a_log[0]),
                  b_dsk=f(b_d[0]), b_nw=f(b_norm_w[0]))
    in_maps = []
    for c in range(8):
        m = dict(shared)
        m['st_ssm'] = f(np.asarray(state_ssm)[0, 16 * c:16 * (c + 1)])
        m['st_conv'] = f(np.asarray(state_conv)[0, 16 * c:16 * (c + 1)].reshape(48, 6144))
        m['xp'] = f(x_prompt[c % 4])
        m['xs'] = f(np.asarray(x_sample)[16 * c:16 * (c + 1)].reshape(128, D))
        m['call'] = f(np.concatenate([np.asarray(c_prompt)[c % 4][None], np.asarray(c_sample)[16 * c:16 * (c + 1)]], 0))
        in_maps.append(m)
    res = run_bass_kernel_spmd(nc, in_maps, core_ids=list(range(8)))
    R = res.results
    y_prompt = np.stack([R[c]['y_p'] for c in range(4)], 0)
    y_sample = np.concatenate([R[c]['y_s'].reshape(16, 8, D) for c in range(8)], 0)
    v_prompt = np.stack([R[c]['v_p'] for c in range(4)], 0)[None]
    v_sample = np.concatenate([R[c]['v_s'].reshape(16, 8, D) for c in range(8)], 0)[None]
    ssm_prompt = np.stack([R[c]['ssm_p'].reshape(64, 64, 128) for c in range(4)], 0)[None]
    ssm_sample = np.concatenate([R[c]['ssm_s'] for c in range(8)], 0)[None]
    conv_prompt = np.stack([R[c]['conv_p'] for c in range(4)], 0)[None]
    conv_sample = np.concatenate([R[c]['conv_s'].reshape(16, 3, 6144) for c in range(8)], 0)[None]
    return (y_prompt, y_sample, v_prompt, v_sample, ssm_prompt, ssm_sample, conv_prompt, conv_sample)
```

```python
import numpy as np
from contextlib import ExitStack
import concourse.bass as bass
import concourse.mybir as mybir
from concourse.bass_utils import run_bass_kernel_spmd

F32 = mybir.dt.float32
BF16 = mybir.dt.bfloat16
AF = mybir.ActivationFunctionType
ALU = mybir.AluOpType
AX = mybir.AxisListType

D = 2048
KC = 16
NPASS = 4
NCH = 4
SSQ = 4
NS = SSQ * 8
NPT = NCH * 128
NTOK = NPT + NS
NTILE = NCH + 1
TBS = [(0, 256), (256, NTOK)]
FFN_H = 5632
SSD_IN = 10304
NORM_EPS = 1e-6
LN_EPS = 1e-5
SAME_ENGINE_SYNC = True
STAGE = 2


class Sched:
    ENG = ('pe', 'dve', 'act', 'pool', 'sp')

    def __init__(self, nc, n_dma_sems=(24, 8, 16)):
        self.nc = nc
        self.ins = []
        self.last_w = {}
        self.readers = {}
        self.n_dma_sems = dict(sp=n_dma_sems[0], act=n_dma_sems[1], pool=n_dma_sems[2])

    def _add(self, eng, fn, reads, writes, kind):
        idx = len(self.ins)
        deps = set()
        raw = set()
        for k in reads:
            w = self.last_w.get(k)
            if w is not None:
                deps.add(w)
                raw.add(w)
        for k in writes:
            w = self.last_w.get(k)
            if w is not None:
                deps.add(w)
            for r in self.readers.get(k, {}).values():
                if isinstance(r, list):
                    deps.update(r)
                else:
                    deps.add(r)
        deps.discard(idx)
        if eng in ('dve', 'act', 'pool'):
            deps = set(d for d in deps if d in raw or self.ins[d]['eng'] != eng or self.ins[d]['kind'] == 'dma')
        self.ins.append(dict(eng=eng, fn=fn, deps=deps, kind=kind, needed=False))
        for k in writes:
            self.last_w[k] = idx
            self.readers[k] = {}
        for k in reads:
            if k not in writes:
                rd = self.readers.setdefault(k, {})
                if kind == 'dma':
                    rd.setdefault('dma', []).append(idx)
                else:
                    rd[eng] = idx
        return idx

    def op(self, eng, fn, reads=(), writes=()):
        return self._add(eng, fn, list(reads), list(writes), 'op')

    def dma(self, eng, out, in_, reads=(), writes=(), **kw):
        def fn(e, out=out, in_=in_, kw=kw):
            return e.dma_start(out=out, in_=in_, **kw)
        return self._add(eng, fn, list(reads), list(writes), 'dma')

    def barrier(self):
        last = {}
        for i, it in enumerate(self.ins):
            if it['kind'] != 'dma':
                last[it['eng']] = i
        deps = set(last.values())
        for k, w in self.last_w.items():
            if self.ins[w]['kind'] == 'dma':
                deps.add(w)
        for k, rs in self.readers.items():
            deps.update(rs.get('dma', []))
        for e in self.ENG:
            self.ins.append(dict(eng=e, fn=None, deps=set(deps), kind='nop', needed=False))
        self.last_w.clear()
        self.readers.clear()

    def emit(self):
        nc = self.nc
        ins = self.ins
        deps = set()
        last = {}
        for i, it in enumerate(ins):
            if it['kind'] == 'dma':
                deps.add(i)
            elif it['kind'] == 'op':
                last[it['eng']] = i
        deps |= set(last.values())
        ins.append(dict(eng='sp', fn=None, deps=deps, kind='nop', needed=False))
        for it in ins:
            for d in it['deps']:
                p = ins[d]
                if p['kind'] == 'dma':
                    p['needed'] = True
                elif p['eng'] == it['eng'] and (p['eng'] in ('pe', 'sp') or not SAME_ENGINE_SYNC):
                    pass
                else:
                    p['needed'] = True
        cnt = {e: 0 for e in self.ENG}
        dcnt = {e: 0 for e in self.ENG}
        for it in ins:
            e = it['eng']
            if it['kind'] == 'dma':
                n = self.n_dma_sems[e]
                j = dcnt[e]
                dcnt[e] += 1
                it['sig'] = ('d', e, j % n, 16 * (j // n + 1))
                it['prev_on_sem'] = 16 * (j // n)
            elif it['kind'] == 'op' and it['needed']:
                cnt[e] += 1
                it['sig'] = ('e', e, 0, cnt[e])
            else:
                it['sig'] = None
        self.stats = dict(cnt=cnt, dcnt=dcnt, n=len(ins))
        with ExitStack() as es:
            esem = {e: es.enter_context(nc.semaphore('s_' + e)) for e in self.ENG}
            dsem = {}
            for e in ('sp', 'act', 'pool'):
                nd = min(self.n_dma_sems[e], max(dcnt[e], 1))
                dsem[e] = [es.enter_context(nc.semaphore('d_%s%d' % (e, i))) for i in range(nd)]
            per = {e: [] for e in self.ENG}
            for i, it in enumerate(ins):
                per[it['eng']].append(i)
            block = es.enter_context(nc.Block())

            def make(e):
                def body(engobj):
                    wm = {}
                    for i in per[e]:
                        it = ins[i]
                        waits = {}
                        for d in it['deps']:
                            p = ins[d]
                            sg = p.get('sig')
                            if sg is None:
                                continue
                            if sg[0] == 'e' and p['eng'] == e and (e in ('pe', 'sp') or not SAME_ENGINE_SYNC):
                                continue
                            key = sg[:3]
                            if wm.get(key, 0) >= sg[3]:
                                continue
                            waits[key] = max(waits.get(key, 0), sg[3])
                        if it['kind'] == 'dma' and it['prev_on_sem'] > 0:
                            key = it['sig'][:3]
                            if wm.get(key, 0) < it['prev_on_sem']:
                                waits[key] = max(waits.get(key, 0), it['prev_on_sem'])
                        for key, v in waits.items():
                            sem = esem[key[1]] if key[0] == 'e' else dsem[key[1]][key[2]]
                            engobj.wait_ge(sem, v)
                            wm[key] = v
                        if it['fn'] is None:
                            continue
                        r = it['fn'](engobj)
                        sg = it['sig']
                        if sg is not None:
                            if sg[0] == 'e':
                                r.then_inc(esem[e], 1)
                            else:
                                r.then_inc(dsem[e][sg[2]], 16)
                return body

            block.tensor(make('pe'))
            block.vector(make('dve'))
            block.scalar(make('act'))
            block.gpsimd(make('pool'))
            block.sync(make('sp'))


VEC_ROWS = {}


def _vec_layout():
    off = 0
    lay = {}
    for name, n in [('mod_b0', 96), ('mod_b1', 96), ('nmw0', 16), ('nmw1', 16), ('nfw0', 16),
                    ('nfw1', 16), ('a_b_in', 32), ('fnw', 16), ('conv_w', 192), ('conv_b', 48)]:
        lay[name] = (off, n)
        off += n
    tot = ((off + 127) // 128) * 128
    return lay, tot


VEC_LAY, VEC_TOT = _vec_layout()


def build_program():
    nc = bass.Bass("TRN2", target_bir_lowering=False)
    din = lambda name, shape: nc.dram_tensor(name, list(shape), F32, kind="ExternalInput").ap()
    dout = lambda name, shape: nc.dram_tensor(name, list(shape), F32, kind="ExternalOutput").ap()
    xp = din("xp", [2048, D])
    xs = din("xs", [128, D])
    call = din("call", [17, D])
    vecs = din("vecs", [VEC_TOT, 128])
    mod_w = din("mod_w", [2, D, 6 * D])
    a_w_in = din("a_w_in", [D, 2 * D])
    a_ln_w = din("a_ln_w", [D])
    a_ln_b = din("a_ln_b", [D])
    a_b_in_r = din("a_b_in_r", [2 * D])
    a_w_s = din("a_w_s", [16, 128, 128])
    a_b_s = din("a_b_s", [16, 128])
    a_w_out = din("a_w_out", [D, D])
    f_w_in = din("f_w_in", [2, D, 2 * FFN_H])
    f_w_out = din("f_w_out", [2, FFN_H, D])
    y_p = dout("y_p", [2048, D])
    y_s = dout("y_s", [128, D])
    v_p = dout("v_p", [128, D])
    v_s = dout("v_s", [128, D])
    st_ssm = din("st_ssm", [16, 64, 64, 128])
    st_conv = din("st_conv", [16 * 3, 6144])
    b_w_in = din("b_w_in", [D, SSD_IN])
    b_w_out = din("b_w_out", [2 * D, D])
    b_dtb = din("b_dtb", [64])
    b_alog = din("b_alog", [64])
    b_dsk = din("b_dsk", [64])
    b_nw = din("b_nw", [2 * D])
    ssm_p = dout("ssm_p", [64 * 64, 128])
    ssm_s = dout("ssm_s", [16, 64, 64, 128])
    conv_p = dout("conv_p", [3, 6144])
    conv_s = dout("conv_s", [16 * 3, 6144])
    hst_d = nc.dram_tensor("hst_d", [8, 128, 512], F32).ap()

    S = Sched(nc)
    es = ExitStack()
    with es:
        def sb(name, shape, dt=F32):
            return es.enter_context(nc.sbuf_tensor(name, list(shape), dt))

        XT = sb("XT", [128, KC, NTOK])
        HT = sb("HT", [128, KC, NTOK], BF16)
        MOD = [sb("MOD%d" % l, [128, 96, 17]) for l in range(2)]
        COLS = sb("COLS", [128, VEC_TOT])
        WT = [sb("WT%d" % i, [128, KC, 256], BF16) for i in range(4)]
        WO = [sb("WO%d" % i, [128, 4, 512], BF16) for i in range(2)]
        GB = sb("GB", [128, 4, NTOK], BF16)
        IDF = sb("IDF", [128, 128])
        IDB = sb("IDB", [128, 128], BF16)
        ONESB = sb("ONESB", [128, 128], BF16)
        RSTD = sb("RSTD", [128, NTOK])
        ACOL = sb("ACOL", [128, KC, 17])
        TMPF = [sb("TMPF%d" % i, [128, 2048]) for i in range(2)]
        SQ = [sb("SQ%d" % i, [128, NTOK], BF16) for i in range(2)]
        EV = [sb("EV%d" % i, [128, 512]) for i in range(3)]
        CT = sb("CT", [128, KC, 17], BF16)
        VN = sb("VN", [128, NTILE, 2048], BF16)
        BINV = sb("BINV", [128, 2048], BF16)
        LNW = sb("LNW", [128, 2048], BF16)
        LNB = sb("LNB", [128, 2048], BF16)
        WST = sb("WST", [128, 16, 128], BF16)
        BDS = sb("BDS", [NS, 16, NS], BF16)
        BSR = sb("BSR", [1, 16, 128], BF16)
        BSS = sb("BSS", [1, 16, NS], BF16)
        CMASK = sb("CMASK", [128, 128])
        SEQM = sb("SEQM", [NS, SSQ])
        E8 = sb("E8", [8, SSQ, 8], BF16)
        STATS = sb("STATS", [128, NTILE, 4, 6])
        MV = sb("MV", [128, NTILE, 2])
        SMALL = sb("SMALL", [128, 64])
        UT = sb("UT", [128, 128])
        USB = sb("USB", [NS, NS])
        ONESF = sb("ONESF", [128, 128])
        SEQ32 = sb("SEQ32", [NS, NS])
        MASK3 = sb("MASK3", [128, SSQ, NS])
        DTB = sb("DTB", [128, 64]); AROW = sb("AROW", [128, 64]); DROW = sb("DROW", [128, 64])
        DT = sb("DT", [128, NTILE, 64]); DA = sb("DA", [128, NTILE, 64]); ACS = sb("ACS", [128, NTILE, 64])
        TOT = sb("TOT", [128, NTILE, 64]); EXPA = sb("EXPA", [128, NTILE, 64]); DTE = sb("DTE", [128, NTILE, 64])
        DEC = sb("DEC", [128, NTILE, 64])
        CTAIL = sb("CTAIL", [128, 48, 3])
        HB = sb("HB", [128, 512], BF16)
        BCT = sb("BCT", [128, 2, NTOK], BF16)
        DECS = sb("DECS", [128, 4, SSQ])
        DAB = sb("DAB", [NS, 128])
        H0T = sb("H0T", [128, 512], BF16)
        CMS = sb("CMS", [128, SSQ, NS], BF16)
        WM = sb("WM", [NS, 512], BF16)
        CBM = sb("CBM", [128, 128], BF16)
        XDT = sb("XDT", [128, 512], BF16)
        XDB = sb("XDB", [128, 512], BF16)
        XC2 = sb("XC2", [128, NTOK])
        WW = sb("WW", [128, 512], BF16)
        SM12 = [sb("SM12_%d" % i, [128, 128]) for i in range(2)]
        VNF = VN[:].rearrange("p t c -> p (t c)")
        SZ = VNF[:, 0:NTILE * 512].rearrange("p (t c) -> p t c", c=512)
        XTOK = VNF[:, NTILE * 512:2 * NTILE * 512].rearrange("p (t c) -> p t c", c=512)
        BTOK = VNF[:, 2 * NTILE * 512:2 * NTILE * 512 + NTILE * 128].rearrange("p (t c) -> p t c", c=128)
        _o = 2 * NTILE * 512 + NTILE * 128
        EE = VNF[:, _o:_o + 2048].bitcast(F32).rearrange("p (r i) -> p r i", i=128)
        LT = VNF[:, _o + 2048:_o + 3072].rearrange("p (r i) -> p r i", i=128)
        MT = VNF[:, _o + 3072:_o + 4096].rearrange("p (r i) -> p r i", i=128)
        assert _o + 4096 <= NTILE * 2048
        RAW = TMPF[0][:, 0:3 + NPT + SSQ * 11]
        XC_A = TMPF[0][:, 560:560 + NTOK]
        HSTG = TMPF[0][:, 1104:1616]
        NWR = TMPF[1][:, 0:512]
        H0 = TMPF[1][:, 512:1024].rearrange("p (q n) -> p q n", n=128)
        SCV = TMPF[1][:, 1024:1024 + 48 * SSQ * 3].rearrange("p (c k) -> p c k", k=SSQ * 3)
        CSO = SCV

        PS = [es.enter_context(nc.psum_tensor("PS%d" % i, [128, 512], F32)) for i in range(8)]
        ps_ctr = [0]

        def next_ps(excl=()):
            while True:
                i = ps_ctr[0] % 8
                ps_ctr[0] += 1
                if i not in excl:
                    return i

        ctr = {'wt': 0, 'wo': 0, 'ev': 0, 'evb': 0, 'tmpf': 0, 'sq': 0, 'sm12': 0}

        def rot(name, n):
            i = ctr[name] % n
            ctr[name] += 1
            return i

        def affine_mask(tile_ap, key, pattern, base, cm):
            S.op('pool', lambda e: e.memset(tile_ap, 1.0), writes=[key])
            S.op('pool', lambda e: e.affine_select(out=tile_ap, in_=tile_ap, pattern=pattern,
                                                   compare_op=ALU.is_ge, fill=0.0, base=base,
                                                   channel_multiplier=cm),
                 reads=[key], writes=[key])

        S.op('pool', lambda e: e.memset(IDF[:], 1.0), writes=['IDF'])
        S.op('pool', lambda e: e.affine_select(out=IDF[:], in_=IDF[:], pattern=[[-1, 128]], compare_op=ALU.is_ge,
                                               fill=0.0, base=0, channel_multiplier=1), reads=['IDF'], writes=['IDF'])
        S.op('pool', lambda e: e.affine_select(out=IDF[:], in_=IDF[:], pattern=[[1, 128]], compare_op=ALU.is_ge,
                                               fill=0.0, base=0, channel_multiplier=-1), reads=['IDF'], writes=['IDF'])
        S.op('dve', lambda e: e.tensor_copy(out=IDB[:], in_=IDF[:]), reads=['IDF'], writes=['IDB'])
        S.op('pool', lambda e: e.memset(ONESB[:], 1.0), writes=['ONESB'])
        affine_mask(CMASK[:], 'CMASK', [[1, 128]], 0, -1)
        S.op('pool', lambda e: e.memset(SEQM[:], 1.0), writes=['SEQM'])
        S.op('pool', lambda e: e.affine_select(out=SEQM[:], in_=SEQM[:], pattern=[[-8, SSQ]], compare_op=ALU.is_ge,
                                               fill=0.0, base=0, channel_multiplier=1), reads=['SEQM'], writes=['SEQM'])
        S.op('pool', lambda e: e.affine_select(out=SEQM[:], in_=SEQM[:], pattern=[[8, SSQ]], compare_op=ALU.is_ge,
                                               fill=0.0, base=7, channel_multiplier=-1), reads=['SEQM'], writes=['SEQM'])
        S.op('dve', lambda e: e.tensor_copy(out=E8[:], in_=IDF[0:8, 0:8].unsqueeze(1).to_broadcast([8, SSQ, 8])),
             reads=['IDF'], writes=['E8'])

        for i in range(VEC_TOT // 128):
            t = TMPF[rot('tmpf', 2)]
            tk = 'TMPF%d' % ((ctr['tmpf'] - 1) % 2)
            S.dma('sp', t[:, 0:128], vecs[i * 128:(i + 1) * 128, :], writes=[tk])
            b = next_ps()
            S.op('pe', lambda e, b=b, t=t: e.transpose(out=PS[b][:, 0:128], in_=t[:, 0:128], identity=IDF[:]),
                 reads=[tk, 'IDF'], writes=[('ps', b)])
            S.op('dve', lambda e, b=b, i=i: e.tensor_copy(out=COLS[:, i * 128:(i + 1) * 128], in_=PS[b][:, 0:128]),
                 reads=[('ps', b)], writes=['COLS'])

        def col(name, j=0, n=1):
            r0, nr = VEC_LAY[name]
            return COLS[:, r0 + j:r0 + j + n]

        S.dma('pool', BINV[:], a_b_in_r[2048:4096].partition_broadcast(128), writes=['BINV'])
        S.dma('pool', LNW[:], a_ln_w.partition_broadcast(128), writes=['LNW'])
        S.dma('pool', LNB[:], a_ln_b.partition_broadcast(128), writes=['LNB'])
        S.dma('pool', BSR[:], a_b_s.rearrange("(o g) t -> o g t", o=1), writes=['BSR'])
        S.op('dve', lambda e: e.tensor_copy(out=BSS[:].rearrange("o g (b t) -> o g b t", t=8),
                                            in_=BSR[:, :, 0:8].unsqueeze(2).to_broadcast([1, 16, SSQ, 8])),
             reads=['BSR'], writes=['BSS'])

        t = TMPF[rot('tmpf', 2)]
        tk = 'TMPF%d' % ((ctr['tmpf'] - 1) % 2)
        S.dma('sp', t[:].rearrange("p (g s) -> p g s", g=16), a_w_s.rearrange("g t s -> t g s"), writes=[tk])
        for g in range(16):
            b = next_ps()
            S.op('pe', lambda e, b=b, g=g, t=t: e.transpose(out=PS[b][:, 0:128], in_=t[:, g * 128:(g + 1) * 128], identity=IDF[:]),
                 reads=[tk, 'IDF'], writes=[('ps', b)])
            S.op('dve', lambda e, b=b, g=g: e.tensor_tensor(out=WST[:, g, :], in0=PS[b][:, 0:128], in1=CMASK[:], op=ALU.mult),
                 reads=[('ps', b), 'CMASK'], writes=['WST'])
        b = next_ps()
        S.op('pe', lambda e, b=b: e.matmul(PS[b][0:NS, 0:128], lhsT=E8[:].rearrange("s b t -> s (b t)"),
                                           rhs=WST[0:8, :, 0:8], start=True, stop=True),
             reads=['E8', 'WST'], writes=[('ps', b)])
        S.op('dve', lambda e, b=b: e.tensor_tensor(
            out=BDS[:].rearrange("p g (b t) -> p g b t", t=8),
            in0=PS[b][0:NS, 0:128].rearrange("p (g t) -> p g t", t=8).unsqueeze(2).to_broadcast([NS, 16, SSQ, 8]),
            in1=SEQM[:].unsqueeze(1).unsqueeze(3).to_broadcast([NS, 16, SSQ, 8]), op=ALU.mult),
            reads=[('ps', b), 'SEQM'], writes=['BDS'])

        class WTile:
            def __init__(self, halves):
                self.halves = halves

            def c(self, k, mi):
                tl, key = self.halves[mi // 2]
                o = (mi % 2) * 128
                return tl[:, k, o:o + 128]

            def key(self, mi):
                return self.halves[mi // 2][1]

        def load_w(wap, r0, nk, c0, ncols=512, pool='wt'):
            if pool == 'wo':
                i = rot('wo', 2)
                tl, key = WO[i], 'WO%d' % i
                src = wap[r0:r0 + nk * 128, c0:c0 + ncols].rearrange("(k p) c -> p k c", p=128)
                S.dma('pool', tl[:, 0:nk, 0:ncols], src, writes=[key])
                return tl, key
            halves = []
            for h0 in range(0, ncols, 256):
                w = min(256, ncols - h0)
                i = rot('wt', 4)
                tl, key = WT[i], 'WT%d' % i
                src = wap[r0:r0 + nk * 128, c0 + h0:c0 + h0 + w].rearrange("(k p) c -> p k c", p=128)
                S.dma('pool', tl[:, 0:nk, 0:w], src, writes=[key])
                halves.append((tl, key))
            return WTile(halves), None

        t = TMPF[rot('tmpf', 2)]
        tk = 'TMPF%d' % ((ctr['tmpf'] - 1) % 2)
        S.dma('sp', t[0:17, :], call, writes=[tk])
        S.op('act', lambda e, t=t: e.activation(out=t[0:17, :], in_=t[0:17, :], func=AF.Silu), reads=[tk], writes=[tk])
        for k4 in range(4):
            b = next_ps()
            for kk in range(4):
                k = k4 * 4 + kk
                S.op('pe', lambda e, b=b, k=k, kk=kk, t=t: e.transpose(out=PS[b][:, kk * 17:(kk + 1) * 17],
                                                                      in_=t[0:17, k * 128:(k + 1) * 128], identity=IDF[0:17, 0:17]),
                     reads=[tk, 'IDF'], writes=[('ps', b)])
            S.op('dve', lambda e, b=b, k4=k4: e.tensor_copy(out=CT[:, k4 * 4:(k4 + 1) * 4, :].rearrange("p k c -> p (k c)"),
                                                           in_=PS[b][:, 0:68]),
                 reads=[('ps', b)], writes=['CT'])
        for l in range(2):
            for nb in range(24):
                tl, key = load_w(mod_w[l], 0, KC, nb * 512)
                b = next_ps()
                for mi in range(4):
                    for k in range(KC):
                        S.op('pe', lambda e, b=b, mi=mi, k=k, tl=tl: e.matmul(
                            PS[b][:, mi * 17:(mi + 1) * 17], lhsT=tl.c(k, mi), rhs=CT[:, k, :],
                            start=(k == 0), stop=(k == KC - 1)),
                            reads=[tl.key(mi), 'CT'], writes=[('ps', b)])
                for mi in range(4):
                    m = nb * 4 + mi
                    S.op('act', lambda e, b=b, mi=mi, m=m, l=l: e.activation(
                        out=MOD[l][:, m, :], in_=PS[b][:, mi * 17:(mi + 1) * 17], func=AF.Identity,
                        bias=col('mod_b%d' % l, m), scale=1.0),
                        reads=[('ps', b), 'COLS'], writes=['MOD%d' % l])

        def load_x(hf):
            for tt in range(NTILE):
                rows = 128 if tt < NCH else NS
                t = TMPF[rot('tmpf', 2)]
                tk = 'TMPF%d' % ((ctr['tmpf'] - 1) % 2)
                if tt < NCH:
                    src = xp[hf * NPT + tt * 128: hf * NPT + (tt + 1) * 128, :]
                else:
                    src = xs[hf * NS:(hf + 1) * NS, :]
                S.dma('sp', t[0:rows, :], src, writes=[tk])
                for k4 in range(4):
                    b = next_ps()
                    for kk in range(4):
                        k = k4 * 4 + kk
                        S.op('pe', lambda e, b=b, k=k, kk=kk, t=t, rows=rows: e.transpose(
                            out=PS[b][:, kk * 128:kk * 128 + rows], in_=t[0:rows, k * 128:(k + 1) * 128],
                            identity=IDF[0:rows, 0:rows]),
                            reads=[tk, 'IDF'], writes=[('ps', b)])
                    S.op('dve', lambda e, b=b, k4=k4, tt=tt, rows=rows: e.tensor_copy(
                        out=XT[:, k4 * 4:(k4 + 1) * 4, tt * 128:tt * 128 + rows],
                        in_=PS[b][:].rearrange("p (k t) -> p k t", t=128)[:, :, 0:rows]),
                        reads=[('ps', b)], writes=[('XT', k4 * 4 + kk_, tb_of(tt * 128)) for kk_ in range(4)])

        def tb_of(tok):
            return 0 if tok < TBS[0][1] else 1

        def xk(m):
            return [('XT', m, 0), ('XT', m, 1)]

        def compute_rstd(eps):
            bs = [next_ps() for _ in TBS]
            for k in range(KC):
                i = rot('sq', 2)
                S.op('act', lambda e, i=i, k=k: e.activation(out=SQ[i][:], in_=XT[:, k, :], func=AF.Square),
                     reads=xk(k), writes=['SQ%d' % i])
                for bi, (t0, t1) in enumerate(TBS):
                    S.op('pe', lambda e, b=bs[bi], i=i, t0=t0, t1=t1, k=k: e.matmul(
                        PS[b][:, 0:t1 - t0], lhsT=ONESB[:], rhs=SQ[i][:, t0:t1], start=(k == 0), stop=(k == KC - 1)),
                        reads=['SQ%d' % i, 'ONESB'], writes=[('ps', bs[bi])])
            for bi, (t0, t1) in enumerate(TBS):
                b = bs[bi]
                S.op('dve', lambda e, b=b, t0=t0, t1=t1: e.tensor_scalar(
                    out=RSTD[:, t0:t1], in0=PS[b][:, 0:t1 - t0], scalar1=1.0 / D, scalar2=eps, op0=ALU.mult, op1=ALU.add),
                    reads=[('ps', b)], writes=['RSTD'])
            S.op('act', lambda e: e.activation(out=RSTD[:], in_=RSTD[:], func=AF.Ln), reads=['RSTD'], writes=['RSTD'])
            S.op('act', lambda e: e.activation(out=RSTD[:], in_=RSTD[:], func=AF.Exp, scale=-0.5), reads=['RSTD'], writes=['RSTD'])

        def norm_mod(hf, l, which, nw_name):
            compute_rstd(NORM_EPS)
            sh0, sc0 = which * 48, which * 48 + 16
            for k in range(KC):
                S.op('dve', lambda e, k=k: e.tensor_scalar(
                    out=ACOL[:, k, :], in0=MOD[l][:, sc0 + k, :], scalar1=1.0, scalar2=col(nw_name, k),
                    op0=ALU.add, op1=ALU.mult),
                    reads=['MOD%d' % l, 'COLS'], writes=['ACOL'])
            for k in range(KC):
                i = rot('tmpf', 2)
                t, tk = TMPF[i], 'TMPF%d' % i
                S.op('dve', lambda e, t=t, k=k: e.tensor_tensor(out=t[:, 0:NTOK], in0=XT[:, k, :], in1=RSTD[:], op=ALU.mult),
                     reads=xk(k) + ['RSTD'], writes=[tk])
                S.op('act', lambda e, t=t, k=k: e.activation(
                    out=HT[:, k, 0:NPT], in_=t[:, 0:NPT], func=AF.Identity,
                    bias=MOD[l][:, sh0 + k, 0:1], scale=ACOL[:, k, 0:1]),
                    reads=[tk, 'ACOL', 'MOD%d' % l], writes=[('HT', k)])
                c0 = 1 + hf * SSQ
                S.op('dve', lambda e, t=t, k=k, c0=c0: e.tensor_tensor(
                    out=t[:, NPT:NTOK].rearrange("p (b t) -> p b t", t=8),
                    in0=t[:, NPT:NTOK].rearrange("p (b t) -> p b t", t=8),
                    in1=ACOL[:, k, c0:c0 + SSQ].unsqueeze(2).to_broadcast([128, SSQ, 8]), op=ALU.mult),
                    reads=[tk, 'ACOL'], writes=[tk])
                S.op('dve', lambda e, t=t, k=k, c0=c0: e.tensor_tensor(
                    out=HT[:, k, NPT:NTOK].rearrange("p (b t) -> p b t", t=8),
                    in0=t[:, NPT:NTOK].rearrange("p (b t) -> p b t", t=8),
                    in1=MOD[l][:, sh0 + k, c0:c0 + SSQ].unsqueeze(2).to_broadcast([128, SSQ, 8]), op=ALU.add),
                    reads=[tk, 'MOD%d' % l], writes=[('HT', k)])

        def ht_keys():
            return [('HT', k) for k in range(KC)]

        def resid_evac(hf, l, gate0, b, m, t0, t1):
            n = t1 - t0
            npr = min(t1, NPT) - t0
            S.op('dve', lambda e: e.scalar_tensor_tensor(
                out=XT[:, m, t0:t0 + npr], in0=PS[b][:, 0:npr], scalar=MOD[l][:, gate0 + m, 0:1],
                in1=XT[:, m, t0:t0 + npr], op0=ALU.mult, op1=ALU.add),
                reads=[('ps', b), 'MOD%d' % l, ('XT', m, tb_of(t0))], writes=[('XT', m, tb_of(t0))])
            if t1 > NPT:
                c0 = 1 + hf * SSQ
                i = rot('ev', 3)
                S.op('dve', lambda e: e.tensor_tensor(
                    out=EV[i][:, 0:NS].rearrange("p (b t) -> p b t", t=8),
                    in0=PS[b][:, npr:n].rearrange("p (b t) -> p b t", t=8),
                    in1=MOD[l][:, gate0 + m, c0:c0 + SSQ].unsqueeze(2).to_broadcast([128, SSQ, 8]), op=ALU.mult),
                    reads=[('ps', b), 'MOD%d' % l], writes=['EV%d' % i])
                S.op('dve', lambda e: e.tensor_tensor(out=XT[:, m, NPT:NTOK], in0=XT[:, m, NPT:NTOK], in1=EV[i][:, 0:NS], op=ALU.add),
                     reads=['EV%d' % i, ('XT', m, 1)], writes=[('XT', m, 1)])

        def out_proj_partial(hf, l, gate0, wap, r0):
            for cb in range(4):
                tl, key = load_w(wap, r0, 4, cb * 512, pool='wo')
                for mi in range(4):
                    m = cb * 4 + mi
                    for (t0, t1) in TBS:
                        b = next_ps()
                        for k in range(4):
                            S.op('pe', lambda e, b=b, k=k, mi=mi, tl=tl, t0=t0, t1=t1: e.matmul(
                                PS[b][:, 0:t1 - t0], lhsT=tl[:, k, mi * 128:(mi + 1) * 128], rhs=GB[:, k, t0:t1],
                                start=(k == 0), stop=(k == 3)),
                                reads=[key, 'GB'], writes=[('ps', b)])
                        resid_evac(hf, l, gate0, b, m, t0, t1)

        def gmlp(hf):
            hk = ht_keys()
            for nb in range(4):
                tl, key = load_w(a_w_in, 0, KC, 2048 + nb * 512)
                for tt in range(NTILE):
                    rows = 128 if tt < NCH else NS
                    b = next_ps()
                    for hh, (th, kh) in enumerate(tl.halves):
                        for k in range(KC):
                            S.op('pe', lambda e, b=b, k=k, th=th, hh=hh, tt=tt, rows=rows: e.matmul(
                                PS[b][0:rows, hh * 256:(hh + 1) * 256], lhsT=HT[:, k, tt * 128:tt * 128 + rows], rhs=th[:, k, :],
                                start=(k == 0), stop=(k == KC - 1)),
                                reads=[kh, ('HT', k)], writes=[('ps', b)])
                    i = rot('ev', 3)
                    S.op('dve', lambda e, b=b, i=i, nb=nb, rows=rows: e.tensor_tensor(
                        out=EV[i][0:rows, :], in0=PS[b][0:rows, :], in1=BINV[0:rows, nb * 512:(nb + 1) * 512], op=ALU.add),
                        reads=[('ps', b), 'BINV'], writes=['EV%d' % i])
                    S.op('act', lambda e, i=i, rows=rows: e.activation(out=EV[i][0:rows, :], in_=EV[i][0:rows, :], func=AF.Gelu),
                         reads=['EV%d' % i], writes=['EV%d' % i])
                    S.op('dve', lambda e, i=i, tt=tt, nb=nb, rows=rows: e.bn_stats(out=STATS[0:rows, tt, nb, :], in_=EV[i][0:rows, :]),
                         reads=['EV%d' % i], writes=[('STATS', tt)])
                    S.op('act', lambda e, i=i, tt=tt, nb=nb, rows=rows: e.activation(
                        out=VN[0:rows, tt, nb * 512:(nb + 1) * 512], in_=EV[i][0:rows, :], func=AF.Copy),
                        reads=['EV%d' % i], writes=[('VN', tt)])
            for tt in range(NTILE):
                rows = 128 if tt < NCH else NS
                S.op('dve', lambda e, tt=tt, rows=rows: e.bn_aggr(out=MV[0:rows, tt, :], in_=STATS[0:rows, tt, :, :].rearrange("p a b -> p (a b)")),
                     reads=[('STATS', tt)], writes=[('MV', tt)])
                S.op('dve', lambda e, tt=tt, rows=rows: e.tensor_scalar(out=SMALL[0:rows, tt:tt + 1], in0=MV[0:rows, tt, 1:2], scalar1=LN_EPS, scalar2=None, op0=ALU.add),
                     reads=[('MV', tt)], writes=[('SM', tt)])
                S.op('act', lambda e, tt=tt, rows=rows: e.activation(out=SMALL[0:rows, tt:tt + 1], in_=SMALL[0:rows, tt:tt + 1], func=AF.Sqrt),
                     reads=[('SM', tt)], writes=[('SM', tt)])
                S.op('dve', lambda e, tt=tt, rows=rows: e.reciprocal(out=SMALL[0:rows, tt:tt + 1], in_=SMALL[0:rows, tt:tt + 1]),
                     reads=[('SM', tt)], writes=[('SM', tt)])
                i = rot('tmpf', 2)
                t, tk = TMPF[i], 'TMPF%d' % i
                S.op('dve', lambda e, tt=tt, rows=rows, t=t: e.tensor_scalar(
                    out=t[0:rows, :], in0=VN[0:rows, tt, :], scalar1=MV[0:rows, tt, 0:1], scalar2=SMALL[0:rows, tt:tt + 1],
                    op0=ALU.subtract, op1=ALU.mult),
                    reads=[('VN', tt), ('MV', tt), ('SM', tt)], writes=[tk])
                S.op('dve', lambda e, rows=rows, t=t: e.tensor_tensor(out=t[0:rows, :], in0=t[0:rows, :], in1=LNW[0:rows, :], op=ALU.mult),
                     reads=[tk, 'LNW'], writes=[tk])
                S.op('dve', lambda e, rows=rows, t=t: e.tensor_tensor(out=t[0:rows, :], in0=t[0:rows, :], in1=LNB[0:rows, :], op=ALU.add),
                     reads=[tk, 'LNB'], writes=[tk])
                S.op('act', lambda e, tt=tt, rows=rows, t=t: e.activation(out=VN[0:rows, tt, :], in_=t[0:rows, :], func=AF.Copy),
                     reads=[tk], writes=[('VN', tt)])
                if tt == NCH:
                    S.dma('sp', v_s[hf * NS:(hf + 1) * NS, :], t[0:NS, :], reads=[tk])
                elif tt == NCH - 1 and hf == NPASS - 1:
                    S.dma('sp', v_p, t[:, :], reads=[tk])
            for nb in range(4):
                tl, key = load_w(a_w_in, 0, KC, nb * 512)
                for mi in range(4):
                    m = nb * 4 + mi
                    for (t0, t1) in TBS:
                        n = t1 - t0
                        bu = next_ps()
                        for k in range(KC):
                            S.op('pe', lambda e, b=bu, k=k, mi=mi, tl=tl, t0=t0, t1=t1: e.matmul(
                                PS[b][:, 0:t1 - t0], lhsT=tl.c(k, mi), rhs=HT[:, k, t0:t1],
                                start=(k == 0), stop=(k == KC - 1)),
                                reads=[tl.key(mi), ('HT', k)], writes=[('ps', bu)])
                        i = rot('ev', 3)
                        S.op('act', lambda e, b=bu, i=i, n=n, m=m: e.activation(
                            out=EV[i][:, 0:n], in_=PS[b][:, 0:n], func=AF.Gelu, bias=col('a_b_in', m), scale=1.0),
                            reads=[('ps', bu), 'COLS'], writes=['EV%d' % i])
                        bs_ = next_ps()
                        for tt in range(t0 // 128, (t1 + 127) // 128):
                            rows = 128 if tt < NCH else NS
                            c0 = tt * 128 - t0
                            rhs = WST[:, m, :] if tt < NCH else BDS[:, m, :]
                            brow = BSR[:, m, :] if tt < NCH else BSS[:, m, :]
                            S.op('pe', lambda e, b=bs_, tt=tt, rows=rows, c0=c0, rhs=rhs, m=m: e.matmul(
                                PS[b][:, c0:c0 + rows], lhsT=VN[0:rows, tt, m * 128:(m + 1) * 128], rhs=rhs,
                                start=True, stop=False),
                                reads=[('VN', tt), 'WST', 'BDS'], writes=[('ps', bs_)])
                            S.op('pe', lambda e, b=bs_, rows=rows, c0=c0, brow=brow: e.matmul(
                                PS[b][:, c0:c0 + rows], lhsT=ONESB[0:1, :], rhs=brow, start=False, stop=True),
                                reads=['ONESB', 'BSR', 'BSS'], writes=[('ps', bs_)])
                        S.op('dve', lambda e, b=bs_, i=i, n=n, mi=mi, t0=t0, t1=t1: e.tensor_tensor(
                            out=GB[:, mi, t0:t1], in0=PS[b][:, 0:n], in1=EV[i][:, 0:n], op=ALU.mult),
                            reads=[('ps', bs_), 'EV%d' % i], writes=['GB'])
                out_proj_partial(hf, 0, 32, a_w_out, nb * 512)


        affine_mask(UT[:], 'UT', [[1, 128]], 0, -1)
        S.op('pool', lambda e: e.memset(ONESF[:], 1.0), writes=['ONESF'])
        S.op('dve', lambda e: e.tensor_copy(out=SEQ32[:].rearrange("p (b t) -> p b t", t=8),
                                            in_=SEQM[:].unsqueeze(2).to_broadcast([NS, SSQ, 8])),
             reads=['SEQM'], writes=['SEQ32'])
        S.op('dve', lambda e: e.tensor_tensor(out=USB[:], in0=SEQ32[:], in1=UT[0:NS, 0:NS], op=ALU.mult),
             reads=['SEQ32', 'UT'], writes=['USB'])
        S.op('pool', lambda e: e.memset(MASK3[:], 1.0), writes=['MASK3'])
        S.op('pool', lambda e: e.affine_select(out=MASK3[:], in_=MASK3[:], pattern=[[-8, SSQ], [1, NS]], compare_op=ALU.is_ge,
                                               fill=0.0, base=0, channel_multiplier=0), reads=['MASK3'], writes=['MASK3'])
        S.op('pool', lambda e: e.affine_select(out=MASK3[:], in_=MASK3[:], pattern=[[8, SSQ], [-1, NS]], compare_op=ALU.is_ge,
                                               fill=0.0, base=7, channel_multiplier=0), reads=['MASK3'], writes=['MASK3'])
        S.dma('sp', DTB[:], b_dtb.partition_broadcast(128), writes=['DTB'])
        S.dma('sp', AROW[:], b_alog.partition_broadcast(128), writes=['AROW'])
        S.dma('sp', DROW[:], b_dsk.partition_broadcast(128), writes=['DROW'])
        S.op('act', lambda e: e.activation(out=AROW[:], in_=AROW[:], func=AF.Exp), reads=['AROW'], writes=['AROW'])
        S.op('dve', lambda e: e.tensor_scalar(out=AROW[:], in0=AROW[:], scalar1=-1.0, scalar2=None, op0=ALU.mult),
             reads=['AROW'], writes=['AROW'])
        S.op('pool', lambda e: e.memset(CTAIL[:], 0.0), writes=['CTAIL'])

        LIVE = [set()]

        def tr_store(src_ap, nrow, dst_ap, rkeys):
            b = next_ps(LIVE[0])
            S.op('pe', lambda e: e.transpose(out=PS[b][0:nrow, 0:128], in_=src_ap, identity=IDF[:]),
                 reads=list(rkeys) + ['IDF'], writes=[('ps', b)])
            j = rot('sm12', 2)
            S.op('dve', lambda e: e.tensor_copy(out=SM12[j][0:nrow, :], in_=PS[b][0:nrow, 0:128]),
                 reads=[('ps', b)], writes=['SM12_%d' % j])
            S.dma('sp', dst_ap, SM12[j][0:nrow, :], reads=['SM12_%d' % j])

        def ssd(hf):
            last = (hf == NPASS - 1)
            rows_of = lambda tt: 128 if tt < NCH else NS
            for cb in range(3):
                t, tk = TMPF[0], 'TMPF0'
                S.dma('sp', t[0:SSQ * 3, :], st_conv[hf * SSQ * 3:(hf + 1) * SSQ * 3, cb * 2048:(cb + 1) * 2048], writes=[tk])
                for c4 in range(4):
                    b = next_ps()
                    for cc in range(4):
                        ch = c4 * 4 + cc
                        S.op('pe', lambda e, b=b, cc=cc, ch=ch, t=t: e.transpose(
                            out=PS[b][:, cc * 12:(cc + 1) * 12], in_=t[0:SSQ * 3, ch * 128:(ch + 1) * 128], identity=IDF[0:SSQ * 3, 0:SSQ * 3]),
                            reads=[tk, 'IDF'], writes=[('ps', b)])
                    S.op('dve', lambda e, b=b, cb=cb, c4=c4: e.tensor_copy(
                        out=SCV[:, cb * 16 + c4 * 4: cb * 16 + c4 * 4 + 4, :].rearrange("p c k -> p (c k)"), in_=PS[b][:, 0:48]),
                        reads=[('ps', b)], writes=['SCV'])
            tl, key = load_w(b_w_in, 0, KC, 10240, ncols=64)
            for tt in range(NTILE):
                rows = rows_of(tt)
                b = next_ps()
                for k in range(KC):
                    S.op('pe', lambda e, b=b, k=k, tt=tt, rows=rows, tl=tl: e.matmul(
                        PS[b][0:rows, 0:64], lhsT=HT[:, k, tt * 128:tt * 128 + rows], rhs=tl.halves[0][0][:, k, 0:64],
                        start=(k == 0), stop=(k == KC - 1)), reads=[tl.halves[0][1], ('HT', k)], writes=[('ps', b)])
                S.op('dve', lambda e, b=b, tt=tt, rows=rows: e.tensor_tensor(out=DT[0:rows, tt, :], in0=PS[b][0:rows, 0:64], in1=DTB[0:rows, :], op=ALU.add),
                     reads=[('ps', b), 'DTB'], writes=['DT'])
            S.op('act', lambda e: e.activation(out=DT[:], in_=DT[:], func=AF.Exp), reads=['DT'], writes=['DT'])
            S.op('act', lambda e: e.activation(out=DT[:], in_=DT[:], func=AF.Ln, bias=1.0, scale=1.0), reads=['DT'], writes=['DT'])
            S.op('dve', lambda e: e.tensor_tensor(out=DA[:], in0=DT[:], in1=AROW[:].unsqueeze(1).to_broadcast([128, NTILE, 64]), op=ALU.mult),
                 reads=['DT', 'AROW'], writes=['DA'])
            for tt in range(NTILE):
                rows = rows_of(tt)
                um = UT if tt < NCH else USB
                om = ONESF if tt < NCH else SEQ32
                b = next_ps()
                S.op('pe', lambda e, b=b, tt=tt, rows=rows, um=um: e.matmul(PS[b][0:rows, 0:64], lhsT=um[0:rows, 0:rows], rhs=DA[0:rows, tt, :], start=True, stop=True),
                     reads=['DA', 'UT', 'USB'], writes=[('ps', b)])
                S.op('pe', lambda e, b=b, tt=tt, rows=rows, om=om: e.matmul(PS[b][0:rows, 64:128], lhsT=om[0:rows, 0:rows], rhs=DA[0:rows, tt, :], start=True, stop=True),
                     reads=['DA', 'ONESF', 'SEQ32'], writes=[('ps', b)])
                S.op('dve', lambda e, b=b, tt=tt, rows=rows: e.tensor_copy(out=ACS[0:rows, tt, :], in_=PS[b][0:rows, 0:64]), reads=[('ps', b)], writes=['ACS'])
                S.op('dve', lambda e, b=b, tt=tt, rows=rows: e.tensor_copy(out=TOT[0:rows, tt, :], in_=PS[b][0:rows, 64:128]), reads=[('ps', b)], writes=['TOT'])
            S.op('act', lambda e: e.activation(out=EXPA[:], in_=ACS[:], func=AF.Exp), reads=['ACS'], writes=['EXPA'])
            S.op('act', lambda e: e.activation(out=DEC[:], in_=TOT[:], func=AF.Exp), reads=['TOT'], writes=['DEC'])
            S.op('dve', lambda e: e.tensor_tensor(out=DTE[:], in0=TOT[:], in1=ACS[:], op=ALU.subtract), reads=['TOT', 'ACS'], writes=['DTE'])
            S.op('act', lambda e: e.activation(out=DTE[:], in_=DTE[:], func=AF.Exp), reads=['DTE'], writes=['DTE'])

            S.barrier()
            for g in range(8):
                tl, key = load_w(b_w_in, 0, KC, g * 512)
                for tt in range(NTILE):
                    rows = rows_of(tt)
                    b = next_ps()
                    for hh, (th, kh) in enumerate(tl.halves):
                        for k in range(KC):
                            S.op('pe', lambda e, b=b, k=k, tt=tt, rows=rows, th=th, hh=hh: e.matmul(
                                PS[b][0:rows, hh * 256:(hh + 1) * 256], lhsT=HT[:, k, tt * 128:tt * 128 + rows], rhs=th[:, k, :],
                                start=(k == 0), stop=(k == KC - 1)), reads=[kh, ('HT', k)], writes=[('ps', b)])
                    S.op('act', lambda e, b=b, tt=tt, rows=rows: e.activation(out=SZ[0:rows, tt, :], in_=PS[b][0:rows, :], func=AF.Silu),
                         reads=[('ps', b)], writes=['SZ'])
                tlx, keyx = load_w(b_w_in, 0, KC, 4096 + g * 512)
                cinfo = []
                for ci in range(6):
                    if ci < 4:
                        cinfo.append(dict(c0=ci * 128, ch=g * 4 + ci, kind='x', mi=ci))
                    elif ci == 4:
                        cinfo.append(dict(c0=0, ch=32 + g, kind='B', mi=0))
                    else:
                        cinfo.append(dict(c0=0, ch=40 + g, kind='C', mi=1))

                def stage_p(ci):
                    inf = cinfo[ci]
                    if ci < 4:
                        tl = tlx
                    elif ci == 4:
                        tl, _ = load_w(b_w_in, 0, KC, 8192 + g * 128, ncols=128)
                    else:
                        tl, _ = load_w(b_w_in, 0, KC, 9216 + g * 128, ncols=128)
                    c0 = inf['c0']
                    banks = []
                    for (t0, t1) in TBS:
                        b = next_ps(live_banks)
                        banks.append(b)
                        for k in range(KC):
                            S.op('pe', lambda e, b=b, k=k, tl=tl, c0=c0, t0=t0, t1=t1: e.matmul(
                                PS[b][:, 0:t1 - t0], lhsT=tl.c(k, c0 // 128), rhs=HT[:, k, t0:t1],
                                start=(k == 0), stop=(k == KC - 1)), reads=[tl.key(c0 // 128), ('HT', k)], writes=[('ps', b)])
                    inf['banks'] = banks
                    live_banks.update(banks)

                def stage_q(ci):
                    inf = cinfo[ci]
                    ch, kind, mi = inf['ch'], inf['kind'], inf['mi']
                    XC = (XC_A, XC2)[ci % 2]
                    xck = 'XC%d' % (ci % 2)
                    S.op('dve', lambda e, ch=ch: e.tensor_copy(out=RAW[:, 0:3], in_=CTAIL[:, ch, :]), reads=['CTAIL'], writes=['RAW'])
                    S.op('dve', lambda e, ch=ch: e.tensor_copy(
                        out=RAW[:, 3 + NPT:].rearrange("p (b t) -> p b t", t=11)[:, :, 0:3],
                        in_=SCV[:, ch, :].rearrange("p (b k) -> p b k", k=3)), reads=['SCV'], writes=['RAW'])
                    for bi, (t0, t1) in enumerate(TBS):
                        n = t1 - t0
                        npr = min(t1, NPT) - t0
                        b = inf['banks'][bi]
                        S.op('act', lambda e, b=b, t0=t0, npr=npr: e.activation(out=RAW[:, 3 + t0:3 + t0 + npr], in_=PS[b][:, 0:npr], func=AF.Copy),
                             reads=[('ps', b)], writes=['RAW'])
                        if t1 > NPT:
                            S.op('act', lambda e, b=b, npr=npr, n=n: e.activation(
                                out=RAW[:, 3 + NPT:].rearrange("p (b t) -> p b t", t=11)[:, :, 3:11],
                                in_=PS[b][:, npr:n].rearrange("p (b t) -> p b t", t=8), func=AF.Copy),
                                reads=[('ps', b)], writes=['RAW'])
                    for b in inf['banks']:
                        live_banks.discard(b)
                    def conv_state_out():
                        S.op('dve', lambda e, ch=ch: e.tensor_copy(out=CTAIL[:, ch, :], in_=RAW[:, NPT:NPT + 3]), reads=['RAW'], writes=['CTAIL'])
                        S.op('dve', lambda e, ch=ch: e.tensor_copy(
                            out=CSO[:, ch, :].rearrange("p (b k) -> p b k", k=3),
                            in_=RAW[:, 3 + NPT:].rearrange("p (b t) -> p b t", t=11)[:, :, 8:11]), reads=['RAW'], writes=['CSO'])
                    cw = lambda kk, ch=ch: col('conv_w', kk * 48 + ch)
                    cbias = col('conv_b', ch)
                    pr_in = lambda kk: RAW[:, kk:kk + NPT]
                    sm_in = lambda kk: RAW[:, 3 + NPT:].rearrange("p (b t) -> p b t", t=11)[:, :, kk:kk + 8]
                    pr_out = XC[:, 0:NPT]
                    sm_out = XC[:, NPT:NTOK].rearrange("p (b t) -> p b t", t=8)
                    for (oin, oout) in ((pr_in, pr_out), (sm_in, sm_out)):
                        S.op('dve', lambda e, oin=oin, oout=oout, cw=cw, cbias=cbias: e.tensor_scalar(
                            out=oout, in0=oin(0), scalar1=cw(0), scalar2=cbias, op0=ALU.mult, op1=ALU.add),
                            reads=['RAW', 'COLS'], writes=[xck])
                        for kk in range(1, 4):
                            S.op('dve', lambda e, oin=oin, oout=oout, cw=cw, kk=kk: e.scalar_tensor_tensor(
                                out=oout, in0=oin(kk), scalar=cw(kk), in1=oout, op0=ALU.mult, op1=ALU.add),
                                reads=['RAW', 'COLS', xck], writes=[xck])
                    conv_state_out()
                    if kind == 'C':
                        S.op('act', lambda e: e.activation(out=BCT[:, 1, :], in_=XC[:], func=AF.Silu), reads=[xck], writes=['BCT'])
                        return
                    if kind == 'B':
                        S.op('act', lambda e: e.activation(out=BCT[:, 0, :], in_=XC[:], func=AF.Silu), reads=[xck], writes=['BCT'])
                    S.op('act', lambda e: e.activation(out=XC[:], in_=XC[:], func=AF.Silu), reads=[xck], writes=[xck])
                    b = next_ps(live_banks)
                    for tt in range(NCH):
                        S.op('pe', lambda e, b=b, tt=tt: e.transpose(out=PS[b][:, tt * 128:(tt + 1) * 128], in_=XC[:, tt * 128:(tt + 1) * 128], identity=IDF[:]),
                             reads=[xck, 'IDF'], writes=[('ps', b)])
                    b2 = next_ps(live_banks)
                    S.op('pe', lambda e, b2=b2: e.transpose(out=PS[b2][0:NS, 0:128], in_=XC[:, NPT:NTOK], identity=IDF[:]),
                         reads=[xck, 'IDF'], writes=[('ps', b2)])
                    if kind == 'x':
                        S.op('dve', lambda e, b=b, mi=mi: e.tensor_copy(out=XTOK[:, 0:NCH, mi * 128:(mi + 1) * 128],
                                                                      in_=PS[b][:, :].rearrange("p (t c) -> p t c", c=128)),
                             reads=[('ps', b)], writes=['XTOK'])
                        S.op('dve', lambda e, b2=b2, mi=mi: e.tensor_copy(out=XTOK[0:NS, NCH, mi * 128:(mi + 1) * 128], in_=PS[b2][0:NS, 0:128]),
                             reads=[('ps', b2)], writes=['XTOK'])
                    else:
                        S.op('dve', lambda e, b=b: e.tensor_copy(out=BTOK[:, 0:NCH, :], in_=PS[b][:, :].rearrange("p (t c) -> p t c", c=128)),
                             reads=[('ps', b)], writes=['BTOK'])
                        S.op('dve', lambda e, b2=b2: e.tensor_copy(out=BTOK[0:NS, NCH, :], in_=PS[b2][0:NS, 0:128]),
                             reads=[('ps', b2)], writes=['BTOK'])

                live_banks = set()
                LIVE[0] = live_banks
                stage_p(0)
                for ci in range(6):
                    if ci + 1 < 6:
                        stage_p(ci + 1)
                    stage_q(ci)
                S.dma('sp', NWR[:], b_nw[g * 512:(g + 1) * 512].partition_broadcast(128), writes=['NWR'])
                if hf == 0:
                    S.op('pool', lambda e: e.memset(HSTG[:], 0.0), writes=['HSTG'])
                else:
                    S.dma('sp', HSTG[:], hst_d[g], reads=[('hst_d', g)], writes=['HSTG'])
                S.op('act', lambda e: e.activation(out=HB[:], in_=HSTG[:], func=AF.Copy), reads=['HSTG'], writes=['HB'])
                hs = slice(g * 8, g * 8 + 8)
                held = set()
                LIVE[0] = held
                tinfo = {}

                def front(tt, g=g, hs=hs):
                    rows = rows_of(tt)
                    samp = (tt == NCH)
                    tk0 = tt * 128
                    um = USB if samp else UT
                    S.op('pool', lambda e: e.tensor_tensor(
                        out=XDT[0:rows, :].rearrange("p (r q) -> p r q", q=64), in0=XTOK[0:rows, tt, :].rearrange("p (r q) -> p r q", q=64),
                        in1=DT[0:rows, tt, hs].unsqueeze(2).to_broadcast([rows, 8, 64]), op=ALU.mult), reads=['XTOK', 'DT'], writes=['XDT'])
                    S.op('pool', lambda e: e.tensor_tensor(
                        out=WW[0:rows, :].rearrange("p (r q) -> p r q", q=64), in0=XDT[0:rows, :].rearrange("p (r q) -> p r q", q=64),
                        in1=DTE[0:rows, tt, hs].unsqueeze(2).to_broadcast([rows, 8, 64]), op=ALU.mult), reads=['XDT', 'DTE'], writes=['WW'])
                    S.op('pool', lambda e: e.tensor_tensor(
                        out=XDB[0:rows, :].rearrange("p (r q) -> p r q", q=64), in0=XTOK[0:rows, tt, :].rearrange("p (r q) -> p r q", q=64),
                        in1=DROW[0:rows, hs].unsqueeze(2).to_broadcast([rows, 8, 64]), op=ALU.mult), reads=['XTOK', 'DROW'], writes=['XDB'])
                    b = next_ps(held)
                    S.op('pe', lambda e, b=b: e.matmul(PS[b][0:rows, 0:rows], lhsT=BCT[:, 0, tk0:tk0 + rows], rhs=BCT[:, 1, tk0:tk0 + rows], start=True, stop=True),
                         reads=['BCT'], writes=[('ps', b)])
                    S.op('dve', lambda e, b=b: e.tensor_tensor(out=CBM[0:rows, 0:rows], in0=PS[b][0:rows, 0:rows], in1=um[0:rows, 0:rows], op=ALU.mult),
                         reads=[('ps', b), 'UT', 'USB'], writes=['CBM'])
                    for r4 in range(2):
                        b = next_ps(held)
                        for rr in range(4):
                            h = g * 8 + r4 * 4 + rr
                            S.op('pe', lambda e, b=b, rr=rr, h=h: e.matmul(
                                PS[b][0:rows, rr * 128:rr * 128 + rows], lhsT=DA[0:rows, tt, h:h + 1].to_broadcast([rows, rows]),
                                rhs=um[0:rows, 0:rows], start=True, stop=True), reads=['DA', 'UT', 'USB'], writes=[('ps', b)])
                        for rr in range(4):
                            r = r4 * 4 + rr
                            h = g * 8 + r
                            S.op('dve', lambda e, b=b, rr=rr, r=r, h=h: e.tensor_scalar(
                                out=EE[0:rows, r, 0:rows], in0=PS[b][0:rows, rr * 128:rr * 128 + rows], scalar1=ACS[0:rows, tt, h:h + 1], scalar2=0.0,
                                op0=ALU.subtract, op1=ALU.min), reads=[('ps', b), 'ACS'], writes=['EE'])
                    S.op('act', lambda e: e.activation(out=LT[0:rows, :, 0:rows], in_=EE[0:rows, :, 0:rows], func=AF.Exp), reads=['EE'], writes=['LT'])
                    S.op('dve', lambda e: e.tensor_tensor(out=MT[0:rows, :, 0:rows], in0=LT[0:rows, :, 0:rows],
                                                         in1=CBM[0:rows, 0:rows].unsqueeze(1).to_broadcast([rows, 8, rows]), op=ALU.mult),
                         reads=['LT', 'CBM'], writes=['MT'])
                    by = next_ps(held)
                    S.op('pe', lambda e: e.matmul(PS[by][0:rows, :], lhsT=IDB[0:rows, 0:rows], rhs=XDB[0:rows, :], start=True, stop=False),
                         reads=['IDB', 'XDB'], writes=[('ps', by)])
                    for r in range(8):
                        S.op('pe', lambda e, r=r: e.matmul(PS[by][0:rows, r * 64:(r + 1) * 64], lhsT=MT[0:rows, r, 0:rows], rhs=XDT[0:rows, r * 64:(r + 1) * 64],
                                                            start=False, stop=(r == 7)),
                             reads=['MT', 'XDT'], writes=[('ps', by)])
                    held.add(by)
                    bs3 = None
                    if not samp:
                        bs3 = next_ps(held)
                        S.op('pe', lambda e: e.matmul(PS[bs3][:, :], lhsT=BTOK[:, tt, :], rhs=WW[:, :], start=True, stop=True),
                             reads=['BTOK', 'WW'], writes=[('ps', bs3)])
                        held.add(bs3)
                    tinfo[tt] = (by, bs3)

                def back(tt, g=g, hs=hs):
                    rows = rows_of(tt)
                    samp = (tt == NCH)
                    tk0 = tt * 128
                    by, bs3 = tinfo[tt]
                    bo = next_ps(held)
                    if not samp:
                        S.op('pe', lambda e: e.matmul(PS[bo][0:rows, :], lhsT=BCT[:, 1, tk0:tk0 + rows], rhs=HB[:], start=True, stop=True),
                             reads=['BCT', 'HB'], writes=[('ps', bo)])
                        S.op('dve', lambda e: e.tensor_tensor(
                            out=HSTG[:].rearrange("p (r q) -> p r q", q=64), in0=HSTG[:].rearrange("p (r q) -> p r q", q=64),
                            in1=DEC[:, tt, hs].unsqueeze(2).to_broadcast([128, 8, 64]), op=ALU.mult), reads=['HSTG', 'DEC'], writes=['HSTG'])
                        S.op('dve', lambda e: e.tensor_tensor(out=HSTG[:], in0=HSTG[:], in1=PS[bs3][:, :], op=ALU.add),
                             reads=['HSTG', ('ps', bs3)], writes=['HSTG'])
                        held.discard(bs3)
                        S.op('act', lambda e: e.activation(out=HB[:], in_=HSTG[:], func=AF.Copy), reads=['HSTG'], writes=['HB'])
                    else:
                        held.add(bo)
                        S.op('dve', lambda e: e.tensor_tensor(out=CMS[:], in0=BCT[:, 1, tk0:tk0 + NS].unsqueeze(1).to_broadcast([128, SSQ, NS]), in1=MASK3[:], op=ALU.mult),
                             reads=['BCT', 'MASK3'], writes=['CMS'])
                        bd = next_ps(held)
                        for q4 in range(4):
                            h2 = g * 8 + q4 * 2
                            S.op('dve', lambda e, h2=h2: e.tensor_copy(
                                out=DAB[:].rearrange("p (h q) -> p h q", q=64), in_=DA[0:NS, tt, h2:h2 + 2].unsqueeze(2).to_broadcast([NS, 2, 64])),
                                reads=['DA'], writes=['DAB'])
                            S.op('pe', lambda e, q4=q4: e.matmul(
                                PS[bd][:, q4 * SSQ:(q4 + 1) * SSQ], lhsT=DAB[:], rhs=SEQM[:], start=True, stop=True),
                                reads=['DAB', 'SEQM'], writes=[('ps', bd)])
                        S.op('act', lambda e: e.activation(out=DECS[:].rearrange("p q b -> p (q b)"), in_=PS[bd][:, 0:4 * SSQ], func=AF.Exp),
                             reads=[('ps', bd)], writes=['DECS'])
                        h0bufs = {}
                        H0x = TMPF[1][:, 512:1024].rearrange("p (q n) -> p q n", n=128)

                        def issue_in(bq):
                            sq_ = hf * SSQ + bq
                            if bq < 3:
                                jh = rot('ev', 3)
                                H0v, hk_ = EV[jh][:, :].rearrange("p (q n) -> p q n", n=128), 'EV%d' % jh
                            else:
                                H0v, hk_ = H0x, 'H0'
                            src = st_ssm[sq_, g * 8:(g + 1) * 8].rearrange("(q h) p n -> (h p) q n", h=2)
                            S.dma('sp', H0v, src, writes=[hk_])
                            h0bufs[bq] = (H0v, hk_)

                        for bq in range(SSQ):
                            issue_in(bq)
                        for bq in range(SSQ):
                            H0v, hk_ = h0bufs[bq]
                            bt_ = next_ps(held)
                            for q4 in range(4):
                                S.op('pe', lambda e, bt_=bt_, q4=q4, H0v=H0v: e.transpose(out=PS[bt_][:, q4 * 128:(q4 + 1) * 128], in_=H0v[:, q4, :], identity=IDF[:]),
                                     reads=[hk_, 'IDF'], writes=[('ps', bt_)])
                            S.op('act', lambda e, bt_=bt_: e.activation(out=H0T[:], in_=PS[bt_][:], func=AF.Copy), reads=[('ps', bt_)], writes=['H0T'])
                            S.op('pe', lambda e, bq=bq: e.matmul(PS[bo][0:NS, :], lhsT=CMS[:, bq, :], rhs=H0T[:], start=(bq == 0), stop=(bq == SSQ - 1)),
                                 reads=['CMS', 'H0T'], writes=[('ps', bo)])
                        for bq in range(SSQ):
                            sq_ = hf * SSQ + bq
                            H0v, hk_ = h0bufs[bq]
                            S.op('dve', lambda e, bq=bq: e.tensor_scalar(out=WM[:], in0=WW[0:NS, :], scalar1=SEQM[:, bq:bq + 1], scalar2=None, op0=ALU.mult),
                                 reads=['WW', 'SEQM'], writes=['WM'])
                            bn = next_ps(held)
                            for q4 in range(4):
                                S.op('pe', lambda e, bn=bn, q4=q4: e.matmul(PS[bn][:, q4 * 128:(q4 + 1) * 128], lhsT=WM[:, q4 * 128:(q4 + 1) * 128], rhs=BTOK[0:NS, tt, :], start=True, stop=True),
                                     reads=['WM', 'BTOK'], writes=[('ps', bn)])
                            for q4 in range(4):
                                S.op('dve', lambda e, bn=bn, q4=q4, bq=bq, H0v=H0v: e.scalar_tensor_tensor(
                                    out=H0v[:, q4, :], in0=H0v[:, q4, :], scalar=DECS[:, q4, bq:bq + 1], in1=PS[bn][:, q4 * 128:(q4 + 1) * 128],
                                    op0=ALU.mult, op1=ALU.add), reads=[hk_, 'DECS', ('ps', bn)], writes=[hk_])
                            dst = ssm_s[sq_, g * 8:(g + 1) * 8].rearrange("(q h) p n -> (h p) q n", h=2)
                            S.dma('sp', dst, H0v, reads=[hk_])
                        held.discard(bo)
                    jy, jo = rot('ev', 3), rot('ev', 3)
                    Y, YO = EV[jy], EV[jo]
                    S.op('dve', lambda e: e.tensor_tensor(
                        out=YO[0:rows, :].rearrange("p (r q) -> p r q", q=64), in0=PS[bo][0:rows, :].rearrange("p (r q) -> p r q", q=64),
                        in1=EXPA[0:rows, tt, hs].unsqueeze(2).to_broadcast([rows, 8, 64]), op=ALU.mult), reads=[('ps', bo), 'EXPA'], writes=['EV%d' % jo])
                    S.op('dve', lambda e: e.tensor_tensor(out=Y[0:rows, :], in0=PS[by][0:rows, :], in1=YO[0:rows, :], op=ALU.add),
                         reads=[('ps', by), 'EV%d' % jo], writes=['EV%d' % jy])
                    held.discard(by)
                    S.op('dve', lambda e: e.tensor_tensor(out=Y[0:rows, :], in0=Y[0:rows, :], in1=SZ[0:rows, tt, :], op=ALU.mult),
                         reads=['SZ', 'EV%d' % jy], writes=['EV%d' % jy])
                    S.op('act', lambda e: e.activation(out=YO[0:rows, :], in_=Y[0:rows, :], func=AF.Square, accum_out=SMALL[0:rows, 32:33]),
                         reads=['EV%d' % jy], writes=['EV%d' % jo, ('SM', 32)])
                    S.op('dve', lambda e: e.tensor_scalar(out=SMALL[0:rows, 32:33], in0=SMALL[0:rows, 32:33], scalar1=1.0 / 512, scalar2=NORM_EPS, op0=ALU.mult, op1=ALU.add),
                         reads=[('SM', 32)], writes=[('SM', 32)])
                    S.op('act', lambda e: e.activation(out=SMALL[0:rows, 32:33], in_=SMALL[0:rows, 32:33], func=AF.Ln), reads=[('SM', 32)], writes=[('SM', 32)])
                    S.op('act', lambda e: e.activation(out=SMALL[0:rows, 32:33], in_=SMALL[0:rows, 32:33], func=AF.Exp, scale=-0.5), reads=[('SM', 32)], writes=[('SM', 32)])
                    S.op('dve', lambda e: e.scalar_tensor_tensor(out=Y[0:rows, :], in0=Y[0:rows, :], scalar=SMALL[0:rows, 32:33], in1=NWR[0:rows, :], op0=ALU.mult, op1=ALU.mult),
                         reads=['EV%d' % jy, ('SM', 32), 'NWR'], writes=['EV%d' % jy])
                    bt2 = next_ps(held)
                    for kk in range(4):
                        S.op('pe', lambda e, kk=kk: e.transpose(out=PS[bt2][:, kk * 128:kk * 128 + rows], in_=Y[0:rows, kk * 128:(kk + 1) * 128], identity=IDF[0:rows, 0:rows]),
                             reads=['EV%d' % jy, 'IDF'], writes=[('ps', bt2)])
                    S.op('act', lambda e: e.activation(
                        out=GB[:, :, tk0:tk0 + rows], in_=PS[bt2][:].rearrange("p (k t) -> p k t", t=128)[:, :, 0:rows], func=AF.Copy),
                        reads=[('ps', bt2)], writes=['GB'])

                front(0)
                for tt in range(NTILE):
                    if tt + 1 < NTILE:
                        front(tt + 1)
                    back(tt)
                LIVE[0] = set()
                if not last:
                    S.dma('sp', hst_d[g], HSTG[:], reads=['HSTG'], writes=[('hst_d', g)])
                else:
                    for q4 in range(4):
                        b = next_ps()
                        S.op('pe', lambda e, b=b, q4=q4: e.transpose(out=PS[b][:, 0:128], in_=HSTG[:, q4 * 128:(q4 + 1) * 128], identity=IDF[:]),
                             reads=['HSTG', 'IDF'], writes=[('ps', b)])
                        j = rot('sm12', 2)
                        S.op('dve', lambda e, b=b, j=j: e.tensor_copy(out=SM12[j][:, :], in_=PS[b][:, 0:128]), reads=[('ps', b)], writes=['SM12_%d' % j])
                        r0 = (g * 8 + q4 * 2) * 64
                        S.dma('sp', ssm_p[r0:r0 + 128, :], SM12[j][:, :], reads=['SM12_%d' % j])
                out_proj_partial(hf, 1, 32, b_w_out, g * 512)
            def store_rows(src3, nrow, dst2, rk):
                for c4 in range(12):
                    b = next_ps()
                    for cc in range(4):
                        S.op('pe', lambda e, b=b, cc=cc, c4=c4: e.transpose(out=PS[b][0:nrow, cc * 128:(cc + 1) * 128], in_=src3[:, c4 * 4 + cc, :], identity=IDF[:]),
                             reads=[rk, 'IDF'], writes=[('ps', b)])
                    j = rot('ev', 3)
                    S.op('dve', lambda e, b=b, j=j: e.tensor_copy(out=EV[j][0:nrow, :], in_=PS[b][0:nrow, :]), reads=[('ps', b)], writes=['EV%d' % j])
                    S.dma('sp', dst2[:, c4 * 512:(c4 + 1) * 512], EV[j][0:nrow, :], reads=['EV%d' % j])
            store_rows(CSO, SSQ * 3, conv_s[hf * SSQ * 3:(hf + 1) * SSQ * 3, :], 'CSO')
            if last:
                store_rows(CTAIL, 3, conv_p, 'CTAIL')

        def ffn(hf, l):
            for blk in range(11):
                for half in range(2):
                    tg, _ = load_w(f_w_in[l], 0, KC, blk * 512 + half * 256, ncols=256)
                    tu, _ = load_w(f_w_in[l], 0, KC, FFN_H + blk * 512 + half * 256, ncols=256)
                    for mi2 in range(2):
                        mi = half * 2 + mi2
                        for (t0, t1) in TBS:
                            n = t1 - t0
                            bg, bu = next_ps(), next_ps()
                            for k in range(KC):
                                S.op('pe', lambda e, b=bg, k=k, mi2=mi2, tl=tg, t0=t0, t1=t1: e.matmul(
                                    PS[b][:, 0:t1 - t0], lhsT=tl.c(k, mi2), rhs=HT[:, k, t0:t1],
                                    start=(k == 0), stop=(k == KC - 1)),
                                    reads=[tg.key(mi2), ('HT', k)], writes=[('ps', bg)])
                            for k in range(KC):
                                S.op('pe', lambda e, b=bu, k=k, mi2=mi2, tl=tu, t0=t0, t1=t1: e.matmul(
                                    PS[b][:, 0:t1 - t0], lhsT=tl.c(k, mi2), rhs=HT[:, k, t0:t1],
                                    start=(k == 0), stop=(k == KC - 1)),
                                    reads=[tu.key(mi2), ('HT', k)], writes=[('ps', bu)])
                            i = rot('ev', 3)
                            S.op('act', lambda e, b=bg, i=i, n=n: e.activation(out=EV[i][:, 0:n], in_=PS[b][:, 0:n], func=AF.Silu),
                                 reads=[('ps', bg)], writes=['EV%d' % i])
                            S.op('dve', lambda e, b=bu, i=i, n=n, mi=mi, t0=t0, t1=t1: e.tensor_tensor(
                                out=GB[:, mi, t0:t1], in0=PS[b][:, 0:n], in1=EV[i][:, 0:n], op=ALU.mult),
                                reads=[('ps', bu), 'EV%d' % i], writes=['GB'])
                out_proj_partial(hf, l, 80, f_w_out[l], blk * 512)

        def final_out(hf):
            compute_rstd(NORM_EPS)
            for tt in range(NTILE):
                rows = 128 if tt < NCH else NS
                i = rot('tmpf', 2)
                t, tk = TMPF[i], 'TMPF%d' % i
                for k4 in range(4):
                    j = rot('ev', 3)
                    S.op('dve', lambda e, j=j, k4=k4, tt=tt, rows=rows: e.tensor_tensor(
                        out=EV[j][:, :].rearrange("p (k t) -> p k t", t=128)[:, :, 0:rows],
                        in0=XT[:, k4 * 4:(k4 + 1) * 4, tt * 128:tt * 128 + rows],
                        in1=RSTD[:, tt * 128:tt * 128 + rows].unsqueeze(1).to_broadcast([128, 4, rows]), op=ALU.mult),
                        reads=[kx for kk_ in range(4) for kx in xk(k4 * 4 + kk_)] + ['RSTD'], writes=['EV%d' % j])
                    b = next_ps()
                    for kk in range(4):
                        k = k4 * 4 + kk
                        S.op('act', lambda e, j=j, kk=kk, k=k, rows=rows: e.activation(
                            out=EV[j][:, kk * 128:kk * 128 + rows], in_=EV[j][:, kk * 128:kk * 128 + rows],
                            func=AF.Copy, scale=col('fnw', k)),
                            reads=['EV%d' % j, 'COLS'], writes=['EV%d' % j])
                        S.op('pe', lambda e, b=b, j=j, kk=kk, rows=rows: e.transpose(
                            out=PS[b][0:rows, kk * 128:(kk + 1) * 128], in_=EV[j][:, kk * 128:kk * 128 + rows], identity=IDF[:]),
                            reads=['EV%d' % j, 'IDF'], writes=[('ps', b)])
                    S.op('dve', lambda e, b=b, k4=k4, rows=rows, t=t: e.tensor_copy(out=t[0:rows, k4 * 512:(k4 + 1) * 512], in_=PS[b][0:rows, :]),
                         reads=[('ps', b)], writes=[tk])
                if tt < NCH:
                    S.dma('sp', y_p[hf * NPT + tt * 128: hf * NPT + (tt + 1) * 128, :], t[:, :], reads=[tk])
                else:
                    S.dma('sp', y_s[hf * NS:(hf + 1) * NS, :], t[0:NS, :], reads=[tk])

        for hf in range(NPASS):
            load_x(hf)
            norm_mod(hf, 0, 0, 'nmw0')
            gmlp(hf)
            norm_mod(hf, 0, 1, 'nfw0')
            ffn(hf, 0)
            if STAGE >= 2:
                norm_mod(hf, 1, 0, 'nmw1')
                S.barrier()
                ssd(hf)
                S.barrier()
                norm_mod(hf, 1, 1, 'nfw1')
                ffn(hf, 1)
            final_out(hf)
            S.barrier()

        S.emit()
    return nc, S


_CACHE = {}


def _get_program():
    if 'nc' not in _CACHE:
        _CACHE['nc'] = build_program()
    return _CACHE['nc']


def kernel(x_prompt, x_sample, c_prompt, c_sample, state_ssm, state_conv,
           mod_w, mod_b, norm_mix_w, norm_ffn_w,
           a_w_in, a_b_in, a_ln_w, a_ln_b, a_w_s, a_b_s, a_w_out,
           b_w_in, b_conv_w, b_conv_b, b_dt_bias, b_a_log, b_d, b_norm_w, b_w_out,
           f_w_in, f_w_out, final_norm_w):
    f = lambda a: np.ascontiguousarray(np.asarray(a, dtype=np.float32))
    nc, S = _get_program()
    vecs = np.zeros((VEC_TOT, 128), np.float32)

    def put(name, arr):
        r0, n = VEC_LAY[name]
        vecs[r0:r0 + n] = np.asarray(arr, np.float32).reshape(n, 128)

    put('mod_b0', mod_b[0]); put('mod_b1', mod_b[1])
    put('nmw0', norm_mix_w[0]); put('nmw1', norm_mix_w[1])
    put('nfw0', norm_ffn_w[0]); put('nfw1', norm_ffn_w[1])
    put('a_b_in', a_b_in[0]); put('fnw', final_norm_w)
    put('conv_w', b_conv_w[0]); put('conv_b', b_conv_b[0])
    shared = dict(vecs=vecs, mod_w=f(mod_w), a_w_in=f(a_w_in[0]), a_ln_w=f(a_ln_w[0]), a_ln_b=f(a_ln_b[0]),
                  a_b_in_r=f(a_b_in[0]), a_w_s=f(a_w_s[0]), a_b_s=f(a_b_s[0]), a_w_out=f(a_w_out[0]),
                  f_w_in=f(f_w_in), f_w_out=f(f_w_out))
    shared.update(b_w_in=f(b_w_in[0]), b_w_out=f(b_w_out[0]), b_dtb=f(b_dt_bias[0]), b_alog=f(b_a_log[0]),
                  b_dsk=f(b_d[0]), b_nw=f(b_norm_w[0]))
    in_maps = []
    for c in range(8):
        m = dict(shared)
        m['st_ssm'] = f(np.asarray(state_ssm)[0, 16 * c:16 * (c + 1)])
        m['st_conv'] = f(np.asarray(state_conv)[0, 16 * c:16 * (c + 1)].reshape(48, 6144))
        m['xp'] = f(x_prompt[c % 4])
        m['xs'] = f(np.asarray(x_sample)[16 * c:16 * (c + 1)].reshape(128, D))
        m['call'] = f(np.concatenate([np.asarray(c_prompt)[c % 4][None], np.asarray(c_sample)[16 * c:16 * (c + 1)]], 0))
        in_maps.append(m)
    res = run_bass_kernel_spmd(nc, in_maps, core_ids=list(range(8)))
    R = res.results
    y_prompt = np.stack([R[c]['y_p'] for c in range(4)], 0)
    y_sample = np.concatenate([R[c]['y_s'].reshape(16, 8, D) for c in range(8)], 0)
    v_prompt = np.stack([R[c]['v_p'] for c in range(4)], 0)[None]
    v_sample = np.concatenate([R[c]['v_s'].reshape(16, 8, D) for c in range(8)], 0)[None]
    ssm_prompt = np.stack([R[c]['ssm_p'].reshape(64, 64, 128) for c in range(4)], 0)[None]
    ssm_sample = np.concatenate([R[c]['ssm_s'] for c in range(8)], 0)[None]
    conv_prompt = np.stack([R[c]['conv_p'] for c in range(4)], 0)[None]
    conv_sample = np.concatenate([R[c]['conv_s'].reshape(16, 3, 6144) for c in range(8)], 0)[None]
    return (y_prompt, y_sample, v_prompt, v_sample, ssm_prompt, ssm_sample, conv_prompt, conv_sample)
```
